# Optimizing a Trainium2 kernel written in Bass

```python
import jax
import jax.numpy as jnp
from jax import lax
import numpy as np

D_MODEL = 1024
BATCH = 8
SEQ = 8192
DEPTH = 2

RW_HEADS = 8
RW_HEAD = 64
RW = RW_HEADS * RW_HEAD
RW_LORA_W = 64
RW_LORA_A = 64
RW_LORA_G = 128
RW_LORA_V = 32
RW_GN_EPS = 64e-5
CV = 512
CONV_WIDTH = 31
POOL_WINDOWS = (2, 4, 8, 16)
N_POOL = len(POOL_WINDOWS)
PW = 512
PG = PW // N_POOL
N_BRANCH = 3
D_FF = 4 * D_MODEL
N_ADA = 6
LN_EPS = 1e-5
ALPHA = (2 * DEPTH) ** 0.25
BETA = (8 * DEPTH) ** -0.25

RW_SHIFT = 3 * RW + RW_LORA_W + RW_LORA_A + RW_LORA_G
CV_OFF = RW_SHIFT
PL_OFF = CV_OFF + 2 * CV
GT_OFF = PL_OFF + PW
C_MAIN = GT_OFF + N_BRANCH * D_MODEL

kernel_name = "hybrid_rwkv7_conformer_pool_deepnorm_block"


def _layernorm(x, g, b, eps=LN_EPS):
    xf = x.astype(jnp.float32)
    mu = xf.mean(-1, keepdims=True)
    var = jnp.square(xf - mu).mean(-1, keepdims=True)
    y = (xf - mu) * lax.rsqrt(var + eps)
    return (y * g.astype(jnp.float32) + b.astype(jnp.float32)).astype(x.dtype)


def _token_shift(z, mu):
    prev = jnp.pad(z[:, :-1], ((0, 0), (1, 0), (0, 0)))
    return z + (prev - z) * mu


def _rwkv7_recurrence(r, w, k, v, a, b):
    Bn, S, H, N = r.shape

    def step(state, inp):
        r_t, w_t, k_t, v_t, a_t, b_t = inp
        sa = jnp.einsum('bhvk,bhk->bhv', state, a_t)
        state = (state * w_t[:, :, None, :] + sa[..., None] * b_t[:, :, None, :]
                 + v_t[..., None] * k_t[:, :, None, :])
        y_t = jnp.einsum('bhvk,bhk->bhv', state, r_t)
        return state, y_t

    s0 = jnp.zeros((Bn, H, N, N), jnp.float32)
    xs = tuple(jnp.moveaxis(t, 1, 0) for t in (r, w, k, v, a, b))
    _, y = lax.scan(step, s0, xs)
    return jnp.moveaxis(y, 0, 1)


def _rwkv7_branch(z, lv, v_first, w0, w2, a0, a2, g2, v0, v2, k_k, k_a, r_k, gn_g, gn_b, w_o):
    Bn, S, _ = z.shape
    f32 = jnp.float32
    r, k, v, lw, la, lg = jnp.split(
        z, [RW, 2 * RW, 3 * RW, 3 * RW + RW_LORA_W, 3 * RW + RW_LORA_W + RW_LORA_A], axis=-1)
    w_log = -jax.nn.softplus(-(w0 + jnp.tanh(lw) @ w2).astype(f32)) - 0.5
    decay = jnp.exp(-jnp.exp(w_log))
    a = jax.nn.sigmoid(a0 + la @ a2)
    g = jax.nn.sigmoid(lg) @ g2
    if v_first is None:
        v_first = v
    else:
        v = v + (v_first - v) * jax.nn.sigmoid(v0 + lv @ v2)

    def heads(t):
        return t.astype(f32).reshape(Bn, S, RW_HEADS, RW_HEAD)

    kk = heads(k * k_k)
    kk = kk / jnp.maximum(jnp.sqrt(jnp.sum(kk * kk, -1, keepdims=True)), 1e-12)
    a_h = heads(a)
    k_h = heads(k * (1 + (a - 1) * k_a))
    r_h = heads(r)
    v_h = heads(v)
    y = _rwkv7_recurrence(r_h, decay.reshape(Bn, S, RW_HEADS, RW_HEAD), k_h, v_h, -kk, kk * a_h)
    mu = y.mean(-1, keepdims=True)
    var = jnp.square(y - mu).mean(-1, keepdims=True)
    y = (y - mu) * lax.rsqrt(var + RW_GN_EPS)
    y = y.reshape(Bn, S, RW) * gn_g.astype(f32) + gn_b.astype(f32)
    bonus = jnp.sum(r_h * k_h * r_k.astype(f32), -1, keepdims=True) * v_h
    y = (y + bonus.reshape(Bn, S, RW)).astype(z.dtype) * g
    return y @ w_o, v_first


def _conv_branch(u, conv_w, conv_b, ln_g, ln_b, w_o):
    val, gate = jnp.split(u, 2, axis=-1)
    h = val * jax.nn.sigmoid(gate)
    h = lax.conv_general_dilated(
        h, conv_w[:, None, :], window_strides=(1,),
        padding=((CONV_WIDTH - 1, 0),),
        dimension_numbers=('NWC', 'WIO', 'NWC'),
        feature_group_count=CV) + conv_b
    h = jax.nn.silu(_layernorm(h, ln_g, ln_b))
    return h @ w_o


def _pool_branch(p, lin_w, scale, w_o):
    Bn, S, _ = p.shape
    pf = p.astype(jnp.float32)
    cs = jnp.cumsum(pf, axis=1)
    t1 = jnp.arange(1, S + 1, dtype=jnp.float32)[None, :, None]
    outs = []
    for gi, win in enumerate(POOL_WINDOWS):
        sl = slice(gi * PG, (gi + 1) * PG)
        csg = cs[..., sl]
        lower = jnp.pad(csg[:, :S - win], ((0, 0), (win, 0), (0, 0)))
        mean = (csg - lower) / jnp.minimum(t1, float(win))
        outs.append(mean - pf[..., sl])
    pooled = jnp.stack(outs, axis=2).astype(p.dtype)
    mixed = jnp.einsum('bsgc,gcd->bsgd', pooled, lin_w).reshape(Bn, S, PW)
    return (mixed * scale) @ w_o


def setup_inputs(seed: int = 0) -> dict:
    key = jax.random.key(seed)
    ks = iter(jax.random.split(key, 40))
    f32 = jnp.float32

    def nrm(shape, s):
        return jax.random.normal(next(ks), shape, f32) * s

    def unif(shape, lo, hi):
        return jax.random.uniform(next(ks), shape, f32, lo, hi)

    L, D, Lv = DEPTH, D_MODEL, DEPTH - 1
    return {
        "x": nrm((BATCH, SEQ, D), 1.0),
        "c": nrm((BATCH, D), 1.0),
        "ada_w": nrm((L, D, N_ADA * D), 0.5 * D ** -0.5),
        "ada_b": nrm((L, N_ADA * D), 0.02),
        "w_in": nrm((L, D, C_MAIN), D ** -0.5),
        "w_in_vres": nrm((Lv, D, RW_LORA_V), D ** -0.5),
        "shift_mu": unif((L, RW_SHIFT), 0.0, 1.0),
        "shift_mu_vres": unif((Lv, RW_LORA_V), 0.0, 1.0),
        "rw_w0": unif((L, RW), -5.0, -1.0),
        "rw_w2": nrm((L, RW_LORA_W, RW), 0.5 * RW_LORA_W ** -0.5),
        "rw_a0": nrm((L, RW), 0.1),
        "rw_a2": nrm((L, RW_LORA_A, RW), 0.5 * RW_LORA_A ** -0.5),
        "rw_g2": nrm((L, RW_LORA_G, RW), RW_LORA_G ** -0.5),
        "rw_v0": 1.0 + nrm((Lv, RW), 0.1),
        "rw_v2": nrm((Lv, RW_LORA_V, RW), 0.5 * RW_LORA_V ** -0.5),
        "rw_kk": 0.85 + nrm((L, RW), 0.05),
        "rw_ka": 1.0 + nrm((L, RW), 0.05),
        "rw_rk": nrm((L, RW_HEADS, RW_HEAD), 0.1),
        "rw_gn_g": 1.0 + nrm((L, RW), 0.05),
        "rw_gn_b": nrm((L, RW), 0.02),
        "rw_wo": nrm((L, RW, D), BETA * RW ** -0.5),
        "cv_w": nrm((L, CONV_WIDTH, CV), CONV_WIDTH ** -0.5),
        "cv_b": nrm((L, CV), 0.02),
        "cv_ln_g": 1.0 + nrm((L, CV), 0.05),
        "cv_ln_b": nrm((L, CV), 0.02),
        "cv_wo": nrm((L, CV, D), BETA * CV ** -0.5),
        "pl_w": nrm((L, N_POOL, PG, PG), PG ** -0.5),
        "pl_scale": 1.0 + nrm((L, PW), 0.1),
        "pl_wo": nrm((L, PW, D), BETA * PW ** -0.5),
        "w_out": nrm((L, D, D), BETA * D ** -0.5),
        "ln_m_g": 1.0 + nrm((L, D), 0.05),
        "ln_m_b": nrm((L, D), 0.02),
        "mlp_w1": nrm((L, D, D_FF), BETA * D ** -0.5),
        "mlp_w2": nrm((L, D_FF, D), BETA * D_FF ** -0.5),
        "ln_f_g": 1.0 + nrm((L, D), 0.05),
        "ln_f_b": nrm((L, D), 0.02),
    }


def reference(x, c, ada_w, ada_b, w_in, w_in_vres, shift_mu, shift_mu_vres,
              rw_w0, rw_w2, rw_a0, rw_a2, rw_g2, rw_v0, rw_v2, rw_kk, rw_ka, rw_rk,
              rw_gn_g, rw_gn_b, rw_wo, cv_w, cv_b, cv_ln_g, cv_ln_b, cv_wo,
              pl_w, pl_scale, pl_wo, w_out, ln_m_g, ln_m_b, mlp_w1, mlp_w2,
              ln_f_g, ln_f_b):
    Bn, S, D = x.shape
    cond = jax.nn.silu(c)
    v_first = None
    for l in range(DEPTH):
        ada = (cond @ ada_w[l] + ada_b[l])[:, None, :]
        sh_m, sc_m, gt_m, sh_f, sc_f, gt_f = jnp.split(ada, N_ADA, axis=-1)

        h = x * (1 + sc_m) + sh_m
        if l == 0:
            proj = h @ w_in[l]
            lv = None
        else:
            proj = h @ jnp.concatenate([w_in[l], w_in_vres[l - 1]], axis=1)
            lv = _token_shift(proj[..., C_MAIN:], shift_mu_vres[l - 1])
        z = _token_shift(proj[..., :RW_SHIFT], shift_mu[l])
        u = proj[..., CV_OFF:PL_OFF]
        p = proj[..., PL_OFF:GT_OFF]
        gates = jax.nn.sigmoid(proj[..., GT_OFF:C_MAIN]).reshape(Bn, S, N_BRANCH, D)

        y_rw, v_first = _rwkv7_branch(
            z, lv, v_first, rw_w0[l], rw_w2[l], rw_a0[l], rw_a2[l], rw_g2[l],
            rw_v0[l - 1] if l > 0 else None, rw_v2[l - 1] if l > 0 else None,
            rw_kk[l], rw_ka[l], rw_rk[l], rw_gn_g[l], rw_gn_b[l], rw_wo[l])
        y_cv = _conv_branch(u, cv_w[l], cv_b[l], cv_ln_g[l], cv_ln_b[l], cv_wo[l])
        y_pl = _pool_branch(p, pl_w[l], pl_scale[l], pl_wo[l])
        merged = gates[:, :, 0] * y_rw + gates[:, :, 1] * y_cv + gates[:, :, 2] * y_pl
        x = _layernorm(ALPHA * x + gt_m * (merged @ w_out[l]), ln_m_g[l], ln_m_b[l])

        h = x * (1 + sc_f) + sh_f
        y_ff = jnp.square(jax.nn.relu(h @ mlp_w1[l])) @ mlp_w2[l]
        x = _layernorm(ALPHA * x + gt_f * y_ff, ln_f_g[l], ln_f_b[l])
    return x
```

```python
import types
import numpy as np
from contextlib import ExitStack
import concourse.bass as bass
import concourse.mybir as mybir
from concourse.bass_utils import run_bass_kernel_spmd

F32 = mybir.dt.float32
BF16 = mybir.dt.bfloat16
I32 = mybir.dt.int32
ALU = mybir.AluOpType
AF = mybir.ActivationFunctionType

D = 1024
RW = 512
C_MAIN = 6400
NG = 37
GW = 4096
ALPHA = 4.0 ** 0.25
CDEC = float(np.exp(-0.5))
LN_EPS = 1e-5
GN_EPS = 64e-5
CONVW = 31
ENG_NAMES = ("pe", "act", "dve", "pool", "sp")


class Sched:
    def __init__(self, nc, sems, dma_sems):
        self.nc = nc
        self.sem = dict(zip(ENG_NAMES, sems))
        self.cnt = {e: 0 for e in ENG_NAMES}
        self.dma_pool = {q: list(v) for q, v in dma_sems.items()}
        self.dma_sems = [s for q in self.dma_pool for s in self.dma_pool[q]]
        self.dma_idx = {}
        i = 0
        for q in self.dma_pool:
            self.dma_idx[q] = list(range(i, i + len(self.dma_pool[q])))
            i += len(self.dma_pool[q])
        self.dma_cnt = [0] * len(self.dma_sems)
        self.dma_rr = {q: 0 for q in self.dma_pool}
        self.streams = {e: [] for e in ENG_NAMES}
        self.seen = {e: {} for e in ENG_NAMES}
        self.last_w = {}
        self.readers = {}
        self.nops = 0
        self.fence_dma = False

    def _deps(self, reads, writes):
        toks = []
        for k in reads:
            t = self.last_w.get(k)
            if t is not None:
                toks.append(t)
        for k in writes:
            t = self.last_w.get(k)
            if t is not None:
                toks.append(t)
            toks.extend(self.readers.get(k, ()))
        return toks

    def _waits_for(self, e, toks, skip_same_pe=True):
        need = {}
        for (sk, v) in toks:
            if sk == e and e == "pe" and skip_same_pe:
                continue
            if self.seen[e].get(sk, 0) >= v:
                continue
            if need.get(sk, 0) < v:
                need[sk] = v
        for sk, v in need.items():
            self.seen[e][sk] = v
        return list(need.items())

    def _commit(self, tok, reads, writes):
        for k in reads:
            self.readers.setdefault(k, []).append(tok)
        for k in writes:
            self.last_w[k] = tok
            self.readers[k] = []

    @staticmethod
    def _freeze(fn):
        if getattr(fn, "__closure__", None) is None:
            return fn
        cells = []
        for c in fn.__closure__:
            try:
                v = c.cell_contents
                if isinstance(v, types.FunctionType):
                    v = Sched._freeze(v)
                cells.append(types.CellType(v))
            except ValueError:
                cells.append(c)
        g = types.FunctionType(fn.__code__, fn.__globals__, fn.__name__, fn.__defaults__, tuple(cells))
        g.__kwdefaults__ = fn.__kwdefaults__
        return g

    def op(self, e, fn, reads=(), writes=()):
        fn = self._freeze(fn)
        reads = list(reads); writes = list(writes)
        toks = self._deps(reads, writes)
        waits = self._waits_for(e, toks)
        self.cnt[e] += 1
        tok = (e, self.cnt[e])
        self.streams[e].append((waits, fn, ("eng", e)))
        self._commit(tok, reads, writes)
        self.nops += 1
        return tok

    def dma(self, q, out, in_, reads=(), writes=(), **kw):
        toks = self._deps(reads, writes)
        i = self.dma_idx[q][self.dma_rr[q]]
        self.dma_rr[q] = (self.dma_rr[q] + 1) % len(self.dma_idx[q])
        sk = ("d", i)
        if self.dma_cnt[i] > 0:
            toks.append((sk, self.dma_cnt[i]))
        waits = self._waits_for(q, toks)
        self.dma_cnt[i] += 16
        tok = (sk, self.dma_cnt[i])

        def fn(eng, out=out, in_=in_, kw=kw):
            return eng.dma_start(out=out, in_=in_, **kw)
        self.streams[q].append((waits, fn, ("dma", i)))
        self._commit(tok, reads, writes)
        return tok

    def fence(self, engines=("pe", "act", "dve", "pool")):
        for e in engines:
            toks = [(en, self.cnt[en]) for en in engines if self.cnt[en] > 0]
            if self.fence_dma:
                toks += [(("d", i), v) for i, v in enumerate(self.dma_cnt) if v > 0]
            waits = self._waits_for(e, toks, skip_same_pe=False)
            if waits:
                self.streams[e].append((waits, None, None))

    def wait_all(self, e):
        toks = [(en, self.cnt[en]) for en in ENG_NAMES if self.cnt[en] > 0 and en != e]
        toks += [(("d", i), v) for i, v in enumerate(self.dma_cnt) if v > 0]
        waits = self._waits_for(e, toks)
        self.streams[e].append((waits, None, None))

    def _semh(self, sk):
        if isinstance(sk, tuple):
            return self.dma_sems[sk[1]]
        return self.sem[sk]

    def emit(self, block):
        eng_of = {"pe": self.nc.tensor, "act": self.nc.scalar, "dve": self.nc.vector,
                  "pool": self.nc.gpsimd, "sp": self.nc.sync}

        def mk(e):
            def body(eng):
                for waits, fn, sig in self.streams[e]:
                    for sk, v in waits:
                        eng.wait_ge(self._semh(sk), v)
                    if fn is None:
                        continue
                    ins = fn(eng)
                    if sig[0] == "eng":
                        ins.then_inc(self.sem[e], 1)
                    else:
                        ins.then_inc(self.dma_sems[sig[1]], 16)
            return body
        block.tensor(mk("pe"))
        block.scalar(mk("act"))
        block.vector(mk("dve"))
        block.gpsimd(mk("pool"))
        block.sync(mk("sp"))


def vec_layout():
    off = {}
    r = 0

    def add(name, n):
        nonlocal r
        off[name] = r
        r += n
    add("c", 8)
    for l in range(2):
        for nm, n in (("ada_b", 48), ("mu", 14), ("w0", 4), ("a0", 4), ("kk", 4), ("ka", 4), ("rk", 4),
                      ("gng", 4), ("gnb", 4), ("cvb", 4), ("cvg", 4), ("cvbb", 4), ("plsc", 4),
                      ("lnmg", 8), ("lnmb", 8), ("lnfg", 8), ("lnfb", 8), ("cvw", 124)):
            add(f"{nm}{l}", n)
    add("v0", 4)
    add("muv", 1)
    return off, r


VOFF, NVROWS = vec_layout()
NVB = (NVROWS + 127) // 128


def pack_vecs(inp, b):
    P = np.zeros((NVB * 128, 128), np.float32)

    def put(name, arr):
        a = np.asarray(arr, np.float32).reshape(-1)
        n = (a.size + 127) // 128
        buf = np.zeros(n * 128, np.float32)
        buf[:a.size] = a
        P[VOFF[name]:VOFF[name] + n] = buf.reshape(n, 128)
    put("c", inp["c"][b])
    for l in range(2):
        put(f"ada_b{l}", inp["ada_b"][l]); put(f"mu{l}", inp["shift_mu"][l])
        put(f"w0{l}", inp["rw_w0"][l]); put(f"a0{l}", inp["rw_a0"][l])
        put(f"kk{l}", inp["rw_kk"][l]); put(f"ka{l}", inp["rw_ka"][l]); put(f"rk{l}", inp["rw_rk"][l])
        put(f"gng{l}", inp["rw_gn_g"][l]); put(f"gnb{l}", inp["rw_gn_b"][l])
        put(f"cvb{l}", inp["cv_b"][l]); put(f"cvg{l}", inp["cv_ln_g"][l]); put(f"cvbb{l}", inp["cv_ln_b"][l])
        put(f"plsc{l}", inp["pl_scale"][l])
        put(f"lnmg{l}", inp["ln_m_g"][l]); put(f"lnmb{l}", inp["ln_m_b"][l])
        put(f"lnfg{l}", inp["ln_f_g"][l]); put(f"lnfb{l}", inp["ln_f_b"][l])
        put(f"cvw{l}", inp["cv_w"][l])
    put("v0", inp["rw_v0"][0])
    put("muv", inp["shift_mu_vres"][0])
    return P


WEIGHT_NAMES = ("ada_w", "w_in", "w_in_vres", "rw_w2", "rw_a2", "rw_g2", "rw_v2", "rw_wo", "cv_wo",
                "pl_wo", "pl_w", "w_out", "mlp_w1", "mlp_w2")
WEIGHT_SHAPES = {"ada_w": [2, 1024, 6144], "w_in": [2, 1024, 6400], "w_in_vres": [1, 1024, 32],
                 "rw_w2": [2, 64, 512], "rw_a2": [2, 64, 512], "rw_g2": [2, 128, 512], "rw_v2": [1, 32, 512],
                 "rw_wo": [2, 512, 1024], "cv_wo": [2, 512, 1024], "pl_wo": [2, 512, 1024],
                 "pl_w": [2, 4, 128, 128], "w_out": [2, 1024, 1024], "mlp_w1": [2, 1024, 4096],
                 "mlp_w2": [2, 4096, 1024]}


def build_nc(S_TOK, T=256, NL=2, debug=None):
    assert S_TOK % T == 0 and T % 128 == 0 and T <= 512
    NT = S_TOK // T
    NCH = T // 64
    TB = T // 128
    nc = bass.Bass("TRN2", target_bir_lowering=False)
    dr = {}
    dr["x"] = nc.dram_tensor("x", [S_TOK, D], F32, kind="ExternalInput").ap()
    dr["vecs"] = nc.dram_tensor("vecs", [NVB * 128, 128], F32, kind="ExternalInput").ap()
    for nm in WEIGHT_NAMES:
        dr[nm] = nc.dram_tensor(nm, WEIGHT_SHAPES[nm], F32, kind="ExternalInput").ap()
    out = nc.dram_tensor("out", [S_TOK, D], F32, kind="ExternalOutput").ap()
    wsc = nc.dram_tensor("wsc", [2, NG, 128, GW], BF16, kind="Internal").ap()
    dbg = {}
    if debug:
        for nm, shp in debug.items():
            dbg[nm] = nc.dram_tensor(nm, list(shp), F32, kind="ExternalOutput").ap()

    with ExitStack() as es:
        def sb(name, shape, dt=F32):
            return es.enter_context(nc.sbuf_tensor(name, list(shape), dt))

        def pst(name, shape, dt=F32):
            return es.enter_context(nc.psum_tensor(name, list(shape), dt))

        X = sb("X", [128, 8, T])
        H = sb("H", [128, 8, T], BF16)
        WS = sb("WS", [128, 4, GW], BF16)
        VF = sb("VF", [128, 4, T])
        VC = sb("VC", [128, NVB * 128])
        PK = sb("PK", [128, NVB, 128])
        ADA = sb("ADA", [128, 2, 48])
        DC = sb("DC", [128, 2, 64])
        IDENT = sb("IDENT", [128, 128])
        I2 = sb("I2", [128, 64])
        MSU = sb("MSU", [128, 128]); MIU = sb("MIU", [128, 128]); MSL = sb("MSL", [128, 128])
        OBD = sb("OBD", [128, 128]); OBD64 = sb("OBD64", [128, 128])
        O512 = sb("O512", [128, 128]); O1024 = sb("O1024", [128, 128])
        RESETM = sb("RESETM", [128, T])
        EPS = sb("EPS", [128, 4])
        RC16 = sb("RC16", [128, 4, 16])
        IOTI = sb("IOTI", [128, 16], I32)
        W2P = sb("W2P", [128, 2, 512], BF16); A2P = sb("A2P", [128, 2, 512], BF16)
        G2 = sb("G2", [128, 2, 512], BF16); V2 = sb("V2", [32, 512], BF16)
        PLW = sb("PLW", [128, 2, 4, 128], BF16)
        CARRY = sb("CARRY", [128, 2, 16])
        HALO = sb("HALO", [128, 2, 4, 30])
        PHALO = sb("PHALO", [128, 2, 4, 16])
        ST = sb("ST", [128, 2, 4, 64])
        CONDT = sb("CONDT", [128, 8])
        XIN = sb("XIN", [128, TB, D])
        YBR = sb("YBR", [128, 3, 4, T], BF16)
        AW = 44 * T + 9600
        AR = sb("AR", [128, AW])

        class Alloc:
            def __init__(self, base=0):
                self.o = base

            def f(self, n, shape=None):
                ap = AR[:, self.o:self.o + n]
                self.o += n
                assert self.o <= AW, (self.o, AW)
                return ap

            def t3(self, a, b):
                return self.f(a * b).rearrange("p (a b) -> p a b", a=a)

            def b3(self, a, b):
                n = (a * b + 1) // 2
                return self.f(n).bitcast(BF16).rearrange("p (a b) -> p a b", a=a)

        A = Alloc()
        ZR = A.t3(4, T); ZK = A.t3(4, T); ZV = A.t3(4, T); YF = A.t3(4, T)
        TA = A.f(T); TBm = A.f(T); TC = A.f(T); TD = A.f(T)
        mark_dead1 = A.o
        SW = A.t3(4, T); LC = A.t3(4, T); PINV = A.t3(4, T); AH = A.t3(4, T); KKN = A.t3(4, T)
        mark_dead1_end = A.o
        RAW = A.f(T + 4); DTMP = A.f(T); ZTMP = A.f(T)
        T12 = A.b3(1, T)[:, 0, :]; LG = A.b3(1, T)[:, 0, :]; LV = A.b3(1, T)[:, 0, :]
        BDA = A.t3(4, 128); BDB = A.t3(4, 128); BDK = A.t3(4, 128); BDR = A.t3(4, 128); BDV = A.t3(4, 128)
        MN = [A.t3(4, 128), A.t3(4, 128)]; MNT = [A.t3(4, 128), A.t3(4, 128)]
        MKA = A.t3(4, 128); MBR = A.t3(4, 128); MKR = A.t3(4, 128); GM = A.t3(4, 128)
        BDBT = A.t3(4, 128); BDKT = A.t3(4, 128)
        VT = A.t3(4, 64); XTt = A.t3(4, 64); UT = A.t3(4, 64)
        YTBD = A.t3(4, 128)
        PC = A.t3(4, 8)
        arena_mixer_end = A.o
        B2 = Alloc(mark_dead1)
        HGLU = B2.t3(4, T + 30); ACC = B2.t3(4, T); CVV = B2.t3(4, T)
        PP = B2.t3(4, T + 16); SA = B2.f(T + 16); SBm = B2.f(T + 16)
        PLB = B2.b3(1, T)[:, 0, :]
        assert B2.o <= mark_dead1_end, (B2.o, mark_dead1_end)
        B3 = Alloc(0)
        MERG = B3.t3(4, T); MERGED = B3.b3(8, T); SIG = B3.f(T); TMG = B3.f(T)
        assert B3.o <= 12 * T
        B4 = Alloc(mark_dead1)
        U = B4.t3(8, T); RTMP = B4.f(T); RTMP2 = B4.f(T)
        assert B4.o <= mark_dead1_end, (B4.o, AW)
        H1 = Alloc(0).b3(32, T)
        B5 = Alloc(0)
        AWS = [B5.t3(8, 512), B5.t3(8, 512)]
        assert B5.o <= AW

        PSB = [pst(f"PS{i}", [128, 512]) for i in range(8)]
        ps_rr = [0]

        def bank():
            i = ps_rr[0]
            ps_rr[0] = (i + 1) % 8
            return i, PSB[i], f"PS{i}"

        sems = [es.enter_context(nc.semaphore(f"s_{e}")) for e in ENG_NAMES]
        dsems = {"sp": [es.enter_context(nc.semaphore(f"dsp{i}")) for i in range(8)],
                 "pool": [es.enter_context(nc.semaphore(f"dpl{i}")) for i in range(4)]}
        block = es.enter_context(nc.Block())
        S = Sched(nc, sems, dsems)
        S.fence_dma = bool(debug)

        def vc(name, col=0):
            c0 = VOFF[name] + col
            return VC[:, c0:c0 + 1]

        def conv_dma(l, g, src_ap, kc, n, col0=0, ncols=None):
            ncols = n if ncols is None else ncols
            dst = wsc[l, g][:, 0:kc * n].rearrange("p (kc n) -> p kc n", kc=kc)[:, :, col0:col0 + ncols]
            S.dma("pool", dst, src_ap, writes=[f"wsc{l}_{g}"])

        def group_src(l, g):
            w_in = dr["w_in"][l].rearrange("(kc p) n -> p kc n", p=128)
            if g < 6:
                return [(w_in[:, :, g * 512:(g + 1) * 512], 8, 512, 0, 512)]
            if g == 6:
                r = [(w_in[:, :, 3072:3328], 8, 288, 0, 256)]
                if l >= 1:
                    r.append((dr["w_in_vres"][l - 1].rearrange("(kc p) n -> p kc n", p=128), 8, 288, 256, 32))
                return r
            if g < 13:
                i = g - 7
                b, q = i // 2, i % 2
                c0 = 3328 + (8 * b + 4 * q) * 128
                return [(w_in[:, :, c0:c0 + 512], 8, 512, 0, 512)]
            if g < 19:
                i = g - 13
                b, q = i // 2, i % 2
                w = dr[("rw_wo", "cv_wo", "pl_wo")[b]][l].rearrange("(kc p) n -> p kc n", p=128)
                return [(w[:, :, q * 512:(q + 1) * 512], 4, 512, 0, 512)]
            if g < 21:
                q = g - 19
                w = dr["w_out"][l].rearrange("(kc p) n -> p kc n", p=128)
                return [(w[:, :, q * 512:(q + 1) * 512], 8, 512, 0, 512)]
            if g < 29:
                j = g - 21
                w = dr["mlp_w1"][l].rearrange("(kc p) n -> p kc n", p=128)
                return [(w[:, :, j * 512:(j + 1) * 512], 8, 512, 0, 512)]
            m = g - 29
            w = dr["mlp_w2"][l].rearrange("(kc p) n -> p kc n", p=128)
            return [(w[:, :, m * 128:(m + 1) * 128], 32, 128, 0, 128)]

        for l in range(NL):
            for g in range(NG):
                for (src, kc, n, c0, ncol) in group_src(l, g):
                    conv_dma(l, g, src, kc, n, c0, ncol)

        S.op("pool", lambda e: e.memset(IDENT[:], 0.0), writes=["IDENT"])
        S.op("pool", lambda e: e.affine_select(out=IDENT[:], in_=IDENT[:], pattern=[[-1, 128]], compare_op=ALU.not_equal,
                                               fill=1.0, base=0, channel_multiplier=1), reads=["IDENT"], writes=["IDENT"])
        S.op("pool", lambda e: e.tensor_tensor(out=I2[:], in0=IDENT[:, 0:64], in1=IDENT[:, 64:128], op=ALU.add),
             reads=["IDENT"], writes=["I2"])

        def tri(M, key, cmp_op, sgn=1):
            S.op("pool", lambda e: e.memset(M[:], 1.0), writes=[key])
            S.op("pool", lambda e: e.affine_select(out=M[:], in_=M[:], pattern=[[sgn, 128]], compare_op=cmp_op,
                                                   fill=0.0, base=0, channel_multiplier=-sgn), reads=[key], writes=[key])
            S.op("pool", lambda e: e.memset(M[0:64, 64:128], 0.0), reads=[key], writes=[key])
            S.op("pool", lambda e: e.memset(M[64:128, 0:64], 0.0), reads=[key], writes=[key])
        tri(MSU, "MSU", ALU.is_gt)
        tri(MIU, "MIU", ALU.is_ge)
        tri(MSL, "MSL", ALU.is_gt, -1)
        for M, key, val in ((OBD, "OBD", 1.0), (OBD64, "OBD64", 1.0 / 64)):
            S.op("pool", lambda e, M=M, val=val: e.memset(M[:], val), writes=[key])
            S.op("pool", lambda e, M=M: e.memset(M[0:64, 64:128], 0.0), reads=[key], writes=[key])
            S.op("pool", lambda e, M=M: e.memset(M[64:128, 0:64], 0.0), reads=[key], writes=[key])
        S.op("pool", lambda e: e.memset(O512[:], 1.0 / 512), writes=["O512"])
        S.op("pool", lambda e: e.memset(O1024[:], 1.0 / 1024), writes=["O1024"])
        S.op("pool", lambda e: e.memset(RESETM[:], 1.0), writes=["RESETM"])
        S.op("pool", lambda e: e.memset(RESETM[:].rearrange("p (c j) -> p c j", j=64)[:, :, 0:1], 0.0),
             reads=["RESETM"], writes=["RESETM"])
        for i, v in enumerate((GN_EPS, LN_EPS / (ALPHA * ALPHA), LN_EPS, 0.0)):
            S.op("pool", lambda e, i=i, v=v: e.memset(EPS[:, i:i + 1], float(v)), reads=["EPS"], writes=["EPS"])
        S.op("pool", lambda e: e.iota(IOTI[:], pattern=[[1, 16]], base=1, channel_multiplier=0), writes=["IOTI"])
        for g in range(4):
            S.op("pool", lambda e, g=g: e.tensor_copy(out=RC16[:, g, :], in_=IOTI[:]), reads=["IOTI", "RC16"], writes=["RC16"])
            S.op("pool", lambda e, g=g: e.tensor_scalar(out=RC16[:, g, :], in0=RC16[:, g, :], scalar1=float(2 ** (g + 1)), scalar2=None,
                                                      op0=ALU.min), reads=["RC16"], writes=["RC16"])
        S.op("dve", lambda e: e.reciprocal(out=RC16[:], in_=RC16[:]), reads=["RC16"], writes=["RC16"])
        for nm, Tn in (("CARRY", CARRY), ("HALO", HALO), ("PHALO", PHALO), ("ST", ST)):
            S.op("pool", lambda e, Tn=Tn: e.memset(Tn[:], 0.0), writes=[nm])
        for nm, Tn in (("BDA", BDA), ("BDB", BDB), ("BDK", BDK), ("BDR", BDR), ("BDV", BDV), ("YTBD", YTBD)):
            S.op("pool", lambda e, Tn=Tn: e.memset(Tn, 0.0), writes=[nm])
        S.op("pool", lambda e: e.memset(W2P[:], 0.0), writes=["W2P"])
        S.op("pool", lambda e: e.memset(A2P[:], 0.0), writes=["A2P"])
        for l in range(NL):
            S.dma("pool", W2P[0:64, l, :], dr["rw_w2"][l], reads=["W2P"], writes=["W2P"])
            S.dma("pool", A2P[64:128, l, :], dr["rw_a2"][l], reads=["A2P"], writes=["A2P"])
            S.dma("pool", G2[:, l, :], dr["rw_g2"][l], writes=["G2"])
            S.dma("pool", PLW[:, l, :, :], dr["pl_w"][l].rearrange("g c d -> c g d"), writes=["PLW"])
        if NL > 1:
            S.dma("pool", V2[:], dr["rw_v2"][0], writes=["V2"])
        S.dma("sp", PK[:], dr["vecs"].rearrange("(b p) n -> p b n", p=128), writes=["PK"])
        for b in range(NVB):
            bi, pb, pk = bank()
            S.op("pe", lambda e, b=b, pb=pb: e.transpose(pb[:, 0:128], PK[:, b, :], IDENT[:]),
                 reads=["PK", "IDENT"], writes=[pk])
            S.op("act", lambda e, b=b, pb=pb: e.copy(out=VC[:, b * 128:(b + 1) * 128], in_=pb[:, 0:128]),
                 reads=[pk], writes=["VC"])
        S.op("act", lambda e: e.activation(out=CONDT[:], in_=VC[:, VOFF["c"]:VOFF["c"] + 8], func=AF.Silu),
             reads=["VC"], writes=["CONDT"])
        for l in range(NL):
            bi, pb, pk = bank()
            for gq in range(12):
                slot = gq % 2
                S.dma("sp", AWS[slot], dr["ada_w"][l].rearrange("(kc p) n -> p kc n", p=128)[:, :, gq * 512:(gq + 1) * 512],
                      writes=[f"AWS{slot}"])
                for mi in range(4):
                    j = gq * 4 + mi

                    def fn(e, slot=slot, mi=mi, j=j, pb=pb):
                        for kc in range(8):
                            ins = e.matmul(pb[:, j:j + 1], lhsT=AWS[slot][:, kc, mi * 128:(mi + 1) * 128],
                                           rhs=CONDT[:, kc:kc + 1], start=(kc == 0), stop=(kc == 7))
                        return ins
                    S.op("pe", fn, reads=[f"AWS{slot}", "CONDT"], writes=[pk])
            S.op("dve", lambda e, l=l, pb=pb: e.tensor_tensor(out=ADA[:, l, :], in0=pb[:, 0:48],
                                                             in1=VC[:, VOFF[f"ada_b{l}"]:VOFF[f"ada_b{l}"] + 48], op=ALU.add),
                 reads=[pk, "VC"], writes=["ADA"])
        for l in range(NL):
            S.op("dve", lambda e, l=l: e.tensor_scalar(out=DC[:, l, 0:8], in0=ADA[:, l, 8:16], scalar1=1.0, scalar2=None, op0=ALU.add),
                 reads=["ADA"], writes=["DC"])
            S.op("dve", lambda e, l=l: e.tensor_scalar(out=DC[:, l, 8:16], in0=ADA[:, l, 32:40], scalar1=1.0, scalar2=None, op0=ALU.add),
                 reads=["ADA", "DC"], writes=["DC"])
            S.op("dve", lambda e, l=l: e.tensor_scalar(out=DC[:, l, 16:24], in0=ADA[:, l, 16:24], scalar1=1.0 / ALPHA, scalar2=None, op0=ALU.mult),
                 reads=["ADA", "DC"], writes=["DC"])
            S.op("dve", lambda e, l=l: e.tensor_scalar(out=DC[:, l, 24:32], in0=ADA[:, l, 40:48], scalar1=1.0 / ALPHA, scalar2=None, op0=ALU.mult),
                 reads=["ADA", "DC"], writes=["DC"])
            m0 = VOFF[f"mu{l}"]
            S.op("dve", lambda e, l=l, m0=m0: e.tensor_scalar(out=DC[:, l, 32:46], in0=VC[:, m0:m0 + 14], scalar1=-1.0, scalar2=1.0,
                                                           op0=ALU.mult, op1=ALU.add), reads=["VC", "DC"], writes=["DC"])
            k0 = VOFF[f"ka{l}"]
            S.op("dve", lambda e, l=l, k0=k0: e.tensor_scalar(out=DC[:, l, 48:52], in0=VC[:, k0:k0 + 4], scalar1=-1.0, scalar2=1.0,
                                                           op0=ALU.mult, op1=ALU.add), reads=["VC", "DC"], writes=["DC"])
        S.fence()

        ws_rr = [0]

        def wload(l, g):
            s = ws_rr[0]
            ws_rr[0] = (s + 1) % 4
            if g == 6:
                kc, n, nv = 8, 288, (288 if l >= 1 else 256)
            elif 13 <= g < 19:
                kc, n, nv = 4, 512, 512
            elif g >= 29:
                kc, n, nv = 32, 128, 128
            else:
                kc, n, nv = 8, 512, 512
            dst = WS[:, s, 0:kc * n].rearrange("p (kc n) -> p kc n", kc=kc)[:, :, 0:nv]
            src = wsc[l, g][:, 0:kc * n].rearrange("p (kc n) -> p kc n", kc=kc)[:, :, 0:nv]
            S.dma("sp", dst, src, reads=[f"wsc{l}_{g}"], writes=[f"WS{s}"])
            return s

        def wview(s, kc, n):
            return WS[:, s, 0:kc * n].rearrange("p (kc n) -> p kc n", kc=kc)

        def dump(name, ap, keys, idx=None):
            if name in dbg:
                dst = dbg[name] if idx is None else dbg[name][idx]
                S.dma("sp", dst, ap, reads=keys)

        def ln_stats(srcs, src_keys, sq_eng_out, ONESM, eps_col):
            n = len(srcs)
            bi1, pb1, pk1 = bank()
            bi2, pb2, pk2 = bank()

            def fm(e):
                for i, s_ in enumerate(srcs):
                    ins = e.matmul(pb1[:, 0:T], lhsT=ONESM[:], rhs=s_, start=(i == 0), stop=(i == n - 1))
                return ins
            S.op("pe", fm, reads=src_keys, writes=[pk1])
            for i, s_ in enumerate(srcs):
                S.op("act", lambda e, s_=s_: e.activation(out=TC, in_=s_, func=AF.Square), reads=[src_keys[i]], writes=["TC"])
                S.op("pe", lambda e, i=i: e.matmul(pb2[:, 0:T], lhsT=ONESM[:], rhs=TC, start=(i == 0), stop=(i == n - 1)),
                     reads=["TC"], writes=[pk2])
            S.op("act", lambda e: e.copy(out=TA, in_=pb1[:, 0:T]), reads=[pk1], writes=["TA"])
            S.op("act", lambda e: e.activation(out=TD, in_=pb1[:, 0:T], func=AF.Square), reads=[pk1], writes=["TD"])
            S.op("dve", lambda e: e.tensor_tensor(out=TBm, in0=pb2[:, 0:T], in1=TD, op=ALU.subtract), reads=[pk2, "TD"], writes=["TB"])
            S.op("dve", lambda e: e.tensor_scalar(out=TBm, in0=TBm, scalar1=0.0, scalar2=None, op0=ALU.max), reads=["TB"], writes=["TB"])
            S.op("act", lambda e: e.activation(out=TBm, in_=TBm, func=AF.Sqrt, bias=EPS[:, eps_col:eps_col + 1], scale=1.0),
                 reads=["TB", "EPS"], writes=["TB"])
            S.op("dve", lambda e: e.reciprocal(out=TBm, in_=TBm), reads=["TB"], writes=["TB"])
            return TA, TBm

        def modulate(l, which):
            o_sc = 0 if which == "m" else 8
            o_sh = 0 if which == "m" else 24
            for m in range(8):
                S.op("act", lambda e, m=m: e.activation(out=H[:, m, :], in_=X[:, m, :], func=AF.Identity,
                                                        bias=ADA[:, l, o_sh + m:o_sh + m + 1], scale=DC[:, l, o_sc + m:o_sc + m + 1]),
                     reads=[f"X{m}", "ADA", "DC"], writes=[f"H{m}"])

        def residual_ln(l, which, get_ps):
            o_gt = 16 if which == "m" else 24
            gname, bname = (f"lnmg{l}", f"lnmb{l}") if which == "m" else (f"lnfg{l}", f"lnfb{l}")
            for m in range(8):
                pb, pk = get_ps(m)
                S.op("dve", lambda e, m=m, pb=pb: e.scalar_tensor_tensor(out=U[:, m, :], in0=pb[:, 0:T], scalar=DC[:, l, o_gt + m:o_gt + m + 1],
                                                                     in1=X[:, m, :], op0=ALU.mult, op1=ALU.add),
                     reads=[pk, "DC", f"X{m}"], writes=[f"U{m}"])
            mean, rstd = ln_stats([U[:, m, :] for m in range(8)], [f"U{m}" for m in range(8)], None, O1024, 1)
            for m in range(8):
                S.op("dve", lambda e, m=m: e.tensor_tensor(out=U[:, m, :], in0=U[:, m, :], in1=mean, op=ALU.subtract),
                     reads=[f"U{m}", "TA"], writes=[f"U{m}"])
                S.op("pool", lambda e, m=m: e.tensor_tensor(out=U[:, m, :], in0=U[:, m, :], in1=rstd, op=ALU.mult),
                     reads=[f"U{m}", "TB"], writes=[f"U{m}"])
                S.op("dve", lambda e, m=m: e.tensor_scalar(out=X[:, m, :], in0=U[:, m, :], scalar1=vc(gname, m), scalar2=vc(bname, m),
                                                         op0=ALU.mult, op1=ALU.add), reads=[f"U{m}", "VC"], writes=[f"X{m}"])

        for ti in range(NT):
            t0 = ti * T
            S.dma("sp", XIN[:], dr["x"][t0:t0 + T, :].rearrange("(b p) d -> p b d", p=128), writes=["XIN"])
            for fc in range(8):
                bi, pb, pk = bank()

                def ft(e, fc=fc, pb=pb):
                    for tb in range(TB):
                        ins = e.transpose(pb[:, tb * 128:(tb + 1) * 128], XIN[:, tb, fc * 128:(fc + 1) * 128], IDENT[:])
                    return ins
                S.op("pe", ft, reads=["XIN", "IDENT"], writes=[pk])
                S.op("act" if fc % 2 else "dve",
                     (lambda e, fc=fc, pb=pb: e.copy(out=X[:, fc, :], in_=pb[:, 0:T])) if fc % 2 else
                     (lambda e, fc=fc, pb=pb: e.tensor_copy(out=X[:, fc, :], in_=pb[:, 0:T])),
                     reads=[pk], writes=[f"X{fc}"])
            for l in range(NL):
                modulate(l, "m")
                slot_of = {}

                def proj_chunk(j, ncols=128, col_in_group=None):
                    g = j // 4 if j < 24 else 6
                    if g not in slot_of:
                        slot_of[g] = wload(l, g)
                    s = slot_of[g]
                    n = 512 if g < 6 else 288
                    wv = wview(s, 8, n)
                    c0 = (j % 4) * 128 if j < 24 else (j - 24) * 128
                    bi, pb, pk = bank()

                    def fn(e, pb=pb, wv=wv, c0=c0, ncols=ncols):
                        for kc in range(8):
                            ins = e.matmul(pb[0:ncols, 0:T], lhsT=wv[:, kc, c0:c0 + ncols], rhs=H[:, kc, :],
                                           start=(kc == 0), stop=(kc == 7))
                        return ins
                    S.op("pe", fn, reads=[f"WS{s}"] + [f"H{m}" for m in range(8)], writes=[pk])
                    return pb, pk

                def token_shift(pb, pk, cidx, mu_ap, omu_ap, dst, dst_keys, np_=128):
                    S.op("act", lambda e: e.copy(out=RAW[0:np_, 1:T + 1], in_=pb[0:np_, 0:T]), reads=[pk], writes=["RAW"])
                    S.op("pool", lambda e: e.tensor_copy(out=RAW[0:np_, 0:1], in_=CARRY[0:np_, l, cidx:cidx + 1]),
                         reads=["CARRY", "RAW"], writes=["RAW"])
                    S.op("dve", lambda e: e.tensor_tensor(out=DTMP[0:np_, :], in0=RAW[0:np_, 0:T], in1=RAW[0:np_, 1:T + 1], op=ALU.subtract),
                         reads=["RAW"], writes=["DTMP"])
                    S.op("dve", lambda e: e.scalar_tensor_tensor(out=dst, in0=DTMP[0:np_, :], scalar=mu_ap, in1=RAW[0:np_, 1:T + 1],
                                                                 op0=ALU.mult, op1=ALU.add), reads=["DTMP", "RAW", "VC"], writes=dst_keys)
                    S.op("pool", lambda e: e.tensor_copy(out=CARRY[0:np_, l, cidx:cidx + 1], in_=RAW[0:np_, T:T + 1]),
                         reads=["RAW", "CARRY"], writes=["CARRY"])

                for j in range(26):
                    pb, pk = proj_chunk(j)
                    if j < 12:
                        dstT = (ZR, ZK, ZV)[j // 4]
                        nm = ("ZR", "ZK", "ZV")[j // 4]
                        token_shift(pb, pk, j, vc(f"mu{l}", j), None, dstT[:, j % 4, :], [f"{nm}{j % 4}"])
                    elif j == 12:
                        token_shift(pb, pk, j, vc(f"mu{l}", j), None, ZTMP, ["ZTMP"])
                        S.op("act", lambda e: e.activation(out=T12[0:64, :], in_=ZTMP[0:64, :], func=AF.Tanh), reads=["ZTMP"], writes=["T12"])
                        S.op("dve", lambda e: e.tensor_copy(out=T12[64:128, :], in_=ZTMP[64:128, :]), reads=["ZTMP", "T12"], writes=["T12"])
                    elif j == 13:
                        token_shift(pb, pk, j, vc(f"mu{l}", j), None, ZTMP, ["ZTMP"])
                        S.op("act", lambda e: e.activation(out=LG, in_=ZTMP, func=AF.Sigmoid), reads=["ZTMP"], writes=["LG"])
                    elif j < 18:
                        cc = j - 14
                        S.op("act", lambda e, cc=cc, pb=pb: e.copy(out=CVV[:, cc, :], in_=pb[:, 0:T]), reads=[pk], writes=[f"CVV{cc}"])
                    elif j < 22:
                        cc = j - 18
                        S.op("act", lambda e, pb=pb: e.activation(out=TA, in_=pb[:, 0:T], func=AF.Sigmoid), reads=[pk], writes=["TA"])
                        if cc == 0:
                            S.op("pool", lambda e: e.tensor_copy(out=HGLU[:, :, 0:30], in_=HALO[:, l, :, :]), reads=["HALO"],
                                 writes=[f"HGLU{c_}" for c_ in range(4)])
                        S.op("dve", lambda e, cc=cc: e.tensor_tensor(out=HGLU[:, cc, 30:30 + T], in0=CVV[:, cc, :], in1=TA, op=ALU.mult),
                             reads=[f"CVV{cc}", "TA"], writes=[f"HGLU{cc}"])
                    else:
                        gg = j - 22
                        if gg == 0:
                            S.op("pool", lambda e: e.tensor_copy(out=PP[:, :, 0:16], in_=PHALO[:, l, :, :]), reads=["PHALO"],
                                 writes=[f"PP{c_}" for c_ in range(4)])
                        S.op("act", lambda e, gg=gg, pb=pb: e.copy(out=PP[:, gg, 16:16 + T], in_=pb[:, 0:T]), reads=[pk], writes=[f"PP{gg}"])
                if l >= 1:
                    s = slot_of[6]
                    wv = wview(s, 8, 288)
                    bi, pb, pk = bank()

                    def fnv(e, pb=pb, wv=wv):
                        for kc in range(8):
                            ins = e.matmul(pb[0:32, 0:T], lhsT=wv[:, kc, 256:288], rhs=H[:, kc, :], start=(kc == 0), stop=(kc == 7))
                        return ins
                    S.op("pe", fnv, reads=[f"WS{s}"] + [f"H{m}" for m in range(8)], writes=[pk])
                    token_shift(pb, pk, 14, VC[0:32, VOFF["muv"]:VOFF["muv"] + 1], None, ZTMP[0:32, :], ["ZTMP"], np_=32)
                    S.op("act", lambda e: e.copy(out=LV[0:32, :], in_=ZTMP[0:32, :]), reads=["ZTMP"], writes=["LV"])
                dump(f"z_r{l}", ZR, ["ZR0", "ZR1", "ZR2", "ZR3"], ti)
                dump(f"z_k{l}", ZK, ["ZK0", "ZK1", "ZK2", "ZK3"], ti)
                dump(f"z_v{l}", ZV, ["ZV0", "ZV1", "ZV2", "ZV3"], ti)

                for cc in range(4):
                    eng = "dve" if cc % 2 == 0 else "pool"
                    wbase = VOFF[f"cvw{l}"]
                    S.op(eng, lambda e, cc=cc: e.tensor_scalar(out=ACC[:, cc, :], in0=HGLU[:, cc, 0:T], scalar1=VC[:, wbase + cc:wbase + cc + 1],
                                                              scalar2=vc(f"cvb{l}", cc), op0=ALU.mult, op1=ALU.add),
                         reads=[f"HGLU{cc}", "VC"], writes=[f"ACC{cc}"])
                    for jt in range(1, CONVW):
                        if eng == "dve":
                            S.op(eng, lambda e, cc=cc, jt=jt: e.scalar_tensor_tensor(
                                out=ACC[:, cc, :], in0=HGLU[:, cc, jt:jt + T], scalar=VC[:, wbase + jt * 4 + cc:wbase + jt * 4 + cc + 1],
                                in1=ACC[:, cc, :], op0=ALU.mult, op1=ALU.add), reads=[f"HGLU{cc}", "VC", f"ACC{cc}"], writes=[f"ACC{cc}"])
                        else:
                            ct, ck = ((SA, "SA"), (SBm, "SB"))[jt % 2]
                            S.op("act", lambda e, cc=cc, jt=jt, ct=ct: e.activation(
                                out=ct[:, 0:T], in_=HGLU[:, cc, jt:jt + T], func=AF.Identity,
                                scale=VC[:, wbase + jt * 4 + cc:wbase + jt * 4 + cc + 1]), reads=[f"HGLU{cc}", "VC"], writes=[ck])
                            S.op("pool", lambda e, cc=cc, ct=ct: e.tensor_tensor(out=ACC[:, cc, :], in0=ACC[:, cc, :], in1=ct[:, 0:T], op=ALU.add),
                                 reads=[ck, f"ACC{cc}"], writes=[f"ACC{cc}"])
                S.op("pool", lambda e: e.tensor_copy(out=HALO[:, l, :, :], in_=HGLU[:, :, T:T + 30]),
                     reads=[f"HGLU{c_}" for c_ in range(4)], writes=["HALO"])
                dump(f"conv{l}", ACC, [f"ACC{c_}" for c_ in range(4)], ti)
                mean, rstd = ln_stats([ACC[:, cc, :] for cc in range(4)], [f"ACC{cc}" for cc in range(4)], None, O512, 2)
                for cc in range(4):
                    S.op("dve", lambda e, cc=cc: e.tensor_tensor(out=ACC[:, cc, :], in0=ACC[:, cc, :], in1=mean, op=ALU.subtract),
                         reads=[f"ACC{cc}", "TA"], writes=[f"ACC{cc}"])
                    S.op("pool", lambda e, cc=cc: e.tensor_tensor(out=ACC[:, cc, :], in0=ACC[:, cc, :], in1=rstd, op=ALU.mult),
                         reads=[f"ACC{cc}", "TB"], writes=[f"ACC{cc}"])
                    S.op("dve", lambda e, cc=cc: e.tensor_scalar(out=ACC[:, cc, :], in0=ACC[:, cc, :], scalar1=vc(f"cvg{l}", cc),
                                                                scalar2=vc(f"cvbb{l}", cc), op0=ALU.mult, op1=ALU.add),
                         reads=[f"ACC{cc}", "VC"], writes=[f"ACC{cc}"])
                    S.op("act", lambda e, cc=cc: e.activation(out=YBR[:, 1, cc, :], in_=ACC[:, cc, :], func=AF.Silu),
                         reads=[f"ACC{cc}"], writes=[f"YBR1_{cc}"])
                for gg in range(4):
                    win = 2 ** (gg + 1)
                    eng = "pool" if gg % 2 == 0 else "dve"
                    src = PP[:, gg, :]
                    cur_key = f"PP{gg}"
                    bufs = [(SA, "SA"), (SBm, "SB")]
                    sh = 1
                    k_ = 0
                    while sh < win:
                        dstb, dkey = bufs[k_ % 2]
                        lo = 2 * sh - 1
                        S.op(eng, lambda e, src=src, dstb=dstb, sh=sh, lo=lo: e.tensor_tensor(
                            out=dstb[:, lo:T + 16], in0=src[:, lo:T + 16], in1=src[:, lo - sh:T + 16 - sh], op=ALU.add),
                            reads=[cur_key], writes=[dkey])
                        src, cur_key = dstb, dkey
                        sh *= 2
                        k_ += 1
                    dstb, dkey = bufs[k_ % 2]
                    S.op("dve", lambda e, src=src, dstb=dstb, gg=gg, win=win: e.scalar_tensor_tensor(
                        out=dstb[:, 16:16 + T], in0=src[:, 16:16 + T], scalar=1.0 / win, in1=PP[:, gg, 16:16 + T],
                        op0=ALU.mult, op1=ALU.subtract), reads=[cur_key, f"PP{gg}"], writes=[dkey])
                    if ti == 0:
                        S.op(eng, lambda e, src=src, gg=gg: e.tensor_tensor(out=src[:, 16:32], in0=src[:, 16:32], in1=RC16[:, gg, :], op=ALU.mult),
                             reads=[cur_key, "RC16", dkey], writes=[cur_key])
                        S.op(eng, lambda e, src=src, dstb=dstb, gg=gg: e.tensor_tensor(out=dstb[:, 16:32], in0=src[:, 16:32], in1=PP[:, gg, 16:32],
                                                                                  op=ALU.subtract), reads=[cur_key, f"PP{gg}", dkey], writes=[dkey])
                    S.op("act", lambda e, dstb=dstb: e.copy(out=PLB[:, :], in_=dstb[:, 16:16 + T]), reads=[dkey], writes=["PLB"])
                    bi, pb, pk = bank()
                    S.op("pe", lambda e, gg=gg, pb=pb: e.matmul(pb[:, 0:T], lhsT=PLW[:, l, gg, :], rhs=PLB[:, :], start=True, stop=True),
                         reads=["PLB", "PLW"], writes=[pk])
                    S.op("act", lambda e, gg=gg, pb=pb: e.activation(out=YBR[:, 2, gg, :], in_=pb[:, 0:T], func=AF.Identity,
                                                                   scale=vc(f"plsc{l}", gg)), reads=[pk, "VC"], writes=[f"YBR2_{gg}"])
                S.op("pool", lambda e: e.tensor_copy(out=PHALO[:, l, :, :], in_=PP[:, :, T:T + 16]),
                     reads=[f"PP{c_}" for c_ in range(4)], writes=["PHALO"])
                S.fence()

                for p in range(4):
                    bi, pb, pk = bank()
                    S.op("pe", lambda e, p=p, pb=pb: e.matmul(pb[:, 0:T], lhsT=W2P[:, l, p * 128:(p + 1) * 128], rhs=T12, start=True, stop=True),
                         reads=["T12", "W2P"], writes=[pk])
                    S.op("act", lambda e, p=p, pb=pb: e.activation(out=SW[:, p, :], in_=pb[:, 0:T], func=AF.Sigmoid, bias=vc(f"w0{l}", p), scale=1.0),
                         reads=[pk, "VC"], writes=[f"SW{p}"])
                    bi, pb, pk = bank()
                    S.op("pe", lambda e, p=p, pb=pb: e.matmul(pb[:, 0:T], lhsT=A2P[:, l, p * 128:(p + 1) * 128], rhs=T12, start=True, stop=True),
                         reads=["T12", "A2P"], writes=[pk])
                    S.op("act", lambda e, p=p, pb=pb: e.activation(out=AH[:, p, :], in_=pb[:, 0:T], func=AF.Sigmoid, bias=vc(f"a0{l}", p), scale=1.0),
                         reads=[pk, "VC"], writes=[f"AH{p}"])
                    if l == 0:
                        S.op("pool", lambda e, p=p: e.tensor_copy(out=VF[:, p, :], in_=ZV[:, p, :]), reads=[f"ZV{p}"], writes=[f"VF{p}"])
                    else:
                        bi, pb, pk = bank()
                        S.op("pe", lambda e, p=p, pb=pb: e.matmul(pb[:, 0:T], lhsT=V2[:, p * 128:(p + 1) * 128], rhs=LV[0:32, :], start=True, stop=True),
                             reads=["LV", "V2"], writes=[pk])
                        S.op("act", lambda e, p=p, pb=pb: e.activation(out=TA, in_=pb[:, 0:T], func=AF.Sigmoid, bias=vc("v0", p), scale=1.0),
                             reads=[pk, "VC"], writes=["TA"])
                        S.op("pool", lambda e, p=p: e.tensor_tensor(out=TD, in0=VF[:, p, :], in1=ZV[:, p, :], op=ALU.subtract),
                             reads=[f"VF{p}", f"ZV{p}"], writes=["TD"])
                        S.op("dve", lambda e: e.tensor_tensor(out=TD, in0=TD, in1=TA, op=ALU.mult), reads=["TD", "TA"], writes=["TD"])
                        S.op("dve", lambda e, p=p: e.tensor_tensor(out=ZV[:, p, :], in0=ZV[:, p, :], in1=TD, op=ALU.add),
                             reads=[f"ZV{p}", "TD"], writes=[f"ZV{p}"])
                    S.op("act", lambda e, p=p: e.activation(out=TC, in_=ZK[:, p, :], func=AF.Square, scale=vc(f"kk{l}", p)),
                         reads=[f"ZK{p}", "VC"], writes=["TC"])
                    bi, pb, pk = bank()
                    S.op("pe", lambda e, pb=pb: e.matmul(pb[:, 0:T], lhsT=OBD[:], rhs=TC, start=True, stop=True), reads=["TC", "OBD"], writes=[pk])
                    S.op("act", lambda e, pb=pb: e.activation(out=TD, in_=pb[:, 0:T], func=AF.Sqrt), reads=[pk], writes=["TD"])
                    S.op("dve", lambda e: e.tensor_scalar(out=TD, in0=TD, scalar1=1e-12, scalar2=-1.0, op0=ALU.max, op1=ALU.mult), reads=["TD"], writes=["TD"])
                    S.op("dve", lambda e: e.reciprocal(out=TD, in_=TD), reads=["TD"], writes=["TD"])
                    S.op("dve", lambda e, p=p: e.scalar_tensor_tensor(out=KKN[:, p, :], in0=ZK[:, p, :], scalar=vc(f"kk{l}", p), in1=TD,
                                                                      op0=ALU.mult, op1=ALU.mult), reads=[f"ZK{p}", "TD", "VC"], writes=[f"KKN{p}"])
                    S.op("pool", lambda e, p=p: e.tensor_scalar(out=TC, in0=AH[:, p, :], scalar1=vc(f"ka{l}", p), scalar2=DC[:, l, 48 + p:49 + p],
                                                               op0=ALU.mult, op1=ALU.add), reads=[f"AH{p}", "VC", "DC"], writes=["TC"])
                    S.op("dve", lambda e, p=p: e.tensor_tensor(out=ZK[:, p, :], in0=ZK[:, p, :], in1=TC, op=ALU.mult),
                         reads=[f"ZK{p}", "TC"], writes=[f"ZK{p}"])
                    S.op("dve", lambda e, p=p: e.scalar_tensor_tensor(out=AH[:, p, :], in0=AH[:, p, :], scalar=-1.0, in1=KKN[:, p, :], op0=ALU.mult, op1=ALU.mult),
                         reads=[f"AH{p}", f"KKN{p}"], writes=[f"AH{p}"])
                    S.op("dve", lambda e, p=p: e.tensor_tensor_scan(out=LC[:, p, :], data0=RESETM[:], data1=SW[:, p, :], initial=0.0,
                                                                    op0=ALU.mult, op1=ALU.add), reads=["RESETM", f"SW{p}"], writes=[f"LC{p}"])
                    S.op("pool", lambda e, p=p: e.tensor_tensor(out=SW[:, p, :], in0=LC[:, p, :], in1=SW[:, p, :], op=ALU.subtract),
                         reads=[f"LC{p}", f"SW{p}"], writes=[f"SW{p}"])
                    S.op("act", lambda e, p=p: e.activation(out=SW[:, p, :], in_=SW[:, p, :], func=AF.Exp, scale=-CDEC), reads=[f"SW{p}"], writes=[f"SW{p}"])
                    S.op("act", lambda e, p=p: e.activation(out=PINV[:, p, :], in_=LC[:, p, :], func=AF.Exp, scale=CDEC), reads=[f"LC{p}"], writes=[f"PINV{p}"])
                    S.op("act", lambda e, p=p: e.activation(out=LC[:, p, :], in_=LC[:, p, :], func=AF.Exp, scale=-CDEC), reads=[f"LC{p}"], writes=[f"LC{p}"])
                    S.op("pool", lambda e, p=p: e.tensor_copy(out=PC[:, p, 0:NCH], in_=LC[:, p, :].rearrange("p (c j) -> p c j", j=64)[:, :, 63]),
                         reads=[f"LC{p}"], writes=["PC"])
                PEX, PIN, BBt, KH = SW, LC, AH, ZK
                allk = lambda nm: [f"{nm}{p}" for p in range(4)]
                for c in range(NCH):
                    cs = slice(c * 64, (c + 1) * 64)
                    for hh in range(2):
                        rws = slice(64 * hh, 64 * hh + 64)
                        cls = slice(64 * hh, 64 * hh + 64)
                        S.op("dve", lambda e, rws=rws, cls=cls, cs=cs: e.tensor_tensor(out=BDR[rws, :, cls], in0=ZR[rws, :, cs], in1=PIN[rws, :, cs], op=ALU.mult),
                             reads=allk("ZR") + allk("LC"), writes=["BDR"])
                        S.op("pool", lambda e, rws=rws, cls=cls, cs=cs: e.tensor_tensor(out=BDK[rws, :, cls], in0=KH[rws, :, cs], in1=PINV[rws, :, cs], op=ALU.mult),
                             reads=allk("ZK") + allk("PINV"), writes=["BDK"])
                        S.op("dve", lambda e, rws=rws, cls=cls, cs=cs: e.tensor_tensor(out=BDB[rws, :, cls], in0=BBt[rws, :, cs], in1=PINV[rws, :, cs], op=ALU.mult),
                             reads=allk("AH") + allk("PINV"), writes=["BDB"])
                        S.op("pool", lambda e, rws=rws, cls=cls, cs=cs: e.tensor_tensor(out=BDA[rws, :, cls], in0=KKN[rws, :, cs], in1=PEX[rws, :, cs], op=ALU.mult),
                             reads=allk("KKN") + allk("SW"), writes=["BDA"])
                        S.op("pool", lambda e, rws=rws, cls=cls, cs=cs: e.tensor_copy(out=BDV[rws, :, cls], in_=ZV[rws, :, cs]),
                             reads=allk("ZV"), writes=["BDV"])

                    def mm4(outb, lT, rT, ncol=128):
                        def fn(e):
                            for p in range(4):
                                ins = e.matmul(outb[:, p * ncol:(p + 1) * ncol], lhsT=lT[:, p, :], rhs=rT[:, p, :], start=True, stop=True)
                            return ins
                        return fn

                    def v4(pb, n=128):
                        return pb[:, 0:4 * n].rearrange("p (a b) -> p a b", a=4)

                    def bc4(M):
                        return M[:].unsqueeze(1).to_broadcast([128, 4, 128])
                    for (lT, lk, rT, rk_, dstM, dk, msk, mk_) in (
                            (BDB, "BDB", BDA, "BDA", MN[0], "MN0", MSU, "MSU"),
                            (BDA, "BDA", BDB, "BDB", MNT[0], "MNT0", MSL, "MSL"),
                            (BDK, "BDK", BDA, "BDA", MKA, "MKA", MSU, "MSU"),
                            (BDB, "BDB", BDR, "BDR", MBR, "MBR", MIU, "MIU"),
                            (BDK, "BDK", BDR, "BDR", MKR, "MKR", MIU, "MIU")):
                        bi, pb, pk = bank()
                        S.op("pe", mm4(pb, lT, rT), reads=[lk, rk_], writes=[pk])
                        S.op("dve", lambda e, pb=pb, dstM=dstM, msk=msk: e.tensor_tensor(out=dstM, in0=v4(pb), in1=bc4(msk), op=ALU.mult),
                             reads=[pk, mk_], writes=[dk])
                    S.op("pool", lambda e: e.tensor_tensor(out=GM, in0=MN[0], in1=bc4(IDENT), op=ALU.add), reads=["MN0", "IDENT"], writes=["GM"])
                    cur = 0
                    for jl in range(1, 6):
                        nxt = 1 - cur
                        if jl < 5:
                            bi, pb, pk = bank()
                            S.op("pe", mm4(pb, MNT[cur], MN[cur]), reads=[f"MN{cur}", f"MNT{cur}"], writes=[pk])
                            S.op("act", lambda e, pb=pb, nxt=nxt: e.copy(out=MN[nxt], in_=v4(pb)), reads=[pk], writes=[f"MN{nxt}"])
                        bi, pb, pk = bank()
                        S.op("pe", mm4(pb, MN[cur], MNT[cur]), reads=[f"MN{cur}", f"MNT{cur}"], writes=[pk])
                        S.op("dve", lambda e, pb=pb, nxt=nxt: e.tensor_copy(out=MNT[nxt], in_=v4(pb)), reads=[pk], writes=[f"MNT{nxt}"])
                        bi, pb, pk = bank()
                        S.op("pe", mm4(pb, MNT[nxt], GM), reads=[f"MNT{nxt}", "GM"], writes=[pk])
                        S.op("dve", lambda e, pb=pb: e.tensor_tensor(out=GM, in0=v4(pb), in1=GM, op=ALU.add), reads=[pk, "GM"], writes=["GM"])
                        cur = nxt
                    for (srcT, sk_, dstT_, dk) in ((BDB, "BDB", BDBT, "BDBT"), (BDK, "BDK", BDKT, "BDKT")):
                        bi, pb, pk = bank()

                        def ftr(e, pb=pb, srcT=srcT):
                            for p in range(4):
                                ins = e.transpose(pb[:, p * 128:(p + 1) * 128], srcT[:, p, :], IDENT[:])
                            return ins
                        S.op("pe", ftr, reads=[sk_, "IDENT"], writes=[pk])
                        S.op("act", lambda e, pb=pb, dstT_=dstT_: e.copy(out=dstT_, in_=v4(pb)), reads=[pk], writes=[dk])
                    bi, pb, pk = bank()

                    def fvt(e, pb=pb):
                        for p in range(4):
                            ins = e.matmul(pb[:, p * 64:(p + 1) * 64], lhsT=BDV[:, p, :], rhs=I2[:], start=True, stop=True)
                        return ins
                    S.op("pe", fvt, reads=["BDV", "I2"], writes=[pk])
                    S.op("act", lambda e, pb=pb: e.copy(out=VT, in_=v4(pb, 64)), reads=[pk], writes=["VT"])
                    STl = ST[:, l, :, :]
                    bi, pb, pk = bank()

                    def f1(e, pb=pb):
                        for p in range(4):
                            e.matmul(pb[:, p * 64:(p + 1) * 64], lhsT=BDA[:, p, :], rhs=STl[:, p, :], start=True, stop=False)
                            ins = e.matmul(pb[:, p * 64:(p + 1) * 64], lhsT=MKA[:, p, :], rhs=VT[:, p, :], start=False, stop=True)
                        return ins
                    S.op("pe", f1, reads=["BDA", "ST", "MKA", "VT"], writes=[pk])
                    S.op("act", lambda e, pb=pb: e.copy(out=XTt, in_=v4(pb, 64)), reads=[pk], writes=["XT"])
                    bi, pb, pk = bank()
                    S.op("pe", mm4(pb, GM, XTt, 64), reads=["GM", "XT"], writes=[pk])
                    S.op("dve", lambda e, pb=pb: e.tensor_copy(out=UT, in_=v4(pb, 64)), reads=[pk], writes=["UT"])
                    bi, pby, pky = bank()

                    def f3(e, pby=pby):
                        for p in range(4):
                            o = pby[:, p * 64:(p + 1) * 64]
                            e.matmul(o, lhsT=BDR[:, p, :], rhs=STl[:, p, :], start=True, stop=False)
                            e.matmul(o, lhsT=MBR[:, p, :], rhs=UT[:, p, :], start=False, stop=False)
                            ins = e.matmul(o, lhsT=MKR[:, p, :], rhs=VT[:, p, :], start=False, stop=True)
                        return ins
                    S.op("pe", f3, reads=["BDR", "ST", "MBR", "UT", "MKR", "VT"], writes=[pky])
                    bi, pbs, pks = bank()

                    def f4(e, pbs=pbs):
                        for p in range(4):
                            o = pbs[:, p * 64:(p + 1) * 64]
                            e.matmul(o, lhsT=IDENT[:], rhs=STl[:, p, :], start=True, stop=False)
                            e.matmul(o, lhsT=BDBT[:, p, :], rhs=UT[:, p, :], start=False, stop=False)
                            ins = e.matmul(o, lhsT=BDKT[:, p, :], rhs=VT[:, p, :], start=False, stop=True)
                        return ins
                    S.op("pe", f4, reads=["IDENT", "ST", "BDBT", "UT", "BDKT", "VT"], writes=[pks])
                    S.op("act", lambda e, pby=pby: e.copy(out=YTBD[0:64, :, 0:64], in_=v4(pby, 64)[0:64]), reads=[pky, "YTBD"], writes=["YTBD"])
                    S.op("dve", lambda e, pby=pby: e.tensor_copy(out=YTBD[64:128, :, 64:128], in_=v4(pby, 64)[64:128]), reads=[pky, "YTBD"], writes=["YTBD"])
                    S.op("dve", lambda e, pbs=pbs, c=c: e.tensor_tensor(out=STl, in0=v4(pbs, 64), in1=PC[:, :, c:c + 1].to_broadcast([128, 4, 64]), op=ALU.mult),
                         reads=[pks, "PC"], writes=["ST"])
                    bi, pb, pk = bank()

                    def f5(e, pb=pb):
                        for p in range(4):
                            ins = e.matmul(pb[:, p * 64:(p + 1) * 64], lhsT=YTBD[:, p, :], rhs=I2[:], start=True, stop=True)
                        return ins
                    S.op("pe", f5, reads=["YTBD", "I2"], writes=[pk])
                    S.op("act", lambda e, pb=pb, cs=cs: e.copy(out=YF[:, :, cs], in_=v4(pb, 64)), reads=[pk], writes=allk("YF"))
                dump(f"yrec{l}", YF, allk("YF"), ti)
                for p in range(4):
                    S.op("dve", lambda e, p=p: e.scalar_tensor_tensor(out=TC, in0=ZR[:, p, :], scalar=vc(f"rk{l}", p), in1=KH[:, p, :],
                                                                      op0=ALU.mult, op1=ALU.mult), reads=[f"ZR{p}", f"ZK{p}", "VC"], writes=["TC"])
                    bi, pbb, pkb = bank()
                    S.op("pe", lambda e, pbb=pbb: e.matmul(pbb[:, 0:T], lhsT=OBD[:], rhs=TC, start=True, stop=True), reads=["TC", "OBD"], writes=[pkb])
                    S.op("dve", lambda e, p=p, pbb=pbb: e.tensor_tensor(out=KKN[:, p, :], in0=pbb[:, 0:T], in1=ZV[:, p, :], op=ALU.mult),
                         reads=[pkb, f"ZV{p}"], writes=[f"KKN{p}"])
                    mean, rstd = ln_stats([YF[:, p, :]], [f"YF{p}"], None, OBD64, 0)
                    S.op("dve", lambda e, p=p: e.tensor_tensor(out=YF[:, p, :], in0=YF[:, p, :], in1=mean, op=ALU.subtract),
                         reads=[f"YF{p}", "TA"], writes=[f"YF{p}"])
                    S.op("pool", lambda e, p=p: e.tensor_tensor(out=YF[:, p, :], in0=YF[:, p, :], in1=rstd, op=ALU.mult),
                         reads=[f"YF{p}", "TB"], writes=[f"YF{p}"])
                    S.op("dve", lambda e, p=p: e.tensor_scalar(out=YF[:, p, :], in0=YF[:, p, :], scalar1=vc(f"gng{l}", p), scalar2=vc(f"gnb{l}", p),
                                                             op0=ALU.mult, op1=ALU.add), reads=[f"YF{p}", "VC"], writes=[f"YF{p}"])
                    S.op("pool", lambda e, p=p: e.tensor_tensor(out=YF[:, p, :], in0=YF[:, p, :], in1=KKN[:, p, :], op=ALU.add),
                         reads=[f"YF{p}", f"KKN{p}"], writes=[f"YF{p}"])
                    bi, pbg, pkg = bank()
                    S.op("pe", lambda e, p=p, pbg=pbg: e.matmul(pbg[:, 0:T], lhsT=G2[:, l, p * 128:(p + 1) * 128], rhs=LG, start=True, stop=True),
                         reads=["LG", "G2"], writes=[pkg])
                    S.op("dve", lambda e, p=p, pbg=pbg: e.tensor_tensor(out=YBR[:, 0, p, :], in0=pbg[:, 0:T], in1=YF[:, p, :], op=ALU.mult),
                         reads=[pkg, f"YF{p}"], writes=[f"YBR0_{p}"])
                S.fence()
                for q in range(2):
                    for b in range(3):
                        sg = wload(l, 7 + b * 2 + q)
                        so = wload(l, 13 + b * 2 + q)
                        wg = wview(sg, 8, 512)
                        wo = wview(so, 4, 512)
                        for mi in range(4):
                            bi, pbg, pkg = bank()

                            def fg(e, pbg=pbg, wg=wg, mi=mi):
                                for kc in range(8):
                                    ins = e.matmul(pbg[:, 0:T], lhsT=wg[:, kc, mi * 128:(mi + 1) * 128], rhs=H[:, kc, :], start=(kc == 0), stop=(kc == 7))
                                return ins
                            S.op("pe", fg, reads=[f"WS{sg}"] + [f"H{m}" for m in range(8)], writes=[pkg])
                            bi, pbo, pko = bank()

                            def fo(e, pbo=pbo, wo=wo, mi=mi, b=b):
                                for kc in range(4):
                                    ins = e.matmul(pbo[:, 0:T], lhsT=wo[:, kc, mi * 128:(mi + 1) * 128], rhs=YBR[:, b, kc, :], start=(kc == 0), stop=(kc == 3))
                                return ins
                            S.op("pe", fo, reads=[f"WS{so}"] + [f"YBR{b}_{k_}" for k_ in range(4)], writes=[pko])
                            S.op("act", lambda e, pbg=pbg: e.activation(out=SIG, in_=pbg[:, 0:T], func=AF.Sigmoid), reads=[pkg], writes=["SIG"])
                            if b == 0:
                                S.op("dve", lambda e, pbo=pbo, mi=mi: e.tensor_tensor(out=MERG[:, mi, :], in0=pbo[:, 0:T], in1=SIG, op=ALU.mult),
                                     reads=[pko, "SIG"], writes=[f"MERG{mi}"])
                            else:
                                S.op("dve", lambda e, pbo=pbo: e.tensor_tensor(out=TMG, in0=pbo[:, 0:T], in1=SIG, op=ALU.mult),
                                     reads=[pko, "SIG"], writes=["TMG"])
                                if b == 1:
                                    S.op("pool", lambda e, mi=mi: e.tensor_tensor(out=MERG[:, mi, :], in0=MERG[:, mi, :], in1=TMG, op=ALU.add),
                                         reads=[f"MERG{mi}", "TMG"], writes=[f"MERG{mi}"])
                                else:
                                    S.op("pool", lambda e, mi=mi, q=q: e.tensor_tensor(out=MERGED[:, q * 4 + mi, :], in0=MERG[:, mi, :], in1=TMG, op=ALU.add),
                                         reads=[f"MERG{mi}", "TMG"], writes=[f"MERGED{q * 4 + mi}"])
                S.fence()
                sA = wload(l, 19)
                sB = wload(l, 20)
                psl = {}
                for m in range(8):
                    s = sA if m < 4 else sB
                    wv = wview(s, 8, 512)
                    bi, pb, pk = bank()

                    def fw(e, pb=pb, wv=wv, m=m):
                        for kc in range(8):
                            ins = e.matmul(pb[:, 0:T], lhsT=wv[:, kc, (m % 4) * 128:(m % 4 + 1) * 128], rhs=MERGED[:, kc, :], start=(kc == 0), stop=(kc == 7))
                        return ins
                    S.op("pe", fw, reads=[f"WS{s}"] + [f"MERGED{k_}" for k_ in range(8)], writes=[pk])
                    S.op("dve", lambda e, m=m, pb=pb: e.scalar_tensor_tensor(out=U[:, m, :], in0=pb[:, 0:T], scalar=DC[:, l, 16 + m:17 + m],
                                                                         in1=X[:, m, :], op0=ALU.mult, op1=ALU.add),
                         reads=[pk, "DC", f"X{m}"], writes=[f"U{m}"])

                def ln_apply(gname, bname):
                    mean, rstd = ln_stats([U[:, m, :] for m in range(8)], [f"U{m}" for m in range(8)], None, O1024, 1)
                    for m in range(8):
                        S.op("dve", lambda e, m=m: e.tensor_tensor(out=U[:, m, :], in0=U[:, m, :], in1=mean, op=ALU.subtract),
                             reads=[f"U{m}", "TA"], writes=[f"U{m}"])
                        S.op("pool", lambda e, m=m: e.tensor_tensor(out=U[:, m, :], in0=U[:, m, :], in1=rstd, op=ALU.mult),
                             reads=[f"U{m}", "TB"], writes=[f"U{m}"])
                        S.op("dve", lambda e, m=m: e.tensor_scalar(out=X[:, m, :], in0=U[:, m, :], scalar1=vc(gname, m), scalar2=vc(bname, m),
                                                                 op0=ALU.mult, op1=ALU.add), reads=[f"U{m}", "VC"], writes=[f"X{m}"])
                ln_apply(f"lnmg{l}", f"lnmb{l}")
                dump(f"xm{l}", X[:], [f"X{m}" for m in range(8)], ti)
                S.fence()
                modulate(l, "f")
                for jg in range(8):
                    s = wload(l, 21 + jg)
                    wv = wview(s, 8, 512)
                    for ji in range(4):
                        j = jg * 4 + ji
                        bi, pb, pk = bank()

                        def f1m(e, pb=pb, wv=wv, ji=ji):
                            for kc in range(8):
                                ins = e.matmul(pb[:, 0:T], lhsT=wv[:, kc, ji * 128:(ji + 1) * 128], rhs=H[:, kc, :], start=(kc == 0), stop=(kc == 7))
                            return ins
                        S.op("pe", f1m, reads=[f"WS{s}"] + [f"H{m}" for m in range(8)], writes=[pk])
                        rt, rk_ = (RTMP, "RTMP") if j % 2 == 0 else (RTMP2, "RTMP2")
                        S.op("act", lambda e, pb=pb, rt=rt: e.activation(out=rt, in_=pb[:, 0:T], func=AF.Relu), reads=[pk], writes=[rk_])
                        S.op("dve" if j % 2 == 0 else "pool", lambda e, j=j, rt=rt: e.tensor_tensor(out=H1[:, j, :], in0=rt, in1=rt, op=ALU.mult),
                             reads=[rk_], writes=[f"H1_{j}"])
                for m in range(8):
                    s = wload(l, 29 + m)
                    wv = wview(s, 32, 128)
                    bi, pb, pk = bank()

                    def f2m(e, pb=pb, wv=wv):
                        for kc in range(32):
                            ins = e.matmul(pb[:, 0:T], lhsT=wv[:, kc, :], rhs=H1[:, kc, :], start=(kc == 0), stop=(kc == 31))
                        return ins
                    S.op("pe", f2m, reads=[f"WS{s}"] + [f"H1_{k_}" for k_ in range(32)], writes=[pk])
                    S.op("dve", lambda e, m=m, pb=pb: e.scalar_tensor_tensor(out=U[:, m, :], in0=pb[:, 0:T], scalar=DC[:, l, 24 + m:25 + m],
                                                                         in1=X[:, m, :], op0=ALU.mult, op1=ALU.add),
                         reads=[pk, "DC", f"X{m}"], writes=[f"U{m}"])
                ln_apply(f"lnfg{l}", f"lnfb{l}")
                dump(f"xf{l}", X[:], [f"X{m}" for m in range(8)], ti)
                S.fence()
            for tb in range(TB):
                for half in range(2):
                    bi, pb, pk = bank()

                    def fo_(e, pb=pb, tb=tb, half=half):
                        for f4_ in range(4):
                            fc = half * 4 + f4_
                            ins = e.transpose(pb[:, f4_ * 128:(f4_ + 1) * 128], X[:, fc, tb * 128:(tb + 1) * 128], IDENT[:])
                        return ins
                    S.op("pe", fo_, reads=[f"X{m}" for m in range(8)] + ["IDENT"], writes=[pk])
                    S.op("act" if half else "dve",
                         (lambda e, pb=pb, tb=tb, half=half: e.copy(out=XIN[:, tb, half * 512:(half + 1) * 512], in_=pb[:, :])) if half else
                         (lambda e, pb=pb, tb=tb, half=half: e.tensor_copy(out=XIN[:, tb, half * 512:(half + 1) * 512], in_=pb[:, :])),
                         reads=[pk], writes=["XIN"])
            S.dma("sp", out[t0:t0 + T, :].rearrange("(b p) d -> p b d", p=128), XIN[:], reads=["XIN"])
        S.wait_all("sp")
        S.emit(block)
        build_nc.last_stats = {"nops": S.nops, "cnt": dict(S.cnt)}
    return nc


def kernel(**inputs):
    B, S_TOK, _ = inputs["x"].shape
    nc = build_nc(S_TOK, T=256, NL=2)
    in_maps = []
    for b in range(B):
        m = {"x": np.ascontiguousarray(inputs["x"][b], dtype=np.float32), "vecs": pack_vecs(inputs, b)}
        for nm in WEIGHT_NAMES:
            m[nm] = np.ascontiguousarray(inputs[nm], dtype=np.float32)
        in_maps.append(m)
    res = run_bass_kernel_spmd(nc, in_maps, core_ids=list(range(B)))
    return np.stack([np.asarray(r["out"]).reshape(S_TOK, D) for r in res.results], axis=0).astype(np.float32)
```

```python
import types
import numpy as np
from contextlib import ExitStack
import concourse.bass as bass
import concourse.mybir as mybir
from concourse.bass_utils import run_bass_kernel_spmd

F32 = mybir.dt.float32
BF16 = mybir.dt.bfloat16
I32 = mybir.dt.int32
ALU = mybir.AluOpType
AF = mybir.ActivationFunctionType

D = 1024
RW = 512
C_MAIN = 6400
NG = 37
GW = 4096
ALPHA = 4.0 ** 0.25
CDEC = float(np.exp(-0.5))
LN_EPS = 1e-5
GN_EPS = 64e-5
CONVW = 31
ENG_NAMES = ("pe", "act", "dve", "pool", "sp")


class Sched:
    def __init__(self, nc, sems, dma_sems):
        self.nc = nc
        self.sem = dict(zip(ENG_NAMES, sems))
        self.cnt = {e: 0 for e in ENG_NAMES}
        self.dma_pool = {q: list(v) for q, v in dma_sems.items()}
        self.dma_sems = [s for q in self.dma_pool for s in self.dma_pool[q]]
        self.dma_idx = {}
        i = 0
        for q in self.dma_pool:
            self.dma_idx[q] = list(range(i, i + len(self.dma_pool[q])))
            i += len(self.dma_pool[q])
        self.dma_cnt = [0] * len(self.dma_sems)
        self.dma_rr = {q: 0 for q in self.dma_pool}
        self.streams = {e: [] for e in ENG_NAMES}
        self.seen = {e: {} for e in ENG_NAMES}
        self.last_w = {}
        self.readers = {}
        self.nops = 0
        self.fence_dma = False

    def _deps(self, reads, writes):
        toks = []
        for k in reads:
            t = self.last_w.get(k)
            if t is not None:
                toks.append(t)
        for k in writes:
            t = self.last_w.get(k)
            if t is not None:
                toks.append(t)
            toks.extend(self.readers.get(k, ()))
        return toks

    def _waits_for(self, e, toks, skip_same_pe=True):
        need = {}
        for (sk, v) in toks:
            if sk == e and e == "pe" and skip_same_pe:
                continue
            if self.seen[e].get(sk, 0) >= v:
                continue
            if need.get(sk, 0) < v:
                need[sk] = v
        for sk, v in need.items():
            self.seen[e][sk] = v
        return list(need.items())

    def _commit(self, tok, reads, writes):
        for k in reads:
            self.readers.setdefault(k, []).append(tok)
        for k in writes:
            self.last_w[k] = tok
            self.readers[k] = []

    @staticmethod
    def _freeze(fn):
        if getattr(fn, "__closure__", None) is None:
            return fn
        cells = []
        for c in fn.__closure__:
            try:
                v = c.cell_contents
                if isinstance(v, types.FunctionType):
                    v = Sched._freeze(v)
                cells.append(types.CellType(v))
            except ValueError:
                cells.append(c)
        g = types.FunctionType(fn.__code__, fn.__globals__, fn.__name__, fn.__defaults__, tuple(cells))
        g.__kwdefaults__ = fn.__kwdefaults__
        return g

    def op(self, e, fn, reads=(), writes=()):
        fn = self._freeze(fn)
        reads = list(reads); writes = list(writes)
        toks = self._deps(reads, writes)
        waits = self._waits_for(e, toks)
        self.cnt[e] += 1
        tok = (e, self.cnt[e])
        self.streams[e].append((waits, fn, ("eng", e)))
        self._commit(tok, reads, writes)
        self.nops += 1
        return tok

    def dma(self, q, out, in_, reads=(), writes=(), **kw):
        toks = self._deps(reads, writes)
        i = self.dma_idx[q][self.dma_rr[q]]
        self.dma_rr[q] = (self.dma_rr[q] + 1) % len(self.dma_idx[q])
        sk = ("d", i)
        if self.dma_cnt[i] > 0:
            toks.append((sk, self.dma_cnt[i]))
        waits = self._waits_for(q, toks)
        self.dma_cnt[i] += 16
        tok = (sk, self.dma_cnt[i])

        def fn(eng, out=out, in_=in_, kw=kw):
            return eng.dma_start(out=out, in_=in_, **kw)
        self.streams[q].append((waits, fn, ("dma", i)))
        self._commit(tok, reads, writes)
        return tok

    def fence(self, engines=("pe", "act", "dve", "pool")):
        for e in engines:
            toks = [(en, self.cnt[en]) for en in engines if self.cnt[en] > 0]
            if self.fence_dma:
                toks += [(("d", i), v) for i, v in enumerate(self.dma_cnt) if v > 0]
            waits = self._waits_for(e, toks, skip_same_pe=False)
            if waits:
                self.streams[e].append((waits, None, None))

    def wait_all(self, e):
        toks = [(en, self.cnt[en]) for en in ENG_NAMES if self.cnt[en] > 0 and en != e]
        toks += [(("d", i), v) for i, v in enumerate(self.dma_cnt) if v > 0]
        waits = self._waits_for(e, toks)
        self.streams[e].append((waits, None, None))

    def _semh(self, sk):
        if isinstance(sk, tuple):
            return self.dma_sems[sk[1]]
        return self.sem[sk]

    def emit(self, block):
        eng_of = {"pe": self.nc.tensor, "act": self.nc.scalar, "dve": self.nc.vector,
                  "pool": self.nc.gpsimd, "sp": self.nc.sync}

        def mk(e):
            def body(eng):
                for waits, fn, sig in self.streams[e]:
                    for sk, v in waits:
                        eng.wait_ge(self._semh(sk), v)
                    if fn is None:
                        continue
                    ins = fn(eng)
                    if sig[0] == "eng":
                        ins.then_inc(self.sem[e], 1)
                    else:
                        ins.then_inc(self.dma_sems[sig[1]], 16)
            return body
        block.tensor(mk("pe"))
        block.scalar(mk("act"))
        block.vector(mk("dve"))
        block.gpsimd(mk("pool"))
        block.sync(mk("sp"))


def vec_layout():
    off = {}
    r = 0

    def add(name, n):
        nonlocal r
        off[name] = r
        r += n
    add("c", 8)
    for l in range(2):
        for nm, n in (("ada_b", 48), ("mu", 14), ("w0", 4), ("a0", 4), ("kk", 4), ("ka", 4), ("rk", 4),
                      ("gng", 4), ("gnb", 4), ("cvb", 4), ("cvg", 4), ("cvbb", 4), ("plsc", 4),
                      ("lnmg", 8), ("lnmb", 8), ("lnfg", 8), ("lnfb", 8), ("cvw", 124)):
            add(f"{nm}{l}", n)
    add("v0", 4)
    add("muv", 1)
    return off, r


VOFF, NVROWS = vec_layout()
NVB = (NVROWS + 127) // 128


def pack_vecs(inp, b):
    P = np.zeros((NVB * 128, 128), np.float32)

    def put(name, arr):
        a = np.asarray(arr, np.float32).reshape(-1)
        n = (a.size + 127) // 128
        buf = np.zeros(n * 128, np.float32)
        buf[:a.size] = a
        P[VOFF[name]:VOFF[name] + n] = buf.reshape(n, 128)
    put("c", inp["c"][b])
    for l in range(2):
        put(f"ada_b{l}", inp["ada_b"][l]); put(f"mu{l}", inp["shift_mu"][l])
        put(f"w0{l}", inp["rw_w0"][l]); put(f"a0{l}", inp["rw_a0"][l])
        put(f"kk{l}", inp["rw_kk"][l]); put(f"ka{l}", inp["rw_ka"][l]); put(f"rk{l}", inp["rw_rk"][l])
        put(f"gng{l}", inp["rw_gn_g"][l]); put(f"gnb{l}", inp["rw_gn_b"][l])
        put(f"cvb{l}", inp["cv_b"][l]); put(f"cvg{l}", inp["cv_ln_g"][l]); put(f"cvbb{l}", inp["cv_ln_b"][l])
        put(f"plsc{l}", inp["pl_scale"][l])
        put(f"lnmg{l}", inp["ln_m_g"][l]); put(f"lnmb{l}", inp["ln_m_b"][l])
        put(f"lnfg{l}", inp["ln_f_g"][l]); put(f"lnfb{l}", inp["ln_f_b"][l])
        put(f"cvw{l}", inp["cv_w"][l])
    put("v0", inp["rw_v0"][0])
    put("muv", inp["shift_mu_vres"][0])
    return P


WEIGHT_NAMES = ("ada_w", "w_in", "w_in_vres", "rw_w2", "rw_a2", "rw_g2", "rw_v2", "rw_wo", "cv_wo",
                "pl_wo", "pl_w", "w_out", "mlp_w1", "mlp_w2")
WEIGHT_SHAPES = {"ada_w": [2, 1024, 6144], "w_in": [2, 1024, 6400], "w_in_vres": [1, 1024, 32],
                 "rw_w2": [2, 64, 512], "rw_a2": [2, 64, 512], "rw_g2": [2, 128, 512], "rw_v2": [1, 32, 512],
                 "rw_wo": [2, 512, 1024], "cv_wo": [2, 512, 1024], "pl_wo": [2, 512, 1024],
                 "pl_w": [2, 4, 128, 128], "w_out": [2, 1024, 1024], "mlp_w1": [2, 1024, 4096],
                 "mlp_w2": [2, 4096, 1024]}


def build_nc(S_TOK, T=256, NL=2, debug=None, REC_BF16=True, NEU_BF16=True, MM2_F32=False, ST_F32=False):
    assert S_TOK % T == 0 and T % 128 == 0 and T <= 512
    NT = S_TOK // T
    NCH = T // 64
    TB = T // 128
    nc = bass.Bass("TRN2", target_bir_lowering=False)
    dr = {}
    dr["x"] = nc.dram_tensor("x", [S_TOK, D], F32, kind="ExternalInput").ap()
    dr["vecs"] = nc.dram_tensor("vecs", [NVB * 128, 128], F32, kind="ExternalInput").ap()
    for nm in WEIGHT_NAMES:
        dr[nm] = nc.dram_tensor(nm, WEIGHT_SHAPES[nm], F32, kind="ExternalInput").ap()
    out = nc.dram_tensor("out", [S_TOK, D], F32, kind="ExternalOutput").ap()
    wsc = nc.dram_tensor("wsc", [2, NG, 128, GW], BF16, kind="Internal").ap()
    dbg = {}
    if debug:
        for nm, shp in debug.items():
            dbg[nm] = nc.dram_tensor(nm, list(shp), F32, kind="ExternalOutput").ap()

    with ExitStack() as es:
        def sb(name, shape, dt=F32):
            return es.enter_context(nc.sbuf_tensor(name, list(shape), dt))

        def pst(name, shape, dt=F32):
            return es.enter_context(nc.psum_tensor(name, list(shape), dt))

        X = sb("X", [128, 8, T])
        H = sb("H", [128, 8, T], BF16)
        WS = sb("WS", [128, 4, GW], BF16)
        VF = sb("VF", [128, 4, T])
        VC = sb("VC", [128, NVB * 128])
        PK = sb("PK", [128, NVB, 128])
        ADA = sb("ADA", [128, 2, 48])
        DC = sb("DC", [128, 2, 64])
        IDENT = sb("IDENT", [128, 128])
        I2 = sb("I2", [128, 64])
        MSU = sb("MSU", [128, 128]); MIU = sb("MIU", [128, 128]); MSL = sb("MSL", [128, 128])
        OBD = sb("OBD", [128, 128]); OBD64 = sb("OBD64", [128, 128])
        O512 = sb("O512", [128, 128]); O1024 = sb("O1024", [128, 128])
        RESETM = sb("RESETM", [128, T])
        EPS = sb("EPS", [128, 4])
        RC16 = sb("RC16", [128, 4, 16])
        IOTI = sb("IOTI", [128, 16], I32)
        W2P = sb("W2P", [128, 2, 512], BF16); A2P = sb("A2P", [128, 2, 512], BF16)
        G2 = sb("G2", [128, 2, 512], BF16); V2 = sb("V2", [32, 512], BF16)
        PLW = sb("PLW", [128, 2, 4, 128], BF16)
        CARRY = sb("CARRY", [128, 2, 16])
        HALO = sb("HALO", [128, 2, 4, 30])
        PHALO = sb("PHALO", [128, 2, 4, 16])
        ST = sb("ST", [128, 2, 4, 64])
        CONDT = sb("CONDT", [128, 8])
        XIN = sb("XIN", [128, TB, D])
        YBR = sb("YBR", [128, 3, 4, T], BF16)
        AW = 44 * T + 9600
        AR = sb("AR", [128, AW])

        class Alloc:
            def __init__(self, base=0):
                self.o = base

            def f(self, n, shape=None):
                ap = AR[:, self.o:self.o + n]
                self.o += n
                assert self.o <= AW, (self.o, AW)
                return ap

            def t3(self, a, b):
                return self.f(a * b).rearrange("p (a b) -> p a b", a=a)

            def b3(self, a, b):
                n = (a * b + 1) // 2
                return self.f(n).bitcast(BF16).rearrange("p (a b) -> p a b", a=a)

        A = Alloc()
        ZR = A.t3(4, T); ZK = A.t3(4, T); ZV = A.t3(4, T); YF = A.t3(4, T)
        TA = A.f(T); TBm = A.f(T); TC = A.f(T); TD = A.f(T)
        mark_dead1 = A.o
        SW = A.t3(4, T); LC = A.t3(4, T); PINV = A.t3(4, T); AH = A.t3(4, T); KKN = A.t3(4, T)
        mark_dead1_end = A.o
        RAWS = [A.f(T), A.f(T), A.f(T)]; ZTMP = A.f(T)
        raw_rr = [0]
        T12 = A.b3(1, T)[:, 0, :]; LG = A.b3(1, T)[:, 0, :]; LV = A.b3(1, T)[:, 0, :]
        rt3 = A.b3 if REC_BF16 else A.t3
        nt3 = A.b3 if (REC_BF16 and NEU_BF16) else A.t3
        BDA = rt3(4, 128); BDB = rt3(4, 128); BDK = rt3(4, 128); BDR = rt3(4, 128); BDV = rt3(4, 128)
        BDA2 = nt3(4, 128); BDB2 = nt3(4, 128)
        MN = [nt3(4, 128), nt3(4, 128)]; MNT = [nt3(4, 128), nt3(4, 128)]
        MKA = rt3(4, 128); MBR = rt3(4, 128); MKR = rt3(4, 128); GM = nt3(4, 128); GMr = rt3(4, 128)
        BDBT = rt3(4, 128); BDKT = rt3(4, 128)
        VT = rt3(4, 64); XTt = (A.t3 if MM2_F32 else rt3)(4, 64); UT = rt3(4, 64)
        YTBD = rt3(4, 128)
        STb = rt3(4, 64); TMPS = A.t3(4, 64)
        PC = A.t3(4, 8)
        arena_mixer_end = A.o
        B2 = Alloc(mark_dead1)
        HGLU = B2.t3(4, T + 30); ACC = B2.t3(4, T); CVV = B2.t3(4, T)
        PP = B2.t3(4, T + 16); SA = B2.f(T + 16); SBm = B2.f(T + 16)
        PLB = B2.b3(1, T)[:, 0, :]
        assert B2.o <= mark_dead1_end, (B2.o, mark_dead1_end)
        B3 = Alloc(0)
        MERG = B3.t3(4, T); MERGED = B3.b3(8, T); SIG = B3.f(T); TMG = B3.f(T)
        assert B3.o <= 12 * T
        B4 = Alloc(mark_dead1)
        U = B4.t3(8, T); RTMP = B4.f(T); RTMP2 = B4.f(T)
        assert B4.o <= mark_dead1_end, (B4.o, AW)
        H1 = Alloc(0).b3(32, T)
        B5 = Alloc(0)
        AWS = [B5.t3(8, 512), B5.t3(8, 512)]
        assert B5.o <= AW

        PSB = [pst(f"PS{i}", [128, 512]) for i in range(8)]
        ps_rr = [0]

        def bank():
            i = ps_rr[0]
            ps_rr[0] = (i + 1) % 8
            return i, PSB[i], f"PS{i}"

        sems = [es.enter_context(nc.semaphore(f"s_{e}")) for e in ENG_NAMES]
        dsems = {"sp": [es.enter_context(nc.semaphore(f"dsp{i}")) for i in range(8)],
                 "pool": [es.enter_context(nc.semaphore(f"dpl{i}")) for i in range(4)]}
        block = es.enter_context(nc.Block())
        S = Sched(nc, sems, dsems)
        S.fence_dma = bool(debug)

        def vc(name, col=0):
            c0 = VOFF[name] + col
            return VC[:, c0:c0 + 1]

        def conv_dma(l, g, src_ap, kc, n, col0=0, ncols=None):
            ncols = n if ncols is None else ncols
            dst = wsc[l, g][:, 0:kc * n].rearrange("p (kc n) -> p kc n", kc=kc)[:, :, col0:col0 + ncols]
            S.dma("pool", dst, src_ap, writes=[f"wsc{l}_{g}"])

        def group_src(l, g):
            w_in = dr["w_in"][l].rearrange("(kc p) n -> p kc n", p=128)
            if g < 6:
                return [(w_in[:, :, g * 512:(g + 1) * 512], 8, 512, 0, 512)]
            if g == 6:
                r = [(w_in[:, :, 3072:3328], 8, 288, 0, 256)]
                if l >= 1:
                    r.append((dr["w_in_vres"][l - 1].rearrange("(kc p) n -> p kc n", p=128), 8, 288, 256, 32))
                return r
            if g < 13:
                i = g - 7
                b, q = i // 2, i % 2
                c0 = 3328 + (8 * b + 4 * q) * 128
                return [(w_in[:, :, c0:c0 + 512], 8, 512, 0, 512)]
            if g < 19:
                i = g - 13
                b, q = i // 2, i % 2
                w = dr[("rw_wo", "cv_wo", "pl_wo")[b]][l].rearrange("(kc p) n -> p kc n", p=128)
                return [(w[:, :, q * 512:(q + 1) * 512], 4, 512, 0, 512)]
            if g < 21:
                q = g - 19
                w = dr["w_out"][l].rearrange("(kc p) n -> p kc n", p=128)
                return [(w[:, :, q * 512:(q + 1) * 512], 8, 512, 0, 512)]
            if g < 29:
                j = g - 21
                w = dr["mlp_w1"][l].rearrange("(kc p) n -> p kc n", p=128)
                return [(w[:, :, j * 512:(j + 1) * 512], 8, 512, 0, 512)]
            m = g - 29
            w = dr["mlp_w2"][l].rearrange("(kc p) n -> p kc n", p=128)
            return [(w[:, :, m * 128:(m + 1) * 128], 32, 128, 0, 128)]

        for l in range(NL):
            for g in range(NG):
                for (src, kc, n, c0, ncol) in group_src(l, g):
                    conv_dma(l, g, src, kc, n, c0, ncol)

        S.op("pool", lambda e: e.memset(IDENT[:], 0.0), writes=["IDENT"])
        S.op("pool", lambda e: e.affine_select(out=IDENT[:], in_=IDENT[:], pattern=[[-1, 128]], compare_op=ALU.not_equal,
                                               fill=1.0, base=0, channel_multiplier=1), reads=["IDENT"], writes=["IDENT"])
        S.op("pool", lambda e: e.tensor_tensor(out=I2[:], in0=IDENT[:, 0:64], in1=IDENT[:, 64:128], op=ALU.add),
             reads=["IDENT"], writes=["I2"])
        IDR = sb("IDR", [128, 128], BF16 if REC_BF16 else F32)
        I2R = sb("I2R", [128, 64], BF16 if REC_BF16 else F32)
        S.op("pool", lambda e: e.tensor_copy(out=IDR[:], in_=IDENT[:]), reads=["IDENT"], writes=["IDR"])
        S.op("pool", lambda e: e.tensor_copy(out=I2R[:], in_=I2[:]), reads=["I2"], writes=["I2R"])

        def tri(M, key, cmp_op, sgn=1):
            S.op("pool", lambda e: e.memset(M[:], 1.0), writes=[key])
            S.op("pool", lambda e: e.affine_select(out=M[:], in_=M[:], pattern=[[sgn, 128]], compare_op=cmp_op,
                                                   fill=0.0, base=0, channel_multiplier=-sgn), reads=[key], writes=[key])
            S.op("pool", lambda e: e.memset(M[0:64, 64:128], 0.0), reads=[key], writes=[key])
            S.op("pool", lambda e: e.memset(M[64:128, 0:64], 0.0), reads=[key], writes=[key])
        tri(MSU, "MSU", ALU.is_gt)
        tri(MIU, "MIU", ALU.is_ge)
        tri(MSL, "MSL", ALU.is_gt, -1)
        for M, key, val in ((OBD, "OBD", 1.0), (OBD64, "OBD64", 1.0 / 64)):
            S.op("pool", lambda e, M=M, val=val: e.memset(M[:], val), writes=[key])
            S.op("pool", lambda e, M=M: e.memset(M[0:64, 64:128], 0.0), reads=[key], writes=[key])
            S.op("pool", lambda e, M=M: e.memset(M[64:128, 0:64], 0.0), reads=[key], writes=[key])
        S.op("pool", lambda e: e.memset(O512[:], 1.0 / 512), writes=["O512"])
        S.op("pool", lambda e: e.memset(O1024[:], 1.0 / 1024), writes=["O1024"])
        S.op("pool", lambda e: e.memset(RESETM[:], 1.0), writes=["RESETM"])
        S.op("pool", lambda e: e.memset(RESETM[:].rearrange("p (c j) -> p c j", j=64)[:, :, 0:1], 0.0),
             reads=["RESETM"], writes=["RESETM"])
        for i, v in enumerate((GN_EPS, LN_EPS / (ALPHA * ALPHA), LN_EPS, 0.0)):
            S.op("pool", lambda e, i=i, v=v: e.memset(EPS[:, i:i + 1], float(v)), reads=["EPS"], writes=["EPS"])
        S.op("pool", lambda e: e.iota(IOTI[:], pattern=[[1, 16]], base=1, channel_multiplier=0), writes=["IOTI"])
        for g in range(4):
            S.op("pool", lambda e, g=g: e.tensor_copy(out=RC16[:, g, :], in_=IOTI[:]), reads=["IOTI", "RC16"], writes=["RC16"])
            S.op("pool", lambda e, g=g: e.tensor_scalar(out=RC16[:, g, :], in0=RC16[:, g, :], scalar1=float(2 ** (g + 1)), scalar2=None,
                                                      op0=ALU.min), reads=["RC16"], writes=["RC16"])
        S.op("dve", lambda e: e.reciprocal(out=RC16[:], in_=RC16[:]), reads=["RC16"], writes=["RC16"])
        for nm, Tn in (("CARRY", CARRY), ("HALO", HALO), ("PHALO", PHALO), ("ST", ST)):
            S.op("pool", lambda e, Tn=Tn: e.memset(Tn[:], 0.0), writes=[nm])
        for nm, Tn in (("BDA", BDA), ("BDB", BDB), ("BDK", BDK), ("BDR", BDR), ("BDV", BDV), ("YTBD", YTBD), ("BDA2", BDA2), ("BDB2", BDB2), ("STb", STb)):
            S.op("pool", lambda e, Tn=Tn: e.memset(Tn, 0.0), writes=[nm])
        S.op("pool", lambda e: e.memset(W2P[:], 0.0), writes=["W2P"])
        S.op("pool", lambda e: e.memset(A2P[:], 0.0), writes=["A2P"])
        for l in range(NL):
            S.dma("pool", W2P[0:64, l, :], dr["rw_w2"][l], reads=["W2P"], writes=["W2P"])
            S.dma("pool", A2P[64:128, l, :], dr["rw_a2"][l], reads=["A2P"], writes=["A2P"])
            S.dma("pool", G2[:, l, :], dr["rw_g2"][l], writes=["G2"])
            S.dma("pool", PLW[:, l, :, :], dr["pl_w"][l].rearrange("g c d -> c g d"), writes=["PLW"])
        if NL > 1:
            S.dma("pool", V2[:], dr["rw_v2"][0], writes=["V2"])
        S.dma("sp", PK[:], dr["vecs"].rearrange("(b p) n -> p b n", p=128), writes=["PK"])
        for b in range(NVB):
            bi, pb, pk = bank()
            S.op("pe", lambda e, b=b, pb=pb: e.transpose(pb[:, 0:128], PK[:, b, :], IDENT[:]),
                 reads=["PK", "IDENT"], writes=[pk])
            S.op("act", lambda e, b=b, pb=pb: e.copy(out=VC[:, b * 128:(b + 1) * 128], in_=pb[:, 0:128]),
                 reads=[pk], writes=["VC"])
        S.op("act", lambda e: e.activation(out=CONDT[:], in_=VC[:, VOFF["c"]:VOFF["c"] + 8], func=AF.Silu),
             reads=["VC"], writes=["CONDT"])
        for l in range(NL):
            bi, pb, pk = bank()
            for gq in range(12):
                slot = gq % 2
                S.dma("sp", AWS[slot], dr["ada_w"][l].rearrange("(kc p) n -> p kc n", p=128)[:, :, gq * 512:(gq + 1) * 512],
                      writes=[f"AWS{slot}"])
                for mi in range(4):
                    j = gq * 4 + mi

                    def fn(e, slot=slot, mi=mi, j=j, pb=pb):
                        for kc in range(8):
                            ins = e.matmul(pb[:, j:j + 1], lhsT=AWS[slot][:, kc, mi * 128:(mi + 1) * 128],
                                           rhs=CONDT[:, kc:kc + 1], start=(kc == 0), stop=(kc == 7))
                        return ins
                    S.op("pe", fn, reads=[f"AWS{slot}", "CONDT"], writes=[pk])
            S.op("dve", lambda e, l=l, pb=pb: e.tensor_tensor(out=ADA[:, l, :], in0=pb[:, 0:48],
                                                             in1=VC[:, VOFF[f"ada_b{l}"]:VOFF[f"ada_b{l}"] + 48], op=ALU.add),
                 reads=[pk, "VC"], writes=["ADA"])
        for l in range(NL):
            S.op("dve", lambda e, l=l: e.tensor_scalar(out=DC[:, l, 0:8], in0=ADA[:, l, 8:16], scalar1=1.0, scalar2=None, op0=ALU.add),
                 reads=["ADA"], writes=["DC"])
            S.op("dve", lambda e, l=l: e.tensor_scalar(out=DC[:, l, 8:16], in0=ADA[:, l, 32:40], scalar1=1.0, scalar2=None, op0=ALU.add),
                 reads=["ADA", "DC"], writes=["DC"])
            S.op("dve", lambda e, l=l: e.tensor_scalar(out=DC[:, l, 16:24], in0=ADA[:, l, 16:24], scalar1=1.0 / ALPHA, scalar2=None, op0=ALU.mult),
                 reads=["ADA", "DC"], writes=["DC"])
            S.op("dve", lambda e, l=l: e.tensor_scalar(out=DC[:, l, 24:32], in0=ADA[:, l, 40:48], scalar1=1.0 / ALPHA, scalar2=None, op0=ALU.mult),
                 reads=["ADA", "DC"], writes=["DC"])
            m0 = VOFF[f"mu{l}"]
            S.op("dve", lambda e, l=l, m0=m0: e.tensor_scalar(out=DC[:, l, 32:46], in0=VC[:, m0:m0 + 14], scalar1=-1.0, scalar2=1.0,
                                                           op0=ALU.mult, op1=ALU.add), reads=["VC", "DC"], writes=["DC"])
            mv0 = VOFF["muv"]
            S.op("dve", lambda e, l=l, mv0=mv0: e.tensor_scalar(out=DC[:, l, 46:47], in0=VC[:, mv0:mv0 + 1], scalar1=-1.0, scalar2=1.0,
                                                             op0=ALU.mult, op1=ALU.add), reads=["VC", "DC"], writes=["DC"])
            k0 = VOFF[f"ka{l}"]
            S.op("dve", lambda e, l=l, k0=k0: e.tensor_scalar(out=DC[:, l, 48:52], in0=VC[:, k0:k0 + 4], scalar1=-1.0, scalar2=1.0,
                                                           op0=ALU.mult, op1=ALU.add), reads=["VC", "DC"], writes=["DC"])
        S.fence()

        ws_rr = [0]

        def wload(l, g):
            s = ws_rr[0]
            ws_rr[0] = (s + 1) % 4
            if g == 6:
                kc, n, nv = 8, 288, (288 if l >= 1 else 256)
            elif 13 <= g < 19:
                kc, n, nv = 4, 512, 512
            elif g >= 29:
                kc, n, nv = 32, 128, 128
            else:
                kc, n, nv = 8, 512, 512
            dst = WS[:, s, 0:kc * n].rearrange("p (kc n) -> p kc n", kc=kc)[:, :, 0:nv]
            src = wsc[l, g][:, 0:kc * n].rearrange("p (kc n) -> p kc n", kc=kc)[:, :, 0:nv]
            S.dma("sp", dst, src, reads=[f"wsc{l}_{g}"], writes=[f"WS{s}"])
            return s

        def wview(s, kc, n):
            return WS[:, s, 0:kc * n].rearrange("p (kc n) -> p kc n", kc=kc)

        def dump(name, ap, keys, idx=None):
            if name in dbg:
                dst = dbg[name] if idx is None else dbg[name][idx]
                S.dma("sp", dst, ap, reads=keys)

        def ln_stats(srcs, src_keys, sq_eng_out, ONESM, eps_col):
            n = len(srcs)
            bi1, pb1, pk1 = bank()
            bi2, pb2, pk2 = bank()

            def fm(e):
                for i, s_ in enumerate(srcs):
                    ins = e.matmul(pb1[:, 0:T], lhsT=ONESM[:], rhs=s_, start=(i == 0), stop=(i == n - 1))
                return ins
            S.op("pe", fm, reads=src_keys, writes=[pk1])
            for i, s_ in enumerate(srcs):
                S.op("act", lambda e, s_=s_: e.activation(out=TC, in_=s_, func=AF.Square), reads=[src_keys[i]], writes=["TC"])
                S.op("pe", lambda e, i=i: e.matmul(pb2[:, 0:T], lhsT=ONESM[:], rhs=TC, start=(i == 0), stop=(i == n - 1)),
                     reads=["TC"], writes=[pk2])
            S.op("act", lambda e: e.copy(out=TA, in_=pb1[:, 0:T]), reads=[pk1], writes=["TA"])
            S.op("act", lambda e: e.activation(out=TD, in_=pb1[:, 0:T], func=AF.Square), reads=[pk1], writes=["TD"])
            S.op("dve", lambda e: e.tensor_tensor(out=TBm, in0=pb2[:, 0:T], in1=TD, op=ALU.subtract), reads=[pk2, "TD"], writes=["TB"])
            S.op("dve", lambda e: e.tensor_scalar(out=TBm, in0=TBm, scalar1=0.0, scalar2=None, op0=ALU.max), reads=["TB"], writes=["TB"])
            S.op("act", lambda e: e.activation(out=TBm, in_=TBm, func=AF.Sqrt, bias=EPS[:, eps_col:eps_col + 1], scale=1.0),
                 reads=["TB", "EPS"], writes=["TB"])
            S.op("dve", lambda e: e.reciprocal(out=TBm, in_=TBm), reads=["TB"], writes=["TB"])
            return TA, TBm

        def modulate(l, which):
            o_sc = 0 if which == "m" else 8
            o_sh = 0 if which == "m" else 24
            for m in range(8):
                S.op("act", lambda e, m=m: e.activation(out=H[:, m, :], in_=X[:, m, :], func=AF.Identity,
                                                        bias=ADA[:, l, o_sh + m:o_sh + m + 1], scale=DC[:, l, o_sc + m:o_sc + m + 1]),
                     reads=[f"X{m}", "ADA", "DC"], writes=[f"H{m}"])

        def residual_ln(l, which, get_ps):
            o_gt = 16 if which == "m" else 24
            gname, bname = (f"lnmg{l}", f"lnmb{l}") if which == "m" else (f"lnfg{l}", f"lnfb{l}")
            for m in range(8):
                pb, pk = get_ps(m)
                S.op("dve", lambda e, m=m, pb=pb: e.scalar_tensor_tensor(out=U[:, m, :], in0=pb[:, 0:T], scalar=DC[:, l, o_gt + m:o_gt + m + 1],
                                                                     in1=X[:, m, :], op0=ALU.mult, op1=ALU.add),
                     reads=[pk, "DC", f"X{m}"], writes=[f"U{m}"])
            mean, rstd = ln_stats([U[:, m, :] for m in range(8)], [f"U{m}" for m in range(8)], None, O1024, 1)
            for m in range(8):
                S.op("dve", lambda e, m=m: e.tensor_tensor(out=U[:, m, :], in0=U[:, m, :], in1=mean, op=ALU.subtract),
                     reads=[f"U{m}", "TA"], writes=[f"U{m}"])
                S.op("pool", lambda e, m=m: e.tensor_tensor(out=U[:, m, :], in0=U[:, m, :], in1=rstd, op=ALU.mult),
                     reads=[f"U{m}", "TB"], writes=[f"U{m}"])
                S.op("dve", lambda e, m=m: e.tensor_scalar(out=X[:, m, :], in0=U[:, m, :], scalar1=vc(gname, m), scalar2=vc(bname, m),
                                                         op0=ALU.mult, op1=ALU.add), reads=[f"U{m}", "VC"], writes=[f"X{m}"])

        for ti in range(NT):
            t0 = ti * T
            S.dma("sp", XIN[:], dr["x"][t0:t0 + T, :].rearrange("(b p) d -> p b d", p=128), writes=["XIN"])
            for fc in range(8):
                bi, pb, pk = bank()

                def ft(e, fc=fc, pb=pb):
                    for tb in range(TB):
                        ins = e.transpose(pb[:, tb * 128:(tb + 1) * 128], XIN[:, tb, fc * 128:(fc + 1) * 128], IDENT[:])
                    return ins
                S.op("pe", ft, reads=["XIN", "IDENT"], writes=[pk])
                S.op("act" if fc % 2 else "dve",
                     (lambda e, fc=fc, pb=pb: e.copy(out=X[:, fc, :], in_=pb[:, 0:T])) if fc % 2 else
                     (lambda e, fc=fc, pb=pb: e.tensor_copy(out=X[:, fc, :], in_=pb[:, 0:T])),
                     reads=[pk], writes=[f"X{fc}"])
            for l in range(NL):
                modulate(l, "m")
                slot_of = {}

                def proj_chunk(j, ncols=128, col_in_group=None):
                    g = j // 4 if j < 24 else 6
                    if g not in slot_of:
                        slot_of[g] = wload(l, g)
                    s = slot_of[g]
                    n = 512 if g < 6 else 288
                    wv = wview(s, 8, n)
                    c0 = (j % 4) * 128 if j < 24 else (j - 24) * 128
                    bi, pb, pk = bank()

                    def fn(e, pb=pb, wv=wv, c0=c0, ncols=ncols):
                        for kc in range(8):
                            ins = e.matmul(pb[0:ncols, 0:T], lhsT=wv[:, kc, c0:c0 + ncols], rhs=H[:, kc, :],
                                           start=(kc == 0), stop=(kc == 7))
                        return ins
                    S.op("pe", fn, reads=[f"WS{s}"] + [f"H{m}" for m in range(8)], writes=[pk])
                    return pb, pk

                def token_shift(pb, pk, cidx, mu_ap, omu_ap, dst, dst_keys, np_=128):
                    k = raw_rr[0]
                    raw_rr[0] = (k + 1) % 3
                    Rk, rk_key = RAWS[k], f"RAW{k}"
                    ck = f"CARRY{l}_{cidx}"
                    S.op("act", lambda e: e.activation(out=Rk[0:np_, 0:T], in_=pb[0:np_, 0:T], func=AF.Identity, scale=omu_ap),
                         reads=[pk, "DC"], writes=[rk_key])
                    S.op("dve", lambda e: e.scalar_tensor_tensor(out=dst[:, 1:T], in0=pb[0:np_, 0:T - 1], scalar=mu_ap, in1=Rk[0:np_, 1:T],
                                                                 op0=ALU.mult, op1=ALU.add), reads=[pk, rk_key, "VC"], writes=dst_keys)
                    S.op("dve", lambda e: e.scalar_tensor_tensor(out=dst[:, 0:1], in0=CARRY[0:np_, l, cidx:cidx + 1], scalar=mu_ap, in1=Rk[0:np_, 0:1],
                                                                 op0=ALU.mult, op1=ALU.add), reads=[ck, rk_key, "VC"] + list(dst_keys), writes=dst_keys)
                    S.op("act", lambda e: e.copy(out=CARRY[0:np_, l, cidx:cidx + 1], in_=pb[0:np_, T - 1:T]), reads=[pk, ck], writes=[ck])

                for j in range(26):
                    pb, pk = proj_chunk(j)
                    if j < 12:
                        dstT = (ZR, ZK, ZV)[j // 4]
                        nm = ("ZR", "ZK", "ZV")[j // 4]
                        token_shift(pb, pk, j, vc(f"mu{l}", j), DC[:, l, 32 + j:33 + j], dstT[:, j % 4, :], [f"{nm}{j % 4}"])
                    elif j == 12:
                        token_shift(pb, pk, j, vc(f"mu{l}", j), DC[:, l, 32 + j:33 + j], ZTMP, ["ZTMP"])
                        S.op("act", lambda e: e.activation(out=T12[0:64, :], in_=ZTMP[0:64, :], func=AF.Tanh), reads=["ZTMP"], writes=["T12"])
                        S.op("dve", lambda e: e.tensor_copy(out=T12[64:128, :], in_=ZTMP[64:128, :]), reads=["ZTMP", "T12"], writes=["T12"])
                    elif j == 13:
                        token_shift(pb, pk, j, vc(f"mu{l}", j), DC[:, l, 32 + j:33 + j], ZTMP, ["ZTMP"])
                        S.op("act", lambda e: e.activation(out=LG, in_=ZTMP, func=AF.Sigmoid), reads=["ZTMP"], writes=["LG"])
                    elif j < 18:
                        cc = j - 14
                        S.op("act", lambda e, cc=cc, pb=pb: e.copy(out=CVV[:, cc, :], in_=pb[:, 0:T]), reads=[pk], writes=[f"CVV{cc}"])
                    elif j < 22:
                        cc = j - 18
                        S.op("act", lambda e, pb=pb: e.activation(out=TA, in_=pb[:, 0:T], func=AF.Sigmoid), reads=[pk], writes=["TA"])
                        if cc == 0:
                            S.op("pool", lambda e: e.tensor_copy(out=HGLU[:, :, 0:30], in_=HALO[:, l, :, :]), reads=["HALO"],
                                 writes=[f"HGLU{c_}" for c_ in range(4)])
                        S.op("dve", lambda e, cc=cc: e.tensor_tensor(out=HGLU[:, cc, 30:30 + T], in0=CVV[:, cc, :], in1=TA, op=ALU.mult),
                             reads=[f"CVV{cc}", "TA"], writes=[f"HGLU{cc}"])
                    else:
                        gg = j - 22
                        if gg == 0:
                            S.op("pool", lambda e: e.tensor_copy(out=PP[:, :, 0:16], in_=PHALO[:, l, :, :]), reads=["PHALO"],
                                 writes=[f"PP{c_}" for c_ in range(4)])
                        S.op("act", lambda e, gg=gg, pb=pb: e.copy(out=PP[:, gg, 16:16 + T], in_=pb[:, 0:T]), reads=[pk], writes=[f"PP{gg}"])
                if l >= 1:
                    s = slot_of[6]
                    wv = wview(s, 8, 288)
                    bi, pb, pk = bank()

                    def fnv(e, pb=pb, wv=wv):
                        for kc in range(8):
                            ins = e.matmul(pb[0:32, 0:T], lhsT=wv[:, kc, 256:288], rhs=H[:, kc, :], start=(kc == 0), stop=(kc == 7))
                        return ins
                    S.op("pe", fnv, reads=[f"WS{s}"] + [f"H{m}" for m in range(8)], writes=[pk])
                    token_shift(pb, pk, 14, VC[0:32, VOFF["muv"]:VOFF["muv"] + 1], DC[0:32, l, 46:47], ZTMP[0:32, :], ["ZTMP"], np_=32)
                    S.op("act", lambda e: e.copy(out=LV[0:32, :], in_=ZTMP[0:32, :]), reads=["ZTMP"], writes=["LV"])
                dump(f"z_r{l}", ZR, ["ZR0", "ZR1", "ZR2", "ZR3"], ti)
                dump(f"z_k{l}", ZK, ["ZK0", "ZK1", "ZK2", "ZK3"], ti)
                dump(f"z_v{l}", ZV, ["ZV0", "ZV1", "ZV2", "ZV3"], ti)

                for cc in range(4):
                    eng = "dve" if cc % 2 == 0 else "pool"
                    wbase = VOFF[f"cvw{l}"]
                    S.op(eng, lambda e, cc=cc: e.tensor_scalar(out=ACC[:, cc, :], in0=HGLU[:, cc, 0:T], scalar1=VC[:, wbase + cc:wbase + cc + 1],
                                                              scalar2=vc(f"cvb{l}", cc), op0=ALU.mult, op1=ALU.add),
                         reads=[f"HGLU{cc}", "VC"], writes=[f"ACC{cc}"])
                    for jt in range(1, CONVW):
                        if eng == "dve":
                            S.op(eng, lambda e, cc=cc, jt=jt: e.scalar_tensor_tensor(
                                out=ACC[:, cc, :], in0=HGLU[:, cc, jt:jt + T], scalar=VC[:, wbase + jt * 4 + cc:wbase + jt * 4 + cc + 1],
                                in1=ACC[:, cc, :], op0=ALU.mult, op1=ALU.add), reads=[f"HGLU{cc}", "VC", f"ACC{cc}"], writes=[f"ACC{cc}"])
                        else:
                            ct, ck = ((SA, "SA"), (SBm, "SB"))[jt % 2]
                            S.op("act", lambda e, cc=cc, jt=jt, ct=ct: e.activation(
                                out=ct[:, 0:T], in_=HGLU[:, cc, jt:jt + T], func=AF.Identity,
                                scale=VC[:, wbase + jt * 4 + cc:wbase + jt * 4 + cc + 1]), reads=[f"HGLU{cc}", "VC"], writes=[ck])
                            S.op("pool", lambda e, cc=cc, ct=ct: e.tensor_tensor(out=ACC[:, cc, :], in0=ACC[:, cc, :], in1=ct[:, 0:T], op=ALU.add),
                                 reads=[ck, f"ACC{cc}"], writes=[f"ACC{cc}"])
                S.op("pool", lambda e: e.tensor_copy(out=HALO[:, l, :, :], in_=HGLU[:, :, T:T + 30]),
                     reads=[f"HGLU{c_}" for c_ in range(4)], writes=["HALO"])
                dump(f"conv{l}", ACC, [f"ACC{c_}" for c_ in range(4)], ti)
                mean, rstd = ln_stats([ACC[:, cc, :] for cc in range(4)], [f"ACC{cc}" for cc in range(4)], None, O512, 2)
                for cc in range(4):
                    S.op("dve", lambda e, cc=cc: e.tensor_tensor(out=ACC[:, cc, :], in0=ACC[:, cc, :], in1=mean, op=ALU.subtract),
                         reads=[f"ACC{cc}", "TA"], writes=[f"ACC{cc}"])
                    S.op("pool", lambda e, cc=cc: e.tensor_tensor(out=ACC[:, cc, :], in0=ACC[:, cc, :], in1=rstd, op=ALU.mult),
                         reads=[f"ACC{cc}", "TB"], writes=[f"ACC{cc}"])
                    S.op("dve", lambda e, cc=cc: e.tensor_scalar(out=ACC[:, cc, :], in0=ACC[:, cc, :], scalar1=vc(f"cvg{l}", cc),
                                                                scalar2=vc(f"cvbb{l}", cc), op0=ALU.mult, op1=ALU.add),
                         reads=[f"ACC{cc}", "VC"], writes=[f"ACC{cc}"])
                    S.op("act", lambda e, cc=cc: e.activation(out=YBR[:, 1, cc, :], in_=ACC[:, cc, :], func=AF.Silu),
                         reads=[f"ACC{cc}"], writes=[f"YBR1_{cc}"])
                for gg in range(4):
                    win = 2 ** (gg + 1)
                    eng = "pool" if gg % 2 == 0 else "dve"
                    src = PP[:, gg, :]
                    cur_key = f"PP{gg}"
                    bufs = [(SA, "SA"), (SBm, "SB")]
                    sh = 1
                    k_ = 0
                    while sh < win:
                        dstb, dkey = bufs[k_ % 2]
                        lo = 2 * sh - 1
                        S.op(eng, lambda e, src=src, dstb=dstb, sh=sh, lo=lo: e.tensor_tensor(
                            out=dstb[:, lo:T + 16], in0=src[:, lo:T + 16], in1=src[:, lo - sh:T + 16 - sh], op=ALU.add),
                            reads=[cur_key], writes=[dkey])
                        src, cur_key = dstb, dkey
                        sh *= 2
                        k_ += 1
                    dstb, dkey = bufs[k_ % 2]
                    S.op("dve", lambda e, src=src, dstb=dstb, gg=gg, win=win: e.scalar_tensor_tensor(
                        out=dstb[:, 16:16 + T], in0=src[:, 16:16 + T], scalar=1.0 / win, in1=PP[:, gg, 16:16 + T],
                        op0=ALU.mult, op1=ALU.subtract), reads=[cur_key, f"PP{gg}"], writes=[dkey])
                    if ti == 0:
                        S.op(eng, lambda e, src=src, gg=gg: e.tensor_tensor(out=src[:, 16:32], in0=src[:, 16:32], in1=RC16[:, gg, :], op=ALU.mult),
                             reads=[cur_key, "RC16", dkey], writes=[cur_key])
                        S.op(eng, lambda e, src=src, dstb=dstb, gg=gg: e.tensor_tensor(out=dstb[:, 16:32], in0=src[:, 16:32], in1=PP[:, gg, 16:32],
                                                                                  op=ALU.subtract), reads=[cur_key, f"PP{gg}", dkey], writes=[dkey])
                    S.op("act", lambda e, dstb=dstb: e.copy(out=PLB[:, :], in_=dstb[:, 16:16 + T]), reads=[dkey], writes=["PLB"])
                    bi, pb, pk = bank()
                    S.op("pe", lambda e, gg=gg, pb=pb: e.matmul(pb[:, 0:T], lhsT=PLW[:, l, gg, :], rhs=PLB[:, :], start=True, stop=True),
                         reads=["PLB", "PLW"], writes=[pk])
                    S.op("act", lambda e, gg=gg, pb=pb: e.activation(out=YBR[:, 2, gg, :], in_=pb[:, 0:T], func=AF.Identity,
                                                                   scale=vc(f"plsc{l}", gg)), reads=[pk, "VC"], writes=[f"YBR2_{gg}"])
                S.op("pool", lambda e: e.tensor_copy(out=PHALO[:, l, :, :], in_=PP[:, :, T:T + 16]),
                     reads=[f"PP{c_}" for c_ in range(4)], writes=["PHALO"])
                S.fence()

                for p in range(4):
                    bi, pb, pk = bank()
                    S.op("pe", lambda e, p=p, pb=pb: e.matmul(pb[:, 0:T], lhsT=W2P[:, l, p * 128:(p + 1) * 128], rhs=T12, start=True, stop=True),
                         reads=["T12", "W2P"], writes=[pk])
                    S.op("act", lambda e, p=p, pb=pb: e.activation(out=SW[:, p, :], in_=pb[:, 0:T], func=AF.Sigmoid, bias=vc(f"w0{l}", p), scale=1.0),
                         reads=[pk, "VC"], writes=[f"SW{p}"])
                    bi, pb, pk = bank()
                    S.op("pe", lambda e, p=p, pb=pb: e.matmul(pb[:, 0:T], lhsT=A2P[:, l, p * 128:(p + 1) * 128], rhs=T12, start=True, stop=True),
                         reads=["T12", "A2P"], writes=[pk])
                    S.op("act", lambda e, p=p, pb=pb: e.activation(out=AH[:, p, :], in_=pb[:, 0:T], func=AF.Sigmoid, bias=vc(f"a0{l}", p), scale=1.0),
                         reads=[pk, "VC"], writes=[f"AH{p}"])
                    if l == 0:
                        S.op("pool", lambda e, p=p: e.tensor_copy(out=VF[:, p, :], in_=ZV[:, p, :]), reads=[f"ZV{p}"], writes=[f"VF{p}"])
                    else:
                        bi, pb, pk = bank()
                        S.op("pe", lambda e, p=p, pb=pb: e.matmul(pb[:, 0:T], lhsT=V2[:, p * 128:(p + 1) * 128], rhs=LV[0:32, :], start=True, stop=True),
                             reads=["LV", "V2"], writes=[pk])
                        S.op("act", lambda e, p=p, pb=pb: e.activation(out=TA, in_=pb[:, 0:T], func=AF.Sigmoid, bias=vc("v0", p), scale=1.0),
                             reads=[pk, "VC"], writes=["TA"])
                        S.op("pool", lambda e, p=p: e.tensor_tensor(out=TD, in0=VF[:, p, :], in1=ZV[:, p, :], op=ALU.subtract),
                             reads=[f"VF{p}", f"ZV{p}"], writes=["TD"])
                        S.op("dve", lambda e: e.tensor_tensor(out=TD, in0=TD, in1=TA, op=ALU.mult), reads=["TD", "TA"], writes=["TD"])
                        S.op("dve", lambda e, p=p: e.tensor_tensor(out=ZV[:, p, :], in0=ZV[:, p, :], in1=TD, op=ALU.add),
                             reads=[f"ZV{p}", "TD"], writes=[f"ZV{p}"])
                    S.op("act", lambda e, p=p: e.activation(out=TC, in_=ZK[:, p, :], func=AF.Square, scale=vc(f"kk{l}", p)),
                         reads=[f"ZK{p}", "VC"], writes=["TC"])
                    bi, pb, pk = bank()
                    S.op("pe", lambda e, pb=pb: e.matmul(pb[:, 0:T], lhsT=OBD[:], rhs=TC, start=True, stop=True), reads=["TC", "OBD"], writes=[pk])
                    S.op("act", lambda e, pb=pb: e.activation(out=TD, in_=pb[:, 0:T], func=AF.Sqrt), reads=[pk], writes=["TD"])
                    S.op("dve", lambda e: e.tensor_scalar(out=TD, in0=TD, scalar1=1e-12, scalar2=-1.0, op0=ALU.max, op1=ALU.mult), reads=["TD"], writes=["TD"])
                    S.op("dve", lambda e: e.reciprocal(out=TD, in_=TD), reads=["TD"], writes=["TD"])
                    S.op("dve", lambda e, p=p: e.scalar_tensor_tensor(out=KKN[:, p, :], in0=ZK[:, p, :], scalar=vc(f"kk{l}", p), in1=TD,
                                                                      op0=ALU.mult, op1=ALU.mult), reads=[f"ZK{p}", "TD", "VC"], writes=[f"KKN{p}"])
                    S.op("pool", lambda e, p=p: e.tensor_scalar(out=TC, in0=AH[:, p, :], scalar1=vc(f"ka{l}", p), scalar2=DC[:, l, 48 + p:49 + p],
                                                               op0=ALU.mult, op1=ALU.add), reads=[f"AH{p}", "VC", "DC"], writes=["TC"])
                    S.op("dve", lambda e, p=p: e.tensor_tensor(out=ZK[:, p, :], in0=ZK[:, p, :], in1=TC, op=ALU.mult),
                         reads=[f"ZK{p}", "TC"], writes=[f"ZK{p}"])
                    S.op("dve", lambda e, p=p: e.scalar_tensor_tensor(out=AH[:, p, :], in0=AH[:, p, :], scalar=-1.0, in1=KKN[:, p, :], op0=ALU.mult, op1=ALU.mult),
                         reads=[f"AH{p}", f"KKN{p}"], writes=[f"AH{p}"])
                    S.op("dve", lambda e, p=p: e.tensor_tensor_scan(out=LC[:, p, :], data0=RESETM[:], data1=SW[:, p, :], initial=0.0,
                                                                    op0=ALU.mult, op1=ALU.add), reads=["RESETM", f"SW{p}"], writes=[f"LC{p}"])
                    S.op("pool", lambda e, p=p: e.tensor_tensor(out=SW[:, p, :], in0=LC[:, p, :], in1=SW[:, p, :], op=ALU.subtract),
                         reads=[f"LC{p}", f"SW{p}"], writes=[f"SW{p}"])
                    S.op("act", lambda e, p=p: e.activation(out=SW[:, p, :], in_=SW[:, p, :], func=AF.Exp, scale=-CDEC), reads=[f"SW{p}"], writes=[f"SW{p}"])
                    S.op("act", lambda e, p=p: e.activation(out=PINV[:, p, :], in_=LC[:, p, :], func=AF.Exp, scale=CDEC), reads=[f"LC{p}"], writes=[f"PINV{p}"])
                    S.op("act", lambda e, p=p: e.activation(out=LC[:, p, :], in_=LC[:, p, :], func=AF.Exp, scale=-CDEC), reads=[f"LC{p}"], writes=[f"LC{p}"])
                    S.op("pool", lambda e, p=p: e.tensor_copy(out=PC[:, p, 0:NCH], in_=LC[:, p, :].rearrange("p (c j) -> p c j", j=64)[:, :, 63]),
                         reads=[f"LC{p}"], writes=["PC"])
                PEX, PIN, BBt, KH = SW, LC, AH, ZK
                allk = lambda nm: [f"{nm}{p}" for p in range(4)]
                same_nt = (REC_BF16 == (REC_BF16 and NEU_BF16))
                BDAn, BDBn = (BDA, BDB) if same_nt else (BDA2, BDB2)
                kBDAn, kBDBn = ("BDA", "BDB") if same_nt else ("BDA2", "BDB2")
                GMx = GM if (same_nt or MM2_F32) else GMr
                kGMx = "GM" if (same_nt or MM2_F32) else "GMr"

                def mm4(outb, lT, rT, ncol=128):
                    def fn(e):
                        for p in range(4):
                            ins = e.matmul(outb[:, p * ncol:(p + 1) * ncol], lhsT=lT[:, p, :], rhs=rT[:, p, :], start=True, stop=True)
                        return ins
                    return fn

                def mm4c(outb, lT, rconst, ncol):
                    def fn(e):
                        for p in range(4):
                            ins = e.matmul(outb[:, p * ncol:(p + 1) * ncol], lhsT=lT[:, p, :], rhs=rconst, start=True, stop=True)
                        return ins
                    return fn

                def v4(pb, n=128):
                    return pb[:, 0:4 * n].rearrange("p (a b) -> p a b", a=4)

                def bc4(M):
                    return M[:].unsqueeze(1).to_broadcast([128, 4, 128])
                STl = ST[:, l, :, :]
                S.op("act", lambda e: e.copy(out=STb, in_=STl), reads=["ST", "STb"], writes=["STb"])
                for c in range(NCH):
                    cs = slice(c * 64, (c + 1) * 64)
                    for hh in range(2):
                        rws = slice(64 * hh, 64 * hh + 64)
                        cls = slice(64 * hh, 64 * hh + 64)
                        S.op("dve", lambda e, rws=rws, cls=cls, cs=cs: e.tensor_tensor(out=BDR[rws, :, cls], in0=ZR[rws, :, cs], in1=PIN[rws, :, cs], op=ALU.mult),
                             reads=allk("ZR") + allk("LC"), writes=["BDR"])
                        S.op("pool", lambda e, rws=rws, cls=cls, cs=cs: e.tensor_tensor(out=BDK[rws, :, cls], in0=KH[rws, :, cs], in1=PINV[rws, :, cs], op=ALU.mult),
                             reads=allk("ZK") + allk("PINV"), writes=["BDK"])
                        S.op("dve", lambda e, rws=rws, cls=cls, cs=cs: e.tensor_tensor(out=BDB[rws, :, cls], in0=BBt[rws, :, cs], in1=PINV[rws, :, cs], op=ALU.mult),
                             reads=allk("AH") + allk("PINV"), writes=["BDB"])
                        S.op("pool", lambda e, rws=rws, cls=cls, cs=cs: e.tensor_tensor(out=BDA[rws, :, cls], in0=KKN[rws, :, cs], in1=PEX[rws, :, cs], op=ALU.mult),
                             reads=allk("KKN") + allk("SW"), writes=["BDA"])
                        S.op("pool", lambda e, rws=rws, cls=cls, cs=cs: e.tensor_copy(out=BDV[rws, :, cls], in_=ZV[rws, :, cs]),
                             reads=allk("ZV"), writes=["BDV"])
                        if not same_nt:
                            S.op("dve", lambda e, rws=rws, cls=cls, cs=cs: e.tensor_tensor(out=BDB2[rws, :, cls], in0=BBt[rws, :, cs], in1=PINV[rws, :, cs], op=ALU.mult),
                                 reads=allk("AH") + allk("PINV"), writes=["BDB2"])
                            S.op("pool", lambda e, rws=rws, cls=cls, cs=cs: e.tensor_tensor(out=BDA2[rws, :, cls], in0=KKN[rws, :, cs], in1=PEX[rws, :, cs], op=ALU.mult),
                                 reads=allk("KKN") + allk("SW"), writes=["BDA2"])
                    for (lT, lk, rT, rk_, dstM, dk, msk, mk_) in (
                            (BDBn, kBDBn, BDAn, kBDAn, MN[0], "MN0", MSU, "MSU"),
                            (BDAn, kBDAn, BDBn, kBDBn, MNT[0], "MNT0", MSL, "MSL"),
                            (BDK, "BDK", BDA, "BDA", MKA, "MKA", MSU, "MSU"),
                            (BDB, "BDB", BDR, "BDR", MBR, "MBR", MIU, "MIU"),
                            (BDK, "BDK", BDR, "BDR", MKR, "MKR", MIU, "MIU")):
                        bi, pb, pk = bank()
                        S.op("pe", mm4(pb, lT, rT), reads=[lk, rk_], writes=[pk])
                        S.op("dve", lambda e, pb=pb, dstM=dstM, msk=msk: e.tensor_tensor(out=dstM, in0=v4(pb), in1=bc4(msk), op=ALU.mult),
                             reads=[pk, mk_], writes=[dk])
                    S.op("pool", lambda e: e.tensor_tensor(out=GM, in0=MN[0], in1=bc4(IDENT), op=ALU.add), reads=["MN0", "IDENT"], writes=["GM"])
                    cur = 0
                    for jl in range(1, 6):
                        nxt = 1 - cur
                        if jl < 5:
                            bi, pb, pk = bank()
                            S.op("pe", mm4(pb, MNT[cur], MN[cur]), reads=[f"MN{cur}", f"MNT{cur}"], writes=[pk])
                            S.op("act", lambda e, pb=pb, nxt=nxt: e.copy(out=MN[nxt], in_=v4(pb)), reads=[pk], writes=[f"MN{nxt}"])
                        bi, pb, pk = bank()
                        S.op("pe", mm4(pb, MN[cur], MNT[cur]), reads=[f"MN{cur}", f"MNT{cur}"], writes=[pk])
                        S.op("dve", lambda e, pb=pb, nxt=nxt: e.tensor_copy(out=MNT[nxt], in_=v4(pb)), reads=[pk], writes=[f"MNT{nxt}"])
                        bi, pb, pk = bank()
                        S.op("pe", mm4(pb, MNT[nxt], GM), reads=[f"MNT{nxt}", "GM"], writes=[pk])
                        S.op("dve", lambda e, pb=pb: e.tensor_tensor(out=GM, in0=v4(pb), in1=GM, op=ALU.add), reads=[pk, "GM"], writes=["GM"])
                        cur = nxt
                    if not same_nt:
                        S.op("act", lambda e: e.copy(out=GMr, in_=GM), reads=["GM"], writes=["GMr"])
                    for (srcT, sk_, dstT_, dk) in ((BDB, "BDB", BDBT, "BDBT"), (BDK, "BDK", BDKT, "BDKT")):
                        bi, pb, pk = bank()
                        S.op("pe", mm4c(pb, srcT, IDR[:], 128), reads=[sk_, "IDR"], writes=[pk])
                        S.op("act", lambda e, pb=pb, dstT_=dstT_: e.copy(out=dstT_, in_=v4(pb)), reads=[pk], writes=[dk])
                    bi, pb, pk = bank()
                    S.op("pe", mm4c(pb, BDV, I2R[:], 64), reads=["BDV", "I2R"], writes=[pk])
                    S.op("act", lambda e, pb=pb: e.copy(out=VT, in_=v4(pb, 64)), reads=[pk], writes=["VT"])
                    bi, pb, pk = bank()

                    def f1(e, pb=pb):
                        for p in range(4):
                            e.matmul(pb[:, p * 64:(p + 1) * 64], lhsT=BDA[:, p, :], rhs=STb[:, p, :], start=True, stop=False)
                            ins = e.matmul(pb[:, p * 64:(p + 1) * 64], lhsT=MKA[:, p, :], rhs=VT[:, p, :], start=False, stop=True)
                        return ins
                    S.op("pe", f1, reads=["BDA", "STb", "MKA", "VT"], writes=[pk])
                    S.op("act", lambda e, pb=pb: e.copy(out=XTt, in_=v4(pb, 64)), reads=[pk], writes=["XT"])
                    bi, pb, pk = bank()
                    S.op("pe", mm4(pb, GMx, XTt, 64), reads=[kGMx, "XT"], writes=[pk])
                    S.op("dve", lambda e, pb=pb: e.tensor_copy(out=UT, in_=v4(pb, 64)), reads=[pk], writes=["UT"])
                    bi, pby, pky = bank()

                    def f3(e, pby=pby):
                        for p in range(4):
                            o = pby[:, p * 64:(p + 1) * 64]
                            e.matmul(o, lhsT=BDR[:, p, :], rhs=STb[:, p, :], start=True, stop=False)
                            e.matmul(o, lhsT=MBR[:, p, :], rhs=UT[:, p, :], start=False, stop=False)
                            ins = e.matmul(o, lhsT=MKR[:, p, :], rhs=VT[:, p, :], start=False, stop=True)
                        return ins
                    S.op("pe", f3, reads=["BDR", "STb", "MBR", "UT", "MKR", "VT"], writes=[pky])
                    bi, pbs, pks = bank()

                    def f4(e, pbs=pbs):
                        for p in range(4):
                            o = pbs[:, p * 64:(p + 1) * 64]
                            e.matmul(o, lhsT=BDBT[:, p, :], rhs=UT[:, p, :], start=True, stop=False)
                            ins = e.matmul(o, lhsT=BDKT[:, p, :], rhs=VT[:, p, :], start=False, stop=True)
                        return ins
                    S.op("pe", f4, reads=["BDBT", "UT", "BDKT", "VT"], writes=[pks])
                    S.op("dve", lambda e, pbs=pbs: e.tensor_tensor(out=TMPS, in0=v4(pbs, 64), in1=STl, op=ALU.add), reads=[pks, "ST"], writes=["TMPS"])
                    S.op("dve", lambda e, c=c: e.tensor_tensor(out=STl, in0=TMPS, in1=PC[:, :, c:c + 1].to_broadcast([128, 4, 64]), op=ALU.mult),
                         reads=["TMPS", "PC"], writes=["ST"])
                    S.op("act", lambda e: e.copy(out=STb, in_=STl), reads=["ST"], writes=["STb"])
                    S.op("act", lambda e, pby=pby: e.copy(out=YTBD[0:64, :, 0:64], in_=v4(pby, 64)[0:64]), reads=[pky, "YTBD"], writes=["YTBD"])
                    S.op("pool" if False else "dve", lambda e, pby=pby: e.tensor_copy(out=YTBD[64:128, :, 64:128], in_=v4(pby, 64)[64:128]), reads=[pky, "YTBD"], writes=["YTBD"])
                    bi, pb, pk = bank()
                    S.op("pe", mm4c(pb, YTBD, I2R[:], 64), reads=["YTBD", "I2R"], writes=[pk])
                    S.op("act", lambda e, pb=pb, cs=cs: e.copy(out=YF[:, :, cs], in_=v4(pb, 64)), reads=[pk], writes=allk("YF"))
                dump(f"yrec{l}", YF, allk("YF"), ti)
                for p in range(4):
                    S.op("dve", lambda e, p=p: e.scalar_tensor_tensor(out=TC, in0=ZR[:, p, :], scalar=vc(f"rk{l}", p), in1=KH[:, p, :],
                                                                      op0=ALU.mult, op1=ALU.mult), reads=[f"ZR{p}", f"ZK{p}", "VC"], writes=["TC"])
                    bi, pbb, pkb = bank()
                    S.op("pe", lambda e, pbb=pbb: e.matmul(pbb[:, 0:T], lhsT=OBD[:], rhs=TC, start=True, stop=True), reads=["TC", "OBD"], writes=[pkb])
                    S.op("dve", lambda e, p=p, pbb=pbb: e.tensor_tensor(out=KKN[:, p, :], in0=pbb[:, 0:T], in1=ZV[:, p, :], op=ALU.mult),
                         reads=[pkb, f"ZV{p}"], writes=[f"KKN{p}"])
                    mean, rstd = ln_stats([YF[:, p, :]], [f"YF{p}"], None, OBD64, 0)
                    S.op("dve", lambda e, p=p: e.tensor_tensor(out=YF[:, p, :], in0=YF[:, p, :], in1=mean, op=ALU.subtract),
                         reads=[f"YF{p}", "TA"], writes=[f"YF{p}"])
                    S.op("pool", lambda e, p=p: e.tensor_tensor(out=YF[:, p, :], in0=YF[:, p, :], in1=rstd, op=ALU.mult),
                         reads=[f"YF{p}", "TB"], writes=[f"YF{p}"])
                    S.op("dve", lambda e, p=p: e.tensor_scalar(out=YF[:, p, :], in0=YF[:, p, :], scalar1=vc(f"gng{l}", p), scalar2=vc(f"gnb{l}", p),
                                                             op0=ALU.mult, op1=ALU.add), reads=[f"YF{p}", "VC"], writes=[f"YF{p}"])
                    S.op("pool", lambda e, p=p: e.tensor_tensor(out=YF[:, p, :], in0=YF[:, p, :], in1=KKN[:, p, :], op=ALU.add),
                         reads=[f"YF{p}", f"KKN{p}"], writes=[f"YF{p}"])
                    bi, pbg, pkg = bank()
                    S.op("pe", lambda e, p=p, pbg=pbg: e.matmul(pbg[:, 0:T], lhsT=G2[:, l, p * 128:(p + 1) * 128], rhs=LG, start=True, stop=True),
                         reads=["LG", "G2"], writes=[pkg])
                    S.op("dve", lambda e, p=p, pbg=pbg: e.tensor_tensor(out=YBR[:, 0, p, :], in0=pbg[:, 0:T], in1=YF[:, p, :], op=ALU.mult),
                         reads=[pkg, f"YF{p}"], writes=[f"YBR0_{p}"])
                S.fence()
                for q in range(2):
                    for b in range(3):
                        sg = wload(l, 7 + b * 2 + q)
                        so = wload(l, 13 + b * 2 + q)
                        wg = wview(sg, 8, 512)
                        wo = wview(so, 4, 512)
                        for mi in range(4):
                            bi, pbg, pkg = bank()

                            def fg(e, pbg=pbg, wg=wg, mi=mi):
                                for kc in range(8):
                                    ins = e.matmul(pbg[:, 0:T], lhsT=wg[:, kc, mi * 128:(mi + 1) * 128], rhs=H[:, kc, :], start=(kc == 0), stop=(kc == 7))
                                return ins
                            S.op("pe", fg, reads=[f"WS{sg}"] + [f"H{m}" for m in range(8)], writes=[pkg])
                            bi, pbo, pko = bank()

                            def fo(e, pbo=pbo, wo=wo, mi=mi, b=b):
                                for kc in range(4):
                                    ins = e.matmul(pbo[:, 0:T], lhsT=wo[:, kc, mi * 128:(mi + 1) * 128], rhs=YBR[:, b, kc, :], start=(kc == 0), stop=(kc == 3))
                                return ins
                            S.op("pe", fo, reads=[f"WS{so}"] + [f"YBR{b}_{k_}" for k_ in range(4)], writes=[pko])
                            S.op("act", lambda e, pbg=pbg: e.activation(out=SIG, in_=pbg[:, 0:T], func=AF.Sigmoid), reads=[pkg], writes=["SIG"])
                            if b == 0:
                                S.op("dve", lambda e, pbo=pbo, mi=mi: e.tensor_tensor(out=MERG[:, mi, :], in0=pbo[:, 0:T], in1=SIG, op=ALU.mult),
                                     reads=[pko, "SIG"], writes=[f"MERG{mi}"])
                            else:
                                S.op("dve", lambda e, pbo=pbo: e.tensor_tensor(out=TMG, in0=pbo[:, 0:T], in1=SIG, op=ALU.mult),
                                     reads=[pko, "SIG"], writes=["TMG"])
                                if b == 1:
                                    S.op("pool", lambda e, mi=mi: e.tensor_tensor(out=MERG[:, mi, :], in0=MERG[:, mi, :], in1=TMG, op=ALU.add),
                                         reads=[f"MERG{mi}", "TMG"], writes=[f"MERG{mi}"])
                                else:
                                    S.op("pool", lambda e, mi=mi, q=q: e.tensor_tensor(out=MERGED[:, q * 4 + mi, :], in0=MERG[:, mi, :], in1=TMG, op=ALU.add),
                                         reads=[f"MERG{mi}", "TMG"], writes=[f"MERGED{q * 4 + mi}"])
                S.fence()
                sA = wload(l, 19)
                sB = wload(l, 20)
                psl = {}
                for m in range(8):
                    s = sA if m < 4 else sB
                    wv = wview(s, 8, 512)
                    bi, pb, pk = bank()

                    def fw(e, pb=pb, wv=wv, m=m):
                        for kc in range(8):
                            ins = e.matmul(pb[:, 0:T], lhsT=wv[:, kc, (m % 4) * 128:(m % 4 + 1) * 128], rhs=MERGED[:, kc, :], start=(kc == 0), stop=(kc == 7))
                        return ins
                    S.op("pe", fw, reads=[f"WS{s}"] + [f"MERGED{k_}" for k_ in range(8)], writes=[pk])
                    S.op("dve", lambda e, m=m, pb=pb: e.scalar_tensor_tensor(out=U[:, m, :], in0=pb[:, 0:T], scalar=DC[:, l, 16 + m:17 + m],
                                                                         in1=X[:, m, :], op0=ALU.mult, op1=ALU.add),
                         reads=[pk, "DC", f"X{m}"], writes=[f"U{m}"])

                def ln_apply(gname, bname):
                    mean, rstd = ln_stats([U[:, m, :] for m in range(8)], [f"U{m}" for m in range(8)], None, O1024, 1)
                    for m in range(8):
                        S.op("dve", lambda e, m=m: e.tensor_tensor(out=U[:, m, :], in0=U[:, m, :], in1=mean, op=ALU.subtract),
                             reads=[f"U{m}", "TA"], writes=[f"U{m}"])
                        S.op("pool", lambda e, m=m: e.tensor_tensor(out=U[:, m, :], in0=U[:, m, :], in1=rstd, op=ALU.mult),
                             reads=[f"U{m}", "TB"], writes=[f"U{m}"])
                        S.op("dve", lambda e, m=m: e.tensor_scalar(out=X[:, m, :], in0=U[:, m, :], scalar1=vc(gname, m), scalar2=vc(bname, m),
                                                                 op0=ALU.mult, op1=ALU.add), reads=[f"U{m}", "VC"], writes=[f"X{m}"])
                ln_apply(f"lnmg{l}", f"lnmb{l}")
                dump(f"xm{l}", X[:], [f"X{m}" for m in range(8)], ti)
                S.fence()
                modulate(l, "f")
                for jg in range(8):
                    s = wload(l, 21 + jg)
                    wv = wview(s, 8, 512)
                    for ji in range(4):
                        j = jg * 4 + ji
                        bi, pb, pk = bank()

                        def f1m(e, pb=pb, wv=wv, ji=ji):
                            for kc in range(8):
                                ins = e.matmul(pb[:, 0:T], lhsT=wv[:, kc, ji * 128:(ji + 1) * 128], rhs=H[:, kc, :], start=(kc == 0), stop=(kc == 7))
                            return ins
                        S.op("pe", f1m, reads=[f"WS{s}"] + [f"H{m}" for m in range(8)], writes=[pk])
                        rt, rk_ = (RTMP, "RTMP") if j % 2 == 0 else (RTMP2, "RTMP2")
                        S.op("act", lambda e, pb=pb, rt=rt: e.activation(out=rt, in_=pb[:, 0:T], func=AF.Relu), reads=[pk], writes=[rk_])
                        S.op("dve" if j % 2 == 0 else "pool", lambda e, j=j, rt=rt: e.tensor_tensor(out=H1[:, j, :], in0=rt, in1=rt, op=ALU.mult),
                             reads=[rk_], writes=[f"H1_{j}"])
                for m in range(8):
                    s = wload(l, 29 + m)
                    wv = wview(s, 32, 128)
                    bi, pb, pk = bank()

                    def f2m(e, pb=pb, wv=wv):
                        for kc in range(32):
                            ins = e.matmul(pb[:, 0:T], lhsT=wv[:, kc, :], rhs=H1[:, kc, :], start=(kc == 0), stop=(kc == 31))
                        return ins
                    S.op("pe", f2m, reads=[f"WS{s}"] + [f"H1_{k_}" for k_ in range(32)], writes=[pk])
                    S.op("dve", lambda e, m=m, pb=pb: e.scalar_tensor_tensor(out=U[:, m, :], in0=pb[:, 0:T], scalar=DC[:, l, 24 + m:25 + m],
                                                                         in1=X[:, m, :], op0=ALU.mult, op1=ALU.add),
                         reads=[pk, "DC", f"X{m}"], writes=[f"U{m}"])
                ln_apply(f"lnfg{l}", f"lnfb{l}")
                dump(f"xf{l}", X[:], [f"X{m}" for m in range(8)], ti)
                S.fence()
            for tb in range(TB):
                for half in range(2):
                    bi, pb, pk = bank()

                    def fo_(e, pb=pb, tb=tb, half=half):
                        for f4_ in range(4):
                            fc = half * 4 + f4_
                            ins = e.transpose(pb[:, f4_ * 128:(f4_ + 1) * 128], X[:, fc, tb * 128:(tb + 1) * 128], IDENT[:])
                        return ins
                    S.op("pe", fo_, reads=[f"X{m}" for m in range(8)] + ["IDENT"], writes=[pk])
                    S.op("act" if half else "dve",
                         (lambda e, pb=pb, tb=tb, half=half: e.copy(out=XIN[:, tb, half * 512:(half + 1) * 512], in_=pb[:, :])) if half else
                         (lambda e, pb=pb, tb=tb, half=half: e.tensor_copy(out=XIN[:, tb, half * 512:(half + 1) * 512], in_=pb[:, :])),
                         reads=[pk], writes=["XIN"])
            S.dma("sp", out[t0:t0 + T, :].rearrange("(b p) d -> p b d", p=128), XIN[:], reads=["XIN"])
        S.wait_all("sp")
        S.emit(block)
        build_nc.last_stats = {"nops": S.nops, "cnt": dict(S.cnt)}
    return nc


def kernel(**inputs):
    B, S_TOK, _ = inputs["x"].shape
    nc = build_nc(S_TOK, T=256, NL=2)
    in_maps = []
    for b in range(B):
        m = {"x": np.ascontiguousarray(inputs["x"][b], dtype=np.float32), "vecs": pack_vecs(inputs, b)}
        for nm in WEIGHT_NAMES:
            m[nm] = np.ascontiguousarray(inputs[nm], dtype=np.float32)
        in_maps.append(m)
    res = run_bass_kernel_spmd(nc, in_maps, core_ids=list(range(B)))
    return np.stack([np.asarray(r["out"]).reshape(S_TOK, D) for r in res.results], axis=0).astype(np.float32)
```

```python
import types
import numpy as np
from contextlib import ExitStack
import concourse.bass as bass
import concourse.mybir as mybir
from concourse.bass_utils import run_bass_kernel_spmd

F32 = mybir.dt.float32
BF16 = mybir.dt.bfloat16
I32 = mybir.dt.int32
ALU = mybir.AluOpType
AF = mybir.ActivationFunctionType

D = 1024
RW = 512
C_MAIN = 6400
NG = 37
GW = 4096
ALPHA = 4.0 ** 0.25
CDEC = float(np.exp(-0.5))
LN_EPS = 1e-5
GN_EPS = 64e-5
CONVW = 31
ENG_NAMES = ("pe", "act", "dve", "pool", "sp")


class Sched:
    def __init__(self, nc, sems, dma_sems):
        self.nc = nc
        self.sem = dict(zip(ENG_NAMES, sems))
        self.cnt = {e: 0 for e in ENG_NAMES}
        self.dma_pool = {q: list(v) for q, v in dma_sems.items()}
        self.dma_sems = [s for q in self.dma_pool for s in self.dma_pool[q]]
        self.dma_idx = {}
        i = 0
        for q in self.dma_pool:
            self.dma_idx[q] = list(range(i, i + len(self.dma_pool[q])))
            i += len(self.dma_pool[q])
        self.dma_cnt = [0] * len(self.dma_sems)
        self.dma_rr = {q: 0 for q in self.dma_pool}
        self.streams = {e: [] for e in ENG_NAMES}
        self.seen = {e: {} for e in ENG_NAMES}
        self.last_w = {}
        self.readers = {}
        self.nops = 0
        self.fence_dma = False

    def _deps(self, reads, writes):
        toks = []
        for k in reads:
            t = self.last_w.get(k)
            if t is not None:
                toks.append(t)
        for k in writes:
            t = self.last_w.get(k)
            if t is not None:
                toks.append(t)
            toks.extend(self.readers.get(k, ()))
        return toks

    def _waits_for(self, e, toks, skip_same_pe=True):
        need = {}
        for (sk, v) in toks:
            if sk == e and e == "pe" and skip_same_pe:
                continue
            if self.seen[e].get(sk, 0) >= v:
                continue
            if need.get(sk, 0) < v:
                need[sk] = v
        for sk, v in need.items():
            self.seen[e][sk] = v
        return list(need.items())

    def _commit(self, tok, reads, writes):
        for k in reads:
            self.readers.setdefault(k, []).append(tok)
        for k in writes:
            self.last_w[k] = tok
            self.readers[k] = []

    @staticmethod
    def _freeze(fn):
        if getattr(fn, "__closure__", None) is None:
            return fn
        cells = []
        for c in fn.__closure__:
            try:
                v = c.cell_contents
                if isinstance(v, types.FunctionType):
                    v = Sched._freeze(v)
                cells.append(types.CellType(v))
            except ValueError:
                cells.append(c)
        g = types.FunctionType(fn.__code__, fn.__globals__, fn.__name__, fn.__defaults__, tuple(cells))
        g.__kwdefaults__ = fn.__kwdefaults__
        return g

    def op(self, e, fn, reads=(), writes=()):
        fn = self._freeze(fn)
        reads = list(reads); writes = list(writes)
        toks = self._deps(reads, writes)
        waits = self._waits_for(e, toks)
        self.cnt[e] += 1
        tok = (e, self.cnt[e])
        self.streams[e].append((waits, fn, ("eng", e)))
        self._commit(tok, reads, writes)
        self.nops += 1
        return tok

    def dma(self, q, out, in_, reads=(), writes=(), **kw):
        toks = self._deps(reads, writes)
        i = self.dma_idx[q][self.dma_rr[q]]
        self.dma_rr[q] = (self.dma_rr[q] + 1) % len(self.dma_idx[q])
        sk = ("d", i)
        if self.dma_cnt[i] > 0:
            toks.append((sk, self.dma_cnt[i]))
        waits = self._waits_for(q, toks)
        self.dma_cnt[i] += 16
        tok = (sk, self.dma_cnt[i])

        def fn(eng, out=out, in_=in_, kw=kw):
            return eng.dma_start(out=out, in_=in_, **kw)
        self.streams[q].append((waits, fn, ("dma", i)))
        self._commit(tok, reads, writes)
        return tok

    def fence(self, engines=("pe", "act", "dve", "pool")):
        for e in engines:
            toks = [(en, self.cnt[en]) for en in engines if self.cnt[en] > 0]
            if self.fence_dma:
                toks += [(("d", i), v) for i, v in enumerate(self.dma_cnt) if v > 0]
            waits = self._waits_for(e, toks, skip_same_pe=False)
            if waits:
                self.streams[e].append((waits, None, None))

    def wait_all(self, e):
        toks = [(en, self.cnt[en]) for en in ENG_NAMES if self.cnt[en] > 0 and en != e]
        toks += [(("d", i), v) for i, v in enumerate(self.dma_cnt) if v > 0]
        waits = self._waits_for(e, toks)
        self.streams[e].append((waits, None, None))

    def _semh(self, sk):
        if isinstance(sk, tuple):
            return self.dma_sems[sk[1]]
        return self.sem[sk]

    def emit(self, block):
        eng_of = {"pe": self.nc.tensor, "act": self.nc.scalar, "dve": self.nc.vector,
                  "pool": self.nc.gpsimd, "sp": self.nc.sync}

        def mk(e):
            def body(eng):
                for waits, fn, sig in self.streams[e]:
                    for sk, v in waits:
                        eng.wait_ge(self._semh(sk), v)
                    if fn is None:
                        continue
                    ins = fn(eng)
                    if sig[0] == "eng":
                        ins.then_inc(self.sem[e], 1)
                    else:
                        ins.then_inc(self.dma_sems[sig[1]], 16)
            return body
        block.tensor(mk("pe"))
        block.scalar(mk("act"))
        block.vector(mk("dve"))
        block.gpsimd(mk("pool"))
        block.sync(mk("sp"))


class Rec:
    def __init__(self):
        self.items = []

    def op(self, e, fn, reads=(), writes=()):
        self.items.append(("op", e, Sched._freeze(fn), list(reads), list(writes)))

    def dma(self, q, out, in_, reads=(), writes=(), **kw):
        self.items.append(("dma", q, out, in_, list(reads), list(writes), kw))


def replay_merged(S, recs):
    pos = [0] * len(recs)
    tot = [max(1, len(r.items)) for r in recs]
    while True:
        best, bf = None, None
        for i, r in enumerate(recs):
            if pos[i] < len(r.items):
                f = pos[i] / tot[i]
                if bf is None or f < bf:
                    best, bf = i, f
        if best is None:
            break
        it = recs[best].items[pos[best]]
        pos[best] += 1
        if it[0] == "op":
            S.op(it[1], it[2], reads=it[3], writes=it[4])
        else:
            S.dma(it[1], it[2], it[3], reads=it[4], writes=it[5], **it[6])


def vec_layout():
    off = {}
    r = 0

    def add(name, n):
        nonlocal r
        off[name] = r
        r += n
    add("c", 8)
    for l in range(2):
        for nm, n in (("ada_b", 48), ("mu", 14), ("w0", 4), ("a0", 4), ("kk", 4), ("ka", 4), ("rk", 4),
                      ("gng", 4), ("gnb", 4), ("cvb", 4), ("cvg", 4), ("cvbb", 4), ("plsc", 4),
                      ("lnmg", 8), ("lnmb", 8), ("lnfg", 8), ("lnfb", 8), ("cvw", 124)):
            add(f"{nm}{l}", n)
    add("v0", 4)
    add("muv", 1)
    return off, r


VOFF, NVROWS = vec_layout()
NVB = (NVROWS + 127) // 128


def pack_vecs(inp, b):
    P = np.zeros((NVB * 128, 128), np.float32)

    def put(name, arr):
        a = np.asarray(arr, np.float32).reshape(-1)
        n = (a.size + 127) // 128
        buf = np.zeros(n * 128, np.float32)
        buf[:a.size] = a
        P[VOFF[name]:VOFF[name] + n] = buf.reshape(n, 128)
    put("c", inp["c"][b])
    for l in range(2):
        put(f"ada_b{l}", inp["ada_b"][l]); put(f"mu{l}", inp["shift_mu"][l])
        put(f"w0{l}", inp["rw_w0"][l]); put(f"a0{l}", inp["rw_a0"][l])
        put(f"kk{l}", inp["rw_kk"][l]); put(f"ka{l}", inp["rw_ka"][l]); put(f"rk{l}", inp["rw_rk"][l])
        put(f"gng{l}", inp["rw_gn_g"][l]); put(f"gnb{l}", inp["rw_gn_b"][l])
        put(f"cvb{l}", inp["cv_b"][l]); put(f"cvg{l}", inp["cv_ln_g"][l]); put(f"cvbb{l}", inp["cv_ln_b"][l])
        put(f"plsc{l}", inp["pl_scale"][l])
        put(f"lnmg{l}", inp["ln_m_g"][l]); put(f"lnmb{l}", inp["ln_m_b"][l])
        put(f"lnfg{l}", inp["ln_f_g"][l]); put(f"lnfb{l}", inp["ln_f_b"][l])
        put(f"cvw{l}", inp["cv_w"][l])
    put("v0", inp["rw_v0"][0])
    put("muv", inp["shift_mu_vres"][0])
    return P


WEIGHT_NAMES = ("ada_w", "w_in", "w_in_vres", "rw_w2", "rw_a2", "rw_g2", "rw_v2", "rw_wo", "cv_wo",
                "pl_wo", "pl_w", "w_out", "mlp_w1", "mlp_w2")
WEIGHT_SHAPES = {"ada_w": [2, 1024, 6144], "w_in": [2, 1024, 6400], "w_in_vres": [1, 1024, 32],
                 "rw_w2": [2, 64, 512], "rw_a2": [2, 64, 512], "rw_g2": [2, 128, 512], "rw_v2": [1, 32, 512],
                 "rw_wo": [2, 512, 1024], "cv_wo": [2, 512, 1024], "pl_wo": [2, 512, 1024],
                 "pl_w": [2, 4, 128, 128], "w_out": [2, 1024, 1024], "mlp_w1": [2, 1024, 4096],
                 "mlp_w2": [2, 4096, 1024]}


def build_nc(S_TOK, T=256, NL=2, debug=None, REC_BF16=True, NEU_BF16=True, MM2_F32=False, ST_F32=False):
    assert S_TOK % T == 0 and T % 128 == 0 and T <= 512
    NT = S_TOK // T
    NCH = T // 64
    TB = T // 128
    nc = bass.Bass("TRN2", target_bir_lowering=False)
    dr = {}
    dr["x"] = nc.dram_tensor("x", [S_TOK, D], F32, kind="ExternalInput").ap()
    dr["vecs"] = nc.dram_tensor("vecs", [NVB * 128, 128], F32, kind="ExternalInput").ap()
    for nm in WEIGHT_NAMES:
        dr[nm] = nc.dram_tensor(nm, WEIGHT_SHAPES[nm], F32, kind="ExternalInput").ap()
    out = nc.dram_tensor("out", [S_TOK, D], F32, kind="ExternalOutput").ap()
    wsc = nc.dram_tensor("wsc", [2, NG, 128, GW], BF16, kind="Internal").ap()
    dbg = {}
    if debug:
        for nm, shp in debug.items():
            dbg[nm] = nc.dram_tensor(nm, list(shp), F32, kind="ExternalOutput").ap()

    with ExitStack() as es:
        def sb(name, shape, dt=F32):
            return es.enter_context(nc.sbuf_tensor(name, list(shape), dt))

        def pst(name, shape, dt=F32):
            return es.enter_context(nc.psum_tensor(name, list(shape), dt))

        X = sb("X", [128, 8, T])
        H = sb("H", [128, 8, T], BF16)
        WS = sb("WS", [128, 4, GW], BF16)
        VF = sb("VF", [128, 4, T])
        VC = sb("VC", [128, NVB * 128])
        PK = sb("PK", [128, NVB, 128])
        ADA = sb("ADA", [128, 2, 48])
        DC = sb("DC", [128, 2, 64])
        IDENT = sb("IDENT", [128, 128])
        I2 = sb("I2", [128, 64])
        MSU = sb("MSU", [128, 128]); MIU = sb("MIU", [128, 128]); MSL = sb("MSL", [128, 128])
        OBD = sb("OBD", [128, 128]); OBD64 = sb("OBD64", [128, 128])
        O512 = sb("O512", [128, 128]); O1024 = sb("O1024", [128, 128])
        RESETM = sb("RESETM", [128, T])
        EPS = sb("EPS", [128, 4])
        RC16 = sb("RC16", [128, 4, 16])
        IOTI = sb("IOTI", [128, 16], I32)
        W2P = sb("W2P", [128, 2, 512], BF16); A2P = sb("A2P", [128, 2, 512], BF16)
        G2 = sb("G2", [128, 2, 512], BF16); V2 = sb("V2", [32, 512], BF16)
        PLW = sb("PLW", [128, 2, 4, 128], BF16)
        CARRY = sb("CARRY", [128, 2, 16])
        HALO = sb("HALO", [128, 2, 4, 30])
        PHALO = sb("PHALO", [128, 2, 4, 16])
        ST = sb("ST", [128, 2, 4, 64])
        CONDT = sb("CONDT", [128, 8])
        XIN = sb("XIN", [128, TB, D])
        YBR = sb("YBR", [128, 3, 4, T], BF16)
        AW = 78 * T + 9600
        AR = sb("AR", [128, AW])

        class Alloc:
            def __init__(self, base=0):
                self.o = base

            def f(self, n, shape=None):
                ap = AR[:, self.o:self.o + n]
                self.o += n
                assert self.o <= AW, (self.o, AW)
                return ap

            def t3(self, a, b):
                return self.f(a * b).rearrange("p (a b) -> p a b", a=a)

            def b3(self, a, b):
                n = (a * b + 1) // 2
                return self.f(n).bitcast(BF16).rearrange("p (a b) -> p a b", a=a)

        A = Alloc()
        ZR = A.t3(4, T); ZK = A.t3(4, T); ZV = A.t3(4, T); YF = A.t3(4, T)
        TA = A.f(T); TBm = A.f(T); TC = A.f(T); TD = A.f(T)
        mark_dead1 = A.o
        SW = A.t3(4, T); LC = A.t3(4, T); PINV = A.t3(4, T); AH = A.t3(4, T); KKN = A.t3(4, T)
        mark_dead1_end = A.o
        RAWS = [A.f(T), A.f(T), A.f(T)]; ZTMP = A.f(T)
        raw_rr = [0]
        T12 = A.b3(1, T)[:, 0, :]; LG = A.b3(1, T)[:, 0, :]; LV = A.b3(1, T)[:, 0, :]
        rt3 = A.b3 if REC_BF16 else A.t3
        nt3 = A.b3 if (REC_BF16 and NEU_BF16) else A.t3
        BDA = rt3(4, 128); BDB = rt3(4, 128); BDK = rt3(4, 128); BDR = rt3(4, 128); BDV = rt3(4, 128)
        BDA2 = nt3(4, 128); BDB2 = nt3(4, 128)
        MN = [nt3(4, 128), nt3(4, 128)]; MNT = [nt3(4, 128), nt3(4, 128)]
        MKA = rt3(4, 128); MBR = rt3(4, 128); MKR = rt3(4, 128); GM = nt3(4, 128); GMr = rt3(4, 128)
        BDBT = rt3(4, 128); BDKT = rt3(4, 128)
        VT = rt3(4, 64); XTt = (A.t3 if MM2_F32 else rt3)(4, 64); UT = rt3(4, 64)
        YTBD = rt3(4, 128)
        STb = rt3(4, 64); TMPS = A.t3(4, 64)
        PC = A.t3(4, 8)
        arena_mixer_end = A.o
        B2 = Alloc(arena_mixer_end)
        HGLU = B2.t3(4, T + 30); ACC = B2.t3(4, T); CVV = B2.t3(4, T)
        PP = B2.t3(4, T + 16); SA = B2.f(T + 16); SBm = B2.f(T + 16)
        PLB = B2.b3(1, T)[:, 0, :]
        TA2 = B2.f(T); TB2 = B2.f(T); TC2 = B2.f(T); TD2 = B2.f(T)
        TCq = B2.t3(4, T); TDq = B2.t3(4, T)
        RESET4 = B2.f(4 * T)
        assert B2.o <= AW, (B2.o, AW)
        LT = {"TA": TA, "TB": TBm, "TC": TC, "TD": TD, "kTA": "TA", "kTB": "TB", "kTC": "TC", "kTD": "TD"}
        LT1 = dict(LT)
        LT2 = {"TA": TA2, "TB": TB2, "TC": TC2, "TD": TD2, "kTA": "TA2", "kTB": "TB2", "kTC": "TC2", "kTD": "TD2"}
        B3 = Alloc(0)
        MERG = B3.t3(4, T); MERGED = B3.b3(8, T); SIG = B3.f(T); TMG = B3.f(T)
        assert B3.o <= 12 * T
        B4 = Alloc(mark_dead1)
        U = B4.t3(8, T); RTMP = B4.f(T); RTMP2 = B4.f(T)
        assert B4.o <= mark_dead1_end, (B4.o, AW)
        H1 = Alloc(0).b3(32, T)
        B5 = Alloc(0)
        AWS = [B5.t3(8, 512), B5.t3(8, 512)]
        assert B5.o <= AW

        PSB = [pst(f"PS{i}", [128, 512]) for i in range(8)]
        ps_rr = [0]

        bank_pools = {"ALL": list(range(8)), "A": [0, 1, 2, 3, 4], "B": [5, 6, 7]}
        bank_rr = {"ALL": 0, "A": 0, "B": 0}
        cur_pool = ["ALL"]

        def bank():
            pl = bank_pools[cur_pool[0]]
            i = pl[bank_rr[cur_pool[0]] % len(pl)]
            bank_rr[cur_pool[0]] += 1
            return i, PSB[i], f"PS{i}"

        sems = [es.enter_context(nc.semaphore(f"s_{e}")) for e in ENG_NAMES]
        dsems = {"sp": [es.enter_context(nc.semaphore(f"dsp{i}")) for i in range(8)],
                 "pool": [es.enter_context(nc.semaphore(f"dpl{i}")) for i in range(4)]}
        block = es.enter_context(nc.Block())
        S = Sched(nc, sems, dsems)
        S.fence_dma = bool(debug)

        def vc(name, col=0):
            c0 = VOFF[name] + col
            return VC[:, c0:c0 + 1]

        def conv_dma(l, g, src_ap, kc, n, col0=0, ncols=None):
            ncols = n if ncols is None else ncols
            dst = wsc[l, g][:, 0:kc * n].rearrange("p (kc n) -> p kc n", kc=kc)[:, :, col0:col0 + ncols]
            S.dma("pool", dst, src_ap, writes=[f"wsc{l}_{g}"])

        def group_src(l, g):
            w_in = dr["w_in"][l].rearrange("(kc p) n -> p kc n", p=128)
            if g < 6:
                return [(w_in[:, :, g * 512:(g + 1) * 512], 8, 512, 0, 512)]
            if g == 6:
                r = [(w_in[:, :, 3072:3328], 8, 288, 0, 256)]
                if l >= 1:
                    r.append((dr["w_in_vres"][l - 1].rearrange("(kc p) n -> p kc n", p=128), 8, 288, 256, 32))
                return r
            if g < 13:
                i = g - 7
                b, q = i // 2, i % 2
                c0 = 3328 + (8 * b + 4 * q) * 128
                return [(w_in[:, :, c0:c0 + 512], 8, 512, 0, 512)]
            if g < 19:
                i = g - 13
                b, q = i // 2, i % 2
                w = dr[("rw_wo", "cv_wo", "pl_wo")[b]][l].rearrange("(kc p) n -> p kc n", p=128)
                return [(w[:, :, q * 512:(q + 1) * 512], 4, 512, 0, 512)]
            if g < 21:
                q = g - 19
                w = dr["w_out"][l].rearrange("(kc p) n -> p kc n", p=128)
                return [(w[:, :, q * 512:(q + 1) * 512], 8, 512, 0, 512)]
            if g < 29:
                j = g - 21
                w = dr["mlp_w1"][l].rearrange("(kc p) n -> p kc n", p=128)
                return [(w[:, :, j * 512:(j + 1) * 512], 8, 512, 0, 512)]
            m = g - 29
            w = dr["mlp_w2"][l].rearrange("(kc p) n -> p kc n", p=128)
            return [(w[:, :, m * 128:(m + 1) * 128], 32, 128, 0, 128)]

        for l in range(NL):
            for g in range(NG):
                for (src, kc, n, c0, ncol) in group_src(l, g):
                    conv_dma(l, g, src, kc, n, c0, ncol)

        S.op("pool", lambda e: e.memset(IDENT[:], 0.0), writes=["IDENT"])
        S.op("pool", lambda e: e.affine_select(out=IDENT[:], in_=IDENT[:], pattern=[[-1, 128]], compare_op=ALU.not_equal,
                                               fill=1.0, base=0, channel_multiplier=1), reads=["IDENT"], writes=["IDENT"])
        S.op("pool", lambda e: e.tensor_tensor(out=I2[:], in0=IDENT[:, 0:64], in1=IDENT[:, 64:128], op=ALU.add),
             reads=["IDENT"], writes=["I2"])
        IDR = sb("IDR", [128, 128], BF16 if REC_BF16 else F32)
        I2R = sb("I2R", [128, 64], BF16 if REC_BF16 else F32)
        S.op("pool", lambda e: e.tensor_copy(out=IDR[:], in_=IDENT[:]), reads=["IDENT"], writes=["IDR"])
        S.op("pool", lambda e: e.tensor_copy(out=I2R[:], in_=I2[:]), reads=["I2"], writes=["I2R"])

        def tri(M, key, cmp_op, sgn=1):
            S.op("pool", lambda e: e.memset(M[:], 1.0), writes=[key])
            S.op("pool", lambda e: e.affine_select(out=M[:], in_=M[:], pattern=[[sgn, 128]], compare_op=cmp_op,
                                                   fill=0.0, base=0, channel_multiplier=-sgn), reads=[key], writes=[key])
            S.op("pool", lambda e: e.memset(M[0:64, 64:128], 0.0), reads=[key], writes=[key])
            S.op("pool", lambda e: e.memset(M[64:128, 0:64], 0.0), reads=[key], writes=[key])
        tri(MSU, "MSU", ALU.is_gt)
        tri(MIU, "MIU", ALU.is_ge)
        tri(MSL, "MSL", ALU.is_gt, -1)
        for M, key, val in ((OBD, "OBD", 1.0), (OBD64, "OBD64", 1.0 / 64)):
            S.op("pool", lambda e, M=M, val=val: e.memset(M[:], val), writes=[key])
            S.op("pool", lambda e, M=M: e.memset(M[0:64, 64:128], 0.0), reads=[key], writes=[key])
            S.op("pool", lambda e, M=M: e.memset(M[64:128, 0:64], 0.0), reads=[key], writes=[key])
        S.op("pool", lambda e: e.memset(O512[:], 1.0 / 512), writes=["O512"])
        S.op("pool", lambda e: e.memset(O1024[:], 1.0 / 1024), writes=["O1024"])
        S.op("pool", lambda e: e.memset(RESETM[:], 1.0), writes=["RESETM"])
        S.op("pool", lambda e: e.memset(RESETM[:].rearrange("p (c j) -> p c j", j=64)[:, :, 0:1], 0.0),
             reads=["RESETM"], writes=["RESETM"])
        S.op("pool", lambda e: e.memset(RESET4, 1.0), writes=["RESET4"])
        S.op("pool", lambda e: e.memset(RESET4.rearrange("p (c j) -> p c j", j=64)[:, :, 0:1], 0.0), reads=["RESET4"], writes=["RESET4"])
        for i, v in enumerate((GN_EPS, LN_EPS / (ALPHA * ALPHA), LN_EPS, 0.0)):
            S.op("pool", lambda e, i=i, v=v: e.memset(EPS[:, i:i + 1], float(v)), reads=["EPS"], writes=["EPS"])
        S.op("pool", lambda e: e.iota(IOTI[:], pattern=[[1, 16]], base=1, channel_multiplier=0), writes=["IOTI"])
        for g in range(4):
            S.op("pool", lambda e, g=g: e.tensor_copy(out=RC16[:, g, :], in_=IOTI[:]), reads=["IOTI", "RC16"], writes=["RC16"])
            S.op("pool", lambda e, g=g: e.tensor_scalar(out=RC16[:, g, :], in0=RC16[:, g, :], scalar1=float(2 ** (g + 1)), scalar2=None,
                                                      op0=ALU.min), reads=["RC16"], writes=["RC16"])
        S.op("dve", lambda e: e.reciprocal(out=RC16[:], in_=RC16[:]), reads=["RC16"], writes=["RC16"])
        for nm, Tn in (("CARRY", CARRY), ("HALO", HALO), ("PHALO", PHALO), ("ST", ST)):
            S.op("pool", lambda e, Tn=Tn: e.memset(Tn[:], 0.0), writes=[nm])
        for nm, Tn in (("BDA", BDA), ("BDB", BDB), ("BDK", BDK), ("BDR", BDR), ("BDV", BDV), ("YTBD", YTBD), ("BDA2", BDA2), ("BDB2", BDB2), ("STb", STb)):
            S.op("pool", lambda e, Tn=Tn: e.memset(Tn, 0.0), writes=[nm])
        S.op("pool", lambda e: e.memset(W2P[:], 0.0), writes=["W2P"])
        S.op("pool", lambda e: e.memset(A2P[:], 0.0), writes=["A2P"])
        for l in range(NL):
            S.dma("pool", W2P[0:64, l, :], dr["rw_w2"][l], reads=["W2P"], writes=["W2P"])
            S.dma("pool", A2P[64:128, l, :], dr["rw_a2"][l], reads=["A2P"], writes=["A2P"])
            S.dma("pool", G2[:, l, :], dr["rw_g2"][l], writes=["G2"])
            S.dma("pool", PLW[:, l, :, :], dr["pl_w"][l].rearrange("g c d -> c g d"), writes=["PLW"])
        if NL > 1:
            S.dma("pool", V2[:], dr["rw_v2"][0], writes=["V2"])
        S.dma("sp", PK[:], dr["vecs"].rearrange("(b p) n -> p b n", p=128), writes=["PK"])
        for b in range(NVB):
            bi, pb, pk = bank()
            S.op("pe", lambda e, b=b, pb=pb: e.transpose(pb[:, 0:128], PK[:, b, :], IDENT[:]),
                 reads=["PK", "IDENT"], writes=[pk])
            S.op("act", lambda e, b=b, pb=pb: e.copy(out=VC[:, b * 128:(b + 1) * 128], in_=pb[:, 0:128]),
                 reads=[pk], writes=["VC"])
        S.op("act", lambda e: e.activation(out=CONDT[:], in_=VC[:, VOFF["c"]:VOFF["c"] + 8], func=AF.Silu),
             reads=["VC"], writes=["CONDT"])
        for l in range(NL):
            bi, pb, pk = bank()
            for gq in range(12):
                slot = gq % 2
                S.dma("sp", AWS[slot], dr["ada_w"][l].rearrange("(kc p) n -> p kc n", p=128)[:, :, gq * 512:(gq + 1) * 512],
                      writes=[f"AWS{slot}"])
                for mi in range(4):
                    j = gq * 4 + mi

                    def fn(e, slot=slot, mi=mi, j=j, pb=pb):
                        for kc in range(8):
                            ins = e.matmul(pb[:, j:j + 1], lhsT=AWS[slot][:, kc, mi * 128:(mi + 1) * 128],
                                           rhs=CONDT[:, kc:kc + 1], start=(kc == 0), stop=(kc == 7))
                        return ins
                    S.op("pe", fn, reads=[f"AWS{slot}", "CONDT"], writes=[pk])
            S.op("dve", lambda e, l=l, pb=pb: e.tensor_tensor(out=ADA[:, l, :], in0=pb[:, 0:48],
                                                             in1=VC[:, VOFF[f"ada_b{l}"]:VOFF[f"ada_b{l}"] + 48], op=ALU.add),
                 reads=[pk, "VC"], writes=["ADA"])
        for l in range(NL):
            S.op("dve", lambda e, l=l: e.tensor_scalar(out=DC[:, l, 0:8], in0=ADA[:, l, 8:16], scalar1=1.0, scalar2=None, op0=ALU.add),
                 reads=["ADA"], writes=["DC"])
            S.op("dve", lambda e, l=l: e.tensor_scalar(out=DC[:, l, 8:16], in0=ADA[:, l, 32:40], scalar1=1.0, scalar2=None, op0=ALU.add),
                 reads=["ADA", "DC"], writes=["DC"])
            S.op("dve", lambda e, l=l: e.tensor_scalar(out=DC[:, l, 16:24], in0=ADA[:, l, 16:24], scalar1=1.0 / ALPHA, scalar2=None, op0=ALU.mult),
                 reads=["ADA", "DC"], writes=["DC"])
            S.op("dve", lambda e, l=l: e.tensor_scalar(out=DC[:, l, 24:32], in0=ADA[:, l, 40:48], scalar1=1.0 / ALPHA, scalar2=None, op0=ALU.mult),
                 reads=["ADA", "DC"], writes=["DC"])
            m0 = VOFF[f"mu{l}"]
            S.op("dve", lambda e, l=l, m0=m0: e.tensor_scalar(out=DC[:, l, 32:46], in0=VC[:, m0:m0 + 14], scalar1=-1.0, scalar2=1.0,
                                                           op0=ALU.mult, op1=ALU.add), reads=["VC", "DC"], writes=["DC"])
            mv0 = VOFF["muv"]
            S.op("dve", lambda e, l=l, mv0=mv0: e.tensor_scalar(out=DC[:, l, 46:47], in0=VC[:, mv0:mv0 + 1], scalar1=-1.0, scalar2=1.0,
                                                             op0=ALU.mult, op1=ALU.add), reads=["VC", "DC"], writes=["DC"])
            k0 = VOFF[f"ka{l}"]
            S.op("dve", lambda e, l=l, k0=k0: e.tensor_scalar(out=DC[:, l, 48:52], in0=VC[:, k0:k0 + 4], scalar1=-1.0, scalar2=1.0,
                                                           op0=ALU.mult, op1=ALU.add), reads=["VC", "DC"], writes=["DC"])
        S.fence()

        ws_rr = [0]

        def wload(l, g):
            s = ws_rr[0]
            ws_rr[0] = (s + 1) % 4
            if g == 6:
                kc, n, nv = 8, 288, (288 if l >= 1 else 256)
            elif 13 <= g < 19:
                kc, n, nv = 4, 512, 512
            elif g >= 29:
                kc, n, nv = 32, 128, 128
            else:
                kc, n, nv = 8, 512, 512
            dst = WS[:, s, 0:kc * n].rearrange("p (kc n) -> p kc n", kc=kc)[:, :, 0:nv]
            src = wsc[l, g][:, 0:kc * n].rearrange("p (kc n) -> p kc n", kc=kc)[:, :, 0:nv]
            S.dma("sp", dst, src, reads=[f"wsc{l}_{g}"], writes=[f"WS{s}"])
            return s

        def wview(s, kc, n):
            return WS[:, s, 0:kc * n].rearrange("p (kc n) -> p kc n", kc=kc)

        def dump(name, ap, keys, idx=None):
            if name in dbg:
                dst = dbg[name] if idx is None else dbg[name][idx]
                S.dma("sp", dst, ap, reads=keys)

        def ln_stats(srcs, src_keys, sq_eng_out, ONESM, eps_col):
            n = len(srcs)
            tA, tB, tC, tD = LT["TA"], LT["TB"], LT["TC"], LT["TD"]
            kA, kB, kC, kD = LT["kTA"], LT["kTB"], LT["kTC"], LT["kTD"]
            bi1, pb1, pk1 = bank()
            bi2, pb2, pk2 = bank()

            def fm(e):
                for i, s_ in enumerate(srcs):
                    ins = e.matmul(pb1[:, 0:T], lhsT=ONESM[:], rhs=s_, start=(i == 0), stop=(i == n - 1))
                return ins
            S.op("pe", fm, reads=src_keys, writes=[pk1])
            for i, s_ in enumerate(srcs):
                S.op("act", lambda e, s_=s_: e.activation(out=tC, in_=s_, func=AF.Square), reads=[src_keys[i]], writes=[kC])
                S.op("pe", lambda e, i=i: e.matmul(pb2[:, 0:T], lhsT=ONESM[:], rhs=tC, start=(i == 0), stop=(i == n - 1)),
                     reads=[kC], writes=[pk2])
            S.op("act", lambda e: e.copy(out=tA, in_=pb1[:, 0:T]), reads=[pk1], writes=[kA])
            S.op("act", lambda e: e.activation(out=tD, in_=pb1[:, 0:T], func=AF.Square), reads=[pk1], writes=[kD])
            S.op("dve", lambda e: e.tensor_tensor(out=tB, in0=pb2[:, 0:T], in1=tD, op=ALU.subtract), reads=[pk2, kD], writes=[kB])
            S.op("dve", lambda e: e.tensor_scalar(out=tB, in0=tB, scalar1=0.0, scalar2=None, op0=ALU.max), reads=[kB], writes=[kB])
            S.op("act", lambda e: e.activation(out=tB, in_=tB, func=AF.Sqrt, bias=EPS[:, eps_col:eps_col + 1], scale=1.0),
                 reads=[kB, "EPS"], writes=[kB])
            S.op("dve", lambda e: e.reciprocal(out=tB, in_=tB), reads=[kB], writes=[kB])
            return tA, tB

        def modulate(l, which):
            o_sc = 0 if which == "m" else 8
            o_sh = 0 if which == "m" else 24
            for m in range(8):
                S.op("act", lambda e, m=m: e.activation(out=H[:, m, :], in_=X[:, m, :], func=AF.Identity,
                                                        bias=ADA[:, l, o_sh + m:o_sh + m + 1], scale=DC[:, l, o_sc + m:o_sc + m + 1]),
                     reads=[f"X{m}", "ADA", "DC"], writes=[f"H{m}"])

        def residual_ln(l, which, get_ps):
            o_gt = 16 if which == "m" else 24
            gname, bname = (f"lnmg{l}", f"lnmb{l}") if which == "m" else (f"lnfg{l}", f"lnfb{l}")
            for m in range(8):
                pb, pk = get_ps(m)
                S.op("dve", lambda e, m=m, pb=pb: e.scalar_tensor_tensor(out=U[:, m, :], in0=pb[:, 0:T], scalar=DC[:, l, o_gt + m:o_gt + m + 1],
                                                                     in1=X[:, m, :], op0=ALU.mult, op1=ALU.add),
                     reads=[pk, "DC", f"X{m}"], writes=[f"U{m}"])
            mean, rstd = ln_stats([U[:, m, :] for m in range(8)], [f"U{m}" for m in range(8)], None, O1024, 1)
            for m in range(8):
                S.op("dve", lambda e, m=m: e.tensor_tensor(out=U[:, m, :], in0=U[:, m, :], in1=mean, op=ALU.subtract),
                     reads=[f"U{m}", "TA"], writes=[f"U{m}"])
                S.op("pool", lambda e, m=m: e.tensor_tensor(out=U[:, m, :], in0=U[:, m, :], in1=rstd, op=ALU.mult),
                     reads=[f"U{m}", "TB"], writes=[f"U{m}"])
                S.op("dve", lambda e, m=m: e.tensor_scalar(out=X[:, m, :], in0=U[:, m, :], scalar1=vc(gname, m), scalar2=vc(bname, m),
                                                         op0=ALU.mult, op1=ALU.add), reads=[f"U{m}", "VC"], writes=[f"X{m}"])

        for ti in range(NT):
            t0 = ti * T
            S.dma("sp", XIN[:], dr["x"][t0:t0 + T, :].rearrange("(b p) d -> p b d", p=128), writes=["XIN"])
            for fc in range(8):
                bi, pb, pk = bank()

                def ft(e, fc=fc, pb=pb):
                    for tb in range(TB):
                        ins = e.transpose(pb[:, tb * 128:(tb + 1) * 128], XIN[:, tb, fc * 128:(fc + 1) * 128], IDENT[:])
                    return ins
                S.op("pe", ft, reads=["XIN", "IDENT"], writes=[pk])
                S.op("act" if fc % 2 else "dve",
                     (lambda e, fc=fc, pb=pb: e.copy(out=X[:, fc, :], in_=pb[:, 0:T])) if fc % 2 else
                     (lambda e, fc=fc, pb=pb: e.tensor_copy(out=X[:, fc, :], in_=pb[:, 0:T])),
                     reads=[pk], writes=[f"X{fc}"])
            for l in range(NL):
                modulate(l, "m")
                slot_of = {}

                def proj_chunk(j, ncols=128, col_in_group=None):
                    g = j // 4 if j < 24 else 6
                    if g not in slot_of:
                        slot_of[g] = wload(l, g)
                    s = slot_of[g]
                    n = 512 if g < 6 else 288
                    wv = wview(s, 8, n)
                    c0 = (j % 4) * 128 if j < 24 else (j - 24) * 128
                    bi, pb, pk = bank()

                    def fn(e, pb=pb, wv=wv, c0=c0, ncols=ncols):
                        for kc in range(8):
                            ins = e.matmul(pb[0:ncols, 0:T], lhsT=wv[:, kc, c0:c0 + ncols], rhs=H[:, kc, :],
                                           start=(kc == 0), stop=(kc == 7))
                        return ins
                    S.op("pe", fn, reads=[f"WS{s}"] + [f"H{m}" for m in range(8)], writes=[pk])
                    return pb, pk

                def token_shift(pb, pk, cidx, mu_ap, omu_ap, dst, dst_keys, np_=128):
                    k = raw_rr[0]
                    raw_rr[0] = (k + 1) % 3
                    Rk, rk_key = RAWS[k], f"RAW{k}"
                    ck = f"CARRY{l}_{cidx}"
                    S.op("act", lambda e: e.activation(out=Rk[0:np_, 0:T], in_=pb[0:np_, 0:T], func=AF.Identity, scale=omu_ap),
                         reads=[pk, "DC"], writes=[rk_key])
                    S.op("dve", lambda e: e.scalar_tensor_tensor(out=dst[:, 1:T], in0=pb[0:np_, 0:T - 1], scalar=mu_ap, in1=Rk[0:np_, 1:T],
                                                                 op0=ALU.mult, op1=ALU.add), reads=[pk, rk_key, "VC"], writes=dst_keys)
                    S.op("dve", lambda e: e.scalar_tensor_tensor(out=dst[:, 0:1], in0=CARRY[0:np_, l, cidx:cidx + 1], scalar=mu_ap, in1=Rk[0:np_, 0:1],
                                                                 op0=ALU.mult, op1=ALU.add), reads=[ck, rk_key, "VC"] + list(dst_keys), writes=dst_keys)
                    S.op("act", lambda e: e.copy(out=CARRY[0:np_, l, cidx:cidx + 1], in_=pb[0:np_, T - 1:T]), reads=[pk, ck], writes=[ck])

                for j in range(26):
                    pb, pk = proj_chunk(j)
                    if j < 12:
                        dstT = (ZR, ZK, ZV)[j // 4]
                        nm = ("ZR", "ZK", "ZV")[j // 4]
                        token_shift(pb, pk, j, vc(f"mu{l}", j), DC[:, l, 32 + j:33 + j], dstT[:, j % 4, :], [f"{nm}{j % 4}"])
                    elif j == 12:
                        token_shift(pb, pk, j, vc(f"mu{l}", j), DC[:, l, 32 + j:33 + j], ZTMP, ["ZTMP"])
                        S.op("act", lambda e: e.activation(out=T12[0:64, :], in_=ZTMP[0:64, :], func=AF.Tanh), reads=["ZTMP"], writes=["T12"])
                        S.op("dve", lambda e: e.tensor_copy(out=T12[64:128, :], in_=ZTMP[64:128, :]), reads=["ZTMP", "T12"], writes=["T12"])
                    elif j == 13:
                        token_shift(pb, pk, j, vc(f"mu{l}", j), DC[:, l, 32 + j:33 + j], ZTMP, ["ZTMP"])
                        S.op("act", lambda e: e.activation(out=LG, in_=ZTMP, func=AF.Sigmoid), reads=["ZTMP"], writes=["LG"])
                    elif j < 18:
                        cc = j - 14
                        S.op("act", lambda e, cc=cc, pb=pb: e.copy(out=CVV[:, cc, :], in_=pb[:, 0:T]), reads=[pk], writes=[f"CVV{cc}"])
                    elif j < 22:
                        cc = j - 18
                        S.op("act", lambda e, pb=pb: e.activation(out=TA, in_=pb[:, 0:T], func=AF.Sigmoid), reads=[pk], writes=["TA"])
                        if cc == 0:
                            S.op("pool", lambda e: e.tensor_copy(out=HGLU[:, :, 0:30], in_=HALO[:, l, :, :]), reads=["HALO"],
                                 writes=[f"HGLU{c_}" for c_ in range(4)])
                        S.op("dve", lambda e, cc=cc: e.tensor_tensor(out=HGLU[:, cc, 30:30 + T], in0=CVV[:, cc, :], in1=TA, op=ALU.mult),
                             reads=[f"CVV{cc}", "TA"], writes=[f"HGLU{cc}"])
                    else:
                        gg = j - 22
                        if gg == 0:
                            S.op("pool", lambda e: e.tensor_copy(out=PP[:, :, 0:16], in_=PHALO[:, l, :, :]), reads=["PHALO"],
                                 writes=[f"PP{c_}" for c_ in range(4)])
                        S.op("act", lambda e, gg=gg, pb=pb: e.copy(out=PP[:, gg, 16:16 + T], in_=pb[:, 0:T]), reads=[pk], writes=[f"PP{gg}"])
                if l >= 1:
                    s = slot_of[6]
                    wv = wview(s, 8, 288)
                    bi, pb, pk = bank()

                    def fnv(e, pb=pb, wv=wv):
                        for kc in range(8):
                            ins = e.matmul(pb[0:32, 0:T], lhsT=wv[:, kc, 256:288], rhs=H[:, kc, :], start=(kc == 0), stop=(kc == 7))
                        return ins
                    S.op("pe", fnv, reads=[f"WS{s}"] + [f"H{m}" for m in range(8)], writes=[pk])
                    token_shift(pb, pk, 14, VC[0:32, VOFF["muv"]:VOFF["muv"] + 1], DC[0:32, l, 46:47], ZTMP[0:32, :], ["ZTMP"], np_=32)
                    S.op("act", lambda e: e.copy(out=LV[0:32, :], in_=ZTMP[0:32, :]), reads=["ZTMP"], writes=["LV"])
                dump(f"z_r{l}", ZR, ["ZR0", "ZR1", "ZR2", "ZR3"], ti)
                dump(f"z_k{l}", ZK, ["ZK0", "ZK1", "ZK2", "ZK3"], ti)
                dump(f"z_v{l}", ZV, ["ZV0", "ZV1", "ZV2", "ZV3"], ti)

                S_main = S
                S = Rec(); recB = S
                cur_pool[0] = "B"; LT.update(LT2)
                for cc in range(4):
                    eng = "dve" if cc % 2 == 0 else "pool"
                    wbase = VOFF[f"cvw{l}"]
                    S.op(eng, lambda e, cc=cc: e.tensor_scalar(out=ACC[:, cc, :], in0=HGLU[:, cc, 0:T], scalar1=VC[:, wbase + cc:wbase + cc + 1],
                                                              scalar2=vc(f"cvb{l}", cc), op0=ALU.mult, op1=ALU.add),
                         reads=[f"HGLU{cc}", "VC"], writes=[f"ACC{cc}"])
                    for jt in range(1, CONVW):
                        if eng == "dve":
                            S.op(eng, lambda e, cc=cc, jt=jt: e.scalar_tensor_tensor(
                                out=ACC[:, cc, :], in0=HGLU[:, cc, jt:jt + T], scalar=VC[:, wbase + jt * 4 + cc:wbase + jt * 4 + cc + 1],
                                in1=ACC[:, cc, :], op0=ALU.mult, op1=ALU.add), reads=[f"HGLU{cc}", "VC", f"ACC{cc}"], writes=[f"ACC{cc}"])
                        else:
                            ct, ck = ((SA, "SA"), (SBm, "SB"))[jt % 2]
                            S.op("act", lambda e, cc=cc, jt=jt, ct=ct: e.activation(
                                out=ct[:, 0:T], in_=HGLU[:, cc, jt:jt + T], func=AF.Identity,
                                scale=VC[:, wbase + jt * 4 + cc:wbase + jt * 4 + cc + 1]), reads=[f"HGLU{cc}", "VC"], writes=[ck])
                            S.op("pool", lambda e, cc=cc, ct=ct: e.tensor_tensor(out=ACC[:, cc, :], in0=ACC[:, cc, :], in1=ct[:, 0:T], op=ALU.add),
                                 reads=[ck, f"ACC{cc}"], writes=[f"ACC{cc}"])
                S.op("pool", lambda e: e.tensor_copy(out=HALO[:, l, :, :], in_=HGLU[:, :, T:T + 30]),
                     reads=[f"HGLU{c_}" for c_ in range(4)], writes=["HALO"])
                dump(f"conv{l}", ACC, [f"ACC{c_}" for c_ in range(4)], ti)
                mean, rstd = ln_stats([ACC[:, cc, :] for cc in range(4)], [f"ACC{cc}" for cc in range(4)], None, O512, 2)
                for cc in range(4):
                    S.op("dve", lambda e, cc=cc: e.tensor_tensor(out=ACC[:, cc, :], in0=ACC[:, cc, :], in1=mean, op=ALU.subtract),
                         reads=[f"ACC{cc}", LT["kTA"]], writes=[f"ACC{cc}"])
                    S.op("pool", lambda e, cc=cc: e.tensor_tensor(out=ACC[:, cc, :], in0=ACC[:, cc, :], in1=rstd, op=ALU.mult),
                         reads=[f"ACC{cc}", LT["kTB"]], writes=[f"ACC{cc}"])
                    S.op("dve", lambda e, cc=cc: e.tensor_scalar(out=ACC[:, cc, :], in0=ACC[:, cc, :], scalar1=vc(f"cvg{l}", cc),
                                                                scalar2=vc(f"cvbb{l}", cc), op0=ALU.mult, op1=ALU.add),
                         reads=[f"ACC{cc}", "VC"], writes=[f"ACC{cc}"])
                    S.op("act", lambda e, cc=cc: e.activation(out=YBR[:, 1, cc, :], in_=ACC[:, cc, :], func=AF.Silu),
                         reads=[f"ACC{cc}"], writes=[f"YBR1_{cc}"])
                for gg in range(4):
                    win = 2 ** (gg + 1)
                    eng = "pool" if gg % 2 == 0 else "dve"
                    src = PP[:, gg, :]
                    cur_key = f"PP{gg}"
                    bufs = [(SA, "SA"), (SBm, "SB")]
                    sh = 1
                    k_ = 0
                    while sh < win:
                        dstb, dkey = bufs[k_ % 2]
                        lo = 2 * sh - 1
                        S.op(eng, lambda e, src=src, dstb=dstb, sh=sh, lo=lo: e.tensor_tensor(
                            out=dstb[:, lo:T + 16], in0=src[:, lo:T + 16], in1=src[:, lo - sh:T + 16 - sh], op=ALU.add),
                            reads=[cur_key], writes=[dkey])
                        src, cur_key = dstb, dkey
                        sh *= 2
                        k_ += 1
                    dstb, dkey = bufs[k_ % 2]
                    S.op("dve", lambda e, src=src, dstb=dstb, gg=gg, win=win: e.scalar_tensor_tensor(
                        out=dstb[:, 16:16 + T], in0=src[:, 16:16 + T], scalar=1.0 / win, in1=PP[:, gg, 16:16 + T],
                        op0=ALU.mult, op1=ALU.subtract), reads=[cur_key, f"PP{gg}"], writes=[dkey])
                    if ti == 0:
                        S.op(eng, lambda e, src=src, gg=gg: e.tensor_tensor(out=src[:, 16:32], in0=src[:, 16:32], in1=RC16[:, gg, :], op=ALU.mult),
                             reads=[cur_key, "RC16", dkey], writes=[cur_key])
                        S.op(eng, lambda e, src=src, dstb=dstb, gg=gg: e.tensor_tensor(out=dstb[:, 16:32], in0=src[:, 16:32], in1=PP[:, gg, 16:32],
                                                                                  op=ALU.subtract), reads=[cur_key, f"PP{gg}", dkey], writes=[dkey])
                    S.op("act", lambda e, dstb=dstb: e.copy(out=PLB[:, :], in_=dstb[:, 16:16 + T]), reads=[dkey], writes=["PLB"])
                    bi, pb, pk = bank()
                    S.op("pe", lambda e, gg=gg, pb=pb: e.matmul(pb[:, 0:T], lhsT=PLW[:, l, gg, :], rhs=PLB[:, :], start=True, stop=True),
                         reads=["PLB", "PLW"], writes=[pk])
                    S.op("act", lambda e, gg=gg, pb=pb: e.activation(out=YBR[:, 2, gg, :], in_=pb[:, 0:T], func=AF.Identity,
                                                                   scale=vc(f"plsc{l}", gg)), reads=[pk, "VC"], writes=[f"YBR2_{gg}"])
                S.op("pool", lambda e: e.tensor_copy(out=PHALO[:, l, :, :], in_=PP[:, :, T:T + 16]),
                     reads=[f"PP{c_}" for c_ in range(4)], writes=["PHALO"])

                S = Rec(); recA = S
                cur_pool[0] = "A"; LT.update(LT1)
                allk = lambda nm: [f"{nm}{p}" for p in range(4)]
                f4T = lambda t3: t3.rearrange("p a b -> p (a b)")
                for p in range(4):
                    bi, pb, pk = bank()
                    S.op("pe", lambda e, p=p, pb=pb: e.matmul(pb[:, 0:T], lhsT=W2P[:, l, p * 128:(p + 1) * 128], rhs=T12, start=True, stop=True),
                         reads=["T12", "W2P"], writes=[pk])
                    S.op("act", lambda e, p=p, pb=pb: e.activation(out=SW[:, p, :], in_=pb[:, 0:T], func=AF.Sigmoid, bias=vc(f"w0{l}", p), scale=1.0),
                         reads=[pk, "VC"], writes=[f"SW{p}"])
                    bi, pb, pk = bank()
                    S.op("pe", lambda e, p=p, pb=pb: e.matmul(pb[:, 0:T], lhsT=A2P[:, l, p * 128:(p + 1) * 128], rhs=T12, start=True, stop=True),
                         reads=["T12", "A2P"], writes=[pk])
                    S.op("act", lambda e, p=p, pb=pb: e.activation(out=AH[:, p, :], in_=pb[:, 0:T], func=AF.Sigmoid, bias=vc(f"a0{l}", p), scale=1.0),
                         reads=[pk, "VC"], writes=[f"AH{p}"])
                    if l >= 1:
                        bi, pb, pk = bank()
                        S.op("pe", lambda e, p=p, pb=pb: e.matmul(pb[:, 0:T], lhsT=V2[:, p * 128:(p + 1) * 128], rhs=LV[0:32, :], start=True, stop=True),
                             reads=["LV", "V2"], writes=[pk])
                        S.op("act", lambda e, p=p, pb=pb: e.activation(out=TCq[:, p, :], in_=pb[:, 0:T], func=AF.Sigmoid, bias=vc("v0", p), scale=1.0),
                             reads=[pk, "VC"], writes=[f"TCq{p}"])
                if l == 0:
                    S.op("pool", lambda e: e.tensor_copy(out=VF[:, :, :], in_=ZV), reads=allk("ZV"), writes=allk("VF"))
                else:
                    S.op("pool", lambda e: e.tensor_tensor(out=TDq, in0=VF[:, :, :], in1=ZV, op=ALU.subtract), reads=allk("VF") + allk("ZV"), writes=allk("TDq"))
                    S.op("dve", lambda e: e.tensor_tensor(out=TDq, in0=TDq, in1=TCq, op=ALU.mult), reads=allk("TDq") + allk("TCq"), writes=allk("TDq"))
                    S.op("dve", lambda e: e.tensor_tensor(out=ZV, in0=ZV, in1=TDq, op=ALU.add), reads=allk("ZV") + allk("TDq"), writes=allk("ZV"))
                for p in range(4):
                    S.op("act", lambda e, p=p: e.activation(out=TCq[:, p, :], in_=ZK[:, p, :], func=AF.Square, scale=vc(f"kk{l}", p)),
                         reads=[f"ZK{p}", "VC"], writes=[f"TCq{p}"])
                    bi, pb, pk = bank()
                    S.op("pe", lambda e, p=p, pb=pb: e.matmul(pb[:, 0:T], lhsT=OBD[:], rhs=TCq[:, p, :], start=True, stop=True), reads=[f"TCq{p}", "OBD"], writes=[pk])
                    S.op("act", lambda e, p=p, pb=pb: e.activation(out=TDq[:, p, :], in_=pb[:, 0:T], func=AF.Sqrt), reads=[pk], writes=[f"TDq{p}"])
                S.op("dve", lambda e: e.tensor_scalar(out=TDq, in0=TDq, scalar1=1e-12, scalar2=-1.0, op0=ALU.max, op1=ALU.mult), reads=allk("TDq"), writes=allk("TDq"))
                S.op("dve", lambda e: e.reciprocal(out=TDq, in_=TDq), reads=allk("TDq"), writes=allk("TDq"))
                for p in range(4):
                    S.op("dve", lambda e, p=p: e.scalar_tensor_tensor(out=KKN[:, p, :], in0=ZK[:, p, :], scalar=vc(f"kk{l}", p), in1=TDq[:, p, :],
                                                                      op0=ALU.mult, op1=ALU.mult), reads=[f"ZK{p}", f"TDq{p}", "VC"], writes=[f"KKN{p}"])
                    S.op("pool", lambda e, p=p: e.tensor_scalar(out=TCq[:, p, :], in0=AH[:, p, :], scalar1=vc(f"ka{l}", p), scalar2=DC[:, l, 48 + p:49 + p],
                                                               op0=ALU.mult, op1=ALU.add), reads=[f"AH{p}", "VC", "DC", f"TCq{p}"], writes=[f"TCq{p}"])
                S.op("dve", lambda e: e.tensor_tensor(out=ZK, in0=ZK, in1=TCq, op=ALU.mult), reads=allk("ZK") + allk("TCq"), writes=allk("ZK"))
                S.op("dve", lambda e: e.scalar_tensor_tensor(out=AH, in0=AH, scalar=-1.0, in1=KKN, op0=ALU.mult, op1=ALU.mult),
                     reads=allk("AH") + allk("KKN"), writes=allk("AH"))
                S.op("dve", lambda e: e.tensor_tensor_scan(out=f4T(LC), data0=RESET4, data1=f4T(SW), initial=0.0, op0=ALU.mult, op1=ALU.add),
                     reads=["RESET4"] + allk("SW"), writes=allk("LC"))
                S.op("pool", lambda e: e.tensor_tensor(out=SW, in0=LC, in1=SW, op=ALU.subtract), reads=allk("LC") + allk("SW"), writes=allk("SW"))
                S.op("act", lambda e: e.activation(out=SW, in_=SW, func=AF.Exp, scale=-CDEC), reads=allk("SW"), writes=allk("SW"))
                S.op("act", lambda e: e.activation(out=PINV, in_=LC, func=AF.Exp, scale=CDEC), reads=allk("LC"), writes=allk("PINV"))
                S.op("act", lambda e: e.activation(out=LC, in_=LC, func=AF.Exp, scale=-CDEC), reads=allk("LC"), writes=allk("LC"))
                S.op("pool", lambda e: e.tensor_copy(out=PC[:, :, 0:NCH], in_=LC.rearrange("p a (c j) -> p a c j", j=64)[:, :, :, 63]),
                     reads=allk("LC"), writes=["PC"])
                PEX, PIN, BBt, KH = SW, LC, AH, ZK
                allk = lambda nm: [f"{nm}{p}" for p in range(4)]
                same_nt = (REC_BF16 == (REC_BF16 and NEU_BF16))
                BDAn, BDBn = (BDA, BDB) if same_nt else (BDA2, BDB2)
                kBDAn, kBDBn = ("BDA", "BDB") if same_nt else ("BDA2", "BDB2")
                GMx = GM if (same_nt or MM2_F32) else GMr
                kGMx = "GM" if (same_nt or MM2_F32) else "GMr"

                def mm4(outb, lT, rT, ncol=128):
                    def fn(e):
                        for p in range(4):
                            ins = e.matmul(outb[:, p * ncol:(p + 1) * ncol], lhsT=lT[:, p, :], rhs=rT[:, p, :], start=True, stop=True)
                        return ins
                    return fn

                def mm4c(outb, lT, rconst, ncol):
                    def fn(e):
                        for p in range(4):
                            ins = e.matmul(outb[:, p * ncol:(p + 1) * ncol], lhsT=lT[:, p, :], rhs=rconst, start=True, stop=True)
                        return ins
                    return fn

                def v4(pb, n=128):
                    return pb[:, 0:4 * n].rearrange("p (a b) -> p a b", a=4)

                def bc4(M):
                    return M[:].unsqueeze(1).to_broadcast([128, 4, 128])
                STl = ST[:, l, :, :]
                S.op("act", lambda e: e.copy(out=STb, in_=STl), reads=["ST", "STb"], writes=["STb"])
                for c in range(NCH):
                    cs = slice(c * 64, (c + 1) * 64)
                    for hh in range(2):
                        rws = slice(64 * hh, 64 * hh + 64)
                        cls = slice(64 * hh, 64 * hh + 64)
                        S.op("dve", lambda e, rws=rws, cls=cls, cs=cs: e.tensor_tensor(out=BDR[rws, :, cls], in0=ZR[rws, :, cs], in1=PIN[rws, :, cs], op=ALU.mult),
                             reads=allk("ZR") + allk("LC"), writes=["BDR"])
                        S.op("pool", lambda e, rws=rws, cls=cls, cs=cs: e.tensor_tensor(out=BDK[rws, :, cls], in0=KH[rws, :, cs], in1=PINV[rws, :, cs], op=ALU.mult),
                             reads=allk("ZK") + allk("PINV"), writes=["BDK"])
                        S.op("dve", lambda e, rws=rws, cls=cls, cs=cs: e.tensor_tensor(out=BDB[rws, :, cls], in0=BBt[rws, :, cs], in1=PINV[rws, :, cs], op=ALU.mult),
                             reads=allk("AH") + allk("PINV"), writes=["BDB"])
                        S.op("pool", lambda e, rws=rws, cls=cls, cs=cs: e.tensor_tensor(out=BDA[rws, :, cls], in0=KKN[rws, :, cs], in1=PEX[rws, :, cs], op=ALU.mult),
                             reads=allk("KKN") + allk("SW"), writes=["BDA"])
                        S.op("pool", lambda e, rws=rws, cls=cls, cs=cs: e.tensor_copy(out=BDV[rws, :, cls], in_=ZV[rws, :, cs]),
                             reads=allk("ZV"), writes=["BDV"])
                        if not same_nt:
                            S.op("dve", lambda e, rws=rws, cls=cls, cs=cs: e.tensor_tensor(out=BDB2[rws, :, cls], in0=BBt[rws, :, cs], in1=PINV[rws, :, cs], op=ALU.mult),
                                 reads=allk("AH") + allk("PINV"), writes=["BDB2"])
                            S.op("pool", lambda e, rws=rws, cls=cls, cs=cs: e.tensor_tensor(out=BDA2[rws, :, cls], in0=KKN[rws, :, cs], in1=PEX[rws, :, cs], op=ALU.mult),
                                 reads=allk("KKN") + allk("SW"), writes=["BDA2"])
                    for (lT, lk, rT, rk_, dstM, dk, msk, mk_) in (
                            (BDBn, kBDBn, BDAn, kBDAn, MN[0], "MN0", MSU, "MSU"),
                            (BDAn, kBDAn, BDBn, kBDBn, MNT[0], "MNT0", MSL, "MSL"),
                            (BDK, "BDK", BDA, "BDA", MKA, "MKA", MSU, "MSU"),
                            (BDB, "BDB", BDR, "BDR", MBR, "MBR", MIU, "MIU"),
                            (BDK, "BDK", BDR, "BDR", MKR, "MKR", MIU, "MIU")):
                        bi, pb, pk = bank()
                        S.op("pe", mm4(pb, lT, rT), reads=[lk, rk_], writes=[pk])
                        S.op("dve", lambda e, pb=pb, dstM=dstM, msk=msk: e.tensor_tensor(out=dstM, in0=v4(pb), in1=bc4(msk), op=ALU.mult),
                             reads=[pk, mk_], writes=[dk])
                    S.op("pool", lambda e: e.tensor_tensor(out=GM, in0=MN[0], in1=bc4(IDENT), op=ALU.add), reads=["MN0", "IDENT"], writes=["GM"])
                    cur = 0
                    for jl in range(1, 6):
                        nxt = 1 - cur
                        if jl < 5:
                            bi, pb, pk = bank()
                            S.op("pe", mm4(pb, MNT[cur], MN[cur]), reads=[f"MN{cur}", f"MNT{cur}"], writes=[pk])
                            S.op("act", lambda e, pb=pb, nxt=nxt: e.copy(out=MN[nxt], in_=v4(pb)), reads=[pk], writes=[f"MN{nxt}"])
                        bi, pb, pk = bank()
                        S.op("pe", mm4(pb, MN[cur], MNT[cur]), reads=[f"MN{cur}", f"MNT{cur}"], writes=[pk])
                        S.op("dve", lambda e, pb=pb, nxt=nxt: e.tensor_copy(out=MNT[nxt], in_=v4(pb)), reads=[pk], writes=[f"MNT{nxt}"])
                        bi, pb, pk = bank()
                        S.op("pe", mm4(pb, MNT[nxt], GM), reads=[f"MNT{nxt}", "GM"], writes=[pk])
                        S.op("dve", lambda e, pb=pb: e.tensor_tensor(out=GM, in0=v4(pb), in1=GM, op=ALU.add), reads=[pk, "GM"], writes=["GM"])
                        cur = nxt
                    if not same_nt:
                        S.op("act", lambda e: e.copy(out=GMr, in_=GM), reads=["GM"], writes=["GMr"])
                    for (srcT, sk_, dstT_, dk) in ((BDB, "BDB", BDBT, "BDBT"), (BDK, "BDK", BDKT, "BDKT")):
                        bi, pb, pk = bank()
                        S.op("pe", mm4c(pb, srcT, IDR[:], 128), reads=[sk_, "IDR"], writes=[pk])
                        S.op("act", lambda e, pb=pb, dstT_=dstT_: e.copy(out=dstT_, in_=v4(pb)), reads=[pk], writes=[dk])
                    bi, pb, pk = bank()
                    S.op("pe", mm4c(pb, BDV, I2R[:], 64), reads=["BDV", "I2R"], writes=[pk])
                    S.op("act", lambda e, pb=pb: e.copy(out=VT, in_=v4(pb, 64)), reads=[pk], writes=["VT"])
                    bi, pb, pk = bank()

                    def f1(e, pb=pb):
                        for p in range(4):
                            e.matmul(pb[:, p * 64:(p + 1) * 64], lhsT=BDA[:, p, :], rhs=STb[:, p, :], start=True, stop=False)
                            ins = e.matmul(pb[:, p * 64:(p + 1) * 64], lhsT=MKA[:, p, :], rhs=VT[:, p, :], start=False, stop=True)
                        return ins
                    S.op("pe", f1, reads=["BDA", "STb", "MKA", "VT"], writes=[pk])
                    S.op("act", lambda e, pb=pb: e.copy(out=XTt, in_=v4(pb, 64)), reads=[pk], writes=["XT"])
                    bi, pb, pk = bank()
                    S.op("pe", mm4(pb, GMx, XTt, 64), reads=[kGMx, "XT"], writes=[pk])
                    S.op("dve", lambda e, pb=pb: e.tensor_copy(out=UT, in_=v4(pb, 64)), reads=[pk], writes=["UT"])
                    bi, pby, pky = bank()

                    def f3(e, pby=pby):
                        for p in range(4):
                            o = pby[:, p * 64:(p + 1) * 64]
                            e.matmul(o, lhsT=BDR[:, p, :], rhs=STb[:, p, :], start=True, stop=False)
                            e.matmul(o, lhsT=MBR[:, p, :], rhs=UT[:, p, :], start=False, stop=False)
                            ins = e.matmul(o, lhsT=MKR[:, p, :], rhs=VT[:, p, :], start=False, stop=True)
                        return ins
                    S.op("pe", f3, reads=["BDR", "STb", "MBR", "UT", "MKR", "VT"], writes=[pky])
                    bi, pbs, pks = bank()

                    def f4(e, pbs=pbs):
                        for p in range(4):
                            o = pbs[:, p * 64:(p + 1) * 64]
                            e.matmul(o, lhsT=BDBT[:, p, :], rhs=UT[:, p, :], start=True, stop=False)
                            ins = e.matmul(o, lhsT=BDKT[:, p, :], rhs=VT[:, p, :], start=False, stop=True)
                        return ins
                    S.op("pe", f4, reads=["BDBT", "UT", "BDKT", "VT"], writes=[pks])
                    S.op("dve", lambda e, pbs=pbs: e.tensor_tensor(out=TMPS, in0=v4(pbs, 64), in1=STl, op=ALU.add), reads=[pks, "ST"], writes=["TMPS"])
                    S.op("dve", lambda e, c=c: e.tensor_tensor(out=STl, in0=TMPS, in1=PC[:, :, c:c + 1].to_broadcast([128, 4, 64]), op=ALU.mult),
                         reads=["TMPS", "PC"], writes=["ST"])
                    S.op("act", lambda e: e.copy(out=STb, in_=STl), reads=["ST"], writes=["STb"])
                    S.op("act", lambda e, pby=pby: e.copy(out=YTBD[0:64, :, 0:64], in_=v4(pby, 64)[0:64]), reads=[pky, "YTBD"], writes=["YTBD"])
                    S.op("pool" if False else "dve", lambda e, pby=pby: e.tensor_copy(out=YTBD[64:128, :, 64:128], in_=v4(pby, 64)[64:128]), reads=[pky, "YTBD"], writes=["YTBD"])
                    bi, pb, pk = bank()
                    S.op("pe", mm4c(pb, YTBD, I2R[:], 64), reads=["YTBD", "I2R"], writes=[pk])
                    S.op("act", lambda e, pb=pb, cs=cs: e.copy(out=YF[:, :, cs], in_=v4(pb, 64)), reads=[pk], writes=allk("YF"))
                dump(f"yrec{l}", YF, allk("YF"), ti)
                for p in range(4):
                    S.op("dve", lambda e, p=p: e.scalar_tensor_tensor(out=TC, in0=ZR[:, p, :], scalar=vc(f"rk{l}", p), in1=KH[:, p, :],
                                                                      op0=ALU.mult, op1=ALU.mult), reads=[f"ZR{p}", f"ZK{p}", "VC"], writes=["TC"])
                    bi, pbb, pkb = bank()
                    S.op("pe", lambda e, pbb=pbb: e.matmul(pbb[:, 0:T], lhsT=OBD[:], rhs=TC, start=True, stop=True), reads=["TC", "OBD"], writes=[pkb])
                    S.op("dve", lambda e, p=p, pbb=pbb: e.tensor_tensor(out=KKN[:, p, :], in0=pbb[:, 0:T], in1=ZV[:, p, :], op=ALU.mult),
                         reads=[pkb, f"ZV{p}"], writes=[f"KKN{p}"])
                    mean, rstd = ln_stats([YF[:, p, :]], [f"YF{p}"], None, OBD64, 0)
                    S.op("dve", lambda e, p=p: e.tensor_tensor(out=YF[:, p, :], in0=YF[:, p, :], in1=mean, op=ALU.subtract),
                         reads=[f"YF{p}", LT["kTA"]], writes=[f"YF{p}"])
                    S.op("pool", lambda e, p=p: e.tensor_tensor(out=YF[:, p, :], in0=YF[:, p, :], in1=rstd, op=ALU.mult),
                         reads=[f"YF{p}", LT["kTB"]], writes=[f"YF{p}"])
                    S.op("dve", lambda e, p=p: e.tensor_scalar(out=YF[:, p, :], in0=YF[:, p, :], scalar1=vc(f"gng{l}", p), scalar2=vc(f"gnb{l}", p),
                                                             op0=ALU.mult, op1=ALU.add), reads=[f"YF{p}", "VC"], writes=[f"YF{p}"])
                    S.op("pool", lambda e, p=p: e.tensor_tensor(out=YF[:, p, :], in0=YF[:, p, :], in1=KKN[:, p, :], op=ALU.add),
                         reads=[f"YF{p}", f"KKN{p}"], writes=[f"YF{p}"])
                    bi, pbg, pkg = bank()
                    S.op("pe", lambda e, p=p, pbg=pbg: e.matmul(pbg[:, 0:T], lhsT=G2[:, l, p * 128:(p + 1) * 128], rhs=LG, start=True, stop=True),
                         reads=["LG", "G2"], writes=[pkg])
                    S.op("dve", lambda e, p=p, pbg=pbg: e.tensor_tensor(out=YBR[:, 0, p, :], in0=pbg[:, 0:T], in1=YF[:, p, :], op=ALU.mult),
                         reads=[pkg, f"YF{p}"], writes=[f"YBR0_{p}"])
                S = S_main
                cur_pool[0] = "ALL"; LT.update(LT1)
                replay_merged(S, [recA, recB])
                S.fence()
                for q in range(2):
                    for b in range(3):
                        sg = wload(l, 7 + b * 2 + q)
                        so = wload(l, 13 + b * 2 + q)
                        wg = wview(sg, 8, 512)
                        wo = wview(so, 4, 512)
                        for mi in range(4):
                            bi, pbg, pkg = bank()

                            def fg(e, pbg=pbg, wg=wg, mi=mi):
                                for kc in range(8):
                                    ins = e.matmul(pbg[:, 0:T], lhsT=wg[:, kc, mi * 128:(mi + 1) * 128], rhs=H[:, kc, :], start=(kc == 0), stop=(kc == 7))
                                return ins
                            S.op("pe", fg, reads=[f"WS{sg}"] + [f"H{m}" for m in range(8)], writes=[pkg])
                            bi, pbo, pko = bank()

                            def fo(e, pbo=pbo, wo=wo, mi=mi, b=b):
                                for kc in range(4):
                                    ins = e.matmul(pbo[:, 0:T], lhsT=wo[:, kc, mi * 128:(mi + 1) * 128], rhs=YBR[:, b, kc, :], start=(kc == 0), stop=(kc == 3))
                                return ins
                            S.op("pe", fo, reads=[f"WS{so}"] + [f"YBR{b}_{k_}" for k_ in range(4)], writes=[pko])
                            S.op("act", lambda e, pbg=pbg: e.activation(out=SIG, in_=pbg[:, 0:T], func=AF.Sigmoid), reads=[pkg], writes=["SIG"])
                            if b == 0:
                                S.op("dve", lambda e, pbo=pbo, mi=mi: e.tensor_tensor(out=MERG[:, mi, :], in0=pbo[:, 0:T], in1=SIG, op=ALU.mult),
                                     reads=[pko, "SIG"], writes=[f"MERG{mi}"])
                            else:
                                S.op("dve", lambda e, pbo=pbo: e.tensor_tensor(out=TMG, in0=pbo[:, 0:T], in1=SIG, op=ALU.mult),
                                     reads=[pko, "SIG"], writes=["TMG"])
                                if b == 1:
                                    S.op("pool", lambda e, mi=mi: e.tensor_tensor(out=MERG[:, mi, :], in0=MERG[:, mi, :], in1=TMG, op=ALU.add),
                                         reads=[f"MERG{mi}", "TMG"], writes=[f"MERG{mi}"])
                                else:
                                    S.op("pool", lambda e, mi=mi, q=q: e.tensor_tensor(out=MERGED[:, q * 4 + mi, :], in0=MERG[:, mi, :], in1=TMG, op=ALU.add),
                                         reads=[f"MERG{mi}", "TMG"], writes=[f"MERGED{q * 4 + mi}"])
                S.fence()
                sA = wload(l, 19)
                sB = wload(l, 20)
                psl = {}
                for m in range(8):
                    s = sA if m < 4 else sB
                    wv = wview(s, 8, 512)
                    bi, pb, pk = bank()

                    def fw(e, pb=pb, wv=wv, m=m):
                        for kc in range(8):
                            ins = e.matmul(pb[:, 0:T], lhsT=wv[:, kc, (m % 4) * 128:(m % 4 + 1) * 128], rhs=MERGED[:, kc, :], start=(kc == 0), stop=(kc == 7))
                        return ins
                    S.op("pe", fw, reads=[f"WS{s}"] + [f"MERGED{k_}" for k_ in range(8)], writes=[pk])
                    S.op("dve", lambda e, m=m, pb=pb: e.scalar_tensor_tensor(out=U[:, m, :], in0=pb[:, 0:T], scalar=DC[:, l, 16 + m:17 + m],
                                                                         in1=X[:, m, :], op0=ALU.mult, op1=ALU.add),
                         reads=[pk, "DC", f"X{m}"], writes=[f"U{m}"])

                def ln_apply(gname, bname):
                    mean, rstd = ln_stats([U[:, m, :] for m in range(8)], [f"U{m}" for m in range(8)], None, O1024, 1)
                    for m in range(8):
                        S.op("dve", lambda e, m=m: e.tensor_tensor(out=U[:, m, :], in0=U[:, m, :], in1=mean, op=ALU.subtract),
                             reads=[f"U{m}", "TA"], writes=[f"U{m}"])
                        S.op("pool", lambda e, m=m: e.tensor_tensor(out=U[:, m, :], in0=U[:, m, :], in1=rstd, op=ALU.mult),
                             reads=[f"U{m}", "TB"], writes=[f"U{m}"])
                        S.op("dve", lambda e, m=m: e.tensor_scalar(out=X[:, m, :], in0=U[:, m, :], scalar1=vc(gname, m), scalar2=vc(bname, m),
                                                                 op0=ALU.mult, op1=ALU.add), reads=[f"U{m}", "VC"], writes=[f"X{m}"])
                ln_apply(f"lnmg{l}", f"lnmb{l}")
                dump(f"xm{l}", X[:], [f"X{m}" for m in range(8)], ti)
                S.fence()
                modulate(l, "f")
                for jg in range(8):
                    s = wload(l, 21 + jg)
                    wv = wview(s, 8, 512)
                    for ji in range(4):
                        j = jg * 4 + ji
                        bi, pb, pk = bank()

                        def f1m(e, pb=pb, wv=wv, ji=ji):
                            for kc in range(8):
                                ins = e.matmul(pb[:, 0:T], lhsT=wv[:, kc, ji * 128:(ji + 1) * 128], rhs=H[:, kc, :], start=(kc == 0), stop=(kc == 7))
                            return ins
                        S.op("pe", f1m, reads=[f"WS{s}"] + [f"H{m}" for m in range(8)], writes=[pk])
                        rt, rk_ = (RTMP, "RTMP") if j % 2 == 0 else (RTMP2, "RTMP2")
                        S.op("act", lambda e, pb=pb, rt=rt: e.activation(out=rt, in_=pb[:, 0:T], func=AF.Relu), reads=[pk], writes=[rk_])
                        S.op("dve" if j % 2 == 0 else "pool", lambda e, j=j, rt=rt: e.tensor_tensor(out=H1[:, j, :], in0=rt, in1=rt, op=ALU.mult),
                             reads=[rk_], writes=[f"H1_{j}"])
                for m in range(8):
                    s = wload(l, 29 + m)
                    wv = wview(s, 32, 128)
                    bi, pb, pk = bank()

                    def f2m(e, pb=pb, wv=wv):
                        for kc in range(32):
                            ins = e.matmul(pb[:, 0:T], lhsT=wv[:, kc, :], rhs=H1[:, kc, :], start=(kc == 0), stop=(kc == 31))
                        return ins
                    S.op("pe", f2m, reads=[f"WS{s}"] + [f"H1_{k_}" for k_ in range(32)], writes=[pk])
                    S.op("dve", lambda e, m=m, pb=pb: e.scalar_tensor_tensor(out=U[:, m, :], in0=pb[:, 0:T], scalar=DC[:, l, 24 + m:25 + m],
                                                                         in1=X[:, m, :], op0=ALU.mult, op1=ALU.add),
                         reads=[pk, "DC", f"X{m}"], writes=[f"U{m}"])
                ln_apply(f"lnfg{l}", f"lnfb{l}")
                dump(f"xf{l}", X[:], [f"X{m}" for m in range(8)], ti)
                S.fence()
            for tb in range(TB):
                for half in range(2):
                    bi, pb, pk = bank()

                    def fo_(e, pb=pb, tb=tb, half=half):
                        for f4_ in range(4):
                            fc = half * 4 + f4_
                            ins = e.transpose(pb[:, f4_ * 128:(f4_ + 1) * 128], X[:, fc, tb * 128:(tb + 1) * 128], IDENT[:])
                        return ins
                    S.op("pe", fo_, reads=[f"X{m}" for m in range(8)] + ["IDENT"], writes=[pk])
                    S.op("act" if half else "dve",
                         (lambda e, pb=pb, tb=tb, half=half: e.copy(out=XIN[:, tb, half * 512:(half + 1) * 512], in_=pb[:, :])) if half else
                         (lambda e, pb=pb, tb=tb, half=half: e.tensor_copy(out=XIN[:, tb, half * 512:(half + 1) * 512], in_=pb[:, :])),
                         reads=[pk], writes=["XIN"])
            S.dma("sp", out[t0:t0 + T, :].rearrange("(b p) d -> p b d", p=128), XIN[:], reads=["XIN"])
        S.wait_all("sp")
        S.emit(block)
        build_nc.last_stats = {"nops": S.nops, "cnt": dict(S.cnt)}
    return nc


def kernel(**inputs):
    B, S_TOK, _ = inputs["x"].shape
    nc = build_nc(S_TOK, T=256, NL=2)
    in_maps = []
    for b in range(B):
        m = {"x": np.ascontiguousarray(inputs["x"][b], dtype=np.float32), "vecs": pack_vecs(inputs, b)}
        for nm in WEIGHT_NAMES:
            m[nm] = np.ascontiguousarray(inputs[nm], dtype=np.float32)
        in_maps.append(m)
    res = run_bass_kernel_spmd(nc, in_maps, core_ids=list(range(B)))
    return np.stack([np.asarray(r["out"]).reshape(S_TOK, D) for r in res.results], axis=0).astype(np.float32)
```

```python
import types
import numpy as np
from contextlib import ExitStack
import concourse.bass as bass
import concourse.mybir as mybir
from concourse.bass_utils import run_bass_kernel_spmd

F32 = mybir.dt.float32
BF16 = mybir.dt.bfloat16
I32 = mybir.dt.int32
ALU = mybir.AluOpType
AF = mybir.ActivationFunctionType

D = 1024
RW = 512
C_MAIN = 6400
NG = 37
GW = 4096
ALPHA = 4.0 ** 0.25
CDEC = float(np.exp(-0.5))
LN_EPS = 1e-5
GN_EPS = 64e-5
CONVW = 31
ENG_NAMES = ("pe", "act", "dve", "pool", "sp")


class Sched:
    def __init__(self, nc, sems, dma_sems):
        self.nc = nc
        self.sem = dict(zip(ENG_NAMES, sems))
        self.cnt = {e: 0 for e in ENG_NAMES}
        self.dma_pool = {q: list(v) for q, v in dma_sems.items()}
        self.dma_sems = [s for q in self.dma_pool for s in self.dma_pool[q]]
        self.dma_idx = {}
        i = 0
        for q in self.dma_pool:
            self.dma_idx[q] = list(range(i, i + len(self.dma_pool[q])))
            i += len(self.dma_pool[q])
        self.dma_cnt = [0] * len(self.dma_sems)
        self.dma_rr = {q: 0 for q in self.dma_pool}
        self.streams = {e: [] for e in ENG_NAMES}
        self.seen = {e: {} for e in ENG_NAMES}
        self.last_w = {}
        self.readers = {}
        self.nops = 0
        self.fence_dma = False

    def _deps(self, reads, writes):
        toks = []
        for k in reads:
            t = self.last_w.get(k)
            if t is not None:
                toks.append(t)
        for k in writes:
            t = self.last_w.get(k)
            if t is not None:
                toks.append(t)
            toks.extend(self.readers.get(k, ()))
        return toks

    def _waits_for(self, e, toks, skip_same_pe=True):
        need = {}
        for (sk, v) in toks:
            if sk == e and e == "pe" and skip_same_pe:
                continue
            if self.seen[e].get(sk, 0) >= v:
                continue
            if need.get(sk, 0) < v:
                need[sk] = v
        for sk, v in need.items():
            self.seen[e][sk] = v
        return list(need.items())

    def _commit(self, tok, reads, writes):
        for k in reads:
            self.readers.setdefault(k, []).append(tok)
        for k in writes:
            self.last_w[k] = tok
            self.readers[k] = []

    @staticmethod
    def _freeze(fn):
        if getattr(fn, "__closure__", None) is None:
            return fn
        cells = []
        for c in fn.__closure__:
            try:
                v = c.cell_contents
                if isinstance(v, types.FunctionType):
                    v = Sched._freeze(v)
                cells.append(types.CellType(v))
            except ValueError:
                cells.append(c)
        g = types.FunctionType(fn.__code__, fn.__globals__, fn.__name__, fn.__defaults__, tuple(cells))
        g.__kwdefaults__ = fn.__kwdefaults__
        return g

    def op(self, e, fn, reads=(), writes=()):
        fn = self._freeze(fn)
        reads = list(reads); writes = list(writes)
        toks = self._deps(reads, writes)
        waits = self._waits_for(e, toks)
        self.cnt[e] += 1
        tok = (e, self.cnt[e])
        self.streams[e].append((waits, fn, ("eng", e)))
        self._commit(tok, reads, writes)
        self.nops += 1
        return tok

    def dma(self, q, out, in_, reads=(), writes=(), **kw):
        toks = self._deps(reads, writes)
        i = self.dma_idx[q][self.dma_rr[q]]
        self.dma_rr[q] = (self.dma_rr[q] + 1) % len(self.dma_idx[q])
        sk = ("d", i)
        if self.dma_cnt[i] > 0:
            toks.append((sk, self.dma_cnt[i]))
        waits = self._waits_for(q, toks)
        self.dma_cnt[i] += 16
        tok = (sk, self.dma_cnt[i])

        def fn(eng, out=out, in_=in_, kw=kw):
            return eng.dma_start(out=out, in_=in_, **kw)
        self.streams[q].append((waits, fn, ("dma", i)))
        self._commit(tok, reads, writes)
        return tok

    def fence(self, engines=("pe", "act", "dve", "pool")):
        for e in engines:
            toks = [(en, self.cnt[en]) for en in engines if self.cnt[en] > 0]
            if self.fence_dma:
                toks += [(("d", i), v) for i, v in enumerate(self.dma_cnt) if v > 0]
            waits = self._waits_for(e, toks, skip_same_pe=False)
            if waits:
                self.streams[e].append((waits, None, None))

    def wait_all(self, e):
        toks = [(en, self.cnt[en]) for en in ENG_NAMES if self.cnt[en] > 0 and en != e]
        toks += [(("d", i), v) for i, v in enumerate(self.dma_cnt) if v > 0]
        waits = self._waits_for(e, toks)
        self.streams[e].append((waits, None, None))

    def _semh(self, sk):
        if isinstance(sk, tuple):
            return self.dma_sems[sk[1]]
        return self.sem[sk]

    def emit(self, block):
        eng_of = {"pe": self.nc.tensor, "act": self.nc.scalar, "dve": self.nc.vector,
                  "pool": self.nc.gpsimd, "sp": self.nc.sync}

        def mk(e):
            def body(eng):
                for waits, fn, sig in self.streams[e]:
                    for sk, v in waits:
                        eng.wait_ge(self._semh(sk), v)
                    if fn is None:
                        continue
                    ins = fn(eng)
                    if sig[0] == "eng":
                        ins.then_inc(self.sem[e], 1)
                    else:
                        ins.then_inc(self.dma_sems[sig[1]], 16)
            return body
        block.tensor(mk("pe"))
        block.scalar(mk("act"))
        block.vector(mk("dve"))
        block.gpsimd(mk("pool"))
        block.sync(mk("sp"))


class Rec:
    def __init__(self):
        self.items = []

    def op(self, e, fn, reads=(), writes=()):
        self.items.append(("op", e, Sched._freeze(fn), list(reads), list(writes)))

    def dma(self, q, out, in_, reads=(), writes=(), **kw):
        self.items.append(("dma", q, out, in_, list(reads), list(writes), kw))


def replay_merged(S, recs):
    pos = [0] * len(recs)
    tot = [max(1, len(r.items)) for r in recs]
    while True:
        best, bf = None, None
        for i, r in enumerate(recs):
            if pos[i] < len(r.items):
                f = pos[i] / tot[i]
                if bf is None or f < bf:
                    best, bf = i, f
        if best is None:
            break
        it = recs[best].items[pos[best]]
        pos[best] += 1
        if it[0] == "op":
            S.op(it[1], it[2], reads=it[3], writes=it[4])
        else:
            S.dma(it[1], it[2], it[3], reads=it[4], writes=it[5], **it[6])


def vec_layout():
    off = {}
    r = 0

    def add(name, n):
        nonlocal r
        off[name] = r
        r += n
    add("c", 8)
    for l in range(2):
        for nm, n in (("ada_b", 48), ("mu", 14), ("w0", 4), ("a0", 4), ("kk", 4), ("ka", 4), ("rk", 4),
                      ("gng", 4), ("gnb", 4), ("cvb", 4), ("cvg", 4), ("cvbb", 4), ("plsc", 4),
                      ("lnmg", 8), ("lnmb", 8), ("lnfg", 8), ("lnfb", 8), ("cvw", 124)):
            add(f"{nm}{l}", n)
    add("v0", 4)
    add("muv", 1)
    return off, r


VOFF, NVROWS = vec_layout()
NVB = (NVROWS + 127) // 128


def pack_vecs(inp, b):
    P = np.zeros((NVB * 128, 128), np.float32)

    def put(name, arr):
        a = np.asarray(arr, np.float32).reshape(-1)
        n = (a.size + 127) // 128
        buf = np.zeros(n * 128, np.float32)
        buf[:a.size] = a
        P[VOFF[name]:VOFF[name] + n] = buf.reshape(n, 128)
    put("c", inp["c"][b])
    for l in range(2):
        put(f"ada_b{l}", inp["ada_b"][l]); put(f"mu{l}", inp["shift_mu"][l])
        put(f"w0{l}", inp["rw_w0"][l]); put(f"a0{l}", inp["rw_a0"][l])
        put(f"kk{l}", inp["rw_kk"][l]); put(f"ka{l}", inp["rw_ka"][l]); put(f"rk{l}", inp["rw_rk"][l])
        put(f"gng{l}", inp["rw_gn_g"][l]); put(f"gnb{l}", inp["rw_gn_b"][l])
        put(f"cvb{l}", inp["cv_b"][l]); put(f"cvg{l}", inp["cv_ln_g"][l]); put(f"cvbb{l}", inp["cv_ln_b"][l])
        put(f"plsc{l}", inp["pl_scale"][l])
        put(f"lnmg{l}", inp["ln_m_g"][l]); put(f"lnmb{l}", inp["ln_m_b"][l])
        put(f"lnfg{l}", inp["ln_f_g"][l]); put(f"lnfb{l}", inp["ln_f_b"][l])
        put(f"cvw{l}", inp["cv_w"][l])
    put("v0", inp["rw_v0"][0])
    put("muv", inp["shift_mu_vres"][0])
    return P


WEIGHT_NAMES = ("ada_w", "w_in", "w_in_vres", "rw_w2", "rw_a2", "rw_g2", "rw_v2", "rw_wo", "cv_wo",
                "pl_wo", "pl_w", "w_out", "mlp_w1", "mlp_w2")
WEIGHT_SHAPES = {"ada_w": [2, 1024, 6144], "w_in": [2, 1024, 6400], "w_in_vres": [1, 1024, 32],
                 "rw_w2": [2, 64, 512], "rw_a2": [2, 64, 512], "rw_g2": [2, 128, 512], "rw_v2": [1, 32, 512],
                 "rw_wo": [2, 512, 1024], "cv_wo": [2, 512, 1024], "pl_wo": [2, 512, 1024],
                 "pl_w": [2, 4, 128, 128], "w_out": [2, 1024, 1024], "mlp_w1": [2, 1024, 4096],
                 "mlp_w2": [2, 4096, 1024]}


def build_nc(S_TOK, T=256, NL=2, debug=None, REC_BF16=True, NEU_BF16=True, MM2_F32=False, ST_F32=False):
    assert S_TOK % T == 0 and T % 128 == 0 and T <= 512
    NT = S_TOK // T
    NCH = T // 64
    TB = T // 128
    nc = bass.Bass("TRN2", target_bir_lowering=False)
    dr = {}
    dr["x"] = nc.dram_tensor("x", [S_TOK, D], F32, kind="ExternalInput").ap()
    dr["vecs"] = nc.dram_tensor("vecs", [NVB * 128, 128], F32, kind="ExternalInput").ap()
    for nm in WEIGHT_NAMES:
        dr[nm] = nc.dram_tensor(nm, WEIGHT_SHAPES[nm], F32, kind="ExternalInput").ap()
    out = nc.dram_tensor("out", [S_TOK, D], F32, kind="ExternalOutput").ap()
    wsc = nc.dram_tensor("wsc", [2, NG, 128, GW], BF16, kind="Internal").ap()
    dbg = {}
    if debug:
        for nm, shp in debug.items():
            dbg[nm] = nc.dram_tensor(nm, list(shp), F32, kind="ExternalOutput").ap()

    with ExitStack() as es:
        def sb(name, shape, dt=F32):
            return es.enter_context(nc.sbuf_tensor(name, list(shape), dt))

        def pst(name, shape, dt=F32):
            return es.enter_context(nc.psum_tensor(name, list(shape), dt))

        X = sb("X", [128, 8, T])
        H = sb("H", [128, 8, T], BF16)
        WS = sb("WS", [128, 4, GW], BF16)
        VF = sb("VF", [128, 4, T])
        VC = sb("VC", [128, NVB * 128])
        PK = sb("PK", [128, NVB, 128])
        ADA = sb("ADA", [128, 2, 48])
        DC = sb("DC", [128, 2, 128])
        IDENT = sb("IDENT", [128, 128])
        I2 = sb("I2", [128, 64])
        MSU = sb("MSU", [128, 128]); MIU = sb("MIU", [128, 128]); MSL = sb("MSL", [128, 128])
        OBD = sb("OBD", [128, 128]); OBD64 = sb("OBD64", [128, 128])
        O512 = sb("O512", [128, 128]); O1024 = sb("O1024", [128, 128])
        RESETM = sb("RESETM", [128, T])
        EPS = sb("EPS", [128, 4])
        RC16 = sb("RC16", [128, 4, 16])
        IOTI = sb("IOTI", [128, 16], I32)
        W2P = sb("W2P", [128, 2, 512], BF16); A2P = sb("A2P", [128, 2, 512], BF16)
        G2 = sb("G2", [128, 2, 512], BF16); V2 = sb("V2", [32, 512], BF16)
        PLW = sb("PLW", [128, 2, 4, 128], BF16)
        CARRY = sb("CARRY", [128, 2, 16])
        HALO = sb("HALO", [128, 2, 4, 30])
        PHALO = sb("PHALO", [128, 2, 4, 16])
        ST = sb("ST", [128, 2, 4, 64])
        CONDT = sb("CONDT", [128, 8])
        XIN = sb("XIN", [128, TB, D])
        YBR = sb("YBR", [128, 3, 4, T], BF16)
        AW = 78 * T + 9600
        AR = sb("AR", [128, AW])

        class Alloc:
            def __init__(self, base=0):
                self.o = base

            def f(self, n, shape=None):
                ap = AR[:, self.o:self.o + n]
                self.o += n
                assert self.o <= AW, (self.o, AW)
                return ap

            def t3(self, a, b):
                return self.f(a * b).rearrange("p (a b) -> p a b", a=a)

            def b3(self, a, b):
                n = (a * b + 1) // 2
                return self.f(n).bitcast(BF16).rearrange("p (a b) -> p a b", a=a)

        A = Alloc()
        ZR = A.t3(4, T); ZK = A.t3(4, T); ZV = A.t3(4, T); YF = A.t3(4, T)
        TA = A.f(T); TBm = A.f(T); TC = A.f(T); TD = A.f(T)
        mark_dead1 = A.o
        SW = A.t3(4, T); LC = A.t3(4, T); PINV = A.t3(4, T); AH = A.t3(4, T); KKN = A.t3(4, T)
        mark_dead1_end = A.o
        RAWS = [A.f(T), A.f(T), A.f(T)]; ZTMP = A.f(T)
        raw_rr = [0]
        T12 = A.b3(1, T)[:, 0, :]; LG = A.b3(1, T)[:, 0, :]; LV = A.b3(1, T)[:, 0, :]
        rt3 = A.b3 if REC_BF16 else A.t3
        nt3 = A.b3 if (REC_BF16 and NEU_BF16) else A.t3
        BDA = rt3(4, 128); BDB = rt3(4, 128); BDK = rt3(4, 128); BDR = rt3(4, 128); BDV = rt3(4, 128)
        BDA2 = nt3(4, 128); BDB2 = nt3(4, 128)
        MN = [nt3(4, 128), nt3(4, 128)]; MNT = [nt3(4, 128), nt3(4, 128)]
        MKA = rt3(4, 128); MBR = rt3(4, 128); MKR = rt3(4, 128); GM = nt3(4, 128); GMr = rt3(4, 128)
        BDBT = rt3(4, 128); BDKT = rt3(4, 128)
        VT = rt3(4, 64); XTt = (A.t3 if MM2_F32 else rt3)(4, 64); UT = rt3(4, 64)
        YTBD = rt3(4, 128)
        STb = rt3(4, 64); TMPS = A.t3(4, 64)
        PC = A.t3(4, 8)
        arena_mixer_end = A.o
        B2 = Alloc(arena_mixer_end)
        HGLU = B2.t3(4, T + 30); ACC = B2.t3(4, T); CVV = B2.t3(4, T)
        PP = B2.t3(4, T + 16); SA = B2.f(T + 16); SBm = B2.f(T + 16)
        PLB = B2.b3(1, T)[:, 0, :]
        TA2 = B2.f(T); TB2 = B2.f(T); TC2 = B2.f(T); TD2 = B2.f(T)
        usq_off = B2.o
        TCq = B2.t3(4, T); TDq = B2.t3(4, T)
        USQ = AR[:, usq_off:usq_off + 8 * T].rearrange("p (a b) -> p a b", a=8)
        RESET4 = B2.f(4 * T)
        assert B2.o <= AW, (B2.o, AW)
        LT = {"TA": TA, "TB": TBm, "TC": TC, "TD": TD, "kTA": "TA", "kTB": "TB", "kTC": "TC", "kTD": "TD"}
        LT1 = dict(LT)
        LT2 = {"TA": TA2, "TB": TB2, "TC": TC2, "TD": TD2, "kTA": "TA2", "kTB": "TB2", "kTC": "TC2", "kTD": "TD2"}
        B3 = Alloc(0)
        MERG = B3.t3(4, T); MERGED = B3.b3(8, T); SIG = B3.f(T); TMG = B3.f(T)
        assert B3.o <= 12 * T
        B4 = Alloc(mark_dead1)
        U = B4.t3(8, T); RTMP = B4.f(T); RTMP2 = B4.f(T)
        assert B4.o <= mark_dead1_end, (B4.o, AW)
        H1 = Alloc(0).b3(32, T)
        B5 = Alloc(0)
        AWS = [B5.t3(8, 512), B5.t3(8, 512)]
        assert B5.o <= AW

        PSB = [pst(f"PS{i}", [128, 512]) for i in range(8)]
        ps_rr = [0]

        bank_pools = {"ALL": list(range(8)), "A": [0, 1, 2, 3, 4, 5], "B": [6, 7]}
        bank_rr = {"ALL": 0, "A": 0, "B": 0}
        cur_pool = ["ALL"]

        def bank():
            pl = bank_pools[cur_pool[0]]
            i = pl[bank_rr[cur_pool[0]] % len(pl)]
            bank_rr[cur_pool[0]] += 1
            return i, PSB[i], f"PS{i}"

        sems = [es.enter_context(nc.semaphore(f"s_{e}")) for e in ENG_NAMES]
        dsems = {"sp": [es.enter_context(nc.semaphore(f"dsp{i}")) for i in range(8)],
                 "pool": [es.enter_context(nc.semaphore(f"dpl{i}")) for i in range(4)]}
        block = es.enter_context(nc.Block())
        S = Sched(nc, sems, dsems)
        S.fence_dma = bool(debug)

        def vc(name, col=0):
            c0 = VOFF[name] + col
            return VC[:, c0:c0 + 1]

        def conv_dma(l, g, src_ap, kc, n, col0=0, ncols=None):
            ncols = n if ncols is None else ncols
            dst = wsc[l, g][:, 0:kc * n].rearrange("p (kc n) -> p kc n", kc=kc)[:, :, col0:col0 + ncols]
            S.dma("pool", dst, src_ap, writes=[f"wsc{l}_{g}"])

        def group_src(l, g):
            w_in = dr["w_in"][l].rearrange("(kc p) n -> p kc n", p=128)
            if g < 6:
                return [(w_in[:, :, g * 512:(g + 1) * 512], 8, 512, 0, 512)]
            if g == 6:
                r = [(w_in[:, :, 3072:3328], 8, 288, 0, 256)]
                if l >= 1:
                    r.append((dr["w_in_vres"][l - 1].rearrange("(kc p) n -> p kc n", p=128), 8, 288, 256, 32))
                return r
            if g < 13:
                i = g - 7
                b, q = i // 2, i % 2
                c0 = 3328 + (8 * b + 4 * q) * 128
                return [(w_in[:, :, c0:c0 + 512], 8, 512, 0, 512)]
            if g < 19:
                i = g - 13
                b, q = i // 2, i % 2
                w = dr[("rw_wo", "cv_wo", "pl_wo")[b]][l].rearrange("(kc p) n -> p kc n", p=128)
                return [(w[:, :, q * 512:(q + 1) * 512], 4, 512, 0, 512)]
            if g < 21:
                q = g - 19
                w = dr["w_out"][l].rearrange("(kc p) n -> p kc n", p=128)
                return [(w[:, :, q * 512:(q + 1) * 512], 8, 512, 0, 512)]
            if g < 29:
                j = g - 21
                w = dr["mlp_w1"][l].rearrange("(kc p) n -> p kc n", p=128)
                return [(w[:, :, j * 512:(j + 1) * 512], 8, 512, 0, 512)]
            m = g - 29
            w = dr["mlp_w2"][l].rearrange("(kc p) n -> p kc n", p=128)
            return [(w[:, :, m * 128:(m + 1) * 128], 32, 128, 0, 128)]

        for l in range(NL):
            for g in range(NG):
                for (src, kc, n, c0, ncol) in group_src(l, g):
                    conv_dma(l, g, src, kc, n, c0, ncol)

        S.op("pool", lambda e: e.memset(IDENT[:], 0.0), writes=["IDENT"])
        S.op("pool", lambda e: e.affine_select(out=IDENT[:], in_=IDENT[:], pattern=[[-1, 128]], compare_op=ALU.not_equal,
                                               fill=1.0, base=0, channel_multiplier=1), reads=["IDENT"], writes=["IDENT"])
        S.op("pool", lambda e: e.tensor_tensor(out=I2[:], in0=IDENT[:, 0:64], in1=IDENT[:, 64:128], op=ALU.add),
             reads=["IDENT"], writes=["I2"])
        IDR = sb("IDR", [128, 128], BF16 if REC_BF16 else F32)
        I2R = sb("I2R", [128, 64], BF16 if REC_BF16 else F32)
        S.op("pool", lambda e: e.tensor_copy(out=IDR[:], in_=IDENT[:]), reads=["IDENT"], writes=["IDR"])
        S.op("pool", lambda e: e.tensor_copy(out=I2R[:], in_=I2[:]), reads=["I2"], writes=["I2R"])

        def tri(M, key, cmp_op, sgn=1):
            S.op("pool", lambda e: e.memset(M[:], 1.0), writes=[key])
            S.op("pool", lambda e: e.affine_select(out=M[:], in_=M[:], pattern=[[sgn, 128]], compare_op=cmp_op,
                                                   fill=0.0, base=0, channel_multiplier=-sgn), reads=[key], writes=[key])
            S.op("pool", lambda e: e.memset(M[0:64, 64:128], 0.0), reads=[key], writes=[key])
            S.op("pool", lambda e: e.memset(M[64:128, 0:64], 0.0), reads=[key], writes=[key])
        tri(MSU, "MSU", ALU.is_gt)
        tri(MIU, "MIU", ALU.is_ge)
        tri(MSL, "MSL", ALU.is_gt, -1)
        for M, key, val in ((OBD, "OBD", 1.0), (OBD64, "OBD64", 1.0 / 64)):
            S.op("pool", lambda e, M=M, val=val: e.memset(M[:], val), writes=[key])
            S.op("pool", lambda e, M=M: e.memset(M[0:64, 64:128], 0.0), reads=[key], writes=[key])
            S.op("pool", lambda e, M=M: e.memset(M[64:128, 0:64], 0.0), reads=[key], writes=[key])
        S.op("pool", lambda e: e.memset(O512[:], 1.0 / 512), writes=["O512"])
        S.op("pool", lambda e: e.memset(O1024[:], 1.0 / 1024), writes=["O1024"])
        S.op("pool", lambda e: e.memset(RESETM[:], 1.0), writes=["RESETM"])
        S.op("pool", lambda e: e.memset(RESETM[:].rearrange("p (c j) -> p c j", j=64)[:, :, 0:1], 0.0),
             reads=["RESETM"], writes=["RESETM"])
        S.op("pool", lambda e: e.memset(RESET4, 1.0), writes=["RESET4"])
        S.op("pool", lambda e: e.memset(RESET4.rearrange("p (c j) -> p c j", j=64)[:, :, 0:1], 0.0), reads=["RESET4"], writes=["RESET4"])
        for i, v in enumerate((GN_EPS, LN_EPS / (ALPHA * ALPHA), LN_EPS, 0.0)):
            S.op("pool", lambda e, i=i, v=v: e.memset(EPS[:, i:i + 1], float(v)), reads=["EPS"], writes=["EPS"])
        S.op("pool", lambda e: e.iota(IOTI[:], pattern=[[1, 16]], base=1, channel_multiplier=0), writes=["IOTI"])
        for g in range(4):
            S.op("pool", lambda e, g=g: e.tensor_copy(out=RC16[:, g, :], in_=IOTI[:]), reads=["IOTI", "RC16"], writes=["RC16"])
            S.op("pool", lambda e, g=g: e.tensor_scalar(out=RC16[:, g, :], in0=RC16[:, g, :], scalar1=float(2 ** (g + 1)), scalar2=None,
                                                      op0=ALU.min), reads=["RC16"], writes=["RC16"])
        S.op("dve", lambda e: e.reciprocal(out=RC16[:], in_=RC16[:]), reads=["RC16"], writes=["RC16"])
        for nm, Tn in (("CARRY", CARRY), ("HALO", HALO), ("PHALO", PHALO), ("ST", ST)):
            S.op("pool", lambda e, Tn=Tn: e.memset(Tn[:], 0.0), writes=[nm])
        for nm, Tn in (("BDA", BDA), ("BDB", BDB), ("BDK", BDK), ("BDR", BDR), ("BDV", BDV), ("YTBD", YTBD), ("BDA2", BDA2), ("BDB2", BDB2), ("STb", STb)):
            S.op("pool", lambda e, Tn=Tn: e.memset(Tn, 0.0), writes=[nm])
        S.op("pool", lambda e: e.memset(W2P[:], 0.0), writes=["W2P"])
        S.op("pool", lambda e: e.memset(A2P[:], 0.0), writes=["A2P"])
        for l in range(NL):
            S.dma("pool", W2P[0:64, l, :], dr["rw_w2"][l], reads=["W2P"], writes=["W2P"])
            S.dma("pool", A2P[64:128, l, :], dr["rw_a2"][l], reads=["A2P"], writes=["A2P"])
            S.dma("pool", G2[:, l, :], dr["rw_g2"][l], writes=["G2"])
            S.dma("pool", PLW[:, l, :, :], dr["pl_w"][l].rearrange("g c d -> c g d"), writes=["PLW"])
        if NL > 1:
            S.dma("pool", V2[:], dr["rw_v2"][0], writes=["V2"])
        S.dma("sp", PK[:], dr["vecs"].rearrange("(b p) n -> p b n", p=128), writes=["PK"])
        for b in range(NVB):
            bi, pb, pk = bank()
            S.op("pe", lambda e, b=b, pb=pb: e.transpose(pb[:, 0:128], PK[:, b, :], IDENT[:]),
                 reads=["PK", "IDENT"], writes=[pk])
            S.op("act", lambda e, b=b, pb=pb: e.copy(out=VC[:, b * 128:(b + 1) * 128], in_=pb[:, 0:128]),
                 reads=[pk], writes=["VC"])
        S.op("act", lambda e: e.activation(out=CONDT[:], in_=VC[:, VOFF["c"]:VOFF["c"] + 8], func=AF.Silu),
             reads=["VC"], writes=["CONDT"])
        for l in range(NL):
            bi, pb, pk = bank()
            for gq in range(12):
                slot = gq % 2
                S.dma("sp", AWS[slot], dr["ada_w"][l].rearrange("(kc p) n -> p kc n", p=128)[:, :, gq * 512:(gq + 1) * 512],
                      writes=[f"AWS{slot}"])
                for mi in range(4):
                    j = gq * 4 + mi

                    def fn(e, slot=slot, mi=mi, j=j, pb=pb):
                        for kc in range(8):
                            ins = e.matmul(pb[:, j:j + 1], lhsT=AWS[slot][:, kc, mi * 128:(mi + 1) * 128],
                                           rhs=CONDT[:, kc:kc + 1], start=(kc == 0), stop=(kc == 7))
                        return ins
                    S.op("pe", fn, reads=[f"AWS{slot}", "CONDT"], writes=[pk])
            S.op("dve", lambda e, l=l, pb=pb: e.tensor_tensor(out=ADA[:, l, :], in0=pb[:, 0:48],
                                                             in1=VC[:, VOFF[f"ada_b{l}"]:VOFF[f"ada_b{l}"] + 48], op=ALU.add),
                 reads=[pk, "VC"], writes=["ADA"])
        for l in range(NL):
            S.op("dve", lambda e, l=l: e.tensor_scalar(out=DC[:, l, 0:8], in0=ADA[:, l, 8:16], scalar1=1.0, scalar2=None, op0=ALU.add),
                 reads=["ADA"], writes=["DC"])
            S.op("dve", lambda e, l=l: e.tensor_scalar(out=DC[:, l, 8:16], in0=ADA[:, l, 32:40], scalar1=1.0, scalar2=None, op0=ALU.add),
                 reads=["ADA", "DC"], writes=["DC"])
            S.op("dve", lambda e, l=l: e.tensor_scalar(out=DC[:, l, 16:24], in0=ADA[:, l, 16:24], scalar1=1.0 / ALPHA, scalar2=None, op0=ALU.mult),
                 reads=["ADA", "DC"], writes=["DC"])
            S.op("dve", lambda e, l=l: e.tensor_scalar(out=DC[:, l, 24:32], in0=ADA[:, l, 40:48], scalar1=1.0 / ALPHA, scalar2=None, op0=ALU.mult),
                 reads=["ADA", "DC"], writes=["DC"])
            m0 = VOFF[f"mu{l}"]
            S.op("dve", lambda e, l=l, m0=m0: e.tensor_scalar(out=DC[:, l, 32:46], in0=VC[:, m0:m0 + 14], scalar1=-1.0, scalar2=1.0,
                                                           op0=ALU.mult, op1=ALU.add), reads=["VC", "DC"], writes=["DC"])
            mv0 = VOFF["muv"]
            S.op("dve", lambda e, l=l, mv0=mv0: e.tensor_scalar(out=DC[:, l, 46:47], in0=VC[:, mv0:mv0 + 1], scalar1=-1.0, scalar2=1.0,
                                                             op0=ALU.mult, op1=ALU.add), reads=["VC", "DC"], writes=["DC"])
            gm0, bm0 = VOFF[f"lnmg{l}"], VOFF[f"lnmb{l}"]
            S.op("dve", lambda e, l=l, gm0=gm0: e.tensor_tensor(out=DC[:, l, 64:72], in0=VC[:, gm0:gm0 + 8], in1=DC[:, l, 8:16], op=ALU.mult),
                 reads=["VC", "DC"], writes=["DC"])
            S.op("dve", lambda e, l=l, bm0=bm0: e.tensor_tensor(out=DC[:, l, 72:80], in0=VC[:, bm0:bm0 + 8], in1=DC[:, l, 8:16], op=ALU.mult),
                 reads=["VC", "DC"], writes=["DC"])
            S.op("dve", lambda e, l=l: e.tensor_tensor(out=DC[:, l, 72:80], in0=DC[:, l, 72:80], in1=ADA[:, l, 24:32], op=ALU.add),
                 reads=["ADA", "DC"], writes=["DC"])
            if l >= 1:
                gf0, bf0 = VOFF[f"lnfg{l - 1}"], VOFF[f"lnfb{l - 1}"]
                S.op("dve", lambda e, l=l, gf0=gf0: e.tensor_tensor(out=DC[:, l, 80:88], in0=VC[:, gf0:gf0 + 8], in1=DC[:, l, 0:8], op=ALU.mult),
                     reads=["VC", "DC"], writes=["DC"])
                S.op("dve", lambda e, l=l, bf0=bf0: e.tensor_tensor(out=DC[:, l, 88:96], in0=VC[:, bf0:bf0 + 8], in1=DC[:, l, 0:8], op=ALU.mult),
                     reads=["VC", "DC"], writes=["DC"])
                S.op("dve", lambda e, l=l: e.tensor_tensor(out=DC[:, l, 88:96], in0=DC[:, l, 88:96], in1=ADA[:, l, 0:8], op=ALU.add),
                     reads=["ADA", "DC"], writes=["DC"])
            k0 = VOFF[f"ka{l}"]
            S.op("dve", lambda e, l=l, k0=k0: e.tensor_scalar(out=DC[:, l, 48:52], in0=VC[:, k0:k0 + 4], scalar1=-1.0, scalar2=1.0,
                                                           op0=ALU.mult, op1=ALU.add), reads=["VC", "DC"], writes=["DC"])
        S.fence()

        ws_rr = [0]

        def wload(l, g):
            s = ws_rr[0]
            ws_rr[0] = (s + 1) % 4
            if g == 6:
                kc, n, nv = 8, 288, (288 if l >= 1 else 256)
            elif 13 <= g < 19:
                kc, n, nv = 4, 512, 512
            elif g >= 29:
                kc, n, nv = 32, 128, 128
            else:
                kc, n, nv = 8, 512, 512
            dst = WS[:, s, 0:kc * n].rearrange("p (kc n) -> p kc n", kc=kc)[:, :, 0:nv]
            src = wsc[l, g][:, 0:kc * n].rearrange("p (kc n) -> p kc n", kc=kc)[:, :, 0:nv]
            S.dma("sp", dst, src, reads=[f"wsc{l}_{g}"], writes=[f"WS{s}"])
            return s

        def wview(s, kc, n):
            return WS[:, s, 0:kc * n].rearrange("p (kc n) -> p kc n", kc=kc)

        def dump(name, ap, keys, idx=None):
            if name in dbg:
                dst = dbg[name] if idx is None else dbg[name][idx]
                S.dma("sp", dst, ap, reads=keys)

        def ln_stats(srcs, src_keys, sq_eng_out, ONESM, eps_col):
            n = len(srcs)
            tA, tB, tC, tD = LT["TA"], LT["TB"], LT["TC"], LT["TD"]
            kA, kB, kC, kD = LT["kTA"], LT["kTB"], LT["kTC"], LT["kTD"]
            bi1, pb1, pk1 = bank()
            bi2, pb2, pk2 = bank()

            def fm(e):
                for i, s_ in enumerate(srcs):
                    ins = e.matmul(pb1[:, 0:T], lhsT=ONESM[:], rhs=s_, start=(i == 0), stop=(i == n - 1))
                return ins
            S.op("pe", fm, reads=src_keys, writes=[pk1])
            for i, s_ in enumerate(srcs):
                S.op("act", lambda e, s_=s_: e.activation(out=tC, in_=s_, func=AF.Square), reads=[src_keys[i]], writes=[kC])
                S.op("pe", lambda e, i=i: e.matmul(pb2[:, 0:T], lhsT=ONESM[:], rhs=tC, start=(i == 0), stop=(i == n - 1)),
                     reads=[kC], writes=[pk2])
            S.op("act", lambda e: e.copy(out=tA, in_=pb1[:, 0:T]), reads=[pk1], writes=[kA])
            S.op("act", lambda e: e.activation(out=tD, in_=pb1[:, 0:T], func=AF.Square), reads=[pk1], writes=[kD])
            S.op("dve", lambda e: e.tensor_tensor(out=tB, in0=pb2[:, 0:T], in1=tD, op=ALU.subtract), reads=[pk2, kD], writes=[kB])
            S.op("dve", lambda e: e.tensor_scalar(out=tB, in0=tB, scalar1=0.0, scalar2=None, op0=ALU.max), reads=[kB], writes=[kB])
            S.op("act", lambda e: e.activation(out=tB, in_=tB, func=AF.Sqrt, bias=EPS[:, eps_col:eps_col + 1], scale=1.0),
                 reads=[kB, "EPS"], writes=[kB])
            S.op("dve", lambda e: e.reciprocal(out=tB, in_=tB), reads=[kB], writes=[kB])
            return tA, tB

        def modulate(l, which):
            o_sc = 0 if which == "m" else 8
            o_sh = 0 if which == "m" else 24
            for m in range(8):
                S.op("act", lambda e, m=m: e.activation(out=H[:, m, :], in_=X[:, m, :], func=AF.Identity,
                                                        bias=ADA[:, l, o_sh + m:o_sh + m + 1], scale=DC[:, l, o_sc + m:o_sc + m + 1]),
                     reads=[f"X{m}", "ADA", "DC"], writes=[f"H{m}"])

        def residual_ln(l, which, get_ps):
            o_gt = 16 if which == "m" else 24
            gname, bname = (f"lnmg{l}", f"lnmb{l}") if which == "m" else (f"lnfg{l}", f"lnfb{l}")
            for m in range(8):
                pb, pk = get_ps(m)
                S.op("dve", lambda e, m=m, pb=pb: e.scalar_tensor_tensor(out=U[:, m, :], in0=pb[:, 0:T], scalar=DC[:, l, o_gt + m:o_gt + m + 1],
                                                                     in1=X[:, m, :], op0=ALU.mult, op1=ALU.add),
                     reads=[pk, "DC", f"X{m}"], writes=[f"U{m}"])
            mean, rstd = ln_stats([U[:, m, :] for m in range(8)], [f"U{m}" for m in range(8)], None, O1024, 1)
            for m in range(8):
                S.op("dve", lambda e, m=m: e.tensor_tensor(out=U[:, m, :], in0=U[:, m, :], in1=mean, op=ALU.subtract),
                     reads=[f"U{m}", "TA"], writes=[f"U{m}"])
                S.op("pool", lambda e, m=m: e.tensor_tensor(out=U[:, m, :], in0=U[:, m, :], in1=rstd, op=ALU.mult),
                     reads=[f"U{m}", "TB"], writes=[f"U{m}"])
                S.op("dve", lambda e, m=m: e.tensor_scalar(out=X[:, m, :], in0=U[:, m, :], scalar1=vc(gname, m), scalar2=vc(bname, m),
                                                         op0=ALU.mult, op1=ALU.add), reads=[f"U{m}", "VC"], writes=[f"X{m}"])

        for ti in range(NT):
            t0 = ti * T
            S.dma("sp", XIN[:], dr["x"][t0:t0 + T, :].rearrange("(b p) d -> p b d", p=128), writes=["XIN"])
            for fc in range(8):
                bi, pb, pk = bank()

                def ft(e, fc=fc, pb=pb):
                    for tb in range(TB):
                        ins = e.transpose(pb[:, tb * 128:(tb + 1) * 128], XIN[:, tb, fc * 128:(fc + 1) * 128], IDENT[:])
                    return ins
                S.op("pe", ft, reads=["XIN", "IDENT"], writes=[pk])
                S.op("act" if fc % 2 else "dve",
                     (lambda e, fc=fc, pb=pb: e.copy(out=X[:, fc, :], in_=pb[:, 0:T])) if fc % 2 else
                     (lambda e, fc=fc, pb=pb: e.tensor_copy(out=X[:, fc, :], in_=pb[:, 0:T])),
                     reads=[pk], writes=[f"X{fc}"])
            for l in range(NL):
                if l == 0:
                    modulate(l, "m")
                slot_of = {}

                def proj_chunk(j, ncols=128, col_in_group=None):
                    g = j // 4 if j < 24 else 6
                    if g not in slot_of:
                        slot_of[g] = wload(l, g)
                    s = slot_of[g]
                    n = 512 if g < 6 else 288
                    wv = wview(s, 8, n)
                    c0 = (j % 4) * 128 if j < 24 else (j - 24) * 128
                    bi, pb, pk = bank()

                    def fn(e, pb=pb, wv=wv, c0=c0, ncols=ncols):
                        for kc in range(8):
                            ins = e.matmul(pb[0:ncols, 0:T], lhsT=wv[:, kc, c0:c0 + ncols], rhs=H[:, kc, :],
                                           start=(kc == 0), stop=(kc == 7))
                        return ins
                    S.op("pe", fn, reads=[f"WS{s}"] + [f"H{m}" for m in range(8)], writes=[pk])
                    return pb, pk

                def token_shift(pb, pk, cidx, mu_ap, omu_ap, dst, dst_keys, np_=128):
                    k = raw_rr[0]
                    raw_rr[0] = (k + 1) % 3
                    Rk, rk_key = RAWS[k], f"RAW{k}"
                    ck = f"CARRY{l}_{cidx}"
                    S.op("act", lambda e: e.activation(out=Rk[0:np_, 0:T], in_=pb[0:np_, 0:T], func=AF.Identity, scale=omu_ap),
                         reads=[pk, "DC"], writes=[rk_key])
                    S.op("dve", lambda e: e.scalar_tensor_tensor(out=dst[:, 1:T], in0=pb[0:np_, 0:T - 1], scalar=mu_ap, in1=Rk[0:np_, 1:T],
                                                                 op0=ALU.mult, op1=ALU.add), reads=[pk, rk_key, "VC"], writes=dst_keys)
                    S.op("dve", lambda e: e.scalar_tensor_tensor(out=dst[:, 0:1], in0=CARRY[0:np_, l, cidx:cidx + 1], scalar=mu_ap, in1=Rk[0:np_, 0:1],
                                                                 op0=ALU.mult, op1=ALU.add), reads=[ck, rk_key, "VC"] + list(dst_keys), writes=dst_keys)
                    S.op("act", lambda e: e.copy(out=CARRY[0:np_, l, cidx:cidx + 1], in_=pb[0:np_, T - 1:T]), reads=[pk, ck], writes=[ck])

                for j in range(26):
                    pb, pk = proj_chunk(j)
                    if j < 12:
                        dstT = (ZR, ZK, ZV)[j // 4]
                        nm = ("ZR", "ZK", "ZV")[j // 4]
                        token_shift(pb, pk, j, vc(f"mu{l}", j), DC[:, l, 32 + j:33 + j], dstT[:, j % 4, :], [f"{nm}{j % 4}"])
                    elif j == 12:
                        token_shift(pb, pk, j, vc(f"mu{l}", j), DC[:, l, 32 + j:33 + j], ZTMP, ["ZTMP"])
                        S.op("act", lambda e: e.activation(out=T12[0:64, :], in_=ZTMP[0:64, :], func=AF.Tanh), reads=["ZTMP"], writes=["T12"])
                        S.op("dve", lambda e: e.tensor_copy(out=T12[64:128, :], in_=ZTMP[64:128, :]), reads=["ZTMP", "T12"], writes=["T12"])
                    elif j == 13:
                        token_shift(pb, pk, j, vc(f"mu{l}", j), DC[:, l, 32 + j:33 + j], ZTMP, ["ZTMP"])
                        S.op("act", lambda e: e.activation(out=LG, in_=ZTMP, func=AF.Sigmoid), reads=["ZTMP"], writes=["LG"])
                    elif j < 18:
                        cc = j - 14
                        S.op("act", lambda e, cc=cc, pb=pb: e.copy(out=CVV[:, cc, :], in_=pb[:, 0:T]), reads=[pk], writes=[f"CVV{cc}"])
                    elif j < 22:
                        cc = j - 18
                        S.op("act", lambda e, pb=pb: e.activation(out=TA, in_=pb[:, 0:T], func=AF.Sigmoid), reads=[pk], writes=["TA"])
                        if cc == 0:
                            S.op("pool", lambda e: e.tensor_copy(out=HGLU[:, :, 0:30], in_=HALO[:, l, :, :]), reads=["HALO"],
                                 writes=[f"HGLU{c_}" for c_ in range(4)])
                        S.op("dve", lambda e, cc=cc: e.tensor_tensor(out=HGLU[:, cc, 30:30 + T], in0=CVV[:, cc, :], in1=TA, op=ALU.mult),
                             reads=[f"CVV{cc}", "TA"], writes=[f"HGLU{cc}"])
                    else:
                        gg = j - 22
                        if gg == 0:
                            S.op("pool", lambda e: e.tensor_copy(out=PP[:, :, 0:16], in_=PHALO[:, l, :, :]), reads=["PHALO"],
                                 writes=[f"PP{c_}" for c_ in range(4)])
                        S.op("act", lambda e, gg=gg, pb=pb: e.copy(out=PP[:, gg, 16:16 + T], in_=pb[:, 0:T]), reads=[pk], writes=[f"PP{gg}"])
                if l >= 1:
                    s = slot_of[6]
                    wv = wview(s, 8, 288)
                    bi, pb, pk = bank()

                    def fnv(e, pb=pb, wv=wv):
                        for kc in range(8):
                            ins = e.matmul(pb[0:32, 0:T], lhsT=wv[:, kc, 256:288], rhs=H[:, kc, :], start=(kc == 0), stop=(kc == 7))
                        return ins
                    S.op("pe", fnv, reads=[f"WS{s}"] + [f"H{m}" for m in range(8)], writes=[pk])
                    token_shift(pb, pk, 14, VC[0:32, VOFF["muv"]:VOFF["muv"] + 1], DC[0:32, l, 46:47], ZTMP[0:32, :], ["ZTMP"], np_=32)
                    S.op("act", lambda e: e.copy(out=LV[0:32, :], in_=ZTMP[0:32, :]), reads=["ZTMP"], writes=["LV"])
                dump(f"z_r{l}", ZR, ["ZR0", "ZR1", "ZR2", "ZR3"], ti)
                dump(f"z_k{l}", ZK, ["ZK0", "ZK1", "ZK2", "ZK3"], ti)
                dump(f"z_v{l}", ZV, ["ZV0", "ZV1", "ZV2", "ZV3"], ti)

                S_main = S
                S = Rec(); recB = S
                cur_pool[0] = "B"; LT.update(LT2)
                for cc in range(4):
                    eng = "dve" if cc % 2 == 0 else "pool"
                    wbase = VOFF[f"cvw{l}"]
                    S.op(eng, lambda e, cc=cc: e.tensor_scalar(out=ACC[:, cc, :], in0=HGLU[:, cc, 0:T], scalar1=VC[:, wbase + cc:wbase + cc + 1],
                                                              scalar2=vc(f"cvb{l}", cc), op0=ALU.mult, op1=ALU.add),
                         reads=[f"HGLU{cc}", "VC"], writes=[f"ACC{cc}"])
                    for jt in range(1, CONVW):
                        if eng == "dve":
                            S.op(eng, lambda e, cc=cc, jt=jt: e.scalar_tensor_tensor(
                                out=ACC[:, cc, :], in0=HGLU[:, cc, jt:jt + T], scalar=VC[:, wbase + jt * 4 + cc:wbase + jt * 4 + cc + 1],
                                in1=ACC[:, cc, :], op0=ALU.mult, op1=ALU.add), reads=[f"HGLU{cc}", "VC", f"ACC{cc}"], writes=[f"ACC{cc}"])
                        else:
                            ct, ck = ((SA, "SA"), (SBm, "SB"))[jt % 2]
                            S.op("act", lambda e, cc=cc, jt=jt, ct=ct: e.activation(
                                out=ct[:, 0:T], in_=HGLU[:, cc, jt:jt + T], func=AF.Identity,
                                scale=VC[:, wbase + jt * 4 + cc:wbase + jt * 4 + cc + 1]), reads=[f"HGLU{cc}", "VC"], writes=[ck])
                            S.op("pool", lambda e, cc=cc, ct=ct: e.tensor_tensor(out=ACC[:, cc, :], in0=ACC[:, cc, :], in1=ct[:, 0:T], op=ALU.add),
                                 reads=[ck, f"ACC{cc}"], writes=[f"ACC{cc}"])
                S.op("pool", lambda e: e.tensor_copy(out=HALO[:, l, :, :], in_=HGLU[:, :, T:T + 30]),
                     reads=[f"HGLU{c_}" for c_ in range(4)], writes=["HALO"])
                dump(f"conv{l}", ACC, [f"ACC{c_}" for c_ in range(4)], ti)
                mean, rstd = ln_stats([ACC[:, cc, :] for cc in range(4)], [f"ACC{cc}" for cc in range(4)], None, O512, 2)
                for cc in range(4):
                    S.op("dve", lambda e, cc=cc: e.tensor_tensor(out=ACC[:, cc, :], in0=ACC[:, cc, :], in1=mean, op=ALU.subtract),
                         reads=[f"ACC{cc}", LT["kTA"]], writes=[f"ACC{cc}"])
                    S.op("pool", lambda e, cc=cc: e.tensor_tensor(out=ACC[:, cc, :], in0=ACC[:, cc, :], in1=rstd, op=ALU.mult),
                         reads=[f"ACC{cc}", LT["kTB"]], writes=[f"ACC{cc}"])
                    S.op("dve", lambda e, cc=cc: e.tensor_scalar(out=ACC[:, cc, :], in0=ACC[:, cc, :], scalar1=vc(f"cvg{l}", cc),
                                                                scalar2=vc(f"cvbb{l}", cc), op0=ALU.mult, op1=ALU.add),
                         reads=[f"ACC{cc}", "VC"], writes=[f"ACC{cc}"])
                    S.op("act", lambda e, cc=cc: e.activation(out=YBR[:, 1, cc, :], in_=ACC[:, cc, :], func=AF.Silu),
                         reads=[f"ACC{cc}"], writes=[f"YBR1_{cc}"])
                for gg in range(4):
                    win = 2 ** (gg + 1)
                    eng = "pool" if gg % 2 == 0 else "dve"
                    src = PP[:, gg, :]
                    cur_key = f"PP{gg}"
                    bufs = [(SA, "SA"), (SBm, "SB")]
                    sh = 1
                    k_ = 0
                    while sh < win:
                        dstb, dkey = bufs[k_ % 2]
                        lo = 2 * sh - 1
                        S.op(eng, lambda e, src=src, dstb=dstb, sh=sh, lo=lo: e.tensor_tensor(
                            out=dstb[:, lo:T + 16], in0=src[:, lo:T + 16], in1=src[:, lo - sh:T + 16 - sh], op=ALU.add),
                            reads=[cur_key], writes=[dkey])
                        src, cur_key = dstb, dkey
                        sh *= 2
                        k_ += 1
                    dstb, dkey = bufs[k_ % 2]
                    S.op("dve", lambda e, src=src, dstb=dstb, gg=gg, win=win: e.scalar_tensor_tensor(
                        out=dstb[:, 16:16 + T], in0=src[:, 16:16 + T], scalar=1.0 / win, in1=PP[:, gg, 16:16 + T],
                        op0=ALU.mult, op1=ALU.subtract), reads=[cur_key, f"PP{gg}"], writes=[dkey])
                    if ti == 0:
                        S.op(eng, lambda e, src=src, gg=gg: e.tensor_tensor(out=src[:, 16:32], in0=src[:, 16:32], in1=RC16[:, gg, :], op=ALU.mult),
                             reads=[cur_key, "RC16", dkey], writes=[cur_key])
                        S.op(eng, lambda e, src=src, dstb=dstb, gg=gg: e.tensor_tensor(out=dstb[:, 16:32], in0=src[:, 16:32], in1=PP[:, gg, 16:32],
                                                                                  op=ALU.subtract), reads=[cur_key, f"PP{gg}", dkey], writes=[dkey])
                    S.op("act", lambda e, dstb=dstb: e.copy(out=PLB[:, :], in_=dstb[:, 16:16 + T]), reads=[dkey], writes=["PLB"])
                    bi, pb, pk = bank()
                    S.op("pe", lambda e, gg=gg, pb=pb: e.matmul(pb[:, 0:T], lhsT=PLW[:, l, gg, :], rhs=PLB[:, :], start=True, stop=True),
                         reads=["PLB", "PLW"], writes=[pk])
                    S.op("act", lambda e, gg=gg, pb=pb: e.activation(out=YBR[:, 2, gg, :], in_=pb[:, 0:T], func=AF.Identity,
                                                                   scale=vc(f"plsc{l}", gg)), reads=[pk, "VC"], writes=[f"YBR2_{gg}"])
                S.op("pool", lambda e: e.tensor_copy(out=PHALO[:, l, :, :], in_=PP[:, :, T:T + 16]),
                     reads=[f"PP{c_}" for c_ in range(4)], writes=["PHALO"])

                S = Rec(); recA = S
                cur_pool[0] = "A"; LT.update(LT1)
                allk = lambda nm: [f"{nm}{p}" for p in range(4)]
                f4T = lambda t3: t3.rearrange("p a b -> p (a b)")
                for p in range(4):
                    bi, pb, pk = bank()
                    S.op("pe", lambda e, p=p, pb=pb: e.matmul(pb[:, 0:T], lhsT=W2P[:, l, p * 128:(p + 1) * 128], rhs=T12, start=True, stop=True),
                         reads=["T12", "W2P"], writes=[pk])
                    S.op("act", lambda e, p=p, pb=pb: e.activation(out=SW[:, p, :], in_=pb[:, 0:T], func=AF.Sigmoid, bias=vc(f"w0{l}", p), scale=1.0),
                         reads=[pk, "VC"], writes=[f"SW{p}"])
                    bi, pb, pk = bank()
                    S.op("pe", lambda e, p=p, pb=pb: e.matmul(pb[:, 0:T], lhsT=A2P[:, l, p * 128:(p + 1) * 128], rhs=T12, start=True, stop=True),
                         reads=["T12", "A2P"], writes=[pk])
                    S.op("act", lambda e, p=p, pb=pb: e.activation(out=AH[:, p, :], in_=pb[:, 0:T], func=AF.Sigmoid, bias=vc(f"a0{l}", p), scale=1.0),
                         reads=[pk, "VC"], writes=[f"AH{p}"])
                    if l >= 1:
                        bi, pb, pk = bank()
                        S.op("pe", lambda e, p=p, pb=pb: e.matmul(pb[:, 0:T], lhsT=V2[:, p * 128:(p + 1) * 128], rhs=LV[0:32, :], start=True, stop=True),
                             reads=["LV", "V2"], writes=[pk])
                        S.op("act", lambda e, p=p, pb=pb: e.activation(out=TCq[:, p, :], in_=pb[:, 0:T], func=AF.Sigmoid, bias=vc("v0", p), scale=1.0),
                             reads=[pk, "VC"], writes=[f"TCq{p}"])
                if l == 0:
                    S.op("pool", lambda e: e.tensor_copy(out=VF[:, :, :], in_=ZV), reads=allk("ZV"), writes=allk("VF"))
                else:
                    S.op("pool", lambda e: e.tensor_tensor(out=TDq, in0=VF[:, :, :], in1=ZV, op=ALU.subtract), reads=allk("VF") + allk("ZV"), writes=allk("TDq"))
                    S.op("dve", lambda e: e.tensor_tensor(out=TDq, in0=TDq, in1=TCq, op=ALU.mult), reads=allk("TDq") + allk("TCq"), writes=allk("TDq"))
                    S.op("dve", lambda e: e.tensor_tensor(out=ZV, in0=ZV, in1=TDq, op=ALU.add), reads=allk("ZV") + allk("TDq"), writes=allk("ZV"))
                for p in range(4):
                    S.op("act", lambda e, p=p: e.activation(out=TCq[:, p, :], in_=ZK[:, p, :], func=AF.Square, scale=vc(f"kk{l}", p)),
                         reads=[f"ZK{p}", "VC"], writes=[f"TCq{p}"])
                    bi, pb, pk = bank()
                    S.op("pe", lambda e, p=p, pb=pb: e.matmul(pb[:, 0:T], lhsT=OBD[:], rhs=TCq[:, p, :], start=True, stop=True), reads=[f"TCq{p}", "OBD"], writes=[pk])
                    S.op("act", lambda e, p=p, pb=pb: e.activation(out=TDq[:, p, :], in_=pb[:, 0:T], func=AF.Sqrt), reads=[pk], writes=[f"TDq{p}"])
                S.op("dve", lambda e: e.tensor_scalar(out=TDq, in0=TDq, scalar1=1e-12, scalar2=-1.0, op0=ALU.max, op1=ALU.mult), reads=allk("TDq"), writes=allk("TDq"))
                S.op("dve", lambda e: e.reciprocal(out=TDq, in_=TDq), reads=allk("TDq"), writes=allk("TDq"))
                for p in range(4):
                    S.op("dve", lambda e, p=p: e.scalar_tensor_tensor(out=KKN[:, p, :], in0=ZK[:, p, :], scalar=vc(f"kk{l}", p), in1=TDq[:, p, :],
                                                                      op0=ALU.mult, op1=ALU.mult), reads=[f"ZK{p}", f"TDq{p}", "VC"], writes=[f"KKN{p}"])
                    S.op("pool", lambda e, p=p: e.tensor_scalar(out=TCq[:, p, :], in0=AH[:, p, :], scalar1=vc(f"ka{l}", p), scalar2=DC[:, l, 48 + p:49 + p],
                                                               op0=ALU.mult, op1=ALU.add), reads=[f"AH{p}", "VC", "DC", f"TCq{p}"], writes=[f"TCq{p}"])
                S.op("dve", lambda e: e.tensor_tensor(out=ZK, in0=ZK, in1=TCq, op=ALU.mult), reads=allk("ZK") + allk("TCq"), writes=allk("ZK"))
                S.op("dve", lambda e: e.scalar_tensor_tensor(out=AH, in0=AH, scalar=-1.0, in1=KKN, op0=ALU.mult, op1=ALU.mult),
                     reads=allk("AH") + allk("KKN"), writes=allk("AH"))
                S.op("dve", lambda e: e.tensor_tensor_scan(out=f4T(LC), data0=RESET4, data1=f4T(SW), initial=0.0, op0=ALU.mult, op1=ALU.add),
                     reads=["RESET4"] + allk("SW"), writes=allk("LC"))
                S.op("pool", lambda e: e.tensor_tensor(out=SW, in0=LC, in1=SW, op=ALU.subtract), reads=allk("LC") + allk("SW"), writes=allk("SW"))
                S.op("act", lambda e: e.activation(out=SW, in_=SW, func=AF.Exp, scale=-CDEC), reads=allk("SW"), writes=allk("SW"))
                S.op("act", lambda e: e.activation(out=PINV, in_=LC, func=AF.Exp, scale=CDEC), reads=allk("LC"), writes=allk("PINV"))
                S.op("act", lambda e: e.activation(out=LC, in_=LC, func=AF.Exp, scale=-CDEC), reads=allk("LC"), writes=allk("LC"))
                S.op("pool", lambda e: e.tensor_copy(out=PC[:, :, 0:NCH], in_=LC.rearrange("p a (c j) -> p a c j", j=64)[:, :, :, 63]),
                     reads=allk("LC"), writes=["PC"])
                PEX, PIN, BBt, KH = SW, LC, AH, ZK
                allk = lambda nm: [f"{nm}{p}" for p in range(4)]
                same_nt = (REC_BF16 == (REC_BF16 and NEU_BF16))
                BDAn, BDBn = (BDA, BDB) if same_nt else (BDA2, BDB2)
                kBDAn, kBDBn = ("BDA", "BDB") if same_nt else ("BDA2", "BDB2")
                GMx = GM if (same_nt or MM2_F32) else GMr
                kGMx = "GM" if (same_nt or MM2_F32) else "GMr"

                def mm4(outb, lT, rT, ncol=128):
                    def fn(e):
                        for p in range(4):
                            ins = e.matmul(outb[:, p * ncol:(p + 1) * ncol], lhsT=lT[:, p, :], rhs=rT[:, p, :], start=True, stop=True)
                        return ins
                    return fn

                def mm4c(outb, lT, rconst, ncol):
                    def fn(e):
                        for p in range(4):
                            ins = e.matmul(outb[:, p * ncol:(p + 1) * ncol], lhsT=lT[:, p, :], rhs=rconst, start=True, stop=True)
                        return ins
                    return fn

                def v4(pb, n=128):
                    return pb[:, 0:4 * n].rearrange("p (a b) -> p a b", a=4)

                def bc4(M):
                    return M[:].unsqueeze(1).to_broadcast([128, 4, 128])
                STl = ST[:, l, :, :]
                S.op("act", lambda e: e.copy(out=STb, in_=STl), reads=["ST", "STb"], writes=["STb"])
                for c in range(NCH):
                    cs = slice(c * 64, (c + 1) * 64)
                    for hh in range(2):
                        rws = slice(64 * hh, 64 * hh + 64)
                        cls = slice(64 * hh, 64 * hh + 64)
                        S.op("dve", lambda e, rws=rws, cls=cls, cs=cs: e.tensor_tensor(out=BDR[rws, :, cls], in0=ZR[rws, :, cs], in1=PIN[rws, :, cs], op=ALU.mult),
                             reads=allk("ZR") + allk("LC"), writes=["BDR"])
                        S.op("pool", lambda e, rws=rws, cls=cls, cs=cs: e.tensor_tensor(out=BDK[rws, :, cls], in0=KH[rws, :, cs], in1=PINV[rws, :, cs], op=ALU.mult),
                             reads=allk("ZK") + allk("PINV"), writes=["BDK"])
                        S.op("dve", lambda e, rws=rws, cls=cls, cs=cs: e.tensor_tensor(out=BDB[rws, :, cls], in0=BBt[rws, :, cs], in1=PINV[rws, :, cs], op=ALU.mult),
                             reads=allk("AH") + allk("PINV"), writes=["BDB"])
                        S.op("pool", lambda e, rws=rws, cls=cls, cs=cs: e.tensor_tensor(out=BDA[rws, :, cls], in0=KKN[rws, :, cs], in1=PEX[rws, :, cs], op=ALU.mult),
                             reads=allk("KKN") + allk("SW"), writes=["BDA"])
                        S.op("pool", lambda e, rws=rws, cls=cls, cs=cs: e.tensor_copy(out=BDV[rws, :, cls], in_=ZV[rws, :, cs]),
                             reads=allk("ZV"), writes=["BDV"])
                        if not same_nt:
                            S.op("dve", lambda e, rws=rws, cls=cls, cs=cs: e.tensor_tensor(out=BDB2[rws, :, cls], in0=BBt[rws, :, cs], in1=PINV[rws, :, cs], op=ALU.mult),
                                 reads=allk("AH") + allk("PINV"), writes=["BDB2"])
                            S.op("pool", lambda e, rws=rws, cls=cls, cs=cs: e.tensor_tensor(out=BDA2[rws, :, cls], in0=KKN[rws, :, cs], in1=PEX[rws, :, cs], op=ALU.mult),
                                 reads=allk("KKN") + allk("SW"), writes=["BDA2"])
                    for (lT, lk, rT, rk_, dstM, dk, msk, mk_) in (
                            (BDBn, kBDBn, BDAn, kBDAn, MN[0], "MN0", MSU, "MSU"),
                            (BDAn, kBDAn, BDBn, kBDBn, MNT[0], "MNT0", MSL, "MSL"),
                            (BDK, "BDK", BDA, "BDA", MKA, "MKA", MSU, "MSU"),
                            (BDB, "BDB", BDR, "BDR", MBR, "MBR", MIU, "MIU"),
                            (BDK, "BDK", BDR, "BDR", MKR, "MKR", MIU, "MIU")):
                        bi, pb, pk = bank()
                        S.op("pe", mm4(pb, lT, rT), reads=[lk, rk_], writes=[pk])
                        S.op("dve", lambda e, pb=pb, dstM=dstM, msk=msk: e.tensor_tensor(out=dstM, in0=v4(pb), in1=bc4(msk), op=ALU.mult),
                             reads=[pk, mk_], writes=[dk])
                    S.op("pool", lambda e: e.tensor_tensor(out=GM, in0=MN[0], in1=bc4(IDENT), op=ALU.add), reads=["MN0", "IDENT"], writes=["GM"])
                    cur = 0
                    for jl in range(1, 6):
                        nxt = 1 - cur
                        if jl < 5:
                            bi, pb, pk = bank()
                            S.op("pe", mm4(pb, MNT[cur], MN[cur]), reads=[f"MN{cur}", f"MNT{cur}"], writes=[pk])
                            S.op("act", lambda e, pb=pb, nxt=nxt: e.copy(out=MN[nxt], in_=v4(pb)), reads=[pk], writes=[f"MN{nxt}"])
                        bi, pb, pk = bank()
                        S.op("pe", mm4(pb, MN[cur], MNT[cur]), reads=[f"MN{cur}", f"MNT{cur}"], writes=[pk])
                        S.op("act", lambda e, pb=pb, nxt=nxt: e.copy(out=MNT[nxt], in_=v4(pb)), reads=[pk], writes=[f"MNT{nxt}"])
                        bi, pb, pk = bank()
                        S.op("pe", mm4(pb, MNT[nxt], GM), reads=[f"MNT{nxt}", "GM"], writes=[pk])
                        S.op("dve", lambda e, pb=pb: e.tensor_tensor(out=GM, in0=v4(pb), in1=GM, op=ALU.add), reads=[pk, "GM"], writes=["GM"])
                        cur = nxt
                    if not same_nt:
                        S.op("act", lambda e: e.copy(out=GMr, in_=GM), reads=["GM"], writes=["GMr"])
                    for (srcT, sk_, dstT_, dk) in ((BDB, "BDB", BDBT, "BDBT"), (BDK, "BDK", BDKT, "BDKT")):
                        bi, pb, pk = bank()
                        S.op("pe", mm4c(pb, srcT, IDR[:], 128), reads=[sk_, "IDR"], writes=[pk])
                        S.op("act", lambda e, pb=pb, dstT_=dstT_: e.copy(out=dstT_, in_=v4(pb)), reads=[pk], writes=[dk])
                    bi, pb, pk = bank()
                    S.op("pe", mm4c(pb, BDV, I2R[:], 64), reads=["BDV", "I2R"], writes=[pk])
                    S.op("act", lambda e, pb=pb: e.copy(out=VT, in_=v4(pb, 64)), reads=[pk], writes=["VT"])
                    bi, pb, pk = bank()

                    def f1(e, pb=pb):
                        for p in range(4):
                            e.matmul(pb[:, p * 64:(p + 1) * 64], lhsT=BDA[:, p, :], rhs=STb[:, p, :], start=True, stop=False)
                            ins = e.matmul(pb[:, p * 64:(p + 1) * 64], lhsT=MKA[:, p, :], rhs=VT[:, p, :], start=False, stop=True)
                        return ins
                    S.op("pe", f1, reads=["BDA", "STb", "MKA", "VT"], writes=[pk])
                    S.op("act", lambda e, pb=pb: e.copy(out=XTt, in_=v4(pb, 64)), reads=[pk], writes=["XT"])
                    bi, pb, pk = bank()
                    S.op("pe", mm4(pb, GMx, XTt, 64), reads=[kGMx, "XT"], writes=[pk])
                    S.op("dve", lambda e, pb=pb: e.tensor_copy(out=UT, in_=v4(pb, 64)), reads=[pk], writes=["UT"])
                    bi, pby, pky = bank()

                    def f3(e, pby=pby):
                        for p in range(4):
                            o = pby[:, p * 64:(p + 1) * 64]
                            e.matmul(o, lhsT=BDR[:, p, :], rhs=STb[:, p, :], start=True, stop=False)
                            e.matmul(o, lhsT=MBR[:, p, :], rhs=UT[:, p, :], start=False, stop=False)
                            ins = e.matmul(o, lhsT=MKR[:, p, :], rhs=VT[:, p, :], start=False, stop=True)
                        return ins
                    S.op("pe", f3, reads=["BDR", "STb", "MBR", "UT", "MKR", "VT"], writes=[pky])
                    bi, pbs, pks = bank()

                    def f4(e, pbs=pbs):
                        for p in range(4):
                            o = pbs[:, p * 64:(p + 1) * 64]
                            e.matmul(o, lhsT=BDBT[:, p, :], rhs=UT[:, p, :], start=True, stop=False)
                            ins = e.matmul(o, lhsT=BDKT[:, p, :], rhs=VT[:, p, :], start=False, stop=True)
                        return ins
                    S.op("pe", f4, reads=["BDBT", "UT", "BDKT", "VT"], writes=[pks])
                    S.op("dve", lambda e, pbs=pbs: e.tensor_tensor(out=TMPS, in0=v4(pbs, 64), in1=STl, op=ALU.add), reads=[pks, "ST"], writes=["TMPS"])
                    S.op("dve", lambda e, c=c: e.tensor_tensor(out=STl, in0=TMPS, in1=PC[:, :, c:c + 1].to_broadcast([128, 4, 64]), op=ALU.mult),
                         reads=["TMPS", "PC"], writes=["ST"])
                    S.op("act", lambda e: e.copy(out=STb, in_=STl), reads=["ST"], writes=["STb"])
                    S.op("act", lambda e, pby=pby: e.copy(out=YTBD[0:64, :, 0:64], in_=v4(pby, 64)[0:64]), reads=[pky, "YTBD"], writes=["YTBD"])
                    S.op("pool" if False else "dve", lambda e, pby=pby: e.tensor_copy(out=YTBD[64:128, :, 64:128], in_=v4(pby, 64)[64:128]), reads=[pky, "YTBD"], writes=["YTBD"])
                    bi, pb, pk = bank()
                    S.op("pe", mm4c(pb, YTBD, I2R[:], 64), reads=["YTBD", "I2R"], writes=[pk])
                    S.op("act", lambda e, pb=pb, cs=cs: e.copy(out=YF[:, :, cs], in_=v4(pb, 64)), reads=[pk], writes=allk("YF"))
                dump(f"yrec{l}", YF, allk("YF"), ti)
                for p in range(4):
                    S.op("dve", lambda e, p=p: e.scalar_tensor_tensor(out=TC, in0=ZR[:, p, :], scalar=vc(f"rk{l}", p), in1=KH[:, p, :],
                                                                      op0=ALU.mult, op1=ALU.mult), reads=[f"ZR{p}", f"ZK{p}", "VC"], writes=["TC"])
                    bi, pbb, pkb = bank()
                    S.op("pe", lambda e, pbb=pbb: e.matmul(pbb[:, 0:T], lhsT=OBD[:], rhs=TC, start=True, stop=True), reads=["TC", "OBD"], writes=[pkb])
                    S.op("dve", lambda e, p=p, pbb=pbb: e.tensor_tensor(out=KKN[:, p, :], in0=pbb[:, 0:T], in1=ZV[:, p, :], op=ALU.mult),
                         reads=[pkb, f"ZV{p}"], writes=[f"KKN{p}"])
                    mean, rstd = ln_stats([YF[:, p, :]], [f"YF{p}"], None, OBD64, 0)
                    S.op("dve", lambda e, p=p: e.tensor_tensor(out=YF[:, p, :], in0=YF[:, p, :], in1=mean, op=ALU.subtract),
                         reads=[f"YF{p}", LT["kTA"]], writes=[f"YF{p}"])
                    S.op("pool", lambda e, p=p: e.tensor_tensor(out=YF[:, p, :], in0=YF[:, p, :], in1=rstd, op=ALU.mult),
                         reads=[f"YF{p}", LT["kTB"]], writes=[f"YF{p}"])
                    S.op("dve", lambda e, p=p: e.tensor_scalar(out=YF[:, p, :], in0=YF[:, p, :], scalar1=vc(f"gng{l}", p), scalar2=vc(f"gnb{l}", p),
                                                             op0=ALU.mult, op1=ALU.add), reads=[f"YF{p}", "VC"], writes=[f"YF{p}"])
                    S.op("pool", lambda e, p=p: e.tensor_tensor(out=YF[:, p, :], in0=YF[:, p, :], in1=KKN[:, p, :], op=ALU.add),
                         reads=[f"YF{p}", f"KKN{p}"], writes=[f"YF{p}"])
                    bi, pbg, pkg = bank()
                    S.op("pe", lambda e, p=p, pbg=pbg: e.matmul(pbg[:, 0:T], lhsT=G2[:, l, p * 128:(p + 1) * 128], rhs=LG, start=True, stop=True),
                         reads=["LG", "G2"], writes=[pkg])
                    S.op("dve", lambda e, p=p, pbg=pbg: e.tensor_tensor(out=YBR[:, 0, p, :], in0=pbg[:, 0:T], in1=YF[:, p, :], op=ALU.mult),
                         reads=[pkg, f"YF{p}"], writes=[f"YBR0_{p}"])
                S = S_main
                cur_pool[0] = "ALL"; LT.update(LT1)
                replay_merged(S, [recA, recB])
                S.fence()
                for q in range(2):
                    for b in range(3):
                        sg = wload(l, 7 + b * 2 + q)
                        so = wload(l, 13 + b * 2 + q)
                        wg = wview(sg, 8, 512)
                        wo = wview(so, 4, 512)
                        for mi in range(4):
                            bi, pbg, pkg = bank()

                            def fg(e, pbg=pbg, wg=wg, mi=mi):
                                for kc in range(8):
                                    ins = e.matmul(pbg[:, 0:T], lhsT=wg[:, kc, mi * 128:(mi + 1) * 128], rhs=H[:, kc, :], start=(kc == 0), stop=(kc == 7))
                                return ins
                            S.op("pe", fg, reads=[f"WS{sg}"] + [f"H{m}" for m in range(8)], writes=[pkg])
                            bi, pbo, pko = bank()

                            def fo(e, pbo=pbo, wo=wo, mi=mi, b=b):
                                for kc in range(4):
                                    ins = e.matmul(pbo[:, 0:T], lhsT=wo[:, kc, mi * 128:(mi + 1) * 128], rhs=YBR[:, b, kc, :], start=(kc == 0), stop=(kc == 3))
                                return ins
                            S.op("pe", fo, reads=[f"WS{so}"] + [f"YBR{b}_{k_}" for k_ in range(4)], writes=[pko])
                            S.op("act", lambda e, pbg=pbg: e.activation(out=SIG, in_=pbg[:, 0:T], func=AF.Sigmoid), reads=[pkg], writes=["SIG"])
                            if b == 0:
                                S.op("dve", lambda e, pbo=pbo, mi=mi: e.tensor_tensor(out=MERG[:, mi, :], in0=pbo[:, 0:T], in1=SIG, op=ALU.mult),
                                     reads=[pko, "SIG"], writes=[f"MERG{mi}"])
                            else:
                                S.op("dve", lambda e, pbo=pbo: e.tensor_tensor(out=TMG, in0=pbo[:, 0:T], in1=SIG, op=ALU.mult),
                                     reads=[pko, "SIG"], writes=["TMG"])
                                if b == 1:
                                    S.op("pool", lambda e, mi=mi: e.tensor_tensor(out=MERG[:, mi, :], in0=MERG[:, mi, :], in1=TMG, op=ALU.add),
                                         reads=[f"MERG{mi}", "TMG"], writes=[f"MERG{mi}"])
                                else:
                                    S.op("pool", lambda e, mi=mi, q=q: e.tensor_tensor(out=MERGED[:, q * 4 + mi, :], in0=MERG[:, mi, :], in1=TMG, op=ALU.add),
                                         reads=[f"MERG{mi}", "TMG"], writes=[f"MERGED{q * 4 + mi}"])
                S.fence()
                sA = wload(l, 19)
                sB = wload(l, 20)
                psl = {}
                for m in range(8):
                    s = sA if m < 4 else sB
                    wv = wview(s, 8, 512)
                    bi, pb, pk = bank()

                    def fw(e, pb=pb, wv=wv, m=m):
                        for kc in range(8):
                            ins = e.matmul(pb[:, 0:T], lhsT=wv[:, kc, (m % 4) * 128:(m % 4 + 1) * 128], rhs=MERGED[:, kc, :], start=(kc == 0), stop=(kc == 7))
                        return ins
                    S.op("pe", fw, reads=[f"WS{s}"] + [f"MERGED{k_}" for k_ in range(8)], writes=[pk])
                    S.op("dve", lambda e, m=m, pb=pb: e.scalar_tensor_tensor(out=U[:, m, :], in0=pb[:, 0:T], scalar=DC[:, l, 16 + m:17 + m],
                                                                         in1=X[:, m, :], op0=ALU.mult, op1=ALU.add),
                         reads=[pk, "DC", f"X{m}"], writes=[f"U{m}"])

                def ln_apply(gname, bname, nxt):
                    uk = [f"U{m}" for m in range(8)]
                    qk = allk("TCq") + allk("TDq")
                    tA, tB, tD = LT["TA"], LT["TB"], LT["TD"]
                    S.op("act", lambda e: e.activation(out=USQ, in_=U, func=AF.Square), reads=uk, writes=qk)
                    bi1, pb1, pk1 = bank()
                    bi2, pb2, pk2 = bank()

                    def fm1(e):
                        for m in range(8):
                            ins = e.matmul(pb1[:, 0:T], lhsT=O1024[:], rhs=U[:, m, :], start=(m == 0), stop=(m == 7))
                        return ins

                    def fm2(e):
                        for m in range(8):
                            ins = e.matmul(pb2[:, 0:T], lhsT=O1024[:], rhs=USQ[:, m, :], start=(m == 0), stop=(m == 7))
                        return ins
                    S.op("pe", fm1, reads=uk + ["O1024"], writes=[pk1])
                    S.op("pe", fm2, reads=qk + ["O1024"], writes=[pk2])
                    S.op("act", lambda e: e.copy(out=tA, in_=pb1[:, 0:T]), reads=[pk1], writes=["TA"])
                    S.op("act", lambda e: e.activation(out=tD, in_=pb1[:, 0:T], func=AF.Square), reads=[pk1], writes=["TD"])
                    S.op("dve", lambda e: e.tensor_tensor(out=tB, in0=pb2[:, 0:T], in1=tD, op=ALU.subtract), reads=[pk2, "TD"], writes=["TB"])
                    S.op("dve", lambda e: e.tensor_scalar(out=tB, in0=tB, scalar1=0.0, scalar2=None, op0=ALU.max), reads=["TB"], writes=["TB"])
                    S.op("act", lambda e: e.activation(out=tB, in_=tB, func=AF.Sqrt, bias=EPS[:, 1:2], scale=1.0), reads=["TB", "EPS"], writes=["TB"])
                    S.op("dve", lambda e: e.reciprocal(out=tB, in_=tB), reads=["TB"], writes=["TB"])
                    S.op("dve", lambda e: e.tensor_tensor(out=U, in0=U, in1=tA.unsqueeze(1).to_broadcast([128, 8, T]), op=ALU.subtract),
                         reads=uk + ["TA"], writes=uk)
                    S.op("pool", lambda e: e.tensor_tensor(out=U, in0=U, in1=tB.unsqueeze(1).to_broadcast([128, 8, T]), op=ALU.mult),
                         reads=uk + ["TB"], writes=uk)
                    for m in range(8):
                        S.op("dve", lambda e, m=m: e.tensor_scalar(out=X[:, m, :], in0=U[:, m, :], scalar1=vc(gname, m), scalar2=vc(bname, m),
                                                                 op0=ALU.mult, op1=ALU.add), reads=[f"U{m}", "VC"], writes=[f"X{m}"])
                        if nxt is not None:
                            ll, g0, b0 = nxt
                            S.op("act", lambda e, m=m, ll=ll, g0=g0, b0=b0: e.activation(out=H[:, m, :], in_=U[:, m, :], func=AF.Identity,
                                                                                       bias=DC[:, ll, b0 + m:b0 + m + 1], scale=DC[:, ll, g0 + m:g0 + m + 1]),
                                 reads=[f"U{m}", "DC"], writes=[f"H{m}"])
                ln_apply(f"lnmg{l}", f"lnmb{l}", (l, 64, 72))
                dump(f"xm{l}", X[:], [f"X{m}" for m in range(8)], ti)
                S.fence()
                for jg in range(8):
                    s = wload(l, 21 + jg)
                    wv = wview(s, 8, 512)
                    for ji in range(4):
                        j = jg * 4 + ji
                        bi, pb, pk = bank()

                        def f1m(e, pb=pb, wv=wv, ji=ji):
                            for kc in range(8):
                                ins = e.matmul(pb[:, 0:T], lhsT=wv[:, kc, ji * 128:(ji + 1) * 128], rhs=H[:, kc, :], start=(kc == 0), stop=(kc == 7))
                            return ins
                        S.op("pe", f1m, reads=[f"WS{s}"] + [f"H{m}" for m in range(8)], writes=[pk])
                        rt, rk_ = (RTMP, "RTMP") if j % 2 == 0 else (RTMP2, "RTMP2")
                        S.op("act", lambda e, pb=pb, rt=rt: e.activation(out=rt, in_=pb[:, 0:T], func=AF.Relu), reads=[pk], writes=[rk_])
                        S.op("dve" if j % 2 == 0 else "pool", lambda e, j=j, rt=rt: e.tensor_tensor(out=H1[:, j, :], in0=rt, in1=rt, op=ALU.mult),
                             reads=[rk_], writes=[f"H1_{j}"])
                for m in range(8):
                    s = wload(l, 29 + m)
                    wv = wview(s, 32, 128)
                    bi, pb, pk = bank()

                    def f2m(e, pb=pb, wv=wv):
                        for kc in range(32):
                            ins = e.matmul(pb[:, 0:T], lhsT=wv[:, kc, :], rhs=H1[:, kc, :], start=(kc == 0), stop=(kc == 31))
                        return ins
                    S.op("pe", f2m, reads=[f"WS{s}"] + [f"H1_{k_}" for k_ in range(32)], writes=[pk])
                    S.op("dve", lambda e, m=m, pb=pb: e.scalar_tensor_tensor(out=U[:, m, :], in0=pb[:, 0:T], scalar=DC[:, l, 24 + m:25 + m],
                                                                         in1=X[:, m, :], op0=ALU.mult, op1=ALU.add),
                         reads=[pk, "DC", f"X{m}"], writes=[f"U{m}"])
                ln_apply(f"lnfg{l}", f"lnfb{l}", (l + 1, 80, 88) if l + 1 < NL else None)
                dump(f"xf{l}", X[:], [f"X{m}" for m in range(8)], ti)
                S.fence()
            for tb in range(TB):
                for half in range(2):
                    bi, pb, pk = bank()

                    def fo_(e, pb=pb, tb=tb, half=half):
                        for f4_ in range(4):
                            fc = half * 4 + f4_
                            ins = e.transpose(pb[:, f4_ * 128:(f4_ + 1) * 128], X[:, fc, tb * 128:(tb + 1) * 128], IDENT[:])
                        return ins
                    S.op("pe", fo_, reads=[f"X{m}" for m in range(8)] + ["IDENT"], writes=[pk])
                    S.op("act" if half else "dve",
                         (lambda e, pb=pb, tb=tb, half=half: e.copy(out=XIN[:, tb, half * 512:(half + 1) * 512], in_=pb[:, :])) if half else
                         (lambda e, pb=pb, tb=tb, half=half: e.tensor_copy(out=XIN[:, tb, half * 512:(half + 1) * 512], in_=pb[:, :])),
                         reads=[pk], writes=["XIN"])
            S.dma("sp", out[t0:t0 + T, :].rearrange("(b p) d -> p b d", p=128), XIN[:], reads=["XIN"])
        S.wait_all("sp")
        S.emit(block)
        build_nc.last_stats = {"nops": S.nops, "cnt": dict(S.cnt)}
    return nc


def kernel(**inputs):
    B, S_TOK, _ = inputs["x"].shape
    nc = build_nc(S_TOK, T=256, NL=2)
    in_maps = []
    for b in range(B):
        m = {"x": np.ascontiguousarray(inputs["x"][b], dtype=np.float32), "vecs": pack_vecs(inputs, b)}
        for nm in WEIGHT_NAMES:
            m[nm] = np.ascontiguousarray(inputs[nm], dtype=np.float32)
        in_maps.append(m)
    res = run_bass_kernel_spmd(nc, in_maps, core_ids=list(range(B)))
    return np.stack([np.asarray(r["out"]).reshape(S_TOK, D) for r in res.results], axis=0).astype(np.float32)
```

```python
import os
import types
import numpy as np
from contextlib import ExitStack
import concourse.bass as bass
import concourse.mybir as mybir
from concourse.bass_utils import run_bass_kernel_spmd

F32 = mybir.dt.float32
BF16 = mybir.dt.bfloat16
I32 = mybir.dt.int32
ALU = mybir.AluOpType
AF = mybir.ActivationFunctionType

D = 1024
RW = 512
C_MAIN = 6400
NG = 37
GW = 4096
ALPHA = 4.0 ** 0.25
CDEC = float(np.exp(-0.5))
LN_EPS = 1e-5
GN_EPS = 64e-5
CONVW = 31
ENG_NAMES = ("pe", "act", "dve", "pool", "sp")


class Sched:
    def __init__(self, nc, sems, dma_sems):
        self.nc = nc
        self.sem = dict(zip(ENG_NAMES, sems))
        self.cnt = {e: 0 for e in ENG_NAMES}
        self.dma_pool = {q: list(v) for q, v in dma_sems.items()}
        self.dma_sems = [s for q in self.dma_pool for s in self.dma_pool[q]]
        self.dma_idx = {}
        i = 0
        for q in self.dma_pool:
            self.dma_idx[q] = list(range(i, i + len(self.dma_pool[q])))
            i += len(self.dma_pool[q])
        self.dma_cnt = [0] * len(self.dma_sems)
        self.dma_rr = {q: 0 for q in self.dma_pool}
        self.streams = {e: [] for e in ENG_NAMES}
        self.seen = {e: {} for e in ENG_NAMES}
        self.last_w = {}
        self.readers = {}
        self.nops = 0
        self.fence_dma = False

    def _deps(self, reads, writes):
        toks = []
        for k in reads:
            t = self.last_w.get(k)
            if t is not None:
                toks.append(t)
        for k in writes:
            t = self.last_w.get(k)
            if t is not None:
                toks.append(t)
            toks.extend(self.readers.get(k, ()))
        return toks

    def _waits_for(self, e, toks, skip_same_pe=True):
        need = {}
        for (sk, v) in toks:
            if sk == e and e == "pe" and skip_same_pe:
                continue
            if self.seen[e].get(sk, 0) >= v:
                continue
            if need.get(sk, 0) < v:
                need[sk] = v
        for sk, v in need.items():
            self.seen[e][sk] = v
        return list(need.items())

    def _commit(self, tok, reads, writes):
        for k in reads:
            self.readers.setdefault(k, []).append(tok)
        for k in writes:
            self.last_w[k] = tok
            self.readers[k] = []

    @staticmethod
    def _freeze(fn):
        if getattr(fn, "__closure__", None) is None:
            return fn
        cells = []
        for c in fn.__closure__:
            try:
                v = c.cell_contents
                if isinstance(v, types.FunctionType):
                    v = Sched._freeze(v)
                cells.append(types.CellType(v))
            except ValueError:
                cells.append(c)
        g = types.FunctionType(fn.__code__, fn.__globals__, fn.__name__, fn.__defaults__, tuple(cells))
        g.__kwdefaults__ = fn.__kwdefaults__
        return g

    def op(self, e, fn, reads=(), writes=()):
        fn = self._freeze(fn)
        reads = list(reads); writes = list(writes)
        toks = self._deps(reads, writes)
        waits = self._waits_for(e, toks)
        self.cnt[e] += 1
        tok = (e, self.cnt[e])
        self.streams[e].append((waits, fn, ("eng", e)))
        self._commit(tok, reads, writes)
        self.nops += 1
        return tok

    def dma(self, q, out, in_, reads=(), writes=(), **kw):
        toks = self._deps(reads, writes)
        i = self.dma_idx[q][self.dma_rr[q]]
        self.dma_rr[q] = (self.dma_rr[q] + 1) % len(self.dma_idx[q])
        sk = ("d", i)
        if self.dma_cnt[i] > 0:
            toks.append((sk, self.dma_cnt[i]))
        waits = self._waits_for(q, toks)
        self.dma_cnt[i] += 16
        tok = (sk, self.dma_cnt[i])

        def fn(eng, out=out, in_=in_, kw=kw):
            return eng.dma_start(out=out, in_=in_, **kw)
        self.streams[q].append((waits, fn, ("dma", i)))
        self._commit(tok, reads, writes)
        return tok

    def fence(self, engines=("pe", "act", "dve", "pool")):
        for e in engines:
            toks = [(en, self.cnt[en]) for en in engines if self.cnt[en] > 0]
            if self.fence_dma:
                toks += [(("d", i), v) for i, v in enumerate(self.dma_cnt) if v > 0]
            waits = self._waits_for(e, toks, skip_same_pe=False)
            if waits:
                self.streams[e].append((waits, None, None))

    def wait_all(self, e):
        toks = [(en, self.cnt[en]) for en in ENG_NAMES if self.cnt[en] > 0 and en != e]
        toks += [(("d", i), v) for i, v in enumerate(self.dma_cnt) if v > 0]
        waits = self._waits_for(e, toks)
        self.streams[e].append((waits, None, None))

    def _semh(self, sk):
        if isinstance(sk, tuple):
            return self.dma_sems[sk[1]]
        return self.sem[sk]

    def emit(self, block):
        eng_of = {"pe": self.nc.tensor, "act": self.nc.scalar, "dve": self.nc.vector,
                  "pool": self.nc.gpsimd, "sp": self.nc.sync}

        def mk(e):
            def body(eng):
                for waits, fn, sig in self.streams[e]:
                    for sk, v in waits:
                        eng.wait_ge(self._semh(sk), v)
                    if fn is None:
                        continue
                    ins = fn(eng)
                    if sig[0] == "eng":
                        ins.then_inc(self.sem[e], 1)
                    else:
                        ins.then_inc(self.dma_sems[sig[1]], 16)
            return body
        block.tensor(mk("pe"))
        block.scalar(mk("act"))
        block.vector(mk("dve"))
        block.gpsimd(mk("pool"))
        block.sync(mk("sp"))


class Rec:
    def __init__(self):
        self.items = []

    def op(self, e, fn, reads=(), writes=()):
        self.items.append(("op", e, Sched._freeze(fn), list(reads), list(writes)))

    def dma(self, q, out, in_, reads=(), writes=(), **kw):
        self.items.append(("dma", q, out, in_, list(reads), list(writes), kw))


class RecK(Rec):
    def __init__(self, si, setkeys):
        super().__init__()
        self.si, self.setkeys = si, set(setkeys)

    def _m(self, ks):
        return [f"{k}_{self.si}" if k in self.setkeys else k for k in ks]

    def op(self, e, fn, reads=(), writes=()):
        super().op(e, fn, self._m(reads), self._m(writes))


def replay_merged(S, recs):
    pos = [0] * len(recs)
    tot = [max(1, len(r.items)) for r in recs]
    while True:
        best, bf = None, None
        for i, r in enumerate(recs):
            if pos[i] < len(r.items):
                f = pos[i] / tot[i]
                if bf is None or f < bf:
                    best, bf = i, f
        if best is None:
            break
        it = recs[best].items[pos[best]]
        pos[best] += 1
        if it[0] == "op":
            S.op(it[1], it[2], reads=it[3], writes=it[4])
        else:
            S.dma(it[1], it[2], it[3], reads=it[4], writes=it[5], **it[6])


def vec_layout():
    off = {}
    r = 0

    def add(name, n):
        nonlocal r
        off[name] = r
        r += n
    add("c", 8)
    for l in range(2):
        for nm, n in (("ada_b", 48), ("mu", 14), ("w0", 4), ("a0", 4), ("kk", 4), ("ka", 4), ("rk", 4),
                      ("gng", 4), ("gnb", 4), ("cvb", 4), ("cvg", 4), ("cvbb", 4), ("plsc", 4),
                      ("lnmg", 8), ("lnmb", 8), ("lnfg", 8), ("lnfb", 8), ("cvw", 124)):
            add(f"{nm}{l}", n)
    add("v0", 4)
    add("muv", 1)
    return off, r


VOFF, NVROWS = vec_layout()
NVB = (NVROWS + 127) // 128


def pack_vecs(inp, b):
    P = np.zeros((NVB * 128, 128), np.float32)

    def put(name, arr):
        a = np.asarray(arr, np.float32).reshape(-1)
        n = (a.size + 127) // 128
        buf = np.zeros(n * 128, np.float32)
        buf[:a.size] = a
        P[VOFF[name]:VOFF[name] + n] = buf.reshape(n, 128)
    put("c", inp["c"][b])
    for l in range(2):
        put(f"ada_b{l}", inp["ada_b"][l]); put(f"mu{l}", inp["shift_mu"][l])
        put(f"w0{l}", inp["rw_w0"][l]); put(f"a0{l}", inp["rw_a0"][l])
        put(f"kk{l}", inp["rw_kk"][l]); put(f"ka{l}", inp["rw_ka"][l]); put(f"rk{l}", inp["rw_rk"][l])
        put(f"gng{l}", inp["rw_gn_g"][l]); put(f"gnb{l}", inp["rw_gn_b"][l])
        put(f"cvb{l}", inp["cv_b"][l]); put(f"cvg{l}", inp["cv_ln_g"][l]); put(f"cvbb{l}", inp["cv_ln_b"][l])
        put(f"plsc{l}", inp["pl_scale"][l])
        put(f"lnmg{l}", inp["ln_m_g"][l]); put(f"lnmb{l}", inp["ln_m_b"][l])
        put(f"lnfg{l}", inp["ln_f_g"][l]); put(f"lnfb{l}", inp["ln_f_b"][l])
        put(f"cvw{l}", inp["cv_w"][l])
    put("v0", inp["rw_v0"][0])
    put("muv", inp["shift_mu_vres"][0])
    return P


WEIGHT_NAMES = ("ada_w", "w_in", "w_in_vres", "rw_w2", "rw_a2", "rw_g2", "rw_v2", "rw_wo", "cv_wo",
                "pl_wo", "pl_w", "w_out", "mlp_w1", "mlp_w2")
WEIGHT_SHAPES = {"ada_w": [2, 1024, 6144], "w_in": [2, 1024, 6400], "w_in_vres": [1, 1024, 32],
                 "rw_w2": [2, 64, 512], "rw_a2": [2, 64, 512], "rw_g2": [2, 128, 512], "rw_v2": [1, 32, 512],
                 "rw_wo": [2, 512, 1024], "cv_wo": [2, 512, 1024], "pl_wo": [2, 512, 1024],
                 "pl_w": [2, 4, 128, 128], "w_out": [2, 1024, 1024], "mlp_w1": [2, 1024, 4096],
                 "mlp_w2": [2, 4096, 1024]}


def build_nc(S_TOK, T=256, NL=2, debug=None, REC_BF16=True, NEU_BF16=True, MM2_F32=False, ST_F32=False, KEEP_FENCES=False):
    assert S_TOK % T == 0 and T % 128 == 0 and T <= 512
    NT = S_TOK // T
    NCH = T // 64
    TB = T // 128
    nc = bass.Bass("TRN2", target_bir_lowering=False)
    dr = {}
    dr["x"] = nc.dram_tensor("x", [S_TOK, D], F32, kind="ExternalInput").ap()
    dr["vecs"] = nc.dram_tensor("vecs", [NVB * 128, 128], F32, kind="ExternalInput").ap()
    for nm in WEIGHT_NAMES:
        dr[nm] = nc.dram_tensor(nm, WEIGHT_SHAPES[nm], F32, kind="ExternalInput").ap()
    out = nc.dram_tensor("out", [S_TOK, D], F32, kind="ExternalOutput").ap()
    wsc = nc.dram_tensor("wsc", [2, NG, 128, GW], BF16, kind="Internal").ap()
    dbg = {}
    if debug:
        for nm, shp in debug.items():
            dbg[nm] = nc.dram_tensor(nm, list(shp), F32, kind="ExternalOutput").ap()

    with ExitStack() as es:
        def sb(name, shape, dt=F32):
            return es.enter_context(nc.sbuf_tensor(name, list(shape), dt))

        def pst(name, shape, dt=F32):
            return es.enter_context(nc.psum_tensor(name, list(shape), dt))

        X = sb("X", [128, 8, T])
        H = sb("H", [128, 8, T], BF16)
        WS = sb("WS", [128, 4, GW], BF16)
        VF = sb("VF", [128, 4, T])
        VC = sb("VC", [128, NVB * 128])
        PK = sb("PK", [128, NVB, 128])
        ADA = sb("ADA", [128, 2, 48])
        DC = sb("DC", [128, 2, 128])
        IDENT = sb("IDENT", [128, 128])
        I2 = sb("I2", [128, 64])
        MSU = sb("MSU", [128, 128]); MIU = sb("MIU", [128, 128]); MSL = sb("MSL", [128, 128])
        OBD = sb("OBD", [128, 128]); OBD64 = sb("OBD64", [128, 128])
        O512 = sb("O512", [128, 128]); O1024 = sb("O1024", [128, 128])
        RESETM = sb("RESETM", [128, T])
        EPS = sb("EPS", [128, 4])
        RC16 = sb("RC16", [128, 4, 16])
        IOTI = sb("IOTI", [128, 16], I32)
        W2P = sb("W2P", [128, 2, 512], BF16); A2P = sb("A2P", [128, 2, 512], BF16)
        G2 = sb("G2", [128, 2, 512], BF16); V2 = sb("V2", [32, 512], BF16)
        PLW = sb("PLW", [128, 2, 4, 128], BF16)
        CARRY = sb("CARRY", [128, 2, 16])
        HALO = sb("HALO", [128, 2, 4, 30])
        PHALO = sb("PHALO", [128, 2, 4, 16])
        ST = sb("ST", [128, 2, 4, 64])
        CONDT = sb("CONDT", [128, 8])
        XIN = sb("XIN", [128, TB, D])
        YBR = sb("YBR", [128, 3, 4, T], BF16)
        AW = 78 * T + 9600 + 1024
        AR = sb("AR", [128, AW])

        class Alloc:
            def __init__(self, base=0):
                self.o = base

            def f(self, n, shape=None):
                ap = AR[:, self.o:self.o + n]
                self.o += n
                assert self.o <= AW, (self.o, AW)
                return ap

            def t3(self, a, b):
                return self.f(a * b).rearrange("p (a b) -> p a b", a=a)

            def b3(self, a, b):
                n = (a * b + 1) // 2
                return self.f(n).bitcast(BF16).rearrange("p (a b) -> p a b", a=a)

        A = Alloc()
        ZR = A.t3(4, T); ZK = A.t3(4, T); ZV = A.t3(4, T); YF = A.t3(4, T)
        TA = A.f(T); TBm = A.f(T); TC = A.f(T); TD = A.f(T)
        mark_dead1 = A.o
        SW = A.t3(4, T); LC = A.t3(4, T); PINV = A.t3(4, T); AH = A.t3(4, T); KKN = A.t3(4, T)
        mark_dead1_end = A.o
        RAWS = [A.f(T), A.f(T), A.f(T)]; ZTMP = A.f(T)
        raw_rr = [0]
        T12 = A.b3(1, T)[:, 0, :]; LG = A.b3(1, T)[:, 0, :]; LV = A.b3(1, T)[:, 0, :]
        rt3 = A.b3 if REC_BF16 else A.t3
        nt3 = A.b3 if (REC_BF16 and NEU_BF16) else A.t3
        BDA = rt3(4, 128); BDB = rt3(4, 128); BDK = rt3(4, 128); BDR = rt3(4, 128); BDV = rt3(4, 128)
        BDA2 = nt3(4, 128); BDB2 = nt3(4, 128)
        MN = [nt3(4, 128), nt3(4, 128)]; MNT = [nt3(4, 128), nt3(4, 128)]
        MKA = rt3(4, 128); MBR = rt3(4, 128); MKR = rt3(4, 128); GM = nt3(4, 128); GMr = rt3(4, 128)
        BDBT = rt3(4, 128); BDKT = rt3(4, 128)
        VT = rt3(4, 64); XTt = (A.t3 if MM2_F32 else rt3)(4, 64); UT = rt3(4, 64)
        YTBD = rt3(4, 128)
        STb = rt3(4, 64); TMPS = A.t3(4, 64)
        PC = A.t3(4, 8)
        SET0 = (BDA, BDB, BDK, BDR, BDV, MKA, MBR, MKR, GM, GMr, BDBT, BDKT, VT, BDA2, BDB2)
        SET1 = (rt3(4, 128), rt3(4, 128), rt3(4, 128), rt3(4, 128), rt3(4, 128), rt3(4, 128), rt3(4, 128), rt3(4, 128),
                nt3(4, 128), rt3(4, 128), rt3(4, 128), rt3(4, 128), rt3(4, 64), nt3(4, 128), nt3(4, 128))
        SETS = [SET0, SET1]
        SETKEYS = ("BDA", "BDB", "BDK", "BDR", "BDV", "MKA", "MBR", "MKR", "GM", "GMr", "BDBT", "BDKT", "VT", "BDA2", "BDB2")
        arena_mixer_end = A.o
        B2 = Alloc(arena_mixer_end)
        HGLU = B2.t3(4, T + 30); ACC = B2.t3(4, T); CVV = B2.t3(4, T)
        PP = B2.t3(4, T + 16); SA = B2.f(T + 16); SBm = B2.f(T + 16)
        PLB = B2.b3(1, T)[:, 0, :]
        TA2 = B2.f(T); TB2 = B2.f(T); TC2 = B2.f(T); TD2 = B2.f(T)
        usq_off = B2.o
        TCq = B2.t3(4, T); TDq = B2.t3(4, T)
        USQ = AR[:, usq_off:usq_off + 8 * T].rearrange("p (a b) -> p a b", a=8)
        RESET4 = B2.f(4 * T)
        assert B2.o <= AW, (B2.o, AW)
        LT = {"TA": TA, "TB": TBm, "TC": TC, "TD": TD, "kTA": "TA", "kTB": "TB", "kTC": "TC", "kTD": "TD"}
        LT1 = dict(LT)
        LT2 = {"TA": TA2, "TB": TB2, "TC": TC2, "TD": TD2, "kTA": "TA2", "kTB": "TB2", "kTC": "TC2", "kTD": "TD2"}
        B3 = Alloc(0)
        MERG = B3.t3(4, T); MERGED = B3.b3(8, T); SIG = RAWS[0]; TMG = RAWS[1]
        assert B3.o <= 12 * T
        B4 = Alloc(mark_dead1)
        U = B4.t3(8, T); RTMP = B4.f(T); RTMP2 = B4.f(T)
        assert B4.o <= mark_dead1_end, (B4.o, AW)
        H1 = Alloc(0).b3(32, T)
        B5 = Alloc(0)
        AWS = [B5.t3(8, 512), B5.t3(8, 512)]
        assert B5.o <= AW

        PSB = [pst(f"PS{i}", [128, 512]) for i in range(8)]
        ps_rr = [0]

        bank_pools = {"ALL": list(range(8)), "A": [0, 1, 2, 3, 4, 5], "B": [6, 7], "AP": [0, 1, 2, 3], "AC": [4, 5]}
        bank_rr = {"ALL": 0, "A": 0, "B": 0, "AP": 0, "AC": 0}
        cur_pool = ["ALL"]

        def bank():
            pl = bank_pools[cur_pool[0]]
            i = pl[bank_rr[cur_pool[0]] % len(pl)]
            bank_rr[cur_pool[0]] += 1
            return i, PSB[i], f"PS{i}"

        sems = [es.enter_context(nc.semaphore(f"s_{e}")) for e in ENG_NAMES]
        dsems = {"sp": [es.enter_context(nc.semaphore(f"dsp{i}")) for i in range(8)],
                 "pool": [es.enter_context(nc.semaphore(f"dpl{i}")) for i in range(4)]}
        block = es.enter_context(nc.Block())
        S = Sched(nc, sems, dsems)
        S.fence_dma = bool(debug)

        def vc(name, col=0):
            c0 = VOFF[name] + col
            return VC[:, c0:c0 + 1]

        def conv_dma(l, g, src_ap, kc, n, col0=0, ncols=None):
            ncols = n if ncols is None else ncols
            dst = wsc[l, g][:, 0:kc * n].rearrange("p (kc n) -> p kc n", kc=kc)[:, :, col0:col0 + ncols]
            S.dma("pool", dst, src_ap, writes=[f"wsc{l}_{g}"])

        def group_src(l, g):
            w_in = dr["w_in"][l].rearrange("(kc p) n -> p kc n", p=128)
            if g < 6:
                return [(w_in[:, :, g * 512:(g + 1) * 512], 8, 512, 0, 512)]
            if g == 6:
                r = [(w_in[:, :, 3072:3328], 8, 288, 0, 256)]
                if l >= 1:
                    r.append((dr["w_in_vres"][l - 1].rearrange("(kc p) n -> p kc n", p=128), 8, 288, 256, 32))
                return r
            if g < 13:
                i = g - 7
                b, q = i // 2, i % 2
                c0 = 3328 + (8 * b + 4 * q) * 128
                return [(w_in[:, :, c0:c0 + 512], 8, 512, 0, 512)]
            if g < 19:
                i = g - 13
                b, q = i // 2, i % 2
                w = dr[("rw_wo", "cv_wo", "pl_wo")[b]][l].rearrange("(kc p) n -> p kc n", p=128)
                return [(w[:, :, q * 512:(q + 1) * 512], 4, 512, 0, 512)]
            if g < 21:
                q = g - 19
                w = dr["w_out"][l].rearrange("(kc p) n -> p kc n", p=128)
                return [(w[:, :, q * 512:(q + 1) * 512], 8, 512, 0, 512)]
            if g < 29:
                j = g - 21
                w = dr["mlp_w1"][l].rearrange("(kc p) n -> p kc n", p=128)
                return [(w[:, :, j * 512:(j + 1) * 512], 8, 512, 0, 512)]
            m = g - 29
            w = dr["mlp_w2"][l].rearrange("(kc p) n -> p kc n", p=128)
            return [(w[:, :, m * 128:(m + 1) * 128], 32, 128, 0, 128)]

        for l in range(NL):
            for g in range(NG):
                for (src, kc, n, c0, ncol) in group_src(l, g):
                    conv_dma(l, g, src, kc, n, c0, ncol)

        S.op("pool", lambda e: e.memset(IDENT[:], 0.0), writes=["IDENT"])
        S.op("pool", lambda e: e.affine_select(out=IDENT[:], in_=IDENT[:], pattern=[[-1, 128]], compare_op=ALU.not_equal,
                                               fill=1.0, base=0, channel_multiplier=1), reads=["IDENT"], writes=["IDENT"])
        S.op("pool", lambda e: e.tensor_tensor(out=I2[:], in0=IDENT[:, 0:64], in1=IDENT[:, 64:128], op=ALU.add),
             reads=["IDENT"], writes=["I2"])
        IDR = sb("IDR", [128, 128], BF16 if REC_BF16 else F32)
        I2R = sb("I2R", [128, 64], BF16 if REC_BF16 else F32)
        S.op("pool", lambda e: e.tensor_copy(out=IDR[:], in_=IDENT[:]), reads=["IDENT"], writes=["IDR"])
        S.op("pool", lambda e: e.tensor_copy(out=I2R[:], in_=I2[:]), reads=["I2"], writes=["I2R"])

        def tri(M, key, cmp_op, sgn=1):
            S.op("pool", lambda e: e.memset(M[:], 1.0), writes=[key])
            S.op("pool", lambda e: e.affine_select(out=M[:], in_=M[:], pattern=[[sgn, 128]], compare_op=cmp_op,
                                                   fill=0.0, base=0, channel_multiplier=-sgn), reads=[key], writes=[key])
            S.op("pool", lambda e: e.memset(M[0:64, 64:128], 0.0), reads=[key], writes=[key])
            S.op("pool", lambda e: e.memset(M[64:128, 0:64], 0.0), reads=[key], writes=[key])
        tri(MSU, "MSU", ALU.is_gt)
        tri(MIU, "MIU", ALU.is_ge)
        tri(MSL, "MSL", ALU.is_gt, -1)
        for M, key, val in ((OBD, "OBD", 1.0), (OBD64, "OBD64", 1.0 / 64)):
            S.op("pool", lambda e, M=M, val=val: e.memset(M[:], val), writes=[key])
            S.op("pool", lambda e, M=M: e.memset(M[0:64, 64:128], 0.0), reads=[key], writes=[key])
            S.op("pool", lambda e, M=M: e.memset(M[64:128, 0:64], 0.0), reads=[key], writes=[key])
        S.op("pool", lambda e: e.memset(O512[:], 1.0 / 512), writes=["O512"])
        S.op("pool", lambda e: e.memset(O1024[:], 1.0 / 1024), writes=["O1024"])
        S.op("pool", lambda e: e.memset(RESETM[:], 1.0), writes=["RESETM"])
        S.op("pool", lambda e: e.memset(RESETM[:].rearrange("p (c j) -> p c j", j=64)[:, :, 0:1], 0.0),
             reads=["RESETM"], writes=["RESETM"])
        S.op("pool", lambda e: e.memset(RESET4, 1.0), writes=["RESET4"])
        S.op("pool", lambda e: e.memset(RESET4.rearrange("p (c j) -> p c j", j=64)[:, :, 0:1], 0.0), reads=["RESET4"], writes=["RESET4"])
        for i, v in enumerate((GN_EPS, LN_EPS / (ALPHA * ALPHA), LN_EPS, 0.0)):
            S.op("pool", lambda e, i=i, v=v: e.memset(EPS[:, i:i + 1], float(v)), reads=["EPS"], writes=["EPS"])
        S.op("pool", lambda e: e.iota(IOTI[:], pattern=[[1, 16]], base=1, channel_multiplier=0), writes=["IOTI"])
        for g in range(4):
            S.op("pool", lambda e, g=g: e.tensor_copy(out=RC16[:, g, :], in_=IOTI[:]), reads=["IOTI", "RC16"], writes=["RC16"])
            S.op("pool", lambda e, g=g: e.tensor_scalar(out=RC16[:, g, :], in0=RC16[:, g, :], scalar1=float(2 ** (g + 1)), scalar2=None,
                                                      op0=ALU.min), reads=["RC16"], writes=["RC16"])
        S.op("dve", lambda e: e.reciprocal(out=RC16[:], in_=RC16[:]), reads=["RC16"], writes=["RC16"])
        for nm, Tn in (("CARRY", CARRY), ("HALO", HALO), ("PHALO", PHALO), ("ST", ST)):
            S.op("pool", lambda e, Tn=Tn: e.memset(Tn[:], 0.0), writes=[nm])
        for nm, Tn in (("BDA_0", BDA), ("BDB_0", BDB), ("BDK_0", BDK), ("BDR_0", BDR), ("BDV_0", BDV), ("YTBD", YTBD), ("BDA2_0", BDA2), ("BDB2_0", BDB2), ("STb", STb)):
            S.op("pool", lambda e, Tn=Tn: e.memset(Tn, 0.0), writes=[nm])
        for i_, nm in ((0, "BDA"), (1, "BDB"), (2, "BDK"), (3, "BDR"), (4, "BDV"), (13, "BDA2"), (14, "BDB2")):
            S.op("pool", lambda e, Tn=SET1[i_]: e.memset(Tn, 0.0), writes=[nm + "_1"])
        S.op("pool", lambda e: e.memset(W2P[:], 0.0), writes=["W2P"])
        S.op("pool", lambda e: e.memset(A2P[:], 0.0), writes=["A2P"])
        for l in range(NL):
            S.dma("pool", W2P[0:64, l, :], dr["rw_w2"][l], reads=["W2P"], writes=["W2P"])
            S.dma("pool", A2P[64:128, l, :], dr["rw_a2"][l], reads=["A2P"], writes=["A2P"])
            S.dma("pool", G2[:, l, :], dr["rw_g2"][l], writes=["G2"])
            S.dma("pool", PLW[:, l, :, :], dr["pl_w"][l].rearrange("g c d -> c g d"), writes=["PLW"])
        if NL > 1:
            S.dma("pool", V2[:], dr["rw_v2"][0], writes=["V2"])
        S.dma("sp", PK[:], dr["vecs"].rearrange("(b p) n -> p b n", p=128), writes=["PK"])
        for b in range(NVB):
            bi, pb, pk = bank()
            S.op("pe", lambda e, b=b, pb=pb: e.transpose(pb[:, 0:128], PK[:, b, :], IDENT[:]),
                 reads=["PK", "IDENT"], writes=[pk])
            S.op("act", lambda e, b=b, pb=pb: e.copy(out=VC[:, b * 128:(b + 1) * 128], in_=pb[:, 0:128]),
                 reads=[pk], writes=["VC"])
        S.op("act", lambda e: e.activation(out=CONDT[:], in_=VC[:, VOFF["c"]:VOFF["c"] + 8], func=AF.Silu),
             reads=["VC"], writes=["CONDT"])
        for l in range(NL):
            bi, pb, pk = bank()
            for gq in range(12):
                slot = gq % 2
                S.dma("sp", AWS[slot], dr["ada_w"][l].rearrange("(kc p) n -> p kc n", p=128)[:, :, gq * 512:(gq + 1) * 512],
                      writes=[f"AWS{slot}"])
                for mi in range(4):
                    j = gq * 4 + mi

                    def fn(e, slot=slot, mi=mi, j=j, pb=pb):
                        for kc in range(8):
                            ins = e.matmul(pb[:, j:j + 1], lhsT=AWS[slot][:, kc, mi * 128:(mi + 1) * 128],
                                           rhs=CONDT[:, kc:kc + 1], start=(kc == 0), stop=(kc == 7))
                        return ins
                    S.op("pe", fn, reads=[f"AWS{slot}", "CONDT"], writes=[pk])
            S.op("dve", lambda e, l=l, pb=pb: e.tensor_tensor(out=ADA[:, l, :], in0=pb[:, 0:48],
                                                             in1=VC[:, VOFF[f"ada_b{l}"]:VOFF[f"ada_b{l}"] + 48], op=ALU.add),
                 reads=[pk, "VC"], writes=["ADA"])
        for l in range(NL):
            S.op("dve", lambda e, l=l: e.tensor_scalar(out=DC[:, l, 0:8], in0=ADA[:, l, 8:16], scalar1=1.0, scalar2=None, op0=ALU.add),
                 reads=["ADA"], writes=["DC"])
            S.op("dve", lambda e, l=l: e.tensor_scalar(out=DC[:, l, 8:16], in0=ADA[:, l, 32:40], scalar1=1.0, scalar2=None, op0=ALU.add),
                 reads=["ADA", "DC"], writes=["DC"])
            S.op("dve", lambda e, l=l: e.tensor_scalar(out=DC[:, l, 16:24], in0=ADA[:, l, 16:24], scalar1=1.0 / ALPHA, scalar2=None, op0=ALU.mult),
                 reads=["ADA", "DC"], writes=["DC"])
            S.op("dve", lambda e, l=l: e.tensor_scalar(out=DC[:, l, 24:32], in0=ADA[:, l, 40:48], scalar1=1.0 / ALPHA, scalar2=None, op0=ALU.mult),
                 reads=["ADA", "DC"], writes=["DC"])
            m0 = VOFF[f"mu{l}"]
            S.op("dve", lambda e, l=l, m0=m0: e.tensor_scalar(out=DC[:, l, 32:46], in0=VC[:, m0:m0 + 14], scalar1=-1.0, scalar2=1.0,
                                                           op0=ALU.mult, op1=ALU.add), reads=["VC", "DC"], writes=["DC"])
            mv0 = VOFF["muv"]
            S.op("dve", lambda e, l=l, mv0=mv0: e.tensor_scalar(out=DC[:, l, 46:47], in0=VC[:, mv0:mv0 + 1], scalar1=-1.0, scalar2=1.0,
                                                             op0=ALU.mult, op1=ALU.add), reads=["VC", "DC"], writes=["DC"])
            gm0, bm0 = VOFF[f"lnmg{l}"], VOFF[f"lnmb{l}"]
            S.op("dve", lambda e, l=l, gm0=gm0: e.tensor_tensor(out=DC[:, l, 64:72], in0=VC[:, gm0:gm0 + 8], in1=DC[:, l, 8:16], op=ALU.mult),
                 reads=["VC", "DC"], writes=["DC"])
            S.op("dve", lambda e, l=l, bm0=bm0: e.tensor_tensor(out=DC[:, l, 72:80], in0=VC[:, bm0:bm0 + 8], in1=DC[:, l, 8:16], op=ALU.mult),
                 reads=["VC", "DC"], writes=["DC"])
            S.op("dve", lambda e, l=l: e.tensor_tensor(out=DC[:, l, 72:80], in0=DC[:, l, 72:80], in1=ADA[:, l, 24:32], op=ALU.add),
                 reads=["ADA", "DC"], writes=["DC"])
            if l >= 1:
                gf0, bf0 = VOFF[f"lnfg{l - 1}"], VOFF[f"lnfb{l - 1}"]
                S.op("dve", lambda e, l=l, gf0=gf0: e.tensor_tensor(out=DC[:, l, 80:88], in0=VC[:, gf0:gf0 + 8], in1=DC[:, l, 0:8], op=ALU.mult),
                     reads=["VC", "DC"], writes=["DC"])
                S.op("dve", lambda e, l=l, bf0=bf0: e.tensor_tensor(out=DC[:, l, 88:96], in0=VC[:, bf0:bf0 + 8], in1=DC[:, l, 0:8], op=ALU.mult),
                     reads=["VC", "DC"], writes=["DC"])
                S.op("dve", lambda e, l=l: e.tensor_tensor(out=DC[:, l, 88:96], in0=DC[:, l, 88:96], in1=ADA[:, l, 0:8], op=ALU.add),
                     reads=["ADA", "DC"], writes=["DC"])
            k0 = VOFF[f"ka{l}"]
            S.op("dve", lambda e, l=l, k0=k0: e.tensor_scalar(out=DC[:, l, 48:52], in0=VC[:, k0:k0 + 4], scalar1=-1.0, scalar2=1.0,
                                                           op0=ALU.mult, op1=ALU.add), reads=["VC", "DC"], writes=["DC"])
        S.fence()

        ws_rr = [0]

        def wload(l, g):
            s = ws_rr[0]
            ws_rr[0] = (s + 1) % 4
            if g == 6:
                kc, n, nv = 8, 288, (288 if l >= 1 else 256)
            elif 13 <= g < 19:
                kc, n, nv = 4, 512, 512
            elif g >= 29:
                kc, n, nv = 32, 128, 128
            else:
                kc, n, nv = 8, 512, 512
            dst = WS[:, s, 0:kc * n].rearrange("p (kc n) -> p kc n", kc=kc)[:, :, 0:nv]
            src = wsc[l, g][:, 0:kc * n].rearrange("p (kc n) -> p kc n", kc=kc)[:, :, 0:nv]
            S.dma("sp", dst, src, reads=[f"wsc{l}_{g}"], writes=[f"WS{s}"])
            return s

        def wview(s, kc, n):
            return WS[:, s, 0:kc * n].rearrange("p (kc n) -> p kc n", kc=kc)

        def dump(name, ap, keys, idx=None):
            if name in dbg:
                dst = dbg[name] if idx is None else dbg[name][idx]
                S.dma("sp", dst, ap, reads=keys)

        def ln_stats(srcs, src_keys, sq_eng_out, ONESM, eps_col):
            n = len(srcs)
            tA, tB, tC, tD = LT["TA"], LT["TB"], LT["TC"], LT["TD"]
            kA, kB, kC, kD = LT["kTA"], LT["kTB"], LT["kTC"], LT["kTD"]
            bi1, pb1, pk1 = bank()
            bi2, pb2, pk2 = bank()

            def fm(e):
                for i, s_ in enumerate(srcs):
                    ins = e.matmul(pb1[:, 0:T], lhsT=ONESM[:], rhs=s_, start=(i == 0), stop=(i == n - 1))
                return ins
            S.op("pe", fm, reads=src_keys, writes=[pk1])
            for i, s_ in enumerate(srcs):
                S.op("act", lambda e, s_=s_: e.activation(out=tC, in_=s_, func=AF.Square), reads=[src_keys[i]], writes=[kC])
                S.op("pe", lambda e, i=i: e.matmul(pb2[:, 0:T], lhsT=ONESM[:], rhs=tC, start=(i == 0), stop=(i == n - 1)),
                     reads=[kC], writes=[pk2])
            S.op("act", lambda e: e.copy(out=tA, in_=pb1[:, 0:T]), reads=[pk1], writes=[kA])
            S.op("act", lambda e: e.activation(out=tD, in_=pb1[:, 0:T], func=AF.Square), reads=[pk1], writes=[kD])
            S.op("dve", lambda e: e.tensor_tensor(out=tB, in0=pb2[:, 0:T], in1=tD, op=ALU.subtract), reads=[pk2, kD], writes=[kB])
            S.op("dve", lambda e: e.tensor_scalar(out=tB, in0=tB, scalar1=0.0, scalar2=None, op0=ALU.max), reads=[kB], writes=[kB])
            S.op("act", lambda e: e.activation(out=tB, in_=tB, func=AF.Sqrt, bias=EPS[:, eps_col:eps_col + 1], scale=1.0),
                 reads=[kB, "EPS"], writes=[kB])
            S.op("dve", lambda e: e.reciprocal(out=tB, in_=tB), reads=[kB], writes=[kB])
            return tA, tB

        def modulate(l, which):
            o_sc = 0 if which == "m" else 8
            o_sh = 0 if which == "m" else 24
            for m in range(8):
                S.op("act", lambda e, m=m: e.activation(out=H[:, m, :], in_=X[:, m, :], func=AF.Identity,
                                                        bias=ADA[:, l, o_sh + m:o_sh + m + 1], scale=DC[:, l, o_sc + m:o_sc + m + 1]),
                     reads=[f"X{m}", "ADA", "DC"], writes=[f"H{m}"])

        def residual_ln(l, which, get_ps):
            o_gt = 16 if which == "m" else 24
            gname, bname = (f"lnmg{l}", f"lnmb{l}") if which == "m" else (f"lnfg{l}", f"lnfb{l}")
            for m in range(8):
                pb, pk = get_ps(m)
                S.op("dve", lambda e, m=m, pb=pb: e.scalar_tensor_tensor(out=U[:, m, :], in0=pb[:, 0:T], scalar=DC[:, l, o_gt + m:o_gt + m + 1],
                                                                     in1=X[:, m, :], op0=ALU.mult, op1=ALU.add),
                     reads=[pk, "DC", f"X{m}"], writes=[f"U{m}"])
            mean, rstd = ln_stats([U[:, m, :] for m in range(8)], [f"U{m}" for m in range(8)], None, O1024, 1)
            for m in range(8):
                S.op("dve", lambda e, m=m: e.tensor_tensor(out=U[:, m, :], in0=U[:, m, :], in1=mean, op=ALU.subtract),
                     reads=[f"U{m}", "TA"], writes=[f"U{m}"])
                S.op("pool", lambda e, m=m: e.tensor_tensor(out=U[:, m, :], in0=U[:, m, :], in1=rstd, op=ALU.mult),
                     reads=[f"U{m}", "TB"], writes=[f"U{m}"])
                S.op("dve", lambda e, m=m: e.tensor_scalar(out=X[:, m, :], in0=U[:, m, :], scalar1=vc(gname, m), scalar2=vc(bname, m),
                                                         op0=ALU.mult, op1=ALU.add), reads=[f"U{m}", "VC"], writes=[f"X{m}"])

        for ti in range(NT):
            t0 = ti * T
            S.dma("sp", XIN[:], dr["x"][t0:t0 + T, :].rearrange("(b p) d -> p b d", p=128), writes=["XIN"])
            for fc in range(8):
                bi, pb, pk = bank()

                def ft(e, fc=fc, pb=pb):
                    for tb in range(TB):
                        ins = e.transpose(pb[:, tb * 128:(tb + 1) * 128], XIN[:, tb, fc * 128:(fc + 1) * 128], IDENT[:])
                    return ins
                S.op("pe", ft, reads=["XIN", "IDENT"], writes=[pk])
                S.op("act" if fc % 2 else "dve",
                     (lambda e, fc=fc, pb=pb: e.copy(out=X[:, fc, :], in_=pb[:, 0:T])) if fc % 2 else
                     (lambda e, fc=fc, pb=pb: e.tensor_copy(out=X[:, fc, :], in_=pb[:, 0:T])),
                     reads=[pk], writes=[f"X{fc}"])
            for l in range(NL):
                if l == 0:
                    modulate(l, "m")
                slot_of = {}

                def proj_chunk(j, ncols=128, col_in_group=None):
                    g = j // 4 if j < 24 else 6
                    if g not in slot_of:
                        slot_of[g] = wload(l, g)
                    s = slot_of[g]
                    n = 512 if g < 6 else 288
                    wv = wview(s, 8, n)
                    c0 = (j % 4) * 128 if j < 24 else (j - 24) * 128
                    bi, pb, pk = bank()

                    def fn(e, pb=pb, wv=wv, c0=c0, ncols=ncols):
                        for kc in range(8):
                            ins = e.matmul(pb[0:ncols, 0:T], lhsT=wv[:, kc, c0:c0 + ncols], rhs=H[:, kc, :],
                                           start=(kc == 0), stop=(kc == 7))
                        return ins
                    S.op("pe", fn, reads=[f"WS{s}"] + [f"H{m}" for m in range(8)], writes=[pk])
                    return pb, pk

                def token_shift(pb, pk, cidx, mu_ap, omu_ap, dst, dst_keys, np_=128):
                    k = raw_rr[0]
                    raw_rr[0] = (k + 1) % 3
                    Rk, rk_key = RAWS[k], f"RAW{k}"
                    ck = f"CARRY{l}_{cidx}"
                    S.op("act", lambda e: e.activation(out=Rk[0:np_, 0:T], in_=pb[0:np_, 0:T], func=AF.Identity, scale=omu_ap),
                         reads=[pk, "DC"], writes=[rk_key])
                    S.op("dve", lambda e: e.scalar_tensor_tensor(out=dst[:, 1:T], in0=pb[0:np_, 0:T - 1], scalar=mu_ap, in1=Rk[0:np_, 1:T],
                                                                 op0=ALU.mult, op1=ALU.add), reads=[pk, rk_key, "VC"], writes=dst_keys)
                    S.op("dve", lambda e: e.scalar_tensor_tensor(out=dst[:, 0:1], in0=CARRY[0:np_, l, cidx:cidx + 1], scalar=mu_ap, in1=Rk[0:np_, 0:1],
                                                                 op0=ALU.mult, op1=ALU.add), reads=[ck, rk_key, "VC"] + list(dst_keys), writes=dst_keys)
                    S.op("act", lambda e: e.copy(out=CARRY[0:np_, l, cidx:cidx + 1], in_=pb[0:np_, T - 1:T]), reads=[pk, ck], writes=[ck])

                for j in range(26):
                    pb, pk = proj_chunk(j)
                    if j < 12:
                        dstT = (ZR, ZK, ZV)[j // 4]
                        nm = ("ZR", "ZK", "ZV")[j // 4]
                        token_shift(pb, pk, j, vc(f"mu{l}", j), DC[:, l, 32 + j:33 + j], dstT[:, j % 4, :], [f"{nm}{j % 4}"])
                    elif j == 12:
                        token_shift(pb, pk, j, vc(f"mu{l}", j), DC[:, l, 32 + j:33 + j], ZTMP, ["ZTMP"])
                        S.op("act", lambda e: e.activation(out=T12[0:64, :], in_=ZTMP[0:64, :], func=AF.Tanh), reads=["ZTMP"], writes=["T12"])
                        S.op("dve", lambda e: e.tensor_copy(out=T12[64:128, :], in_=ZTMP[64:128, :]), reads=["ZTMP", "T12"], writes=["T12"])
                    elif j == 13:
                        token_shift(pb, pk, j, vc(f"mu{l}", j), DC[:, l, 32 + j:33 + j], ZTMP, ["ZTMP"])
                        S.op("act", lambda e: e.activation(out=LG, in_=ZTMP, func=AF.Sigmoid), reads=["ZTMP"], writes=["LG"])
                    elif j < 18:
                        cc = j - 14
                        S.op("act", lambda e, cc=cc, pb=pb: e.copy(out=CVV[:, cc, :], in_=pb[:, 0:T]), reads=[pk], writes=[f"CVV{cc}"])
                    elif j < 22:
                        cc = j - 18
                        S.op("act", lambda e, pb=pb: e.activation(out=TA, in_=pb[:, 0:T], func=AF.Sigmoid), reads=[pk], writes=["TA"])
                        if cc == 0:
                            S.op("pool", lambda e: e.tensor_copy(out=HGLU[:, :, 0:30], in_=HALO[:, l, :, :]), reads=["HALO"],
                                 writes=[f"HGLU{c_}" for c_ in range(4)])
                        S.op("dve", lambda e, cc=cc: e.tensor_tensor(out=HGLU[:, cc, 30:30 + T], in0=CVV[:, cc, :], in1=TA, op=ALU.mult),
                             reads=[f"CVV{cc}", "TA"], writes=[f"HGLU{cc}"])
                    else:
                        gg = j - 22
                        if gg == 0:
                            S.op("pool", lambda e: e.tensor_copy(out=PP[:, :, 0:16], in_=PHALO[:, l, :, :]), reads=["PHALO"],
                                 writes=[f"PP{c_}" for c_ in range(4)])
                        S.op("act", lambda e, gg=gg, pb=pb: e.copy(out=PP[:, gg, 16:16 + T], in_=pb[:, 0:T]), reads=[pk], writes=[f"PP{gg}"])
                if l >= 1:
                    s = slot_of[6]
                    wv = wview(s, 8, 288)
                    bi, pb, pk = bank()

                    def fnv(e, pb=pb, wv=wv):
                        for kc in range(8):
                            ins = e.matmul(pb[0:32, 0:T], lhsT=wv[:, kc, 256:288], rhs=H[:, kc, :], start=(kc == 0), stop=(kc == 7))
                        return ins
                    S.op("pe", fnv, reads=[f"WS{s}"] + [f"H{m}" for m in range(8)], writes=[pk])
                    token_shift(pb, pk, 14, VC[0:32, VOFF["muv"]:VOFF["muv"] + 1], DC[0:32, l, 46:47], ZTMP[0:32, :], ["ZTMP"], np_=32)
                    S.op("act", lambda e: e.copy(out=LV[0:32, :], in_=ZTMP[0:32, :]), reads=["ZTMP"], writes=["LV"])
                dump(f"z_r{l}", ZR, ["ZR0", "ZR1", "ZR2", "ZR3"], ti)
                dump(f"z_k{l}", ZK, ["ZK0", "ZK1", "ZK2", "ZK3"], ti)
                dump(f"z_v{l}", ZV, ["ZV0", "ZV1", "ZV2", "ZV3"], ti)

                S_main = S
                S = Rec(); recB = S
                cur_pool[0] = "B"; LT.update(LT2)
                for cc in range(4):
                    eng = "dve" if cc % 2 == 0 else "pool"
                    wbase = VOFF[f"cvw{l}"]
                    S.op(eng, lambda e, cc=cc: e.tensor_scalar(out=ACC[:, cc, :], in0=HGLU[:, cc, 0:T], scalar1=VC[:, wbase + cc:wbase + cc + 1],
                                                              scalar2=vc(f"cvb{l}", cc), op0=ALU.mult, op1=ALU.add),
                         reads=[f"HGLU{cc}", "VC"], writes=[f"ACC{cc}"])
                    for jt in range(1, CONVW):
                        if eng == "dve":
                            S.op(eng, lambda e, cc=cc, jt=jt: e.scalar_tensor_tensor(
                                out=ACC[:, cc, :], in0=HGLU[:, cc, jt:jt + T], scalar=VC[:, wbase + jt * 4 + cc:wbase + jt * 4 + cc + 1],
                                in1=ACC[:, cc, :], op0=ALU.mult, op1=ALU.add), reads=[f"HGLU{cc}", "VC", f"ACC{cc}"], writes=[f"ACC{cc}"])
                        else:
                            ct, ck = ((SA, "SA"), (SBm, "SB"))[jt % 2]
                            S.op("act", lambda e, cc=cc, jt=jt, ct=ct: e.activation(
                                out=ct[:, 0:T], in_=HGLU[:, cc, jt:jt + T], func=AF.Identity,
                                scale=VC[:, wbase + jt * 4 + cc:wbase + jt * 4 + cc + 1]), reads=[f"HGLU{cc}", "VC"], writes=[ck])
                            S.op("pool", lambda e, cc=cc, ct=ct: e.tensor_tensor(out=ACC[:, cc, :], in0=ACC[:, cc, :], in1=ct[:, 0:T], op=ALU.add),
                                 reads=[ck, f"ACC{cc}"], writes=[f"ACC{cc}"])
                S.op("pool", lambda e: e.tensor_copy(out=HALO[:, l, :, :], in_=HGLU[:, :, T:T + 30]),
                     reads=[f"HGLU{c_}" for c_ in range(4)], writes=["HALO"])
                dump(f"conv{l}", ACC, [f"ACC{c_}" for c_ in range(4)], ti)
                mean, rstd = ln_stats([ACC[:, cc, :] for cc in range(4)], [f"ACC{cc}" for cc in range(4)], None, O512, 2)
                for cc in range(4):
                    S.op("dve", lambda e, cc=cc: e.tensor_tensor(out=ACC[:, cc, :], in0=ACC[:, cc, :], in1=mean, op=ALU.subtract),
                         reads=[f"ACC{cc}", LT["kTA"]], writes=[f"ACC{cc}"])
                    S.op("pool", lambda e, cc=cc: e.tensor_tensor(out=ACC[:, cc, :], in0=ACC[:, cc, :], in1=rstd, op=ALU.mult),
                         reads=[f"ACC{cc}", LT["kTB"]], writes=[f"ACC{cc}"])
                    S.op("dve", lambda e, cc=cc: e.tensor_scalar(out=ACC[:, cc, :], in0=ACC[:, cc, :], scalar1=vc(f"cvg{l}", cc),
                                                                scalar2=vc(f"cvbb{l}", cc), op0=ALU.mult, op1=ALU.add),
                         reads=[f"ACC{cc}", "VC"], writes=[f"ACC{cc}"])
                    S.op("act", lambda e, cc=cc: e.activation(out=YBR[:, 1, cc, :], in_=ACC[:, cc, :], func=AF.Silu),
                         reads=[f"ACC{cc}"], writes=[f"YBR1_{cc}"])
                for gg in range(4):
                    win = 2 ** (gg + 1)
                    eng = "pool" if gg % 2 == 0 else "dve"
                    src = PP[:, gg, :]
                    cur_key = f"PP{gg}"
                    bufs = [(SA, "SA"), (SBm, "SB")]
                    sh = 1
                    k_ = 0
                    while sh < win:
                        dstb, dkey = bufs[k_ % 2]
                        lo = 2 * sh - 1
                        S.op(eng, lambda e, src=src, dstb=dstb, sh=sh, lo=lo: e.tensor_tensor(
                            out=dstb[:, lo:T + 16], in0=src[:, lo:T + 16], in1=src[:, lo - sh:T + 16 - sh], op=ALU.add),
                            reads=[cur_key], writes=[dkey])
                        src, cur_key = dstb, dkey
                        sh *= 2
                        k_ += 1
                    dstb, dkey = bufs[k_ % 2]
                    S.op("dve", lambda e, src=src, dstb=dstb, gg=gg, win=win: e.scalar_tensor_tensor(
                        out=dstb[:, 16:16 + T], in0=src[:, 16:16 + T], scalar=1.0 / win, in1=PP[:, gg, 16:16 + T],
                        op0=ALU.mult, op1=ALU.subtract), reads=[cur_key, f"PP{gg}"], writes=[dkey])
                    if ti == 0:
                        S.op(eng, lambda e, src=src, gg=gg: e.tensor_tensor(out=src[:, 16:32], in0=src[:, 16:32], in1=RC16[:, gg, :], op=ALU.mult),
                             reads=[cur_key, "RC16", dkey], writes=[cur_key])
                        S.op(eng, lambda e, src=src, dstb=dstb, gg=gg: e.tensor_tensor(out=dstb[:, 16:32], in0=src[:, 16:32], in1=PP[:, gg, 16:32],
                                                                                  op=ALU.subtract), reads=[cur_key, f"PP{gg}", dkey], writes=[dkey])
                    S.op("act", lambda e, dstb=dstb: e.copy(out=PLB[:, :], in_=dstb[:, 16:16 + T]), reads=[dkey], writes=["PLB"])
                    bi, pb, pk = bank()
                    S.op("pe", lambda e, gg=gg, pb=pb: e.matmul(pb[:, 0:T], lhsT=PLW[:, l, gg, :], rhs=PLB[:, :], start=True, stop=True),
                         reads=["PLB", "PLW"], writes=[pk])
                    S.op("act", lambda e, gg=gg, pb=pb: e.activation(out=YBR[:, 2, gg, :], in_=pb[:, 0:T], func=AF.Identity,
                                                                   scale=vc(f"plsc{l}", gg)), reads=[pk, "VC"], writes=[f"YBR2_{gg}"])
                S.op("pool", lambda e: e.tensor_copy(out=PHALO[:, l, :, :], in_=PP[:, :, T:T + 16]),
                     reads=[f"PP{c_}" for c_ in range(4)], writes=["PHALO"])

                S = Rec(); recA = S
                cur_pool[0] = "A"; LT.update(LT1)
                allk = lambda nm: [f"{nm}{p}" for p in range(4)]
                f4T = lambda t3: t3.rearrange("p a b -> p (a b)")
                for p in range(4):
                    bi, pb, pk = bank()
                    S.op("pe", lambda e, p=p, pb=pb: e.matmul(pb[:, 0:T], lhsT=W2P[:, l, p * 128:(p + 1) * 128], rhs=T12, start=True, stop=True),
                         reads=["T12", "W2P"], writes=[pk])
                    S.op("act", lambda e, p=p, pb=pb: e.activation(out=SW[:, p, :], in_=pb[:, 0:T], func=AF.Sigmoid, bias=vc(f"w0{l}", p), scale=1.0),
                         reads=[pk, "VC"], writes=[f"SW{p}"])
                    bi, pb, pk = bank()
                    S.op("pe", lambda e, p=p, pb=pb: e.matmul(pb[:, 0:T], lhsT=A2P[:, l, p * 128:(p + 1) * 128], rhs=T12, start=True, stop=True),
                         reads=["T12", "A2P"], writes=[pk])
                    S.op("act", lambda e, p=p, pb=pb: e.activation(out=AH[:, p, :], in_=pb[:, 0:T], func=AF.Sigmoid, bias=vc(f"a0{l}", p), scale=1.0),
                         reads=[pk, "VC"], writes=[f"AH{p}"])
                    if l >= 1:
                        bi, pb, pk = bank()
                        S.op("pe", lambda e, p=p, pb=pb: e.matmul(pb[:, 0:T], lhsT=V2[:, p * 128:(p + 1) * 128], rhs=LV[0:32, :], start=True, stop=True),
                             reads=["LV", "V2"], writes=[pk])
                        S.op("act", lambda e, p=p, pb=pb: e.activation(out=TCq[:, p, :], in_=pb[:, 0:T], func=AF.Sigmoid, bias=vc("v0", p), scale=1.0),
                             reads=[pk, "VC"], writes=[f"TCq{p}"])
                if l == 0:
                    S.op("pool", lambda e: e.tensor_copy(out=VF[:, :, :], in_=ZV), reads=allk("ZV"), writes=allk("VF"))
                else:
                    S.op("pool", lambda e: e.tensor_tensor(out=TDq, in0=VF[:, :, :], in1=ZV, op=ALU.subtract), reads=allk("VF") + allk("ZV"), writes=allk("TDq"))
                    S.op("dve", lambda e: e.tensor_tensor(out=TDq, in0=TDq, in1=TCq, op=ALU.mult), reads=allk("TDq") + allk("TCq"), writes=allk("TDq"))
                    S.op("dve", lambda e: e.tensor_tensor(out=ZV, in0=ZV, in1=TDq, op=ALU.add), reads=allk("ZV") + allk("TDq"), writes=allk("ZV"))
                for p in range(4):
                    S.op("act", lambda e, p=p: e.activation(out=TCq[:, p, :], in_=ZK[:, p, :], func=AF.Square, scale=vc(f"kk{l}", p)),
                         reads=[f"ZK{p}", "VC"], writes=[f"TCq{p}"])
                    bi, pb, pk = bank()
                    S.op("pe", lambda e, p=p, pb=pb: e.matmul(pb[:, 0:T], lhsT=OBD[:], rhs=TCq[:, p, :], start=True, stop=True), reads=[f"TCq{p}", "OBD"], writes=[pk])
                    S.op("act", lambda e, p=p, pb=pb: e.activation(out=TDq[:, p, :], in_=pb[:, 0:T], func=AF.Sqrt), reads=[pk], writes=[f"TDq{p}"])
                S.op("dve", lambda e: e.tensor_scalar(out=TDq, in0=TDq, scalar1=1e-12, scalar2=-1.0, op0=ALU.max, op1=ALU.mult), reads=allk("TDq"), writes=allk("TDq"))
                S.op("dve", lambda e: e.reciprocal(out=TDq, in_=TDq), reads=allk("TDq"), writes=allk("TDq"))
                for p in range(4):
                    S.op("dve", lambda e, p=p: e.scalar_tensor_tensor(out=KKN[:, p, :], in0=ZK[:, p, :], scalar=vc(f"kk{l}", p), in1=TDq[:, p, :],
                                                                      op0=ALU.mult, op1=ALU.mult), reads=[f"ZK{p}", f"TDq{p}", "VC"], writes=[f"KKN{p}"])
                    S.op("pool", lambda e, p=p: e.tensor_scalar(out=TCq[:, p, :], in0=AH[:, p, :], scalar1=vc(f"ka{l}", p), scalar2=DC[:, l, 48 + p:49 + p],
                                                               op0=ALU.mult, op1=ALU.add), reads=[f"AH{p}", "VC", "DC", f"TCq{p}"], writes=[f"TCq{p}"])
                S.op("dve", lambda e: e.tensor_tensor(out=ZK, in0=ZK, in1=TCq, op=ALU.mult), reads=allk("ZK") + allk("TCq"), writes=allk("ZK"))
                S.op("dve", lambda e: e.scalar_tensor_tensor(out=AH, in0=AH, scalar=-1.0, in1=KKN, op0=ALU.mult, op1=ALU.mult),
                     reads=allk("AH") + allk("KKN"), writes=allk("AH"))
                S.op("dve", lambda e: e.tensor_tensor_scan(out=f4T(LC), data0=RESET4, data1=f4T(SW), initial=0.0, op0=ALU.mult, op1=ALU.add),
                     reads=["RESET4"] + allk("SW"), writes=allk("LC"))
                S.op("pool", lambda e: e.tensor_tensor(out=SW, in0=LC, in1=SW, op=ALU.subtract), reads=allk("LC") + allk("SW"), writes=allk("SW"))
                S.op("act", lambda e: e.activation(out=SW, in_=SW, func=AF.Exp, scale=-CDEC), reads=allk("SW"), writes=allk("SW"))
                S.op("act", lambda e: e.activation(out=PINV, in_=LC, func=AF.Exp, scale=CDEC), reads=allk("LC"), writes=allk("PINV"))
                S.op("act", lambda e: e.activation(out=LC, in_=LC, func=AF.Exp, scale=-CDEC), reads=allk("LC"), writes=allk("LC"))
                S.op("pool", lambda e: e.tensor_copy(out=PC[:, :, 0:NCH], in_=LC.rearrange("p a (c j) -> p a c j", j=64)[:, :, :, 63]),
                     reads=allk("LC"), writes=["PC"])
                PEX, PIN, BBt, KH = SW, LC, AH, ZK
                allk = lambda nm: [f"{nm}{p}" for p in range(4)]
                same_nt = (REC_BF16 == (REC_BF16 and NEU_BF16))
                BDAn, BDBn = (BDA, BDB) if same_nt else (BDA2, BDB2)
                kBDAn, kBDBn = ("BDA", "BDB") if same_nt else ("BDA2", "BDB2")
                GMx = GM if (same_nt or MM2_F32) else GMr
                kGMx = "GM" if (same_nt or MM2_F32) else "GMr"

                def mm4(outb, lT, rT, ncol=128):
                    def fn(e):
                        for p in range(4):
                            ins = e.matmul(outb[:, p * ncol:(p + 1) * ncol], lhsT=lT[:, p, :], rhs=rT[:, p, :], start=True, stop=True)
                        return ins
                    return fn

                def mm4c(outb, lT, rconst, ncol):
                    def fn(e):
                        for p in range(4):
                            ins = e.matmul(outb[:, p * ncol:(p + 1) * ncol], lhsT=lT[:, p, :], rhs=rconst, start=True, stop=True)
                        return ins
                    return fn

                def v4(pb, n=128):
                    return pb[:, 0:4 * n].rearrange("p (a b) -> p a b", a=4)

                def bc4(M):
                    return M[:].unsqueeze(1).to_broadcast([128, 4, 128])
                STl = ST[:, l, :, :]
                S.op("act", lambda e: e.copy(out=STb, in_=STl), reads=["ST", "STb"], writes=["STb"])
                recP, recC = [], []
                for c in range(NCH):
                    cs = slice(c * 64, (c + 1) * 64)
                    si = c % 2
                    (BDA, BDB, BDK, BDR, BDV, MKA, MBR, MKR, GM, GMr, BDBT, BDKT, VT, BDA2, BDB2) = SETS[si]
                    BDAn, BDBn = (BDA, BDB) if same_nt else (BDA2, BDB2)
                    GMx = GM if (same_nt or MM2_F32) else GMr
                    S = RecK(si, SETKEYS); recP.append(S)
                    cur_pool[0] = "AP"
                    for hh in range(2):
                        rws = slice(64 * hh, 64 * hh + 64)
                        cls = slice(64 * hh, 64 * hh + 64)
                        S.op("dve", lambda e, rws=rws, cls=cls, cs=cs: e.tensor_tensor(out=BDR[rws, :, cls], in0=ZR[rws, :, cs], in1=PIN[rws, :, cs], op=ALU.mult),
                             reads=allk("ZR") + allk("LC"), writes=["BDR"])
                        S.op("pool", lambda e, rws=rws, cls=cls, cs=cs: e.tensor_tensor(out=BDK[rws, :, cls], in0=KH[rws, :, cs], in1=PINV[rws, :, cs], op=ALU.mult),
                             reads=allk("ZK") + allk("PINV"), writes=["BDK"])
                        S.op("dve", lambda e, rws=rws, cls=cls, cs=cs: e.tensor_tensor(out=BDB[rws, :, cls], in0=BBt[rws, :, cs], in1=PINV[rws, :, cs], op=ALU.mult),
                             reads=allk("AH") + allk("PINV"), writes=["BDB"])
                        S.op("pool", lambda e, rws=rws, cls=cls, cs=cs: e.tensor_tensor(out=BDA[rws, :, cls], in0=KKN[rws, :, cs], in1=PEX[rws, :, cs], op=ALU.mult),
                             reads=allk("KKN") + allk("SW"), writes=["BDA"])
                        S.op("pool", lambda e, rws=rws, cls=cls, cs=cs: e.tensor_copy(out=BDV[rws, :, cls], in_=ZV[rws, :, cs]),
                             reads=allk("ZV"), writes=["BDV"])
                        if not same_nt:
                            S.op("dve", lambda e, rws=rws, cls=cls, cs=cs: e.tensor_tensor(out=BDB2[rws, :, cls], in0=BBt[rws, :, cs], in1=PINV[rws, :, cs], op=ALU.mult),
                                 reads=allk("AH") + allk("PINV"), writes=["BDB2"])
                            S.op("pool", lambda e, rws=rws, cls=cls, cs=cs: e.tensor_tensor(out=BDA2[rws, :, cls], in0=KKN[rws, :, cs], in1=PEX[rws, :, cs], op=ALU.mult),
                                 reads=allk("KKN") + allk("SW"), writes=["BDA2"])
                    for (lT, lk, rT, rk_, dstM, dk, msk, mk_) in (
                            (BDBn, kBDBn, BDAn, kBDAn, MN[0], "MN0", MSU, "MSU"),
                            (BDAn, kBDAn, BDBn, kBDBn, MNT[0], "MNT0", MSL, "MSL"),
                            (BDK, "BDK", BDA, "BDA", MKA, "MKA", MSU, "MSU"),
                            (BDB, "BDB", BDR, "BDR", MBR, "MBR", MIU, "MIU"),
                            (BDK, "BDK", BDR, "BDR", MKR, "MKR", MIU, "MIU")):
                        bi, pb, pk = bank()
                        S.op("pe", mm4(pb, lT, rT), reads=[lk, rk_], writes=[pk])
                        S.op("dve", lambda e, pb=pb, dstM=dstM, msk=msk: e.tensor_tensor(out=dstM, in0=v4(pb), in1=bc4(msk), op=ALU.mult),
                             reads=[pk, mk_], writes=[dk])
                    S.op("pool", lambda e: e.tensor_tensor(out=GM, in0=MN[0], in1=bc4(IDENT), op=ALU.add), reads=["MN0", "IDENT"], writes=["GM"])
                    cur = 0
                    for jl in range(1, 6):
                        nxt = 1 - cur
                        if jl < 5:
                            bi, pb, pk = bank()
                            S.op("pe", mm4(pb, MNT[cur], MN[cur]), reads=[f"MN{cur}", f"MNT{cur}"], writes=[pk])
                            S.op("act", lambda e, pb=pb, nxt=nxt: e.copy(out=MN[nxt], in_=v4(pb)), reads=[pk], writes=[f"MN{nxt}"])
                        bi, pb, pk = bank()
                        S.op("pe", mm4(pb, MN[cur], MNT[cur]), reads=[f"MN{cur}", f"MNT{cur}"], writes=[pk])
                        S.op("act", lambda e, pb=pb, nxt=nxt: e.copy(out=MNT[nxt], in_=v4(pb)), reads=[pk], writes=[f"MNT{nxt}"])
                        bi, pb, pk = bank()
                        S.op("pe", mm4(pb, MNT[nxt], GM), reads=[f"MNT{nxt}", "GM"], writes=[pk])
                        S.op("dve", lambda e, pb=pb: e.tensor_tensor(out=GM, in0=v4(pb), in1=GM, op=ALU.add), reads=[pk, "GM"], writes=["GM"])
                        cur = nxt
                    if not same_nt:
                        S.op("act", lambda e: e.copy(out=GMr, in_=GM), reads=["GM"], writes=["GMr"])
                    for (srcT, sk_, dstT_, dk) in ((BDB, "BDB", BDBT, "BDBT"), (BDK, "BDK", BDKT, "BDKT")):
                        bi, pb, pk = bank()
                        S.op("pe", mm4c(pb, srcT, IDR[:], 128), reads=[sk_, "IDR"], writes=[pk])
                        S.op("act", lambda e, pb=pb, dstT_=dstT_: e.copy(out=dstT_, in_=v4(pb)), reads=[pk], writes=[dk])
                    bi, pb, pk = bank()
                    S.op("pe", mm4c(pb, BDV, I2R[:], 64), reads=["BDV", "I2R"], writes=[pk])
                    S.op("act", lambda e, pb=pb: e.copy(out=VT, in_=v4(pb, 64)), reads=[pk], writes=["VT"])
                    S = RecK(si, SETKEYS); recC.append(S)
                    cur_pool[0] = "AC"
                    bi, pb, pk = bank()

                    def f1(e, pb=pb):
                        for p in range(4):
                            e.matmul(pb[:, p * 64:(p + 1) * 64], lhsT=BDA[:, p, :], rhs=STb[:, p, :], start=True, stop=False)
                            ins = e.matmul(pb[:, p * 64:(p + 1) * 64], lhsT=MKA[:, p, :], rhs=VT[:, p, :], start=False, stop=True)
                        return ins
                    S.op("pe", f1, reads=["BDA", "STb", "MKA", "VT"], writes=[pk])
                    S.op("act", lambda e, pb=pb: e.copy(out=XTt, in_=v4(pb, 64)), reads=[pk], writes=["XT"])
                    bi, pb, pk = bank()
                    S.op("pe", mm4(pb, GMx, XTt, 64), reads=[kGMx, "XT"], writes=[pk])
                    S.op("dve", lambda e, pb=pb: e.tensor_copy(out=UT, in_=v4(pb, 64)), reads=[pk], writes=["UT"])
                    bi, pby, pky = bank()

                    def f3(e, pby=pby):
                        for p in range(4):
                            o = pby[:, p * 64:(p + 1) * 64]
                            e.matmul(o, lhsT=BDR[:, p, :], rhs=STb[:, p, :], start=True, stop=False)
                            e.matmul(o, lhsT=MBR[:, p, :], rhs=UT[:, p, :], start=False, stop=False)
                            ins = e.matmul(o, lhsT=MKR[:, p, :], rhs=VT[:, p, :], start=False, stop=True)
                        return ins
                    S.op("pe", f3, reads=["BDR", "STb", "MBR", "UT", "MKR", "VT"], writes=[pky])
                    bi, pbs, pks = bank()

                    def f4(e, pbs=pbs):
                        for p in range(4):
                            o = pbs[:, p * 64:(p + 1) * 64]
                            e.matmul(o, lhsT=BDBT[:, p, :], rhs=UT[:, p, :], start=True, stop=False)
                            ins = e.matmul(o, lhsT=BDKT[:, p, :], rhs=VT[:, p, :], start=False, stop=True)
                        return ins
                    S.op("pe", f4, reads=["BDBT", "UT", "BDKT", "VT"], writes=[pks])
                    S.op("dve", lambda e, pbs=pbs: e.tensor_tensor(out=TMPS, in0=v4(pbs, 64), in1=STl, op=ALU.add), reads=[pks, "ST"], writes=["TMPS"])
                    S.op("dve", lambda e, c=c: e.tensor_tensor(out=STl, in0=TMPS, in1=PC[:, :, c:c + 1].to_broadcast([128, 4, 64]), op=ALU.mult),
                         reads=["TMPS", "PC"], writes=["ST"])
                    S.op("act", lambda e: e.copy(out=STb, in_=STl), reads=["ST"], writes=["STb"])
                    S.op("act", lambda e, pby=pby: e.copy(out=YTBD[0:64, :, 0:64], in_=v4(pby, 64)[0:64]), reads=[pky, "YTBD"], writes=["YTBD"])
                    S.op("pool" if False else "dve", lambda e, pby=pby: e.tensor_copy(out=YTBD[64:128, :, 64:128], in_=v4(pby, 64)[64:128]), reads=[pky, "YTBD"], writes=["YTBD"])
                    bi, pb, pk = bank()
                    S.op("pe", mm4c(pb, YTBD, I2R[:], 64), reads=["YTBD", "I2R"], writes=[pk])
                    S.op("act", lambda e, pb=pb, cs=cs: e.copy(out=YF[:, :, cs], in_=v4(pb, 64)), reads=[pk], writes=allk("YF"))
                S = recA
                cur_pool[0] = "A"
                if os.environ.get("SEQ_CHUNKS"):
                    for c in range(NCH):
                        replay_merged(S, [recP[c]]); replay_merged(S, [recC[c]])
                else:
                    replay_merged(S, [recP[0]])
                    for c in range(NCH):
                        replay_merged(S, [recC[c]] + ([recP[c + 1]] if c + 1 < NCH else []))
                (BDA, BDB, BDK, BDR, BDV, MKA, MBR, MKR, GM, GMr, BDBT, BDKT, VT, BDA2, BDB2) = SETS[0]
                dump(f"yrec{l}", YF, allk("YF"), ti)
                def v2(pb):
                    return pb[:, 0:2 * T].rearrange("p (a b) -> p a b", a=2)
                hk = lambda nm, h: [f"{nm}{2 * h}", f"{nm}{2 * h + 1}"]
                MEANq, MSQq, VARq = SW, LC, PINV
                for p in range(4):
                    S.op("dve", lambda e, p=p: e.scalar_tensor_tensor(out=TCq[:, p, :], in0=ZR[:, p, :], scalar=vc(f"rk{l}", p), in1=KH[:, p, :],
                                                                      op0=ALU.mult, op1=ALU.mult), reads=[f"ZR{p}", f"ZK{p}", "VC", f"TCq{p}"], writes=[f"TCq{p}"])
                for h in range(2):
                    bi, pbb, pkb = bank()

                    def fb(e, h=h, pbb=pbb):
                        for a in range(2):
                            ins = e.matmul(pbb[:, a * T:(a + 1) * T], lhsT=OBD[:], rhs=TCq[:, 2 * h + a, :], start=True, stop=True)
                        return ins
                    S.op("pe", fb, reads=hk("TCq", h) + ["OBD"], writes=[pkb])
                    S.op("dve", lambda e, h=h, pbb=pbb: e.tensor_tensor(out=KKN[:, 2 * h:2 * h + 2, :], in0=v2(pbb), in1=ZV[:, 2 * h:2 * h + 2, :], op=ALU.mult),
                         reads=[pkb] + hk("ZV", h), writes=hk("KKN", h))
                S.op("act", lambda e: e.activation(out=TDq, in_=YF, func=AF.Square), reads=allk("YF"), writes=allk("TDq"))
                for h in range(2):
                    bi, pbm, pkm = bank()
                    bi, pbq, pkq = bank()

                    def fgm(e, h=h, pbm=pbm):
                        for a in range(2):
                            ins = e.matmul(pbm[:, a * T:(a + 1) * T], lhsT=OBD64[:], rhs=YF[:, 2 * h + a, :], start=True, stop=True)
                        return ins

                    def fgq(e, h=h, pbq=pbq):
                        for a in range(2):
                            ins = e.matmul(pbq[:, a * T:(a + 1) * T], lhsT=OBD64[:], rhs=TDq[:, 2 * h + a, :], start=True, stop=True)
                        return ins
                    S.op("pe", fgm, reads=hk("YF", h) + ["OBD64"], writes=[pkm])
                    S.op("pe", fgq, reads=hk("TDq", h) + ["OBD64"], writes=[pkq])
                    S.op("act", lambda e, h=h, pbm=pbm: e.copy(out=MEANq[:, 2 * h:2 * h + 2, :], in_=v2(pbm)), reads=[pkm], writes=hk("SW", h))
                    S.op("act", lambda e, h=h, pbm=pbm: e.activation(out=MSQq[:, 2 * h:2 * h + 2, :], in_=v2(pbm), func=AF.Square), reads=[pkm], writes=hk("LC", h))
                    S.op("dve", lambda e, h=h, pbq=pbq: e.tensor_tensor(out=VARq[:, 2 * h:2 * h + 2, :], in0=v2(pbq), in1=MSQq[:, 2 * h:2 * h + 2, :], op=ALU.subtract),
                         reads=[pkq] + hk("LC", h), writes=hk("PINV", h))
                S.op("dve", lambda e: e.tensor_scalar(out=VARq, in0=VARq, scalar1=0.0, scalar2=float(GN_EPS), op0=ALU.max, op1=ALU.add),
                     reads=allk("PINV"), writes=allk("PINV"))
                S.op("act", lambda e: e.activation(out=VARq, in_=VARq, func=AF.Sqrt), reads=allk("PINV"), writes=allk("PINV"))
                S.op("dve", lambda e: e.reciprocal(out=VARq, in_=VARq), reads=allk("PINV"), writes=allk("PINV"))
                S.op("dve", lambda e: e.tensor_tensor(out=YF, in0=YF, in1=MEANq, op=ALU.subtract), reads=allk("YF") + allk("SW"), writes=allk("YF"))
                S.op("pool", lambda e: e.tensor_tensor(out=YF, in0=YF, in1=VARq, op=ALU.mult), reads=allk("YF") + allk("PINV"), writes=allk("YF"))
                for p in range(4):
                    S.op("dve", lambda e, p=p: e.tensor_scalar(out=YF[:, p, :], in0=YF[:, p, :], scalar1=vc(f"gng{l}", p), scalar2=vc(f"gnb{l}", p),
                                                             op0=ALU.mult, op1=ALU.add), reads=[f"YF{p}", "VC"], writes=[f"YF{p}"])
                S.op("pool", lambda e: e.tensor_tensor(out=YF, in0=YF, in1=KKN, op=ALU.add), reads=allk("YF") + allk("KKN"), writes=allk("YF"))
                for h in range(2):
                    bi, pbg, pkg = bank()

                    def fgg(e, h=h, pbg=pbg):
                        for a in range(2):
                            p = 2 * h + a
                            ins = e.matmul(pbg[:, a * T:(a + 1) * T], lhsT=G2[:, l, p * 128:(p + 1) * 128], rhs=LG, start=True, stop=True)
                        return ins
                    S.op("pe", fgg, reads=["LG", "G2"], writes=[pkg])
                    S.op("dve", lambda e, h=h, pbg=pbg: e.tensor_tensor(out=YBR[:, 0, 2 * h:2 * h + 2, :], in0=v2(pbg), in1=YF[:, 2 * h:2 * h + 2, :], op=ALU.mult),
                         reads=[pkg] + hk("YF", h), writes=[f"YBR0_{2 * h}", f"YBR0_{2 * h + 1}"])
                S = S_main
                cur_pool[0] = "ALL"; LT.update(LT1)
                replay_merged(S, [recA, recB])
                S.fence()
                for q in range(2):
                    for b in range(3):
                        sg = wload(l, 7 + b * 2 + q)
                        so = wload(l, 13 + b * 2 + q)
                        wg = wview(sg, 8, 512)
                        wo = wview(so, 4, 512)
                        for mi in range(4):
                            bi, pbg, pkg = bank()

                            def fg(e, pbg=pbg, wg=wg, mi=mi):
                                for kc in range(8):
                                    ins = e.matmul(pbg[:, 0:T], lhsT=wg[:, kc, mi * 128:(mi + 1) * 128], rhs=H[:, kc, :], start=(kc == 0), stop=(kc == 7))
                                return ins
                            S.op("pe", fg, reads=[f"WS{sg}"] + [f"H{m}" for m in range(8)], writes=[pkg])
                            bi, pbo, pko = bank()

                            def fo(e, pbo=pbo, wo=wo, mi=mi, b=b):
                                for kc in range(4):
                                    ins = e.matmul(pbo[:, 0:T], lhsT=wo[:, kc, mi * 128:(mi + 1) * 128], rhs=YBR[:, b, kc, :], start=(kc == 0), stop=(kc == 3))
                                return ins
                            S.op("pe", fo, reads=[f"WS{so}"] + [f"YBR{b}_{k_}" for k_ in range(4)], writes=[pko])
                            S.op("act", lambda e, pbg=pbg: e.activation(out=SIG, in_=pbg[:, 0:T], func=AF.Sigmoid), reads=[pkg], writes=["RAW0"])
                            if b == 0:
                                S.op("dve", lambda e, pbo=pbo, mi=mi: e.tensor_tensor(out=MERG[:, mi, :], in0=pbo[:, 0:T], in1=SIG, op=ALU.mult),
                                     reads=[pko, "RAW0"], writes=[f"MERG{mi}"])
                            else:
                                S.op("dve", lambda e, pbo=pbo: e.tensor_tensor(out=TMG, in0=pbo[:, 0:T], in1=SIG, op=ALU.mult),
                                     reads=[pko, "RAW0"], writes=["RAW1"])
                                if b == 1:
                                    S.op("pool", lambda e, mi=mi: e.tensor_tensor(out=MERG[:, mi, :], in0=MERG[:, mi, :], in1=TMG, op=ALU.add),
                                         reads=[f"MERG{mi}", "RAW1"], writes=[f"MERG{mi}"])
                                else:
                                    S.op("pool", lambda e, mi=mi, q=q: e.tensor_tensor(out=MERGED[:, q * 4 + mi, :], in0=MERG[:, mi, :], in1=TMG, op=ALU.add),
                                         reads=[f"MERG{mi}", "RAW1"], writes=[f"MERGED{q * 4 + mi}"])
                if KEEP_FENCES:
                    S.fence()
                sA = wload(l, 19)
                sB = wload(l, 20)
                psl = {}
                for m in range(8):
                    s = sA if m < 4 else sB
                    wv = wview(s, 8, 512)
                    bi, pb, pk = bank()

                    def fw(e, pb=pb, wv=wv, m=m):
                        for kc in range(8):
                            ins = e.matmul(pb[:, 0:T], lhsT=wv[:, kc, (m % 4) * 128:(m % 4 + 1) * 128], rhs=MERGED[:, kc, :], start=(kc == 0), stop=(kc == 7))
                        return ins
                    S.op("pe", fw, reads=[f"WS{s}"] + [f"MERGED{k_}" for k_ in range(8)], writes=[pk])
                    S.op("dve", lambda e, m=m, pb=pb: e.scalar_tensor_tensor(out=U[:, m, :], in0=pb[:, 0:T], scalar=DC[:, l, 16 + m:17 + m],
                                                                         in1=X[:, m, :], op0=ALU.mult, op1=ALU.add),
                         reads=[pk, "DC", f"X{m}"], writes=[f"U{m}"])

                def ln_apply(gname, bname, nxt):
                    uk = [f"U{m}" for m in range(8)]
                    qk = allk("TCq") + allk("TDq")
                    tA, tB, tD = LT["TA"], LT["TB"], LT["TD"]
                    S.op("act", lambda e: e.activation(out=USQ, in_=U, func=AF.Square), reads=uk, writes=qk)
                    bi1, pb1, pk1 = bank()
                    bi2, pb2, pk2 = bank()

                    def fm1(e):
                        for m in range(8):
                            ins = e.matmul(pb1[:, 0:T], lhsT=O1024[:], rhs=U[:, m, :], start=(m == 0), stop=(m == 7))
                        return ins

                    def fm2(e):
                        for m in range(8):
                            ins = e.matmul(pb2[:, 0:T], lhsT=O1024[:], rhs=USQ[:, m, :], start=(m == 0), stop=(m == 7))
                        return ins
                    S.op("pe", fm1, reads=uk + ["O1024"], writes=[pk1])
                    S.op("pe", fm2, reads=qk + ["O1024"], writes=[pk2])
                    S.op("act", lambda e: e.copy(out=tA, in_=pb1[:, 0:T]), reads=[pk1], writes=["TA"])
                    S.op("act", lambda e: e.activation(out=tD, in_=pb1[:, 0:T], func=AF.Square), reads=[pk1], writes=["TD"])
                    S.op("dve", lambda e: e.tensor_tensor(out=tB, in0=pb2[:, 0:T], in1=tD, op=ALU.subtract), reads=[pk2, "TD"], writes=["TB"])
                    S.op("dve", lambda e: e.tensor_scalar(out=tB, in0=tB, scalar1=0.0, scalar2=None, op0=ALU.max), reads=["TB"], writes=["TB"])
                    S.op("act", lambda e: e.activation(out=tB, in_=tB, func=AF.Sqrt, bias=EPS[:, 1:2], scale=1.0), reads=["TB", "EPS"], writes=["TB"])
                    S.op("dve", lambda e: e.reciprocal(out=tB, in_=tB), reads=["TB"], writes=["TB"])
                    S.op("dve", lambda e: e.tensor_tensor(out=U, in0=U, in1=tA.unsqueeze(1).to_broadcast([128, 8, T]), op=ALU.subtract),
                         reads=uk + ["TA"], writes=uk)
                    S.op("pool", lambda e: e.tensor_tensor(out=U, in0=U, in1=tB.unsqueeze(1).to_broadcast([128, 8, T]), op=ALU.mult),
                         reads=uk + ["TB"], writes=uk)
                    for m in range(8):
                        S.op("dve", lambda e, m=m: e.tensor_scalar(out=X[:, m, :], in0=U[:, m, :], scalar1=vc(gname, m), scalar2=vc(bname, m),
                                                                 op0=ALU.mult, op1=ALU.add), reads=[f"U{m}", "VC"], writes=[f"X{m}"])
                        if nxt is not None:
                            ll, g0, b0 = nxt
                            S.op("act", lambda e, m=m, ll=ll, g0=g0, b0=b0: e.activation(out=H[:, m, :], in_=U[:, m, :], func=AF.Identity,
                                                                                       bias=DC[:, ll, b0 + m:b0 + m + 1], scale=DC[:, ll, g0 + m:g0 + m + 1]),
                                 reads=[f"U{m}", "DC"], writes=[f"H{m}"])
                ln_apply(f"lnmg{l}", f"lnmb{l}", (l, 64, 72))
                dump(f"xm{l}", X[:], [f"X{m}" for m in range(8)], ti)
                if KEEP_FENCES:
                    S.fence()
                for jg in range(8):
                    s = wload(l, 21 + jg)
                    wv = wview(s, 8, 512)
                    for ji in range(4):
                        j = jg * 4 + ji
                        bi, pb, pk = bank()

                        def f1m(e, pb=pb, wv=wv, ji=ji):
                            for kc in range(8):
                                ins = e.matmul(pb[:, 0:T], lhsT=wv[:, kc, ji * 128:(ji + 1) * 128], rhs=H[:, kc, :], start=(kc == 0), stop=(kc == 7))
                            return ins
                        S.op("pe", f1m, reads=[f"WS{s}"] + [f"H{m}" for m in range(8)], writes=[pk])
                        rt, rk_ = (RTMP, "RTMP") if j % 2 == 0 else (RTMP2, "RTMP2")
                        S.op("act", lambda e, pb=pb, rt=rt: e.activation(out=rt, in_=pb[:, 0:T], func=AF.Relu), reads=[pk], writes=[rk_])
                        S.op("dve" if j % 2 == 0 else "pool", lambda e, j=j, rt=rt: e.tensor_tensor(out=H1[:, j, :], in0=rt, in1=rt, op=ALU.mult),
                             reads=[rk_], writes=[f"H1_{j}"])
                for m in range(8):
                    s = wload(l, 29 + m)
                    wv = wview(s, 32, 128)
                    bi, pb, pk = bank()

                    def f2m(e, pb=pb, wv=wv):
                        for kc in range(32):
                            ins = e.matmul(pb[:, 0:T], lhsT=wv[:, kc, :], rhs=H1[:, kc, :], start=(kc == 0), stop=(kc == 31))
                        return ins
                    S.op("pe", f2m, reads=[f"WS{s}"] + [f"H1_{k_}" for k_ in range(32)], writes=[pk])
                    S.op("dve", lambda e, m=m, pb=pb: e.scalar_tensor_tensor(out=U[:, m, :], in0=pb[:, 0:T], scalar=DC[:, l, 24 + m:25 + m],
                                                                         in1=X[:, m, :], op0=ALU.mult, op1=ALU.add),
                         reads=[pk, "DC", f"X{m}"], writes=[f"U{m}"])
                ln_apply(f"lnfg{l}", f"lnfb{l}", (l + 1, 80, 88) if l + 1 < NL else None)
                dump(f"xf{l}", X[:], [f"X{m}" for m in range(8)], ti)
                if KEEP_FENCES:
                    S.fence()
            for tb in range(TB):
                for half in range(2):
                    bi, pb, pk = bank()

                    def fo_(e, pb=pb, tb=tb, half=half):
                        for f4_ in range(4):
                            fc = half * 4 + f4_
                            ins = e.transpose(pb[:, f4_ * 128:(f4_ + 1) * 128], X[:, fc, tb * 128:(tb + 1) * 128], IDENT[:])
                        return ins
                    S.op("pe", fo_, reads=[f"X{m}" for m in range(8)] + ["IDENT"], writes=[pk])
                    S.op("act" if half else "dve",
                         (lambda e, pb=pb, tb=tb, half=half: e.copy(out=XIN[:, tb, half * 512:(half + 1) * 512], in_=pb[:, :])) if half else
                         (lambda e, pb=pb, tb=tb, half=half: e.tensor_copy(out=XIN[:, tb, half * 512:(half + 1) * 512], in_=pb[:, :])),
                         reads=[pk], writes=["XIN"])
            S.dma("sp", out[t0:t0 + T, :].rearrange("(b p) d -> p b d", p=128), XIN[:], reads=["XIN"])
        S.wait_all("sp")
        S.emit(block)
        build_nc.last_stats = {"nops": S.nops, "cnt": dict(S.cnt)}
    return nc


def kernel(**inputs):
    B, S_TOK, _ = inputs["x"].shape
    nc = build_nc(S_TOK, T=256, NL=2)
    in_maps = []
    for b in range(B):
        m = {"x": np.ascontiguousarray(inputs["x"][b], dtype=np.float32), "vecs": pack_vecs(inputs, b)}
        for nm in WEIGHT_NAMES:
            m[nm] = np.ascontiguousarray(inputs[nm], dtype=np.float32)
        in_maps.append(m)
    res = run_bass_kernel_spmd(nc, in_maps, core_ids=list(range(B)))
    return np.stack([np.asarray(r["out"]).reshape(S_TOK, D) for r in res.results], axis=0).astype(np.float32)
```

```python
import os
import types
import numpy as np
from contextlib import ExitStack
import concourse.bass as bass
import concourse.mybir as mybir
from concourse.bass_utils import run_bass_kernel_spmd

F32 = mybir.dt.float32
BF16 = mybir.dt.bfloat16
I32 = mybir.dt.int32
ALU = mybir.AluOpType
AF = mybir.ActivationFunctionType

D = 1024
RW = 512
C_MAIN = 6400
NG = 37
GW = 4096
ALPHA = 4.0 ** 0.25
CDEC = float(np.exp(-0.5))
LN_EPS = 1e-5
GN_EPS = 64e-5
CONVW = 31
ENG_NAMES = ("pe", "act", "dve", "pool", "sp")


class Sched:
    def __init__(self, nc, sems, dma_sems):
        self.nc = nc
        self.sem = dict(zip(ENG_NAMES, sems))
        self.cnt = {e: 0 for e in ENG_NAMES}
        self.dma_pool = {q: list(v) for q, v in dma_sems.items()}
        self.dma_sems = [s for q in self.dma_pool for s in self.dma_pool[q]]
        self.dma_idx = {}
        i = 0
        for q in self.dma_pool:
            self.dma_idx[q] = list(range(i, i + len(self.dma_pool[q])))
            i += len(self.dma_pool[q])
        self.dma_cnt = [0] * len(self.dma_sems)
        self.dma_rr = {q: 0 for q in self.dma_pool}
        self.streams = {e: [] for e in ENG_NAMES}
        self.seen = {e: {} for e in ENG_NAMES}
        self.last_w = {}
        self.readers = {}
        self.nops = 0
        self.fence_dma = False

    def _deps(self, reads, writes):
        toks = []
        for k in reads:
            t = self.last_w.get(k)
            if t is not None:
                toks.append(t)
        for k in writes:
            t = self.last_w.get(k)
            if t is not None:
                toks.append(t)
            toks.extend(self.readers.get(k, ()))
        return toks

    def _waits_for(self, e, toks, skip_same_pe=True):
        need = {}
        for (sk, v) in toks:
            if sk == e and e == "pe" and skip_same_pe:
                continue
            if self.seen[e].get(sk, 0) >= v:
                continue
            if need.get(sk, 0) < v:
                need[sk] = v
        for sk, v in need.items():
            self.seen[e][sk] = v
        return list(need.items())

    def _commit(self, tok, reads, writes):
        for k in reads:
            self.readers.setdefault(k, []).append(tok)
        for k in writes:
            self.last_w[k] = tok
            self.readers[k] = []

    @staticmethod
    def _freeze(fn):
        if getattr(fn, "__closure__", None) is None:
            return fn
        cells = []
        for c in fn.__closure__:
            try:
                v = c.cell_contents
                if isinstance(v, types.FunctionType):
                    v = Sched._freeze(v)
                cells.append(types.CellType(v))
            except ValueError:
                cells.append(c)
        g = types.FunctionType(fn.__code__, fn.__globals__, fn.__name__, fn.__defaults__, tuple(cells))
        g.__kwdefaults__ = fn.__kwdefaults__
        return g

    def op(self, e, fn, reads=(), writes=()):
        fn = self._freeze(fn)
        reads = list(reads); writes = list(writes)
        toks = self._deps(reads, writes)
        waits = self._waits_for(e, toks)
        self.cnt[e] += 1
        tok = (e, self.cnt[e])
        self.streams[e].append((waits, fn, ("eng", e)))
        self._commit(tok, reads, writes)
        self.nops += 1
        return tok

    def dma(self, q, out, in_, reads=(), writes=(), **kw):
        toks = self._deps(reads, writes)
        i = self.dma_idx[q][self.dma_rr[q]]
        self.dma_rr[q] = (self.dma_rr[q] + 1) % len(self.dma_idx[q])
        sk = ("d", i)
        if self.dma_cnt[i] > 0:
            toks.append((sk, self.dma_cnt[i]))
        waits = self._waits_for(q, toks)
        self.dma_cnt[i] += 16
        tok = (sk, self.dma_cnt[i])

        def fn(eng, out=out, in_=in_, kw=kw):
            return eng.dma_start(out=out, in_=in_, **kw)
        self.streams[q].append((waits, fn, ("dma", i)))
        self._commit(tok, reads, writes)
        return tok

    def fence(self, engines=("pe", "act", "dve", "pool")):
        for e in engines:
            toks = [(en, self.cnt[en]) for en in engines if self.cnt[en] > 0]
            if self.fence_dma:
                toks += [(("d", i), v) for i, v in enumerate(self.dma_cnt) if v > 0]
            waits = self._waits_for(e, toks, skip_same_pe=False)
            if waits:
                self.streams[e].append((waits, None, None))

    def wait_all(self, e):
        toks = [(en, self.cnt[en]) for en in ENG_NAMES if self.cnt[en] > 0 and en != e]
        toks += [(("d", i), v) for i, v in enumerate(self.dma_cnt) if v > 0]
        waits = self._waits_for(e, toks)
        self.streams[e].append((waits, None, None))

    def _semh(self, sk):
        if isinstance(sk, tuple):
            return self.dma_sems[sk[1]]
        return self.sem[sk]

    def emit(self, block):
        eng_of = {"pe": self.nc.tensor, "act": self.nc.scalar, "dve": self.nc.vector,
                  "pool": self.nc.gpsimd, "sp": self.nc.sync}

        def mk(e):
            def body(eng):
                for waits, fn, sig in self.streams[e]:
                    for sk, v in waits:
                        eng.wait_ge(self._semh(sk), v)
                    if fn is None:
                        continue
                    ins = fn(eng)
                    if sig[0] == "eng":
                        ins.then_inc(self.sem[e], 1)
                    else:
                        ins.then_inc(self.dma_sems[sig[1]], 16)
            return body
        block.tensor(mk("pe"))
        block.scalar(mk("act"))
        block.vector(mk("dve"))
        block.gpsimd(mk("pool"))
        block.sync(mk("sp"))


class Rec:
    def __init__(self):
        self.items = []

    def op(self, e, fn, reads=(), writes=()):
        self.items.append(("op", e, Sched._freeze(fn), list(reads), list(writes)))

    def dma(self, q, out, in_, reads=(), writes=(), **kw):
        self.items.append(("dma", q, out, in_, list(reads), list(writes), kw))


class RecK(Rec):
    def __init__(self, si, setkeys):
        super().__init__()
        self.si, self.setkeys = si, set(setkeys)

    def _m(self, ks):
        return [f"{k}_{self.si}" if k in self.setkeys else k for k in ks]

    def op(self, e, fn, reads=(), writes=()):
        super().op(e, fn, self._m(reads), self._m(writes))


def replay_merged(S, recs):
    pos = [0] * len(recs)
    tot = [max(1, len(r.items)) for r in recs]
    while True:
        best, bf = None, None
        for i, r in enumerate(recs):
            if pos[i] < len(r.items):
                f = pos[i] / tot[i]
                if bf is None or f < bf:
                    best, bf = i, f
        if best is None:
            break
        it = recs[best].items[pos[best]]
        pos[best] += 1
        if it[0] == "op":
            S.op(it[1], it[2], reads=it[3], writes=it[4])
        else:
            S.dma(it[1], it[2], it[3], reads=it[4], writes=it[5], **it[6])


def vec_layout():
    off = {}
    r = 0

    def add(name, n):
        nonlocal r
        off[name] = r
        r += n
    add("c", 8)
    for l in range(2):
        for nm, n in (("ada_b", 48), ("mu", 14), ("w0", 4), ("a0", 4), ("kk", 4), ("ka", 4), ("rk", 4),
                      ("gng", 4), ("gnb", 4), ("cvb", 4), ("cvg", 4), ("cvbb", 4), ("plsc", 4),
                      ("lnmg", 8), ("lnmb", 8), ("lnfg", 8), ("lnfb", 8), ("cvw", 124)):
            add(f"{nm}{l}", n)
    add("v0", 4)
    add("muv", 1)
    return off, r


VOFF, NVROWS = vec_layout()
NVB = (NVROWS + 127) // 128


def pack_vecs(inp, b):
    P = np.zeros((NVB * 128, 128), np.float32)

    def put(name, arr):
        a = np.asarray(arr, np.float32).reshape(-1)
        n = (a.size + 127) // 128
        buf = np.zeros(n * 128, np.float32)
        buf[:a.size] = a
        P[VOFF[name]:VOFF[name] + n] = buf.reshape(n, 128)
    put("c", inp["c"][b])
    for l in range(2):
        put(f"ada_b{l}", inp["ada_b"][l]); put(f"mu{l}", inp["shift_mu"][l])
        put(f"w0{l}", inp["rw_w0"][l]); put(f"a0{l}", inp["rw_a0"][l])
        put(f"kk{l}", inp["rw_kk"][l]); put(f"ka{l}", inp["rw_ka"][l]); put(f"rk{l}", inp["rw_rk"][l])
        put(f"gng{l}", inp["rw_gn_g"][l]); put(f"gnb{l}", inp["rw_gn_b"][l])
        put(f"cvb{l}", inp["cv_b"][l]); put(f"cvg{l}", inp["cv_ln_g"][l]); put(f"cvbb{l}", inp["cv_ln_b"][l])
        put(f"plsc{l}", inp["pl_scale"][l])
        put(f"lnmg{l}", inp["ln_m_g"][l]); put(f"lnmb{l}", inp["ln_m_b"][l])
        put(f"lnfg{l}", inp["ln_f_g"][l]); put(f"lnfb{l}", inp["ln_f_b"][l])
        put(f"cvw{l}", inp["cv_w"][l])
    put("v0", inp["rw_v0"][0])
    put("muv", inp["shift_mu_vres"][0])
    return P


WEIGHT_NAMES = ("ada_w", "w_in", "w_in_vres", "rw_w2", "rw_a2", "rw_g2", "rw_v2", "rw_wo", "cv_wo",
                "pl_wo", "pl_w", "w_out", "mlp_w1", "mlp_w2")
WEIGHT_SHAPES = {"ada_w": [2, 1024, 6144], "w_in": [2, 1024, 6400], "w_in_vres": [1, 1024, 32],
                 "rw_w2": [2, 64, 512], "rw_a2": [2, 64, 512], "rw_g2": [2, 128, 512], "rw_v2": [1, 32, 512],
                 "rw_wo": [2, 512, 1024], "cv_wo": [2, 512, 1024], "pl_wo": [2, 512, 1024],
                 "pl_w": [2, 4, 128, 128], "w_out": [2, 1024, 1024], "mlp_w1": [2, 1024, 4096],
                 "mlp_w2": [2, 4096, 1024]}


def build_nc(S_TOK, T=256, NL=2, debug=None, REC_BF16=True, NEU_BF16=True, MM2_F32=False, ST_F32=False, KEEP_FENCES=False):
    assert S_TOK % T == 0 and T % 128 == 0 and T <= 512
    NT = S_TOK // T
    NCH = T // 64
    TB = T // 128
    nc = bass.Bass("TRN2", target_bir_lowering=False)
    dr = {}
    dr["x"] = nc.dram_tensor("x", [S_TOK, D], F32, kind="ExternalInput").ap()
    dr["vecs"] = nc.dram_tensor("vecs", [NVB * 128, 128], F32, kind="ExternalInput").ap()
    for nm in WEIGHT_NAMES:
        dr[nm] = nc.dram_tensor(nm, WEIGHT_SHAPES[nm], F32, kind="ExternalInput").ap()
    out = nc.dram_tensor("out", [S_TOK, D], F32, kind="ExternalOutput").ap()
    wsc = nc.dram_tensor("wsc", [2, NG, 128, GW], BF16, kind="Internal").ap()
    dbg = {}
    if debug:
        for nm, shp in debug.items():
            dbg[nm] = nc.dram_tensor(nm, list(shp), F32, kind="ExternalOutput").ap()

    with ExitStack() as es:
        def sb(name, shape, dt=F32):
            return es.enter_context(nc.sbuf_tensor(name, list(shape), dt))

        def pst(name, shape, dt=F32):
            return es.enter_context(nc.psum_tensor(name, list(shape), dt))

        X = sb("X", [128, 8, T])
        H = sb("H", [128, 8, T], BF16)
        WS = sb("WS", [128, 4, GW], BF16)
        VF = sb("VF", [128, 4, T])
        VC = sb("VC", [128, NVB * 128])
        PK = sb("PK", [128, NVB, 128])
        ADA = sb("ADA", [128, 2, 48])
        DC = sb("DC", [128, 2, 128])
        IDENT = sb("IDENT", [128, 128])
        I2 = sb("I2", [128, 64])
        MSU = sb("MSU", [128, 128]); MIU = sb("MIU", [128, 128]); MSL = sb("MSL", [128, 128])
        OBD = sb("OBD", [128, 128]); OBD64 = sb("OBD64", [128, 128])
        O512 = sb("O512", [128, 128]); O1024 = sb("O1024", [128, 128])
        RESETM = sb("RESETM", [128, T])
        EPS = sb("EPS", [128, 4])
        RC16 = sb("RC16", [128, 4, 16])
        IOTI = sb("IOTI", [128, 16], I32)
        W2P = sb("W2P", [128, 2, 512], BF16); A2P = sb("A2P", [128, 2, 512], BF16)
        G2 = sb("G2", [128, 2, 512], BF16); V2 = sb("V2", [32, 512], BF16)
        PLW = sb("PLW", [128, 2, 4, 128], BF16)
        CARRY = sb("CARRY", [128, 2, 16])
        HALO = sb("HALO", [128, 2, 4, 30])
        PHALO = sb("PHALO", [128, 2, 4, 16])
        ST = sb("ST", [128, 2, 4, 64])
        CONDT = sb("CONDT", [128, 8])
        XIN = sb("XIN", [128, TB, D])
        YBR = sb("YBR", [128, 3, 4, T], BF16)
        AW = 78 * T + 9600 + 1024
        AR = sb("AR", [128, AW])

        class Alloc:
            def __init__(self, base=0):
                self.o = base

            def f(self, n, shape=None):
                ap = AR[:, self.o:self.o + n]
                self.o += n
                assert self.o <= AW, (self.o, AW)
                return ap

            def t3(self, a, b):
                return self.f(a * b).rearrange("p (a b) -> p a b", a=a)

            def b3(self, a, b):
                n = (a * b + 1) // 2
                return self.f(n).bitcast(BF16).rearrange("p (a b) -> p a b", a=a)

        A = Alloc()
        ZR = A.t3(4, T); ZK = A.t3(4, T); ZV = A.t3(4, T); YF = A.t3(4, T)
        TA = A.f(T); TBm = A.f(T); TC = A.f(T); TD = A.f(T)
        mark_dead1 = A.o
        SW = A.t3(4, T); LC = A.t3(4, T); PINV = A.t3(4, T); AH = A.t3(4, T); KKN = A.t3(4, T)
        mark_dead1_end = A.o
        RAWS = [A.f(T), A.f(T), A.f(T)]; ZTMP = A.f(T)
        raw_rr = [0]
        T12 = A.b3(1, T)[:, 0, :]; LG = A.b3(1, T)[:, 0, :]; LV = A.b3(1, T)[:, 0, :]
        rt3 = A.b3 if REC_BF16 else A.t3
        nt3 = A.b3 if (REC_BF16 and NEU_BF16) else A.t3
        BDA = rt3(4, 128); BDB = rt3(4, 128); BDK = rt3(4, 128); BDR = rt3(4, 128); BDV = rt3(4, 128)
        BDA2 = nt3(4, 128); BDB2 = nt3(4, 128)
        MN = [nt3(4, 128), nt3(4, 128)]; MNT = [nt3(4, 128), nt3(4, 128)]
        MKA = rt3(4, 128); MBR = rt3(4, 128); MKR = rt3(4, 128); GM = nt3(4, 128); GMr = rt3(4, 128)
        BDBT = rt3(4, 128); BDKT = rt3(4, 128)
        VT = rt3(4, 64); XTt = (A.t3 if MM2_F32 else rt3)(4, 64); UT = rt3(4, 64)
        YTBD = rt3(4, 128)
        STb = rt3(4, 64); TMPS = A.t3(4, 64)
        PC = A.t3(4, 8)
        SET0 = (BDA, BDB, BDK, BDR, BDV, MKA, MBR, MKR, GM, GMr, BDBT, BDKT, VT, BDA2, BDB2)
        SET1 = (rt3(4, 128), rt3(4, 128), rt3(4, 128), rt3(4, 128), rt3(4, 128), rt3(4, 128), rt3(4, 128), rt3(4, 128),
                nt3(4, 128), rt3(4, 128), rt3(4, 128), rt3(4, 128), rt3(4, 64), nt3(4, 128), nt3(4, 128))
        SETS = [SET0, SET1]
        SETKEYS = ("BDA", "BDB", "BDK", "BDR", "BDV", "MKA", "MBR", "MKR", "GM", "GMr", "BDBT", "BDKT", "VT", "BDA2", "BDB2")
        arena_mixer_end = A.o
        B2 = Alloc(arena_mixer_end)
        HGLU = B2.t3(4, T + 30); ACC = B2.t3(4, T); CVV = B2.t3(4, T)
        PP = B2.t3(4, T + 16); SA = B2.f(T + 16); SBm = B2.f(T + 16)
        PLB = B2.b3(1, T)[:, 0, :]
        TA2 = B2.f(T); TB2 = B2.f(T); TC2 = B2.f(T); TD2 = B2.f(T)
        usq_off = B2.o
        TCq = B2.t3(4, T); TDq = B2.t3(4, T)
        USQ = AR[:, usq_off:usq_off + 8 * T].rearrange("p (a b) -> p a b", a=8)
        RESET4 = B2.f(4 * T)
        assert B2.o <= AW, (B2.o, AW)
        LT = {"TA": TA, "TB": TBm, "TC": TC, "TD": TD, "kTA": "TA", "kTB": "TB", "kTC": "TC", "kTD": "TD"}
        LT1 = dict(LT)
        LT2 = {"TA": TA2, "TB": TB2, "TC": TC2, "TD": TD2, "kTA": "TA2", "kTB": "TB2", "kTC": "TC2", "kTD": "TD2"}
        B3 = Alloc(0)
        MERG = B3.t3(4, T); MERGED = B3.b3(8, T); SIG = RAWS[0]; TMG = RAWS[1]
        assert B3.o <= 12 * T
        B4 = Alloc(mark_dead1)
        U = B4.t3(8, T); RTMP = B4.f(T); RTMP2 = B4.f(T)
        assert B4.o <= mark_dead1_end, (B4.o, AW)
        H1 = Alloc(0).b3(32, T)
        B5 = Alloc(0)
        AWS = [B5.t3(8, 512), B5.t3(8, 512)]
        assert B5.o <= AW

        PSB = [pst(f"PS{i}", [128, 512]) for i in range(8)]
        ps_rr = [0]

        bank_pools = {"ALL": list(range(8)), "A": [0, 1, 2, 3, 4, 5], "B": [6, 7], "AP": [0, 1, 2, 3], "AC": [4, 5]}
        bank_rr = {"ALL": 0, "A": 0, "B": 0, "AP": 0, "AC": 0}
        cur_pool = ["ALL"]

        def bank():
            pl = bank_pools[cur_pool[0]]
            i = pl[bank_rr[cur_pool[0]] % len(pl)]
            bank_rr[cur_pool[0]] += 1
            return i, PSB[i], f"PS{i}"

        sems = [es.enter_context(nc.semaphore(f"s_{e}")) for e in ENG_NAMES]
        dsems = {"sp": [es.enter_context(nc.semaphore(f"dsp{i}")) for i in range(8)],
                 "pool": [es.enter_context(nc.semaphore(f"dpl{i}")) for i in range(4)]}
        block = es.enter_context(nc.Block())
        S = Sched(nc, sems, dsems)
        S.fence_dma = bool(debug)

        def vc(name, col=0):
            c0 = VOFF[name] + col
            return VC[:, c0:c0 + 1]

        def conv_dma(l, g, src_ap, kc, n, col0=0, ncols=None):
            ncols = n if ncols is None else ncols
            dst = wsc[l, g][:, 0:kc * n].rearrange("p (kc n) -> p kc n", kc=kc)[:, :, col0:col0 + ncols]
            S.dma("pool", dst, src_ap, writes=[f"wsc{l}_{g}"])

        def group_src(l, g):
            w_in = dr["w_in"][l].rearrange("(kc p) n -> p kc n", p=128)
            if g < 6:
                return [(w_in[:, :, g * 512:(g + 1) * 512], 8, 512, 0, 512)]
            if g == 6:
                r = [(w_in[:, :, 3072:3328], 8, 288, 0, 256)]
                if l >= 1:
                    r.append((dr["w_in_vres"][l - 1].rearrange("(kc p) n -> p kc n", p=128), 8, 288, 256, 32))
                return r
            if g < 13:
                i = g - 7
                b, q = i // 2, i % 2
                c0 = 3328 + (8 * b + 4 * q) * 128
                return [(w_in[:, :, c0:c0 + 512], 8, 512, 0, 512)]
            if g < 19:
                i = g - 13
                b, q = i // 2, i % 2
                w = dr[("rw_wo", "cv_wo", "pl_wo")[b]][l].rearrange("(kc p) n -> p kc n", p=128)
                return [(w[:, :, q * 512:(q + 1) * 512], 4, 512, 0, 512)]
            if g < 21:
                q = g - 19
                w = dr["w_out"][l].rearrange("(kc p) n -> p kc n", p=128)
                return [(w[:, :, q * 512:(q + 1) * 512], 8, 512, 0, 512)]
            if g < 29:
                j = g - 21
                w = dr["mlp_w1"][l].rearrange("(kc p) n -> p kc n", p=128)
                return [(w[:, :, j * 512:(j + 1) * 512], 8, 512, 0, 512)]
            m = g - 29
            w = dr["mlp_w2"][l].rearrange("(kc p) n -> p kc n", p=128)
            return [(w[:, :, m * 128:(m + 1) * 128], 32, 128, 0, 128)]

        for l in range(NL):
            for g in range(NG):
                for (src, kc, n, c0, ncol) in group_src(l, g):
                    conv_dma(l, g, src, kc, n, c0, ncol)

        S.op("pool", lambda e: e.memset(IDENT[:], 0.0), writes=["IDENT"])
        S.op("pool", lambda e: e.affine_select(out=IDENT[:], in_=IDENT[:], pattern=[[-1, 128]], compare_op=ALU.not_equal,
                                               fill=1.0, base=0, channel_multiplier=1), reads=["IDENT"], writes=["IDENT"])
        S.op("pool", lambda e: e.tensor_tensor(out=I2[:], in0=IDENT[:, 0:64], in1=IDENT[:, 64:128], op=ALU.add),
             reads=["IDENT"], writes=["I2"])
        IDR = sb("IDR", [128, 128], BF16 if REC_BF16 else F32)
        I2R = sb("I2R", [128, 64], BF16 if REC_BF16 else F32)
        S.op("pool", lambda e: e.tensor_copy(out=IDR[:], in_=IDENT[:]), reads=["IDENT"], writes=["IDR"])
        S.op("pool", lambda e: e.tensor_copy(out=I2R[:], in_=I2[:]), reads=["I2"], writes=["I2R"])

        def tri(M, key, cmp_op, sgn=1):
            S.op("pool", lambda e: e.memset(M[:], 1.0), writes=[key])
            S.op("pool", lambda e: e.affine_select(out=M[:], in_=M[:], pattern=[[sgn, 128]], compare_op=cmp_op,
                                                   fill=0.0, base=0, channel_multiplier=-sgn), reads=[key], writes=[key])
            S.op("pool", lambda e: e.memset(M[0:64, 64:128], 0.0), reads=[key], writes=[key])
            S.op("pool", lambda e: e.memset(M[64:128, 0:64], 0.0), reads=[key], writes=[key])
        tri(MSU, "MSU", ALU.is_gt)
        tri(MIU, "MIU", ALU.is_ge)
        tri(MSL, "MSL", ALU.is_gt, -1)
        for M, key, val in ((OBD, "OBD", 1.0), (OBD64, "OBD64", 1.0 / 64)):
            S.op("pool", lambda e, M=M, val=val: e.memset(M[:], val), writes=[key])
            S.op("pool", lambda e, M=M: e.memset(M[0:64, 64:128], 0.0), reads=[key], writes=[key])
            S.op("pool", lambda e, M=M: e.memset(M[64:128, 0:64], 0.0), reads=[key], writes=[key])
        S.op("pool", lambda e: e.memset(O512[:], 1.0 / 512), writes=["O512"])
        S.op("pool", lambda e: e.memset(O1024[:], 1.0 / 1024), writes=["O1024"])
        S.op("pool", lambda e: e.memset(RESETM[:], 1.0), writes=["RESETM"])
        S.op("pool", lambda e: e.memset(RESETM[:].rearrange("p (c j) -> p c j", j=64)[:, :, 0:1], 0.0),
             reads=["RESETM"], writes=["RESETM"])
        S.op("pool", lambda e: e.memset(RESET4, 1.0), writes=["RESET4"])
        S.op("pool", lambda e: e.memset(RESET4.rearrange("p (c j) -> p c j", j=64)[:, :, 0:1], 0.0), reads=["RESET4"], writes=["RESET4"])
        for i, v in enumerate((GN_EPS, LN_EPS / (ALPHA * ALPHA), LN_EPS, 0.0)):
            S.op("pool", lambda e, i=i, v=v: e.memset(EPS[:, i:i + 1], float(v)), reads=["EPS"], writes=["EPS"])
        S.op("pool", lambda e: e.iota(IOTI[:], pattern=[[1, 16]], base=1, channel_multiplier=0), writes=["IOTI"])
        for g in range(4):
            S.op("pool", lambda e, g=g: e.tensor_copy(out=RC16[:, g, :], in_=IOTI[:]), reads=["IOTI", "RC16"], writes=["RC16"])
            S.op("pool", lambda e, g=g: e.tensor_scalar(out=RC16[:, g, :], in0=RC16[:, g, :], scalar1=float(2 ** (g + 1)), scalar2=None,
                                                      op0=ALU.min), reads=["RC16"], writes=["RC16"])
        S.op("dve", lambda e: e.reciprocal(out=RC16[:], in_=RC16[:]), reads=["RC16"], writes=["RC16"])
        for nm, Tn in (("CARRY", CARRY), ("HALO", HALO), ("PHALO", PHALO), ("ST", ST)):
            S.op("pool", lambda e, Tn=Tn: e.memset(Tn[:], 0.0), writes=[nm])
        for nm, Tn in (("BDA_0", BDA), ("BDB_0", BDB), ("BDK_0", BDK), ("BDR_0", BDR), ("BDV_0", BDV), ("YTBD", YTBD), ("BDA2_0", BDA2), ("BDB2_0", BDB2), ("STb", STb)):
            S.op("pool", lambda e, Tn=Tn: e.memset(Tn, 0.0), writes=[nm])
        for i_, nm in ((0, "BDA"), (1, "BDB"), (2, "BDK"), (3, "BDR"), (4, "BDV"), (13, "BDA2"), (14, "BDB2")):
            S.op("pool", lambda e, Tn=SET1[i_]: e.memset(Tn, 0.0), writes=[nm + "_1"])
        S.op("pool", lambda e: e.memset(W2P[:], 0.0), writes=["W2P"])
        S.op("pool", lambda e: e.memset(A2P[:], 0.0), writes=["A2P"])
        for l in range(NL):
            S.dma("pool", W2P[0:64, l, :], dr["rw_w2"][l], reads=["W2P"], writes=["W2P"])
            S.dma("pool", A2P[64:128, l, :], dr["rw_a2"][l], reads=["A2P"], writes=["A2P"])
            S.dma("pool", G2[:, l, :], dr["rw_g2"][l], writes=["G2"])
            S.dma("pool", PLW[:, l, :, :], dr["pl_w"][l].rearrange("g c d -> c g d"), writes=["PLW"])
        if NL > 1:
            S.dma("pool", V2[:], dr["rw_v2"][0], writes=["V2"])
        S.dma("sp", PK[:], dr["vecs"].rearrange("(b p) n -> p b n", p=128), writes=["PK"])
        for b in range(NVB):
            bi, pb, pk = bank()
            S.op("pe", lambda e, b=b, pb=pb: e.transpose(pb[:, 0:128], PK[:, b, :], IDENT[:]),
                 reads=["PK", "IDENT"], writes=[pk])
            S.op("act", lambda e, b=b, pb=pb: e.copy(out=VC[:, b * 128:(b + 1) * 128], in_=pb[:, 0:128]),
                 reads=[pk], writes=["VC"])
        S.op("act", lambda e: e.activation(out=CONDT[:], in_=VC[:, VOFF["c"]:VOFF["c"] + 8], func=AF.Silu),
             reads=["VC"], writes=["CONDT"])
        for l in range(NL):
            bi, pb, pk = bank()
            for gq in range(12):
                slot = gq % 2
                S.dma("sp", AWS[slot], dr["ada_w"][l].rearrange("(kc p) n -> p kc n", p=128)[:, :, gq * 512:(gq + 1) * 512],
                      writes=[f"AWS{slot}"])
                for mi in range(4):
                    j = gq * 4 + mi

                    def fn(e, slot=slot, mi=mi, j=j, pb=pb):
                        for kc in range(8):
                            ins = e.matmul(pb[:, j:j + 1], lhsT=AWS[slot][:, kc, mi * 128:(mi + 1) * 128],
                                           rhs=CONDT[:, kc:kc + 1], start=(kc == 0), stop=(kc == 7))
                        return ins
                    S.op("pe", fn, reads=[f"AWS{slot}", "CONDT"], writes=[pk])
            S.op("dve", lambda e, l=l, pb=pb: e.tensor_tensor(out=ADA[:, l, :], in0=pb[:, 0:48],
                                                             in1=VC[:, VOFF[f"ada_b{l}"]:VOFF[f"ada_b{l}"] + 48], op=ALU.add),
                 reads=[pk, "VC"], writes=["ADA"])
        for l in range(NL):
            S.op("dve", lambda e, l=l: e.tensor_scalar(out=DC[:, l, 0:8], in0=ADA[:, l, 8:16], scalar1=1.0, scalar2=None, op0=ALU.add),
                 reads=["ADA"], writes=["DC"])
            S.op("dve", lambda e, l=l: e.tensor_scalar(out=DC[:, l, 8:16], in0=ADA[:, l, 32:40], scalar1=1.0, scalar2=None, op0=ALU.add),
                 reads=["ADA", "DC"], writes=["DC"])
            S.op("dve", lambda e, l=l: e.tensor_scalar(out=DC[:, l, 16:24], in0=ADA[:, l, 16:24], scalar1=1.0 / ALPHA, scalar2=None, op0=ALU.mult),
                 reads=["ADA", "DC"], writes=["DC"])
            S.op("dve", lambda e, l=l: e.tensor_scalar(out=DC[:, l, 24:32], in0=ADA[:, l, 40:48], scalar1=1.0 / ALPHA, scalar2=None, op0=ALU.mult),
                 reads=["ADA", "DC"], writes=["DC"])
            m0 = VOFF[f"mu{l}"]
            S.op("dve", lambda e, l=l, m0=m0: e.tensor_scalar(out=DC[:, l, 32:46], in0=VC[:, m0:m0 + 14], scalar1=-1.0, scalar2=1.0,
                                                           op0=ALU.mult, op1=ALU.add), reads=["VC", "DC"], writes=["DC"])
            mv0 = VOFF["muv"]
            S.op("dve", lambda e, l=l, mv0=mv0: e.tensor_scalar(out=DC[:, l, 46:47], in0=VC[:, mv0:mv0 + 1], scalar1=-1.0, scalar2=1.0,
                                                             op0=ALU.mult, op1=ALU.add), reads=["VC", "DC"], writes=["DC"])
            gm0, bm0 = VOFF[f"lnmg{l}"], VOFF[f"lnmb{l}"]
            S.op("dve", lambda e, l=l, gm0=gm0: e.tensor_tensor(out=DC[:, l, 64:72], in0=VC[:, gm0:gm0 + 8], in1=DC[:, l, 8:16], op=ALU.mult),
                 reads=["VC", "DC"], writes=["DC"])
            S.op("dve", lambda e, l=l, bm0=bm0: e.tensor_tensor(out=DC[:, l, 72:80], in0=VC[:, bm0:bm0 + 8], in1=DC[:, l, 8:16], op=ALU.mult),
                 reads=["VC", "DC"], writes=["DC"])
            S.op("dve", lambda e, l=l: e.tensor_tensor(out=DC[:, l, 72:80], in0=DC[:, l, 72:80], in1=ADA[:, l, 24:32], op=ALU.add),
                 reads=["ADA", "DC"], writes=["DC"])
            if l >= 1:
                gf0, bf0 = VOFF[f"lnfg{l - 1}"], VOFF[f"lnfb{l - 1}"]
                S.op("dve", lambda e, l=l, gf0=gf0: e.tensor_tensor(out=DC[:, l, 80:88], in0=VC[:, gf0:gf0 + 8], in1=DC[:, l, 0:8], op=ALU.mult),
                     reads=["VC", "DC"], writes=["DC"])
                S.op("dve", lambda e, l=l, bf0=bf0: e.tensor_tensor(out=DC[:, l, 88:96], in0=VC[:, bf0:bf0 + 8], in1=DC[:, l, 0:8], op=ALU.mult),
                     reads=["VC", "DC"], writes=["DC"])
                S.op("dve", lambda e, l=l: e.tensor_tensor(out=DC[:, l, 88:96], in0=DC[:, l, 88:96], in1=ADA[:, l, 0:8], op=ALU.add),
                     reads=["ADA", "DC"], writes=["DC"])
            k0 = VOFF[f"ka{l}"]
            S.op("dve", lambda e, l=l, k0=k0: e.tensor_scalar(out=DC[:, l, 48:52], in0=VC[:, k0:k0 + 4], scalar1=-1.0, scalar2=1.0,
                                                           op0=ALU.mult, op1=ALU.add), reads=["VC", "DC"], writes=["DC"])
        S.fence()

        ws_rr = [0]

        def wload(l, g):
            s = ws_rr[0]
            ws_rr[0] = (s + 1) % 4
            if g == 6:
                kc, n, nv = 8, 288, (288 if l >= 1 else 256)
            elif 13 <= g < 19:
                kc, n, nv = 4, 512, 512
            elif g >= 29:
                kc, n, nv = 32, 128, 128
            else:
                kc, n, nv = 8, 512, 512
            dst = WS[:, s, 0:kc * n].rearrange("p (kc n) -> p kc n", kc=kc)[:, :, 0:nv]
            src = wsc[l, g][:, 0:kc * n].rearrange("p (kc n) -> p kc n", kc=kc)[:, :, 0:nv]
            S.dma("sp", dst, src, reads=[f"wsc{l}_{g}"], writes=[f"WS{s}"])
            return s

        def wview(s, kc, n):
            return WS[:, s, 0:kc * n].rearrange("p (kc n) -> p kc n", kc=kc)

        def dump(name, ap, keys, idx=None):
            if name in dbg:
                dst = dbg[name] if idx is None else dbg[name][idx]
                S.dma("sp", dst, ap, reads=keys)

        def ln_stats(srcs, src_keys, sq_eng_out, ONESM, eps_col):
            n = len(srcs)
            tA, tB, tC, tD = LT["TA"], LT["TB"], LT["TC"], LT["TD"]
            kA, kB, kC, kD = LT["kTA"], LT["kTB"], LT["kTC"], LT["kTD"]
            bi1, pb1, pk1 = bank()
            bi2, pb2, pk2 = bank()

            def fm(e):
                for i, s_ in enumerate(srcs):
                    ins = e.matmul(pb1[:, 0:T], lhsT=ONESM[:], rhs=s_, start=(i == 0), stop=(i == n - 1))
                return ins
            S.op("pe", fm, reads=src_keys, writes=[pk1])
            for i, s_ in enumerate(srcs):
                S.op("act", lambda e, s_=s_: e.activation(out=tC, in_=s_, func=AF.Square), reads=[src_keys[i]], writes=[kC])
                S.op("pe", lambda e, i=i: e.matmul(pb2[:, 0:T], lhsT=ONESM[:], rhs=tC, start=(i == 0), stop=(i == n - 1)),
                     reads=[kC], writes=[pk2])
            S.op("act", lambda e: e.copy(out=tA, in_=pb1[:, 0:T]), reads=[pk1], writes=[kA])
            S.op("act", lambda e: e.activation(out=tD, in_=pb1[:, 0:T], func=AF.Square), reads=[pk1], writes=[kD])
            S.op("dve", lambda e: e.tensor_tensor(out=tB, in0=pb2[:, 0:T], in1=tD, op=ALU.subtract), reads=[pk2, kD], writes=[kB])
            S.op("dve", lambda e: e.tensor_scalar(out=tB, in0=tB, scalar1=0.0, scalar2=None, op0=ALU.max), reads=[kB], writes=[kB])
            S.op("act", lambda e: e.activation(out=tB, in_=tB, func=AF.Sqrt, bias=EPS[:, eps_col:eps_col + 1], scale=1.0),
                 reads=[kB, "EPS"], writes=[kB])
            S.op("dve", lambda e: e.reciprocal(out=tB, in_=tB), reads=[kB], writes=[kB])
            return tA, tB

        def modulate(l, which):
            o_sc = 0 if which == "m" else 8
            o_sh = 0 if which == "m" else 24
            for m in range(8):
                S.op("act", lambda e, m=m: e.activation(out=H[:, m, :], in_=X[:, m, :], func=AF.Identity,
                                                        bias=ADA[:, l, o_sh + m:o_sh + m + 1], scale=DC[:, l, o_sc + m:o_sc + m + 1]),
                     reads=[f"X{m}", "ADA", "DC"], writes=[f"H{m}"])

        def residual_ln(l, which, get_ps):
            o_gt = 16 if which == "m" else 24
            gname, bname = (f"lnmg{l}", f"lnmb{l}") if which == "m" else (f"lnfg{l}", f"lnfb{l}")
            for m in range(8):
                pb, pk = get_ps(m)
                S.op("dve", lambda e, m=m, pb=pb: e.scalar_tensor_tensor(out=U[:, m, :], in0=pb[:, 0:T], scalar=DC[:, l, o_gt + m:o_gt + m + 1],
                                                                     in1=X[:, m, :], op0=ALU.mult, op1=ALU.add),
                     reads=[pk, "DC", f"X{m}"], writes=[f"U{m}"])
            mean, rstd = ln_stats([U[:, m, :] for m in range(8)], [f"U{m}" for m in range(8)], None, O1024, 1)
            for m in range(8):
                S.op("dve", lambda e, m=m: e.tensor_tensor(out=U[:, m, :], in0=U[:, m, :], in1=mean, op=ALU.subtract),
                     reads=[f"U{m}", "TA"], writes=[f"U{m}"])
                S.op("pool", lambda e, m=m: e.tensor_tensor(out=U[:, m, :], in0=U[:, m, :], in1=rstd, op=ALU.mult),
                     reads=[f"U{m}", "TB"], writes=[f"U{m}"])
                S.op("dve", lambda e, m=m: e.tensor_scalar(out=X[:, m, :], in0=U[:, m, :], scalar1=vc(gname, m), scalar2=vc(bname, m),
                                                         op0=ALU.mult, op1=ALU.add), reads=[f"U{m}", "VC"], writes=[f"X{m}"])

        for ti in range(NT):
            t0 = ti * T
            S.dma("sp", XIN[:], dr["x"][t0:t0 + T, :].rearrange("(b p) d -> p b d", p=128), writes=["XIN"])
            for fc in range(8):
                bi, pb, pk = bank()

                def ft(e, fc=fc, pb=pb):
                    for tb in range(TB):
                        ins = e.transpose(pb[:, tb * 128:(tb + 1) * 128], XIN[:, tb, fc * 128:(fc + 1) * 128], IDENT[:])
                    return ins
                S.op("pe", ft, reads=["XIN", "IDENT"], writes=[pk])
                S.op("act" if fc % 2 else "dve",
                     (lambda e, fc=fc, pb=pb: e.copy(out=X[:, fc, :], in_=pb[:, 0:T])) if fc % 2 else
                     (lambda e, fc=fc, pb=pb: e.tensor_copy(out=X[:, fc, :], in_=pb[:, 0:T])),
                     reads=[pk], writes=[f"X{fc}"])
            for l in range(NL):
                if l == 0:
                    modulate(l, "m")
                slot_of = {}

                def proj_chunk(j, ncols=128, col_in_group=None):
                    g = j // 4 if j < 24 else 6
                    if g not in slot_of:
                        slot_of[g] = wload(l, g)
                    s = slot_of[g]
                    n = 512 if g < 6 else 288
                    wv = wview(s, 8, n)
                    c0 = (j % 4) * 128 if j < 24 else (j - 24) * 128
                    bi, pb, pk = bank()

                    def fn(e, pb=pb, wv=wv, c0=c0, ncols=ncols):
                        for kc in range(8):
                            ins = e.matmul(pb[0:ncols, 0:T], lhsT=wv[:, kc, c0:c0 + ncols], rhs=H[:, kc, :],
                                           start=(kc == 0), stop=(kc == 7))
                        return ins
                    S.op("pe", fn, reads=[f"WS{s}"] + [f"H{m}" for m in range(8)], writes=[pk])
                    return pb, pk

                def token_shift(pb, pk, cidx, mu_ap, omu_ap, dst, dst_keys, np_=128):
                    k = raw_rr[0]
                    raw_rr[0] = (k + 1) % 3
                    Rk, rk_key = RAWS[k], f"RAW{k}"
                    ck = f"CARRY{l}_{cidx}"
                    S.op("act", lambda e: e.activation(out=Rk[0:np_, 0:T], in_=pb[0:np_, 0:T], func=AF.Identity, scale=omu_ap),
                         reads=[pk, "DC"], writes=[rk_key])
                    S.op("dve", lambda e: e.scalar_tensor_tensor(out=dst[:, 1:T], in0=pb[0:np_, 0:T - 1], scalar=mu_ap, in1=Rk[0:np_, 1:T],
                                                                 op0=ALU.mult, op1=ALU.add), reads=[pk, rk_key, "VC"], writes=dst_keys)
                    S.op("dve", lambda e: e.scalar_tensor_tensor(out=dst[:, 0:1], in0=CARRY[0:np_, l, cidx:cidx + 1], scalar=mu_ap, in1=Rk[0:np_, 0:1],
                                                                 op0=ALU.mult, op1=ALU.add), reads=[ck, rk_key, "VC"] + list(dst_keys), writes=dst_keys)
                    S.op("act", lambda e: e.copy(out=CARRY[0:np_, l, cidx:cidx + 1], in_=pb[0:np_, T - 1:T]), reads=[pk, ck], writes=[ck])

                for j in range(26):
                    pb, pk = proj_chunk(j)
                    if j < 12:
                        dstT = (ZR, ZK, ZV)[j // 4]
                        nm = ("ZR", "ZK", "ZV")[j // 4]
                        token_shift(pb, pk, j, vc(f"mu{l}", j), DC[:, l, 32 + j:33 + j], dstT[:, j % 4, :], [f"{nm}{j % 4}"])
                    elif j == 12:
                        token_shift(pb, pk, j, vc(f"mu{l}", j), DC[:, l, 32 + j:33 + j], ZTMP, ["ZTMP"])
                        S.op("act", lambda e: e.activation(out=T12[0:64, :], in_=ZTMP[0:64, :], func=AF.Tanh), reads=["ZTMP"], writes=["T12"])
                        S.op("dve", lambda e: e.tensor_copy(out=T12[64:128, :], in_=ZTMP[64:128, :]), reads=["ZTMP", "T12"], writes=["T12"])
                    elif j == 13:
                        token_shift(pb, pk, j, vc(f"mu{l}", j), DC[:, l, 32 + j:33 + j], ZTMP, ["ZTMP"])
                        S.op("act", lambda e: e.activation(out=LG, in_=ZTMP, func=AF.Sigmoid), reads=["ZTMP"], writes=["LG"])
                    elif j < 18:
                        cc = j - 14
                        S.op("act", lambda e, cc=cc, pb=pb: e.copy(out=CVV[:, cc, :], in_=pb[:, 0:T]), reads=[pk], writes=[f"CVV{cc}"])
                    elif j < 22:
                        cc = j - 18
                        S.op("act", lambda e, pb=pb: e.activation(out=TA, in_=pb[:, 0:T], func=AF.Sigmoid), reads=[pk], writes=["TA"])
                        if cc == 0:
                            S.op("pool", lambda e: e.tensor_copy(out=HGLU[:, :, 0:30], in_=HALO[:, l, :, :]), reads=["HALO"],
                                 writes=[f"HGLU{c_}" for c_ in range(4)])
                        S.op("dve", lambda e, cc=cc: e.tensor_tensor(out=HGLU[:, cc, 30:30 + T], in0=CVV[:, cc, :], in1=TA, op=ALU.mult),
                             reads=[f"CVV{cc}", "TA"], writes=[f"HGLU{cc}"])
                    else:
                        gg = j - 22
                        if gg == 0:
                            S.op("pool", lambda e: e.tensor_copy(out=PP[:, :, 0:16], in_=PHALO[:, l, :, :]), reads=["PHALO"],
                                 writes=[f"PP{c_}" for c_ in range(4)])
                        S.op("act", lambda e, gg=gg, pb=pb: e.copy(out=PP[:, gg, 16:16 + T], in_=pb[:, 0:T]), reads=[pk], writes=[f"PP{gg}"])
                if l >= 1:
                    s = slot_of[6]
                    wv = wview(s, 8, 288)
                    bi, pb, pk = bank()

                    def fnv(e, pb=pb, wv=wv):
                        for kc in range(8):
                            ins = e.matmul(pb[0:32, 0:T], lhsT=wv[:, kc, 256:288], rhs=H[:, kc, :], start=(kc == 0), stop=(kc == 7))
                        return ins
                    S.op("pe", fnv, reads=[f"WS{s}"] + [f"H{m}" for m in range(8)], writes=[pk])
                    token_shift(pb, pk, 14, VC[0:32, VOFF["muv"]:VOFF["muv"] + 1], DC[0:32, l, 46:47], ZTMP[0:32, :], ["ZTMP"], np_=32)
                    S.op("act", lambda e: e.copy(out=LV[0:32, :], in_=ZTMP[0:32, :]), reads=["ZTMP"], writes=["LV"])
                dump(f"z_r{l}", ZR, ["ZR0", "ZR1", "ZR2", "ZR3"], ti)
                dump(f"z_k{l}", ZK, ["ZK0", "ZK1", "ZK2", "ZK3"], ti)
                dump(f"z_v{l}", ZV, ["ZV0", "ZV1", "ZV2", "ZV3"], ti)

                S_main = S
                S = Rec(); recB = S
                cur_pool[0] = "B"; LT.update(LT2)
                for cc in range(4):
                    eng = "dve" if cc % 2 == 0 else "pool"
                    wbase = VOFF[f"cvw{l}"]
                    S.op(eng, lambda e, cc=cc: e.tensor_scalar(out=ACC[:, cc, :], in0=HGLU[:, cc, 0:T], scalar1=VC[:, wbase + cc:wbase + cc + 1],
                                                              scalar2=vc(f"cvb{l}", cc), op0=ALU.mult, op1=ALU.add),
                         reads=[f"HGLU{cc}", "VC"], writes=[f"ACC{cc}"])
                    for jt in range(1, CONVW):
                        if eng == "dve":
                            S.op(eng, lambda e, cc=cc, jt=jt: e.scalar_tensor_tensor(
                                out=ACC[:, cc, :], in0=HGLU[:, cc, jt:jt + T], scalar=VC[:, wbase + jt * 4 + cc:wbase + jt * 4 + cc + 1],
                                in1=ACC[:, cc, :], op0=ALU.mult, op1=ALU.add), reads=[f"HGLU{cc}", "VC", f"ACC{cc}"], writes=[f"ACC{cc}"])
                        else:
                            ct, ck = ((SA, "SA"), (SBm, "SB"))[jt % 2]
                            S.op("act", lambda e, cc=cc, jt=jt, ct=ct: e.activation(
                                out=ct[:, 0:T], in_=HGLU[:, cc, jt:jt + T], func=AF.Identity,
                                scale=VC[:, wbase + jt * 4 + cc:wbase + jt * 4 + cc + 1]), reads=[f"HGLU{cc}", "VC"], writes=[ck])
                            S.op("pool", lambda e, cc=cc, ct=ct: e.tensor_tensor(out=ACC[:, cc, :], in0=ACC[:, cc, :], in1=ct[:, 0:T], op=ALU.add),
                                 reads=[ck, f"ACC{cc}"], writes=[f"ACC{cc}"])
                S.op("pool", lambda e: e.tensor_copy(out=HALO[:, l, :, :], in_=HGLU[:, :, T:T + 30]),
                     reads=[f"HGLU{c_}" for c_ in range(4)], writes=["HALO"])
                dump(f"conv{l}", ACC, [f"ACC{c_}" for c_ in range(4)], ti)
                mean, rstd = ln_stats([ACC[:, cc, :] for cc in range(4)], [f"ACC{cc}" for cc in range(4)], None, O512, 2)
                for cc in range(4):
                    S.op("dve", lambda e, cc=cc: e.tensor_tensor(out=ACC[:, cc, :], in0=ACC[:, cc, :], in1=mean, op=ALU.subtract),
                         reads=[f"ACC{cc}", LT["kTA"]], writes=[f"ACC{cc}"])
                    S.op("pool", lambda e, cc=cc: e.tensor_tensor(out=ACC[:, cc, :], in0=ACC[:, cc, :], in1=rstd, op=ALU.mult),
                         reads=[f"ACC{cc}", LT["kTB"]], writes=[f"ACC{cc}"])
                    S.op("dve", lambda e, cc=cc: e.tensor_scalar(out=ACC[:, cc, :], in0=ACC[:, cc, :], scalar1=vc(f"cvg{l}", cc),
                                                                scalar2=vc(f"cvbb{l}", cc), op0=ALU.mult, op1=ALU.add),
                         reads=[f"ACC{cc}", "VC"], writes=[f"ACC{cc}"])
                    S.op("act", lambda e, cc=cc: e.activation(out=YBR[:, 1, cc, :], in_=ACC[:, cc, :], func=AF.Silu),
                         reads=[f"ACC{cc}"], writes=[f"YBR1_{cc}"])
                for gg in range(4):
                    win = 2 ** (gg + 1)
                    eng = "pool" if gg % 2 == 0 else "dve"
                    src = PP[:, gg, :]
                    cur_key = f"PP{gg}"
                    bufs = [(SA, "SA"), (SBm, "SB")]
                    sh = 1
                    k_ = 0
                    while sh < win:
                        dstb, dkey = bufs[k_ % 2]
                        lo = 2 * sh - 1
                        S.op(eng, lambda e, src=src, dstb=dstb, sh=sh, lo=lo: e.tensor_tensor(
                            out=dstb[:, lo:T + 16], in0=src[:, lo:T + 16], in1=src[:, lo - sh:T + 16 - sh], op=ALU.add),
                            reads=[cur_key], writes=[dkey])
                        src, cur_key = dstb, dkey
                        sh *= 2
                        k_ += 1
                    dstb, dkey = bufs[k_ % 2]
                    S.op("dve", lambda e, src=src, dstb=dstb, gg=gg, win=win: e.scalar_tensor_tensor(
                        out=dstb[:, 16:16 + T], in0=src[:, 16:16 + T], scalar=1.0 / win, in1=PP[:, gg, 16:16 + T],
                        op0=ALU.mult, op1=ALU.subtract), reads=[cur_key, f"PP{gg}"], writes=[dkey])
                    if ti == 0:
                        S.op(eng, lambda e, src=src, gg=gg: e.tensor_tensor(out=src[:, 16:32], in0=src[:, 16:32], in1=RC16[:, gg, :], op=ALU.mult),
                             reads=[cur_key, "RC16", dkey], writes=[cur_key])
                        S.op(eng, lambda e, src=src, dstb=dstb, gg=gg: e.tensor_tensor(out=dstb[:, 16:32], in0=src[:, 16:32], in1=PP[:, gg, 16:32],
                                                                                  op=ALU.subtract), reads=[cur_key, f"PP{gg}", dkey], writes=[dkey])
                    S.op("act", lambda e, dstb=dstb: e.copy(out=PLB[:, :], in_=dstb[:, 16:16 + T]), reads=[dkey], writes=["PLB"])
                    bi, pb, pk = bank()
                    S.op("pe", lambda e, gg=gg, pb=pb: e.matmul(pb[:, 0:T], lhsT=PLW[:, l, gg, :], rhs=PLB[:, :], start=True, stop=True),
                         reads=["PLB", "PLW"], writes=[pk])
                    S.op("act", lambda e, gg=gg, pb=pb: e.activation(out=YBR[:, 2, gg, :], in_=pb[:, 0:T], func=AF.Identity,
                                                                   scale=vc(f"plsc{l}", gg)), reads=[pk, "VC"], writes=[f"YBR2_{gg}"])
                S.op("pool", lambda e: e.tensor_copy(out=PHALO[:, l, :, :], in_=PP[:, :, T:T + 16]),
                     reads=[f"PP{c_}" for c_ in range(4)], writes=["PHALO"])

                S = Rec(); recA = S
                cur_pool[0] = "A"; LT.update(LT1)
                allk = lambda nm: [f"{nm}{p}" for p in range(4)]
                f4T = lambda t3: t3.rearrange("p a b -> p (a b)")
                for p in range(4):
                    bi, pb, pk = bank()
                    S.op("pe", lambda e, p=p, pb=pb: e.matmul(pb[:, 0:T], lhsT=W2P[:, l, p * 128:(p + 1) * 128], rhs=T12, start=True, stop=True),
                         reads=["T12", "W2P"], writes=[pk])
                    S.op("act", lambda e, p=p, pb=pb: e.activation(out=SW[:, p, :], in_=pb[:, 0:T], func=AF.Sigmoid, bias=vc(f"w0{l}", p), scale=1.0),
                         reads=[pk, "VC"], writes=[f"SW{p}"])
                    bi, pb, pk = bank()
                    S.op("pe", lambda e, p=p, pb=pb: e.matmul(pb[:, 0:T], lhsT=A2P[:, l, p * 128:(p + 1) * 128], rhs=T12, start=True, stop=True),
                         reads=["T12", "A2P"], writes=[pk])
                    S.op("act", lambda e, p=p, pb=pb: e.activation(out=AH[:, p, :], in_=pb[:, 0:T], func=AF.Sigmoid, bias=vc(f"a0{l}", p), scale=1.0),
                         reads=[pk, "VC"], writes=[f"AH{p}"])
                    if l >= 1:
                        bi, pb, pk = bank()
                        S.op("pe", lambda e, p=p, pb=pb: e.matmul(pb[:, 0:T], lhsT=V2[:, p * 128:(p + 1) * 128], rhs=LV[0:32, :], start=True, stop=True),
                             reads=["LV", "V2"], writes=[pk])
                        S.op("act", lambda e, p=p, pb=pb: e.activation(out=TCq[:, p, :], in_=pb[:, 0:T], func=AF.Sigmoid, bias=vc("v0", p), scale=1.0),
                             reads=[pk, "VC"], writes=[f"TCq{p}"])
                if l == 0:
                    S.op("pool", lambda e: e.tensor_copy(out=VF[:, :, :], in_=ZV), reads=allk("ZV"), writes=allk("VF"))
                else:
                    S.op("pool", lambda e: e.tensor_tensor(out=TDq, in0=VF[:, :, :], in1=ZV, op=ALU.subtract), reads=allk("VF") + allk("ZV"), writes=allk("TDq"))
                    S.op("dve", lambda e: e.tensor_tensor(out=TDq, in0=TDq, in1=TCq, op=ALU.mult), reads=allk("TDq") + allk("TCq"), writes=allk("TDq"))
                    S.op("dve", lambda e: e.tensor_tensor(out=ZV, in0=ZV, in1=TDq, op=ALU.add), reads=allk("ZV") + allk("TDq"), writes=allk("ZV"))
                for p in range(4):
                    S.op("act", lambda e, p=p: e.activation(out=TCq[:, p, :], in_=ZK[:, p, :], func=AF.Square, scale=vc(f"kk{l}", p)),
                         reads=[f"ZK{p}", "VC"], writes=[f"TCq{p}"])
                    bi, pb, pk = bank()
                    S.op("pe", lambda e, p=p, pb=pb: e.matmul(pb[:, 0:T], lhsT=OBD[:], rhs=TCq[:, p, :], start=True, stop=True), reads=[f"TCq{p}", "OBD"], writes=[pk])
                    S.op("act", lambda e, p=p, pb=pb: e.activation(out=TDq[:, p, :], in_=pb[:, 0:T], func=AF.Sqrt), reads=[pk], writes=[f"TDq{p}"])
                S.op("dve", lambda e: e.tensor_scalar(out=TDq, in0=TDq, scalar1=1e-12, scalar2=-1.0, op0=ALU.max, op1=ALU.mult), reads=allk("TDq"), writes=allk("TDq"))
                S.op("dve", lambda e: e.reciprocal(out=TDq, in_=TDq), reads=allk("TDq"), writes=allk("TDq"))
                for p in range(4):
                    S.op("dve", lambda e, p=p: e.scalar_tensor_tensor(out=KKN[:, p, :], in0=ZK[:, p, :], scalar=vc(f"kk{l}", p), in1=TDq[:, p, :],
                                                                      op0=ALU.mult, op1=ALU.mult), reads=[f"ZK{p}", f"TDq{p}", "VC"], writes=[f"KKN{p}"])
                    S.op("pool", lambda e, p=p: e.tensor_scalar(out=TCq[:, p, :], in0=AH[:, p, :], scalar1=vc(f"ka{l}", p), scalar2=DC[:, l, 48 + p:49 + p],
                                                               op0=ALU.mult, op1=ALU.add), reads=[f"AH{p}", "VC", "DC", f"TCq{p}"], writes=[f"TCq{p}"])
                S.op("dve", lambda e: e.tensor_tensor(out=ZK, in0=ZK, in1=TCq, op=ALU.mult), reads=allk("ZK") + allk("TCq"), writes=allk("ZK"))
                S.op("dve", lambda e: e.scalar_tensor_tensor(out=AH, in0=AH, scalar=-1.0, in1=KKN, op0=ALU.mult, op1=ALU.mult),
                     reads=allk("AH") + allk("KKN"), writes=allk("AH"))
                S.op("dve", lambda e: e.tensor_tensor_scan(out=f4T(LC), data0=RESET4, data1=f4T(SW), initial=0.0, op0=ALU.mult, op1=ALU.add),
                     reads=["RESET4"] + allk("SW"), writes=allk("LC"))
                S.op("pool", lambda e: e.tensor_tensor(out=SW, in0=LC, in1=SW, op=ALU.subtract), reads=allk("LC") + allk("SW"), writes=allk("SW"))
                S.op("act", lambda e: e.activation(out=SW, in_=SW, func=AF.Exp, scale=-CDEC), reads=allk("SW"), writes=allk("SW"))
                S.op("act", lambda e: e.activation(out=PINV, in_=LC, func=AF.Exp, scale=CDEC), reads=allk("LC"), writes=allk("PINV"))
                S.op("act", lambda e: e.activation(out=LC, in_=LC, func=AF.Exp, scale=-CDEC), reads=allk("LC"), writes=allk("LC"))
                S.op("pool", lambda e: e.tensor_copy(out=PC[:, :, 0:NCH], in_=LC.rearrange("p a (c j) -> p a c j", j=64)[:, :, :, 63]),
                     reads=allk("LC"), writes=["PC"])
                PEX, PIN, BBt, KH = SW, LC, AH, ZK
                allk = lambda nm: [f"{nm}{p}" for p in range(4)]
                same_nt = (REC_BF16 == (REC_BF16 and NEU_BF16))
                BDAn, BDBn = (BDA, BDB) if same_nt else (BDA2, BDB2)
                kBDAn, kBDBn = ("BDA", "BDB") if same_nt else ("BDA2", "BDB2")
                GMx = GM if (same_nt or MM2_F32) else GMr
                kGMx = "GM" if (same_nt or MM2_F32) else "GMr"

                def mm4(outb, lT, rT, ncol=128):
                    def fn(e):
                        for p in range(4):
                            ins = e.matmul(outb[:, p * ncol:(p + 1) * ncol], lhsT=lT[:, p, :], rhs=rT[:, p, :], start=True, stop=True)
                        return ins
                    return fn

                def mm4c(outb, lT, rconst, ncol):
                    def fn(e):
                        for p in range(4):
                            ins = e.matmul(outb[:, p * ncol:(p + 1) * ncol], lhsT=lT[:, p, :], rhs=rconst, start=True, stop=True)
                        return ins
                    return fn

                def v4(pb, n=128):
                    return pb[:, 0:4 * n].rearrange("p (a b) -> p a b", a=4)

                def bc4(M):
                    return M[:].unsqueeze(1).to_broadcast([128, 4, 128])
                STl = ST[:, l, :, :]
                S.op("act", lambda e: e.copy(out=STb, in_=STl), reads=["ST", "STb"], writes=["STb"])
                recP, recC = [], []
                for c in range(NCH):
                    cs = slice(c * 64, (c + 1) * 64)
                    si = c % 2
                    (BDA, BDB, BDK, BDR, BDV, MKA, MBR, MKR, GM, GMr, BDBT, BDKT, VT, BDA2, BDB2) = SETS[si]
                    BDAn, BDBn = (BDA, BDB) if same_nt else (BDA2, BDB2)
                    GMx = GM if (same_nt or MM2_F32) else GMr
                    S = RecK(si, SETKEYS); recP.append(S)
                    cur_pool[0] = "AP"
                    for hh in range(2):
                        rws = slice(64 * hh, 64 * hh + 64)
                        cls = slice(64 * hh, 64 * hh + 64)
                        S.op("dve", lambda e, rws=rws, cls=cls, cs=cs: e.tensor_tensor(out=BDR[rws, :, cls], in0=ZR[rws, :, cs], in1=PIN[rws, :, cs], op=ALU.mult),
                             reads=allk("ZR") + allk("LC"), writes=["BDR"])
                        S.op("pool", lambda e, rws=rws, cls=cls, cs=cs: e.tensor_tensor(out=BDK[rws, :, cls], in0=KH[rws, :, cs], in1=PINV[rws, :, cs], op=ALU.mult),
                             reads=allk("ZK") + allk("PINV"), writes=["BDK"])
                        S.op("dve", lambda e, rws=rws, cls=cls, cs=cs: e.tensor_tensor(out=BDB[rws, :, cls], in0=BBt[rws, :, cs], in1=PINV[rws, :, cs], op=ALU.mult),
                             reads=allk("AH") + allk("PINV"), writes=["BDB"])
                        S.op("pool", lambda e, rws=rws, cls=cls, cs=cs: e.tensor_tensor(out=BDA[rws, :, cls], in0=KKN[rws, :, cs], in1=PEX[rws, :, cs], op=ALU.mult),
                             reads=allk("KKN") + allk("SW"), writes=["BDA"])
                        S.op("pool", lambda e, rws=rws, cls=cls, cs=cs: e.tensor_copy(out=BDV[rws, :, cls], in_=ZV[rws, :, cs]),
                             reads=allk("ZV"), writes=["BDV"])
                        if not same_nt:
                            S.op("dve", lambda e, rws=rws, cls=cls, cs=cs: e.tensor_tensor(out=BDB2[rws, :, cls], in0=BBt[rws, :, cs], in1=PINV[rws, :, cs], op=ALU.mult),
                                 reads=allk("AH") + allk("PINV"), writes=["BDB2"])
                            S.op("pool", lambda e, rws=rws, cls=cls, cs=cs: e.tensor_tensor(out=BDA2[rws, :, cls], in0=KKN[rws, :, cs], in1=PEX[rws, :, cs], op=ALU.mult),
                                 reads=allk("KKN") + allk("SW"), writes=["BDA2"])
                    for (lT, lk, rT, rk_, dstM, dk, msk, mk_) in (
                            (BDBn, kBDBn, BDAn, kBDAn, MN[0], "MN0", MSU, "MSU"),
                            (BDAn, kBDAn, BDBn, kBDBn, MNT[0], "MNT0", MSL, "MSL"),
                            (BDK, "BDK", BDA, "BDA", MKA, "MKA", MSU, "MSU"),
                            (BDB, "BDB", BDR, "BDR", MBR, "MBR", MIU, "MIU"),
                            (BDK, "BDK", BDR, "BDR", MKR, "MKR", MIU, "MIU")):
                        bi, pb, pk = bank()
                        S.op("pe", mm4(pb, lT, rT), reads=[lk, rk_], writes=[pk])
                        S.op("dve", lambda e, pb=pb, dstM=dstM, msk=msk: e.tensor_tensor(out=dstM, in0=v4(pb), in1=bc4(msk), op=ALU.mult),
                             reads=[pk, mk_], writes=[dk])
                    S.op("pool", lambda e: e.tensor_tensor(out=GM, in0=MN[0], in1=bc4(IDENT), op=ALU.add), reads=["MN0", "IDENT"], writes=["GM"])
                    cur = 0
                    for jl in range(1, 6):
                        nxt = 1 - cur
                        if jl < 5:
                            bi, pb, pk = bank()
                            S.op("pe", mm4(pb, MNT[cur], MN[cur]), reads=[f"MN{cur}", f"MNT{cur}"], writes=[pk])
                            S.op("act", lambda e, pb=pb, nxt=nxt: e.copy(out=MN[nxt], in_=v4(pb)), reads=[pk], writes=[f"MN{nxt}"])
                        bi, pb, pk = bank()
                        S.op("pe", mm4(pb, MN[cur], MNT[cur]), reads=[f"MN{cur}", f"MNT{cur}"], writes=[pk])
                        S.op("act", lambda e, pb=pb, nxt=nxt: e.copy(out=MNT[nxt], in_=v4(pb)), reads=[pk], writes=[f"MNT{nxt}"])
                        bi, pb, pk = bank()
                        S.op("pe", mm4(pb, MNT[nxt], GM), reads=[f"MNT{nxt}", "GM"], writes=[pk])
                        S.op("dve", lambda e, pb=pb: e.tensor_tensor(out=GM, in0=v4(pb), in1=GM, op=ALU.add), reads=[pk, "GM"], writes=["GM"])
                        cur = nxt
                    if not same_nt:
                        S.op("act", lambda e: e.copy(out=GMr, in_=GM), reads=["GM"], writes=["GMr"])
                    for (srcT, sk_, dstT_, dk) in ((BDB, "BDB", BDBT, "BDBT"), (BDK, "BDK", BDKT, "BDKT")):
                        bi, pb, pk = bank()
                        S.op("pe", mm4c(pb, srcT, IDR[:], 128), reads=[sk_, "IDR"], writes=[pk])
                        S.op("act", lambda e, pb=pb, dstT_=dstT_: e.copy(out=dstT_, in_=v4(pb)), reads=[pk], writes=[dk])
                    bi, pb, pk = bank()
                    S.op("pe", mm4c(pb, BDV, I2R[:], 64), reads=["BDV", "I2R"], writes=[pk])
                    S.op("act", lambda e, pb=pb: e.copy(out=VT, in_=v4(pb, 64)), reads=[pk], writes=["VT"])
                    S = RecK(si, SETKEYS); recC.append(S)
                    cur_pool[0] = "AC"
                    bi, pb, pk = bank()

                    def f1(e, pb=pb):
                        for p in range(4):
                            e.matmul(pb[:, p * 64:(p + 1) * 64], lhsT=BDA[:, p, :], rhs=STb[:, p, :], start=True, stop=False)
                            ins = e.matmul(pb[:, p * 64:(p + 1) * 64], lhsT=MKA[:, p, :], rhs=VT[:, p, :], start=False, stop=True)
                        return ins
                    S.op("pe", f1, reads=["BDA", "STb", "MKA", "VT"], writes=[pk])
                    S.op("act", lambda e, pb=pb: e.copy(out=XTt, in_=v4(pb, 64)), reads=[pk], writes=["XT"])
                    bi, pb, pk = bank()
                    S.op("pe", mm4(pb, GMx, XTt, 64), reads=[kGMx, "XT"], writes=[pk])
                    S.op("dve", lambda e, pb=pb: e.tensor_copy(out=UT, in_=v4(pb, 64)), reads=[pk], writes=["UT"])
                    bi, pby, pky = bank()

                    def f3(e, pby=pby):
                        for p in range(4):
                            o = pby[:, p * 64:(p + 1) * 64]
                            e.matmul(o, lhsT=BDR[:, p, :], rhs=STb[:, p, :], start=True, stop=False)
                            e.matmul(o, lhsT=MBR[:, p, :], rhs=UT[:, p, :], start=False, stop=False)
                            ins = e.matmul(o, lhsT=MKR[:, p, :], rhs=VT[:, p, :], start=False, stop=True)
                        return ins
                    S.op("pe", f3, reads=["BDR", "STb", "MBR", "UT", "MKR", "VT"], writes=[pky])
                    bi, pbs, pks = bank()

                    def f4(e, pbs=pbs):
                        for p in range(4):
                            o = pbs[:, p * 64:(p + 1) * 64]
                            e.matmul(o, lhsT=BDBT[:, p, :], rhs=UT[:, p, :], start=True, stop=False)
                            ins = e.matmul(o, lhsT=BDKT[:, p, :], rhs=VT[:, p, :], start=False, stop=True)
                        return ins
                    S.op("pe", f4, reads=["BDBT", "UT", "BDKT", "VT"], writes=[pks])
                    S.op("dve", lambda e, pbs=pbs: e.tensor_tensor(out=TMPS, in0=v4(pbs, 64), in1=STl, op=ALU.add), reads=[pks, "ST"], writes=["TMPS"])
                    S.op("dve", lambda e, c=c: e.tensor_tensor(out=STl, in0=TMPS, in1=PC[:, :, c:c + 1].to_broadcast([128, 4, 64]), op=ALU.mult),
                         reads=["TMPS", "PC"], writes=["ST"])
                    S.op("act", lambda e: e.copy(out=STb, in_=STl), reads=["ST"], writes=["STb"])
                    S.op("act", lambda e, pby=pby: e.copy(out=YTBD[0:64, :, 0:64], in_=v4(pby, 64)[0:64]), reads=[pky, "YTBD"], writes=["YTBD"])
                    S.op("pool" if False else "dve", lambda e, pby=pby: e.tensor_copy(out=YTBD[64:128, :, 64:128], in_=v4(pby, 64)[64:128]), reads=[pky, "YTBD"], writes=["YTBD"])
                    bi, pb, pk = bank()
                    S.op("pe", mm4c(pb, YTBD, I2R[:], 64), reads=["YTBD", "I2R"], writes=[pk])
                    S.op("act", lambda e, pb=pb, cs=cs: e.copy(out=YF[:, :, cs], in_=v4(pb, 64)), reads=[pk], writes=allk("YF"))
                S = recA
                cur_pool[0] = "A"
                if os.environ.get("SEQ_CHUNKS"):
                    for c in range(NCH):
                        replay_merged(S, [recP[c]]); replay_merged(S, [recC[c]])
                else:
                    replay_merged(S, [recP[0]])
                    for c in range(NCH):
                        replay_merged(S, [recC[c]] + ([recP[c + 1]] if c + 1 < NCH else []))
                (BDA, BDB, BDK, BDR, BDV, MKA, MBR, MKR, GM, GMr, BDBT, BDKT, VT, BDA2, BDB2) = SETS[0]
                dump(f"yrec{l}", YF, allk("YF"), ti)
                def v2(pb):
                    return pb[:, 0:2 * T].rearrange("p (a b) -> p a b", a=2)
                hk = lambda nm, h: [f"{nm}{2 * h}", f"{nm}{2 * h + 1}"]
                MEANq, MSQq, VARq = SW, LC, PINV
                for p in range(4):
                    S.op("dve", lambda e, p=p: e.scalar_tensor_tensor(out=TCq[:, p, :], in0=ZR[:, p, :], scalar=vc(f"rk{l}", p), in1=KH[:, p, :],
                                                                      op0=ALU.mult, op1=ALU.mult), reads=[f"ZR{p}", f"ZK{p}", "VC", f"TCq{p}"], writes=[f"TCq{p}"])
                for h in range(2):
                    bi, pbb, pkb = bank()

                    def fb(e, h=h, pbb=pbb):
                        for a in range(2):
                            ins = e.matmul(pbb[:, a * T:(a + 1) * T], lhsT=OBD[:], rhs=TCq[:, 2 * h + a, :], start=True, stop=True)
                        return ins
                    S.op("pe", fb, reads=hk("TCq", h) + ["OBD"], writes=[pkb])
                    S.op("dve", lambda e, h=h, pbb=pbb: e.tensor_tensor(out=KKN[:, 2 * h:2 * h + 2, :], in0=v2(pbb), in1=ZV[:, 2 * h:2 * h + 2, :], op=ALU.mult),
                         reads=[pkb] + hk("ZV", h), writes=hk("KKN", h))
                S.op("act", lambda e: e.activation(out=TDq, in_=YF, func=AF.Square), reads=allk("YF"), writes=allk("TDq"))
                for h in range(2):
                    bi, pbm, pkm = bank()
                    bi, pbq, pkq = bank()

                    def fgm(e, h=h, pbm=pbm):
                        for a in range(2):
                            ins = e.matmul(pbm[:, a * T:(a + 1) * T], lhsT=OBD64[:], rhs=YF[:, 2 * h + a, :], start=True, stop=True)
                        return ins

                    def fgq(e, h=h, pbq=pbq):
                        for a in range(2):
                            ins = e.matmul(pbq[:, a * T:(a + 1) * T], lhsT=OBD64[:], rhs=TDq[:, 2 * h + a, :], start=True, stop=True)
                        return ins
                    S.op("pe", fgm, reads=hk("YF", h) + ["OBD64"], writes=[pkm])
                    S.op("pe", fgq, reads=hk("TDq", h) + ["OBD64"], writes=[pkq])
                    S.op("act", lambda e, h=h, pbm=pbm: e.copy(out=MEANq[:, 2 * h:2 * h + 2, :], in_=v2(pbm)), reads=[pkm], writes=hk("SW", h))
                    S.op("act", lambda e, h=h, pbm=pbm: e.activation(out=MSQq[:, 2 * h:2 * h + 2, :], in_=v2(pbm), func=AF.Square), reads=[pkm], writes=hk("LC", h))
                    S.op("dve", lambda e, h=h, pbq=pbq: e.tensor_tensor(out=VARq[:, 2 * h:2 * h + 2, :], in0=v2(pbq), in1=MSQq[:, 2 * h:2 * h + 2, :], op=ALU.subtract),
                         reads=[pkq] + hk("LC", h), writes=hk("PINV", h))
                S.op("dve", lambda e: e.tensor_scalar(out=VARq, in0=VARq, scalar1=0.0, scalar2=float(GN_EPS), op0=ALU.max, op1=ALU.add),
                     reads=allk("PINV"), writes=allk("PINV"))
                S.op("act", lambda e: e.activation(out=VARq, in_=VARq, func=AF.Sqrt), reads=allk("PINV"), writes=allk("PINV"))
                S.op("dve", lambda e: e.reciprocal(out=VARq, in_=VARq), reads=allk("PINV"), writes=allk("PINV"))
                S.op("dve", lambda e: e.tensor_tensor(out=YF, in0=YF, in1=MEANq, op=ALU.subtract), reads=allk("YF") + allk("SW"), writes=allk("YF"))
                S.op("pool", lambda e: e.tensor_tensor(out=YF, in0=YF, in1=VARq, op=ALU.mult), reads=allk("YF") + allk("PINV"), writes=allk("YF"))
                for p in range(4):
                    S.op("dve", lambda e, p=p: e.tensor_scalar(out=YF[:, p, :], in0=YF[:, p, :], scalar1=vc(f"gng{l}", p), scalar2=vc(f"gnb{l}", p),
                                                             op0=ALU.mult, op1=ALU.add), reads=[f"YF{p}", "VC"], writes=[f"YF{p}"])
                S.op("pool", lambda e: e.tensor_tensor(out=YF, in0=YF, in1=KKN, op=ALU.add), reads=allk("YF") + allk("KKN"), writes=allk("YF"))
                for h in range(2):
                    bi, pbg, pkg = bank()

                    def fgg(e, h=h, pbg=pbg):
                        for a in range(2):
                            p = 2 * h + a
                            ins = e.matmul(pbg[:, a * T:(a + 1) * T], lhsT=G2[:, l, p * 128:(p + 1) * 128], rhs=LG, start=True, stop=True)
                        return ins
                    S.op("pe", fgg, reads=["LG", "G2"], writes=[pkg])
                    S.op("dve", lambda e, h=h, pbg=pbg: e.tensor_tensor(out=YBR[:, 0, 2 * h:2 * h + 2, :], in0=v2(pbg), in1=YF[:, 2 * h:2 * h + 2, :], op=ALU.mult),
                         reads=[pkg] + hk("YF", h), writes=[f"YBR0_{2 * h}", f"YBR0_{2 * h + 1}"])
                S = S_main
                cur_pool[0] = "ALL"; LT.update(LT1)
                replay_merged(S, [recA, recB])
                if KEEP_FENCES:
                    S.fence()
                for q in range(2):
                    for b in range(3):
                        sg = wload(l, 7 + b * 2 + q)
                        so = wload(l, 13 + b * 2 + q)
                        wg = wview(sg, 8, 512)
                        wo = wview(so, 4, 512)
                        for mi in range(4):
                            bi, pbg, pkg = bank()

                            def fg(e, pbg=pbg, wg=wg, mi=mi):
                                for kc in range(8):
                                    ins = e.matmul(pbg[:, 0:T], lhsT=wg[:, kc, mi * 128:(mi + 1) * 128], rhs=H[:, kc, :], start=(kc == 0), stop=(kc == 7))
                                return ins
                            S.op("pe", fg, reads=[f"WS{sg}"] + [f"H{m}" for m in range(8)], writes=[pkg])
                            bi, pbo, pko = bank()

                            def fo(e, pbo=pbo, wo=wo, mi=mi, b=b):
                                for kc in range(4):
                                    ins = e.matmul(pbo[:, 0:T], lhsT=wo[:, kc, mi * 128:(mi + 1) * 128], rhs=YBR[:, b, kc, :], start=(kc == 0), stop=(kc == 3))
                                return ins
                            S.op("pe", fo, reads=[f"WS{so}"] + [f"YBR{b}_{k_}" for k_ in range(4)], writes=[pko])
                            S.op("act", lambda e, pbg=pbg: e.activation(out=SIG, in_=pbg[:, 0:T], func=AF.Sigmoid), reads=[pkg], writes=["RAW0"])
                            if b == 0:
                                S.op("dve", lambda e, pbo=pbo, mi=mi: e.tensor_tensor(out=MERG[:, mi, :], in0=pbo[:, 0:T], in1=SIG, op=ALU.mult),
                                     reads=[pko, "RAW0"], writes=[f"MERG{mi}"])
                            else:
                                S.op("dve", lambda e, pbo=pbo: e.tensor_tensor(out=TMG, in0=pbo[:, 0:T], in1=SIG, op=ALU.mult),
                                     reads=[pko, "RAW0"], writes=["RAW1"])
                                if b == 1:
                                    S.op("pool", lambda e, mi=mi: e.tensor_tensor(out=MERG[:, mi, :], in0=MERG[:, mi, :], in1=TMG, op=ALU.add),
                                         reads=[f"MERG{mi}", "RAW1"], writes=[f"MERG{mi}"])
                                else:
                                    S.op("pool", lambda e, mi=mi, q=q: e.tensor_tensor(out=MERGED[:, q * 4 + mi, :], in0=MERG[:, mi, :], in1=TMG, op=ALU.add),
                                         reads=[f"MERG{mi}", "RAW1"], writes=[f"MERGED{q * 4 + mi}"])
                if KEEP_FENCES:
                    S.fence()
                sA = wload(l, 19)
                sB = wload(l, 20)
                psl = {}
                for m in range(8):
                    s = sA if m < 4 else sB
                    wv = wview(s, 8, 512)
                    bi, pb, pk = bank()

                    def fw(e, pb=pb, wv=wv, m=m):
                        for kc in range(8):
                            ins = e.matmul(pb[:, 0:T], lhsT=wv[:, kc, (m % 4) * 128:(m % 4 + 1) * 128], rhs=MERGED[:, kc, :], start=(kc == 0), stop=(kc == 7))
                        return ins
                    S.op("pe", fw, reads=[f"WS{s}"] + [f"MERGED{k_}" for k_ in range(8)], writes=[pk])
                    S.op("dve", lambda e, m=m, pb=pb: e.scalar_tensor_tensor(out=U[:, m, :], in0=pb[:, 0:T], scalar=DC[:, l, 16 + m:17 + m],
                                                                         in1=X[:, m, :], op0=ALU.mult, op1=ALU.add),
                         reads=[pk, "DC", f"X{m}"], writes=[f"U{m}"])

                def ln_apply(gname, bname, nxt):
                    uk = [f"U{m}" for m in range(8)]
                    qk = allk("TCq") + allk("TDq")
                    tA, tB, tD = LT["TA"], LT["TB"], LT["TD"]
                    S.op("act", lambda e: e.activation(out=USQ, in_=U, func=AF.Square), reads=uk, writes=qk)
                    bi1, pb1, pk1 = bank()
                    bi2, pb2, pk2 = bank()

                    def fm1(e):
                        for m in range(8):
                            ins = e.matmul(pb1[:, 0:T], lhsT=O1024[:], rhs=U[:, m, :], start=(m == 0), stop=(m == 7))
                        return ins

                    def fm2(e):
                        for m in range(8):
                            ins = e.matmul(pb2[:, 0:T], lhsT=O1024[:], rhs=USQ[:, m, :], start=(m == 0), stop=(m == 7))
                        return ins
                    S.op("pe", fm1, reads=uk + ["O1024"], writes=[pk1])
                    S.op("pe", fm2, reads=qk + ["O1024"], writes=[pk2])
                    S.op("act", lambda e: e.copy(out=tA, in_=pb1[:, 0:T]), reads=[pk1], writes=["TA"])
                    S.op("act", lambda e: e.activation(out=tD, in_=pb1[:, 0:T], func=AF.Square), reads=[pk1], writes=["TD"])
                    S.op("dve", lambda e: e.tensor_tensor(out=tB, in0=pb2[:, 0:T], in1=tD, op=ALU.subtract), reads=[pk2, "TD"], writes=["TB"])
                    S.op("dve", lambda e: e.tensor_scalar(out=tB, in0=tB, scalar1=0.0, scalar2=None, op0=ALU.max), reads=["TB"], writes=["TB"])
                    S.op("act", lambda e: e.activation(out=tB, in_=tB, func=AF.Sqrt, bias=EPS[:, 1:2], scale=1.0), reads=["TB", "EPS"], writes=["TB"])
                    S.op("dve", lambda e: e.reciprocal(out=tB, in_=tB), reads=["TB"], writes=["TB"])
                    S.op("dve", lambda e: e.tensor_tensor(out=U, in0=U, in1=tA.unsqueeze(1).to_broadcast([128, 8, T]), op=ALU.subtract),
                         reads=uk + ["TA"], writes=uk)
                    S.op("pool", lambda e: e.tensor_tensor(out=U, in0=U, in1=tB.unsqueeze(1).to_broadcast([128, 8, T]), op=ALU.mult),
                         reads=uk + ["TB"], writes=uk)
                    for m in range(8):
                        S.op("dve", lambda e, m=m: e.tensor_scalar(out=X[:, m, :], in0=U[:, m, :], scalar1=vc(gname, m), scalar2=vc(bname, m),
                                                                 op0=ALU.mult, op1=ALU.add), reads=[f"U{m}", "VC"], writes=[f"X{m}"])
                        if nxt is not None:
                            ll, g0, b0 = nxt
                            S.op("act", lambda e, m=m, ll=ll, g0=g0, b0=b0: e.activation(out=H[:, m, :], in_=U[:, m, :], func=AF.Identity,
                                                                                       bias=DC[:, ll, b0 + m:b0 + m + 1], scale=DC[:, ll, g0 + m:g0 + m + 1]),
                                 reads=[f"U{m}", "DC"], writes=[f"H{m}"])
                ln_apply(f"lnmg{l}", f"lnmb{l}", (l, 64, 72))
                dump(f"xm{l}", X[:], [f"X{m}" for m in range(8)], ti)
                if KEEP_FENCES:
                    S.fence()
                for jg in range(8):
                    s = wload(l, 21 + jg)
                    wv = wview(s, 8, 512)
                    for ji in range(4):
                        j = jg * 4 + ji
                        bi, pb, pk = bank()

                        def f1m(e, pb=pb, wv=wv, ji=ji):
                            for kc in range(8):
                                ins = e.matmul(pb[:, 0:T], lhsT=wv[:, kc, ji * 128:(ji + 1) * 128], rhs=H[:, kc, :], start=(kc == 0), stop=(kc == 7))
                            return ins
                        S.op("pe", f1m, reads=[f"WS{s}"] + [f"H{m}" for m in range(8)], writes=[pk])
                        rt, rk_ = (RTMP, "RTMP") if j % 2 == 0 else (RTMP2, "RTMP2")
                        S.op("act", lambda e, pb=pb, rt=rt: e.activation(out=rt, in_=pb[:, 0:T], func=AF.Relu), reads=[pk], writes=[rk_])
                        S.op("dve" if j % 2 == 0 else "pool", lambda e, j=j, rt=rt: e.tensor_tensor(out=H1[:, j, :], in0=rt, in1=rt, op=ALU.mult),
                             reads=[rk_], writes=[f"H1_{j}"])
                for m in range(8):
                    s = wload(l, 29 + m)
                    wv = wview(s, 32, 128)
                    bi, pb, pk = bank()

                    def f2m(e, pb=pb, wv=wv):
                        for kc in range(32):
                            ins = e.matmul(pb[:, 0:T], lhsT=wv[:, kc, :], rhs=H1[:, kc, :], start=(kc == 0), stop=(kc == 31))
                        return ins
                    S.op("pe", f2m, reads=[f"WS{s}"] + [f"H1_{k_}" for k_ in range(32)], writes=[pk])
                    S.op("dve", lambda e, m=m, pb=pb: e.scalar_tensor_tensor(out=U[:, m, :], in0=pb[:, 0:T], scalar=DC[:, l, 24 + m:25 + m],
                                                                         in1=X[:, m, :], op0=ALU.mult, op1=ALU.add),
                         reads=[pk, "DC", f"X{m}"], writes=[f"U{m}"])
                ln_apply(f"lnfg{l}", f"lnfb{l}", (l + 1, 80, 88) if l + 1 < NL else None)
                dump(f"xf{l}", X[:], [f"X{m}" for m in range(8)], ti)
                if KEEP_FENCES:
                    S.fence()
            for tb in range(TB):
                for half in range(2):
                    bi, pb, pk = bank()

                    def fo_(e, pb=pb, tb=tb, half=half):
                        for f4_ in range(4):
                            fc = half * 4 + f4_
                            ins = e.transpose(pb[:, f4_ * 128:(f4_ + 1) * 128], X[:, fc, tb * 128:(tb + 1) * 128], IDENT[:])
                        return ins
                    S.op("pe", fo_, reads=[f"X{m}" for m in range(8)] + ["IDENT"], writes=[pk])
                    S.op("act" if half else "dve",
                         (lambda e, pb=pb, tb=tb, half=half: e.copy(out=XIN[:, tb, half * 512:(half + 1) * 512], in_=pb[:, :])) if half else
                         (lambda e, pb=pb, tb=tb, half=half: e.tensor_copy(out=XIN[:, tb, half * 512:(half + 1) * 512], in_=pb[:, :])),
                         reads=[pk], writes=["XIN"])
            S.dma("sp", out[t0:t0 + T, :].rearrange("(b p) d -> p b d", p=128), XIN[:], reads=["XIN"])
        S.wait_all("sp")
        S.emit(block)
        build_nc.last_stats = {"nops": S.nops, "cnt": dict(S.cnt)}
    return nc


def kernel(**inputs):
    B, S_TOK, _ = inputs["x"].shape
    nc = build_nc(S_TOK, T=256, NL=2)
    in_maps = []
    for b in range(B):
        m = {"x": np.ascontiguousarray(inputs["x"][b], dtype=np.float32), "vecs": pack_vecs(inputs, b)}
        for nm in WEIGHT_NAMES:
            m[nm] = np.ascontiguousarray(inputs[nm], dtype=np.float32)
        in_maps.append(m)
    res = run_bass_kernel_spmd(nc, in_maps, core_ids=list(range(B)))
    return np.stack([np.asarray(r["out"]).reshape(S_TOK, D) for r in res.results], axis=0).astype(np.float32)
```
